# Optimizing a Trainium2 kernel written in Bass

```python
import math
import jax, jax.numpy as jnp
from jax import lax
import numpy as np

D_MODEL = 1024
BATCH = 4
SEQ = 8192
DEPTH = 4

CHUNK = 64
Q_BLOCK = 128
N_MIXERS = 3
N_HEADS = 16
HEAD_DIM = D_MODEL // N_HEADS
INNER = N_HEADS * HEAD_DIM
IDX_HEADS = 8
IDX_DIM = 64
TOPK_MAX = 256
DIFF_HEADS = N_HEADS // 2
DIFF_DIM = HEAD_DIM
ROPE_THETA = 10000.0
LN_EPS = 1e-5
RMS_EPS = 1e-5
ALPHA = (2.0 * DEPTH) ** 0.25
BETA = (8.0 * DEPTH) ** -0.25
A_IN = 4 * INNER + IDX_HEADS * IDX_DIM + IDX_HEADS + IDX_DIM
B_IN = 4 * INNER
C_IN = 4 * INNER

kernel_name = "hybrid_dsa_stickbreak_diffattn_deepnorm"


def _n_layers_of(m):
    return len(range(m, DEPTH, N_MIXERS))


def rope_tables(seq, dim):
    inv = 1.0 / (ROPE_THETA ** (jnp.arange(0, dim, 2, dtype=jnp.float32) / dim))
    ang = jnp.arange(seq, dtype=jnp.float32)[:, None] * inv[None, :]
    return jnp.cos(ang), jnp.sin(ang)


def apply_rope(t, cos, sin):
    c = cos[None, :, None, :].astype(t.dtype)
    s = sin[None, :, None, :].astype(t.dtype)
    t1, t2 = jnp.split(t, 2, axis=-1)
    return jnp.concatenate([t1 * c - t2 * s, t2 * c + t1 * s], axis=-1)


def to_blocks(a):
    b, s = a.shape[0], a.shape[1]
    return jnp.moveaxis(a.reshape(b, s // Q_BLOCK, Q_BLOCK, *a.shape[2:]), 1, 0)


def from_blocks(o):
    o = jnp.moveaxis(o, 0, 1)
    return o.reshape(o.shape[0], o.shape[1] * o.shape[2], *o.shape[3:])


def layer_norm(x, g, b):
    xf = x.astype(jnp.float32)
    mu = jnp.mean(xf, axis=-1, keepdims=True)
    var = jnp.mean(jnp.square(xf - mu), axis=-1, keepdims=True)
    return ((xf - mu) * lax.rsqrt(var + LN_EPS) * g + b).astype(x.dtype)


def rms_norm(x, g):
    xf = x.astype(jnp.float32)
    return (xf * lax.rsqrt(jnp.mean(xf * xf, axis=-1, keepdims=True) + RMS_EPS) * g).astype(x.dtype)


def dsa_mixer(x, w_in, w_out, cos, sin):
    b, s, _ = x.shape
    k_top = min(TOPK_MAX, s // 4)
    cuts = [INNER, 2 * INNER, 3 * INNER, 4 * INNER,
            4 * INNER + IDX_HEADS * IDX_DIM, 4 * INNER + IDX_HEADS * IDX_DIM + IDX_HEADS]
    q, k, v, g, qi, wi, ki = jnp.split(x @ w_in, cuts, axis=-1)
    q = apply_rope(q.reshape(b, s, N_HEADS, HEAD_DIM), cos, sin)
    k = apply_rope(k.reshape(b, s, N_HEADS, HEAD_DIM), cos, sin)
    v = v.reshape(b, s, N_HEADS, HEAD_DIM)
    qi = apply_rope(qi.reshape(b, s, IDX_HEADS, IDX_DIM), cos, sin)
    ki = apply_rope(ki[:, :, None, :], cos, sin)[:, :, 0, :]
    wi = wi.astype(jnp.float32) * IDX_HEADS ** -0.5
    pos = jnp.arange(s)
    key_chunk = pos // CHUNK

    def block(args):
        qb, qib, wib, pb = args
        q_chunk = pb // CHUNK
        rel = jax.nn.relu(jnp.einsum('bqhd,bsd->bqhs', qib, ki).astype(jnp.float32) * IDX_DIM ** -0.5)
        score = jnp.einsum('bqhs,bqh->bqs', rel, wib)
        adm = key_chunk[None, :] <= q_chunk[:, None]
        score = jnp.where(adm[None], score, -jnp.inf)
        _, idx = lax.top_k(score, k_top)
        ks = jax.vmap(lambda kk, ii: kk[ii])(k, idx)
        vs = jax.vmap(lambda vv, ii: vv[ii])(v, idx)
        logits = jnp.einsum('bqhd,bqkhd->bhqk', qb, ks).astype(jnp.float32) * HEAD_DIM ** -0.5
        sel_ok = (idx // CHUNK) <= q_chunk[None, :, None]
        logits = jnp.where(sel_ok[:, None], logits, -jnp.inf)
        p = jax.nn.softmax(logits, axis=-1).astype(vs.dtype)
        return jnp.einsum('bhqk,bqkhd->bqhd', p, vs)

    o = lax.map(block, (to_blocks(q), to_blocks(qi), to_blocks(wi), pos.reshape(-1, Q_BLOCK)))
    o = from_blocks(o).reshape(b, s, INNER)
    return (o * jax.nn.silu(g)) @ w_out


def stick_breaking_mixer(x, w_in, w_out):
    b, s, _ = x.shape
    q, k, v, g = jnp.split(x @ w_in, 4, axis=-1)
    q = q.reshape(b, s, N_HEADS, HEAD_DIM)
    k = k.reshape(b, s, N_HEADS, HEAD_DIM)
    v = v.reshape(b, s, N_HEADS, HEAD_DIM)
    pos = jnp.arange(s)

    def block(args):
        qb, pb = args
        z = jnp.einsum('bqhd,bshd->bhqs', qb, k).astype(jnp.float32) * HEAD_DIM ** -0.5
        before = pos[None, :] < pb[:, None]
        log_one_minus = jnp.where(before, jax.nn.log_sigmoid(-z), 0.0)
        log_stick = lax.cumsum(log_one_minus, axis=3, reverse=True) - log_one_minus
        a = jnp.where(before, jnp.exp(jax.nn.log_sigmoid(z) + log_stick), 0.0).astype(v.dtype)
        return jnp.einsum('bhqs,bshd->bqhd', a, v)

    o = lax.map(block, (to_blocks(q), pos.reshape(-1, Q_BLOCK)))
    o = from_blocks(o).reshape(b, s, INNER)
    return (o * jax.nn.silu(g)) @ w_out


def diff_mixer(x, w_in, w_out, lq1, lk1, lq2, lk2, sub_g, lambda_init, cos, sin):
    b, s, _ = x.shape
    q, k, v, g = jnp.split(x @ w_in, 4, axis=-1)
    q = apply_rope(q.reshape(b, s, 2 * DIFF_HEADS, DIFF_DIM), cos, sin).reshape(b, s, DIFF_HEADS, 2, DIFF_DIM)
    k = apply_rope(k.reshape(b, s, 2 * DIFF_HEADS, DIFF_DIM), cos, sin).reshape(b, s, DIFF_HEADS, 2, DIFF_DIM)
    v = v.reshape(b, s, DIFF_HEADS, 2 * DIFF_DIM)
    lam = (jnp.exp(jnp.sum(lq1.astype(jnp.float32) * lk1.astype(jnp.float32)))
           - jnp.exp(jnp.sum(lq2.astype(jnp.float32) * lk2.astype(jnp.float32))) + lambda_init)
    pos = jnp.arange(s)
    key_chunk = pos // CHUNK

    def block(args):
        qb, pb = args
        logits = jnp.einsum('bqhmd,bshmd->bhmqs', qb, k).astype(jnp.float32) * DIFF_DIM ** -0.5
        adm = key_chunk[None, :] <= (pb // CHUNK)[:, None]
        p = jax.nn.softmax(jnp.where(adm, logits, -jnp.inf), axis=-1)
        wdiff = (p[:, :, 0] - lam * p[:, :, 1]).astype(v.dtype)
        return jnp.einsum('bhqs,bshe->bqhe', wdiff, v)

    o = from_blocks(lax.map(block, (to_blocks(q), pos.reshape(-1, Q_BLOCK))))
    o = rms_norm(o, sub_g) * (1.0 - lambda_init)
    o = o.reshape(b, s, INNER)
    return (o * jax.nn.silu(g)) @ w_out


def setup_inputs(seed: int = 0) -> dict:
    key = jax.random.key(seed)
    ks = jax.random.split(key, 14)
    na, nb_, nc = _n_layers_of(0), _n_layers_of(1), _n_layers_of(2)

    def in_proj(k, n, width):
        col = jnp.ones((width,), jnp.float32).at[2 * INNER:3 * INNER].set(BETA)
        return jax.random.normal(k, (n, D_MODEL, width), jnp.float32) * D_MODEL ** -0.5 * col

    def out_proj(k, n):
        return jax.random.normal(k, (n, INNER, D_MODEL), jnp.float32) * INNER ** -0.5 * BETA

    return {
        "x": jax.random.normal(ks[0], (BATCH, SEQ, D_MODEL), jnp.float32),
        "w_in_a": in_proj(ks[1], na, A_IN),
        "w_out_a": out_proj(ks[2], na),
        "w_in_b": in_proj(ks[3], nb_, B_IN),
        "w_out_b": out_proj(ks[4], nb_),
        "w_in_c": in_proj(ks[5], nc, C_IN),
        "w_out_c": out_proj(ks[6], nc),
        "lambda_q1": 0.1 * jax.random.normal(ks[7], (nc, DIFF_DIM), jnp.float32),
        "lambda_k1": 0.1 * jax.random.normal(ks[8], (nc, DIFF_DIM), jnp.float32),
        "lambda_q2": 0.1 * jax.random.normal(ks[9], (nc, DIFF_DIM), jnp.float32),
        "lambda_k2": 0.1 * jax.random.normal(ks[10], (nc, DIFF_DIM), jnp.float32),
        "subln_g": 1.0 + 0.02 * jax.random.normal(ks[11], (nc, 2 * DIFF_DIM), jnp.float32),
        "ln_g": 1.0 + 0.02 * jax.random.normal(ks[12], (DEPTH, D_MODEL), jnp.float32),
        "ln_b": 0.02 * jax.random.normal(ks[13], (DEPTH, D_MODEL), jnp.float32),
    }


def reference(x, w_in_a, w_out_a, w_in_b, w_out_b, w_in_c, w_out_c,
              lambda_q1, lambda_k1, lambda_q2, lambda_k2, subln_g, ln_g, ln_b):
    s = x.shape[1]
    cos_a, sin_a = rope_tables(s, HEAD_DIM)
    cos_i, sin_i = rope_tables(s, IDX_DIM)
    cos_c, sin_c = rope_tables(s, DIFF_DIM)
    for i in range(DEPTH):
        m, j = i % N_MIXERS, i // N_MIXERS
        if m == 0:
            y = dsa_mixer(x, w_in_a[j], w_out_a[j], cos_a, sin_a) if HEAD_DIM == IDX_DIM else None
        elif m == 1:
            y = stick_breaking_mixer(x, w_in_b[j], w_out_b[j])
        else:
            lambda_init = 0.8 - 0.6 * math.exp(-0.3 * i)
            y = diff_mixer(x, w_in_c[j], w_out_c[j], lambda_q1[j], lambda_k1[j],
                           lambda_q2[j], lambda_k2[j], subln_g[j], lambda_init, cos_c, sin_c)
        x = layer_norm(ALPHA * x + y, ln_g[i], ln_b[i])
    return x
```

```python
import math
from contextlib import ExitStack
import numpy as np
import concourse.bass as bass
import concourse.mybir as mybir
from concourse.bass_utils import run_bass_kernel_spmd

F32 = mybir.dt.float32
BF16 = mybir.dt.bfloat16
I32 = mybir.dt.int32
AF = mybir.ActivationFunctionType
ALU = mybir.AluOpType

D = 1024
NH = 16
HD = 64
DEPTH = 4
ALPHA = (2.0 * DEPTH) ** 0.25
LN_EPS = 1e-5
RMS_EPS = 1e-5
A_IN = 4 * D + 8 * 64 + 8 + 64
TOPK = 256
SB_WIN = 3
BIS_ITERS = 24
BIS_R = 32.0


class Buf:
    __slots__ = ("w", "r")

    def __init__(self):
        self.w = None
        self.r = {}


class Q:
    def __init__(self, fw, eng, name, is_dma, nsems, same_engine_waits=True):
        self.fw = fw
        self.eng = eng
        self.is_dma = is_dma
        self.inc = 16 if is_dma else 1
        self.sems = []
        for i in range(nsems):
            h = fw.nc.alloc_semaphore(name=f"s_{name}{i}")
            self.sems.append(len(fw.semh))
            fw.semh.append(h)
        self.cnt = [0] * nsems
        self.rr = 0
        self.waited = {}
        self.sew = same_engine_waits
        self.pending = False

    def _wait(self, sem, val):
        if self.waited.get(sem, 0) < val:
            self.eng.wait_ge(self.fw.semh[sem], val)
            self.waited[sem] = val

    def op(self, fn, reads=(), writes=(), inc=True):
        deps = {}
        for b in reads:
            if b.w is not None:
                s, v = b.w
                if deps.get(s, 0) < v:
                    deps[s] = v
        for b in writes:
            if b.w is not None:
                s, v = b.w
                if deps.get(s, 0) < v:
                    deps[s] = v
            for s, v in b.r.items():
                if deps.get(s, 0) < v:
                    deps[s] = v
        i = self.rr
        if self.is_dma:
            self.rr = (self.rr + 1) % len(self.sems)
            if self.cnt[i] > 0:
                deps[self.sems[i]] = max(deps.get(self.sems[i], 0), self.cnt[i])
        for s, v in deps.items():
            if (not self.sew) and s == self.sems[0]:
                continue
            self._wait(s, v)
        ins = fn()
        if inc:
            self.cnt[i] += self.inc
            ins.then_inc(self.fw.semh[self.sems[i]], self.inc)
            tag = (self.sems[i], self.cnt[i])
            self.pending = False
        else:
            tag = (self.sems[i], self.cnt[i] + self.inc)
            self.pending = True
        for b in reads:
            if b.r.get(tag[0], 0) < tag[1]:
                b.r[tag[0]] = tag[1]
        for b in writes:
            b.w = tag
            b.r = {}
        return tag

    def wait_all_of(self, other):
        for i, s in enumerate(other.sems):
            if other.cnt[i] > 0:
                self._wait(s, other.cnt[i])


class FW:
    def __init__(self, nc):
        self.nc = nc
        self.semh = []
        self.pe = Q(self, nc.tensor, "pe", False, 1, same_engine_waits=False)
        self.act = Q(self, nc.scalar, "act", False, 1)
        self.dve = Q(self, nc.vector, "dve", False, 1)
        self.pool = Q(self, nc.gpsimd, "pool", False, 1)
        self.ld = Q(self, nc.sync, "ld", True, 8)
        self.st = Q(self, nc.gpsimd, "st", True, 8)
        self.qs = [self.pe, self.act, self.dve, self.pool, self.ld, self.st]

    def barrier(self):
        for q in self.qs:
            assert not q.pending
        for q in self.qs:
            for o in self.qs:
                if o is not q:
                    q.wait_all_of(o)


def pipeline(items, nstages):
    n = len(items)
    for step in range(n + nstages - 1):
        for s in range(nstages - 1, -1, -1):
            i = step - s
            if 0 <= i < n and items[i][s] is not None:
                items[i][s]()


class Prog:
    def __init__(self, S, layers, debug=False):
        self.S = S
        IK = "ExternalOutput" if debug else "Internal"
        self.NB = S // 128
        self.NT = S // 512
        self.layers = layers
        nc = self.nc = bass.Bass("TRN2", target_bir_lowering=False)
        self.fw = FW(nc)
        S_ = S
        dt = nc.dram_tensor
        self.x_in = dt("x", [S_, D], F32, kind="ExternalInput").ap()
        self.out = dt("out", [S_, D], F32, kind="ExternalOutput").ap()
        self.w_in = {}
        self.w_out = {}
        for li, (m, j, L) in enumerate(layers):
            width = A_IN if m == 0 else 4 * D
            self.w_in[li] = dt(f"w_in{li}", [D, width], F32, kind="ExternalInput").ap()
            self.w_out[li] = dt(f"w_out{li}", [D, D], F32, kind="ExternalInput").ap()
        self.lng = dt("ln_g", [len(layers), D], F32, kind="ExternalInput").ap()
        self.lnb = dt("ln_b", [len(layers), D], F32, kind="ExternalInput").ap()
        self.lam = dt("lam", [1, 256], F32, kind="ExternalInput").ap()
        self.subg = dt("subg", [128, 1], F32, kind="ExternalInput").ap()
        self.ropec = dt("ropec", [128, 2], F32, kind="ExternalInput").ap()
        self.ident_in = dt("ident", [128, 128], F32, kind="ExternalInput").ap()
        self.xres = [dt(f"xres{i}", [S_, D], F32, kind="Internal").ap() for i in range(2)]
        self.xT = dt("xT", [D, S_], BF16, kind=IK).ap()
        self.QT = dt("QT", [24 * 64, S_], BF16, kind=IK).ap()
        self.KT = dt("KT", [17 * 64, S_], BF16, kind=IK).ap()
        self.GT = dt("GT", [D, S_], BF16, kind=IK).ap()
        self.OT = dt("OT", [D, S_], BF16, kind=IK).ap()
        self.Vs = dt("Vs", [NH, 128, self.NB, 64], BF16, kind=IK).ap()
        self.WI = dt("WI", [S_, 8], F32, kind=IK).ap()
        self.CC = dt("CCt", [128, S_], F32, kind=IK).ap()
        self.SS = dt("SSt", [128, S_], F32, kind=IK).ap()
        self.debug = debug
        if debug:
            self.dbg = [dt(f"dbg{i}", [128, 512], F32, kind="ExternalOutput").ap() for i in range(6)]
        self.b_xres = [Buf(), Buf()]
        self.b_xT, self.b_QT, self.b_KT, self.b_GT, self.b_OT, self.b_Vs, self.b_WI, self.b_tab = (Buf() for _ in range(8))
        self.psall = nc.alloc_psum_tensor("psall", [128, 4096], F32)
        self.bps = [Buf() for _ in range(8)]
        self.build()

    def dump(self, i, src, bsrc):
        if self.debug:
            self.fw.st.op(lambda: self.nc.gpsimd.dma_start(out=self.dbg[i][0:src.shape[0], 0:src.shape[1]], in_=src), reads=[bsrc])

    def sbf(self, es):
        def f(n, sh, d):
            self._uid = getattr(self, "_uid", 0) + 1
            return es.enter_context(self.nc.sbuf_tensor(f"{n}_u{self._uid}", sh, d))
        return f

    def ps(self, i, parts=128, n=512):
        return self.psall[0:parts, i * 512:i * 512 + n]

    def build(self):
        nc, fw = self.nc, self.fw
        with ExitStack() as es:
            self.ident = self.sbf(es)("identsb", [128, 128], F32)
            self.b_ident = Buf()
            fw.ld.op(lambda: nc.sync.dma_start(out=self.ident[:], in_=self.ident_in[:, :]), writes=[self.b_ident])
            self.cst = self.sbf(es)("cst", [128, 4], F32)
            self.b_cst = Buf()
            fw.pool.op(lambda: nc.gpsimd.memset(self.cst[:, 0:1], float(LN_EPS)), writes=[self.b_cst])
            self.rope_tables()
            self.transpose_input()
            cur = 0
            for li, (m, j, L) in enumerate(self.layers):
                last = li == len(self.layers) - 1
                self.phase_a(li, m)
                if m == 0:
                    self.attn_dsa()
                elif m == 1:
                    self.attn_sb()
                else:
                    self.attn_diff(L)
                self.phase_c(li, cur, last)
                cur ^= 1
            fw.barrier()

    def rope_tables(self):
        nc, fw, S = self.nc, self.fw, self.S
        with ExitStack() as es:
            sb = self.sbf(es)
            rc = sb("rc", [128, 2], F32); brc = Buf()
            halfpi = sb("halfpi", [128, 1], F32)
            fw.ld.op(lambda: nc.sync.dma_start(out=rc[:], in_=self.ropec[:, :]), writes=[brc])
            C1 = 6.28125
            C2 = 2 * math.pi - C1
            pools = {}
            for nm, dtp in [("it", F32), ("ang", F32), ("a2", F32), ("ki", I32), ("kf", F32), ("r0", F32), ("r1", F32)]:
                pools[nm] = [(sb(f"rt_{nm}{q}", [128, 512], dtp), Buf()) for q in range(2)]
            for t0 in range(0, S, 512):
                sl = (t0 // 512) % 2
                it, b_it = pools["it"][sl]
                ang, b_ang = pools["ang"][sl]
                for which in range(2):
                    a2, b_a2 = pools["a2"][sl]
                    ki, b_ki = pools["ki"][sl]
                    kf, b_kf = pools["kf"][sl]
                    r, b_r = pools["r0" if which == 0 else "r1"][sl]
                    if which == 0:
                        fw.pool.op(lambda: nc.gpsimd.iota(it[:], pattern=[[1, 512]], base=t0, channel_multiplier=0,
                                                          allow_small_or_imprecise_dtypes=True), writes=[b_it])
                        fw.dve.op(lambda: nc.vector.tensor_scalar(out=ang[:], in0=it[:], scalar1=rc[:, 0:1], scalar2=None,
                                                                  op0=ALU.mult), reads=[b_it, brc], writes=[b_ang])
                        src = ang
                    else:
                        fw.dve.op(lambda: nc.vector.tensor_scalar(out=a2[:], in0=ang[:], scalar1=float(math.pi / 2), scalar2=None,
                                                                  op0=ALU.add), reads=[b_ang], writes=[b_a2])
                        src = a2
                    bsrc = b_ang if which == 0 else b_a2
                    fw.dve.op(lambda: nc.vector.tensor_scalar(out=ki[:], in0=src[:], scalar1=float(1 / (2 * math.pi)), scalar2=None,
                                                              op0=ALU.mult), reads=[bsrc], writes=[b_ki])
                    fw.dve.op(lambda: nc.vector.tensor_copy(kf[:], ki[:]), reads=[b_ki], writes=[b_kf])
                    fw.dve.op(lambda: nc.vector.scalar_tensor_tensor(out=r[:], in0=kf[:], scalar=-C1, in1=src[:], op0=ALU.mult,
                                                                     op1=ALU.add), reads=[b_kf, bsrc], writes=[b_r])
                    fw.dve.op(lambda: nc.vector.scalar_tensor_tensor(out=r[:], in0=kf[:], scalar=-C2, in1=r[:], op0=ALU.mult,
                                                                     op1=ALU.add), reads=[b_kf, b_r], writes=[b_r])
                    fw.dve.op(lambda: nc.vector.tensor_scalar(out=r[:], in0=r[:], scalar1=float(math.pi), scalar2=float(-math.pi),
                                                              op0=ALU.min, op1=ALU.max), reads=[b_r], writes=[b_r])
                    fw.act.op(lambda: nc.scalar.activation(out=r[:], in_=r[:], func=AF.Sin), reads=[b_r], writes=[b_r])
                    if which == 0:
                        fw.dve.op(lambda: nc.vector.tensor_scalar(out=r[:], in0=r[:], scalar1=rc[:, 1:2], scalar2=None,
                                                                  op0=ALU.mult), reads=[b_r, brc], writes=[b_r])
                        fw.st.op(lambda: nc.gpsimd.dma_start(out=self.SS[:, t0:t0 + 512], in_=r[:]), reads=[b_r], writes=[self.b_tab])
                    else:
                        fw.st.op(lambda: nc.gpsimd.dma_start(out=self.CC[:, t0:t0 + 512], in_=r[:]), reads=[b_r], writes=[self.b_tab])
            fw.barrier()

    def emit_transpose_store(self, xn, b_xn, tb, xtt, b_xtt, pbank):
        nc, fw = self.nc, self.fw
        for half in range(2):
            bank = pbank[half]
            for c4 in range(4):
                c = half * 4 + c4
                fw.pe.op(lambda c=c, c4=c4: nc.tensor.transpose(self.ps(bank)[:, c4 * 128:(c4 + 1) * 128], xn[:, c * 128:(c + 1) * 128],
                                                                self.ident[:]),
                         reads=[b_xn, self.b_ident], writes=[self.bps[bank]], inc=(c4 == 3))
            fw.act.op(lambda half=half: nc.scalar.copy(xtt[:, half * 4:(half + 1) * 4, :],
                                                        self.ps(bank).rearrange("p (c t) -> p c t", c=4)),
                      reads=[self.bps[bank]], writes=[b_xtt])
        fw.st.op(lambda: nc.gpsimd.dma_start(out=self.xT[:, tb * 128:(tb + 1) * 128].rearrange("(c p) t -> p c t", p=128), in_=xtt[:]),
                 reads=[b_xtt], writes=[self.b_xT])

    def transpose_input(self):
        nc, fw = self.nc, self.fw
        with ExitStack() as es:
            sb = self.sbf(es)
            xt = [sb(f"ti_x{i}", [128, D], F32) for i in range(2)]
            bx = [Buf(), Buf()]
            xtt = [sb(f"ti_xt{i}", [128, 8, 128], BF16) for i in range(2)]
            bxtt = [Buf(), Buf()]
            for tb in range(self.NB):
                k = tb % 2
                fw.ld.op(lambda: nc.sync.dma_start(out=xt[k][:], in_=self.x_in[tb * 128:(tb + 1) * 128, :]), writes=[bx[k]])
                self.emit_transpose_store(xt[k], bx[k], tb, xtt[k], bxtt[k], (2 * k, 2 * k + 1))
            fw.barrier()

    def phase_a(self, li, m):
        nc, fw, S = self.nc, self.fw, self.S
        width = A_IN if m == 0 else 4 * D
        rope = m in (0, 2)
        rot_blocks = []
        if rope:
            rot_blocks = [(0, 16), (D, 16)]
            if m == 0:
                rot_blocks += [(4 * D, 8), (4 * D + 520, 1)]
        nrot = sum(nh for _, nh in rot_blocks) * 64
        with ExitStack() as es:
            sb = self.sbf(es)
            WIN = sb("WIN", [128, 8, width], BF16); b_win = Buf()
            WROT = sb("WROT", [128, 8, max(nrot, 64)], BF16)
            stg = [sb(f"wstg{i}", [128, width], F32) for i in range(2)]
            bstg = [Buf(), Buf()]
            rot_off = {}
            off = 0
            for col0, nh in rot_blocks:
                rot_off[col0] = off
                off += nh * 64
            for c in range(8):
                k = c % 2
                fw.ld.op(lambda: nc.sync.dma_start(out=stg[k][:], in_=self.w_in[li][c * 128:(c + 1) * 128, :]), writes=[bstg[k]])
                h2 = width // 2
                fw.act.op(lambda: nc.scalar.copy(WIN[:, c, 0:h2], stg[k][:, 0:h2]), reads=[bstg[k]], writes=[b_win])
                fw.dve.op(lambda: nc.vector.tensor_copy(WIN[:, c, h2:width], stg[k][:, h2:width]), reads=[bstg[k]], writes=[b_win])
                for col0, nh in rot_blocks:
                    ro = rot_off[col0]
                    src = stg[k][:, col0:col0 + nh * 64].rearrange("p (h t i) -> p h t i", t=2, i=32)
                    dst = WROT[:, c, ro:ro + nh * 64].rearrange("p (h t i) -> p h t i", t=2, i=32)
                    fw.pool.op(lambda: nc.gpsimd.tensor_copy(dst[:, :, 0, :], src[:, :, 1, :]), reads=[bstg[k]], writes=[b_win])
                    fw.pool.op(lambda: nc.gpsimd.tensor_copy(dst[:, :, 1, :], src[:, :, 0, :]), reads=[bstg[k]], writes=[b_win])
            groups = []
            for g in range(8):
                groups.append((self.QT, self.b_QT, g * 128, g * 128, 128, rope, None))
            for g in range(8):
                groups.append((self.KT, self.b_KT, g * 128, D + g * 128, 128, rope, None))
            for g in range(8):
                groups.append((self.GT, self.b_GT, g * 128, 3 * D + g * 128, 128, False, "silu"))
            if m == 0:
                for g in range(4):
                    groups.append((self.QT, self.b_QT, D + g * 128, 4 * D + g * 128, 128, True, None))
                groups.append((self.KT, self.b_KT, D, 4 * D + 520, 64, True, None))

            def rotcol(col):
                for col0, nh in rot_blocks:
                    if col0 <= col < col0 + nh * 64:
                        return rot_off[col0] + (col - col0)
                raise AssertionError

            XT = [sb(f"pa_xt{i}", [128, 8, 512], BF16) for i in range(2)]; bXT = [Buf(), Buf()]
            CCt = [sb(f"pa_cc{i}", [128, 512], F32) for i in range(2)]; bCC = [Buf(), Buf()]
            SSt = [sb(f"pa_ss{i}", [128, 512], F32) for i in range(2)]; bSS = [Buf(), Buf()]
            T1 = [sb(f"pa_t1{i}", [128, 512], F32) for i in range(2)]; bT1 = [Buf(), Buf()]
            T2 = [sb(f"pa_t2{i}", [128, 512], F32) for i in range(2)]; bT2 = [Buf(), Buf()]
            OB = [sb(f"pa_ob{i}", [128, 512], BF16) for i in range(3)]; bOB = [Buf() for _ in range(3)]
            VT = [sb(f"pa_vt{i}", [128, 4, D], BF16) for i in range(2)]; bVT = [Buf(), Buf()]
            WIt = [sb(f"pa_wi{i}", [128, 8], F32) for i in range(2)]; bWI = [Buf(), Buf()]
            gi = 0
            for tt in range(self.NT):
                k = tt % 2
                t0 = tt * 512
                fw.ld.op(lambda: nc.sync.dma_start(out=XT[k][:], in_=self.xT[:, t0:t0 + 512].rearrange("(c p) t -> p c t", p=128)),
                         reads=[self.b_xT], writes=[bXT[k]])
                if rope:
                    fw.ld.op(lambda: nc.sync.dma_start(out=CCt[k][:], in_=self.CC[:, t0:t0 + 512]), reads=[self.b_tab], writes=[bCC[k]])
                    fw.ld.op(lambda: nc.sync.dma_start(out=SSt[k][:], in_=self.SS[:, t0:t0 + 512]), reads=[self.b_tab], writes=[bSS[k]])
                for (dst, bdst, row0, col0, M, rp, act) in groups:
                    pb = (gi % 2) * 2
                    ob = gi % 3
                    tk = gi % 2
                    gi += 1
                    for c in range(8):
                        fw.pe.op(lambda c=c: nc.tensor.matmul(self.ps(pb, M), lhsT=WIN[:, c, col0:col0 + M], rhs=XT[k][:, c, :],
                                                              start=(c == 0), stop=(c == 7)),
                                 reads=[b_win, bXT[k]], writes=[self.bps[pb]], inc=(c == 7))
                    if rp:
                        rc0 = rotcol(col0)
                        for c in range(8):
                            fw.pe.op(lambda c=c: nc.tensor.matmul(self.ps(pb + 1, M), lhsT=WROT[:, c, rc0:rc0 + M], rhs=XT[k][:, c, :],
                                                                  start=(c == 0), stop=(c == 7)),
                                     reads=[b_win, bXT[k]], writes=[self.bps[pb + 1]], inc=(c == 7))
                        fw.dve.op(lambda: nc.vector.tensor_tensor(out=T1[tk][0:M, :], in0=self.ps(pb, M), in1=CCt[k][0:M, :], op=ALU.mult),
                                  reads=[self.bps[pb], bCC[k]], writes=[bT1[tk]])
                        fw.dve.op(lambda: nc.vector.tensor_tensor(out=T2[tk][0:M, :], in0=self.ps(pb + 1, M), in1=SSt[k][0:M, :], op=ALU.mult),
                                  reads=[self.bps[pb + 1], bSS[k]], writes=[bT2[tk]])
                        fw.pool.op(lambda: nc.gpsimd.tensor_tensor(out=OB[ob][0:M, :], in0=T1[tk][0:M, :], in1=T2[tk][0:M, :], op=ALU.add),
                                   reads=[bT1[tk], bT2[tk]], writes=[bOB[ob]])
                    elif act == "silu":
                        fw.act.op(lambda: nc.scalar.activation(out=OB[ob][0:M, :], in_=self.ps(pb, M), func=AF.Silu),
                                  reads=[self.bps[pb]], writes=[bOB[ob]])
                    else:
                        fw.act.op(lambda: nc.scalar.copy(OB[ob][0:M, :], self.ps(pb, M)), reads=[self.bps[pb]], writes=[bOB[ob]])
                    fw.st.op(lambda: nc.gpsimd.dma_start(out=dst[row0:row0 + M, t0:t0 + 512], in_=OB[ob][0:M, :]),
                             reads=[bOB[ob]], writes=[bdst])
                for tb in range(4):
                    for half in range(2):
                        pb = 4 + half
                        for c in range(8):
                            fw.pe.op(lambda c=c: nc.tensor.matmul(self.ps(pb), lhsT=XT[k][:, c, tb * 128:(tb + 1) * 128],
                                                                  rhs=WIN[:, c, 2 * D + half * 512:2 * D + (half + 1) * 512],
                                                                  start=(c == 0), stop=(c == 7)),
                                     reads=[b_win, bXT[k]], writes=[self.bps[pb]], inc=(c == 7))
                        if half == 0:
                            fw.act.op(lambda: nc.scalar.copy(VT[k][:, tb, 0:512], self.ps(pb)), reads=[self.bps[pb]], writes=[bVT[k]])
                        else:
                            fw.dve.op(lambda: nc.vector.tensor_copy(VT[k][:, tb, 512:1024], self.ps(pb)), reads=[self.bps[pb]], writes=[bVT[k]])
                    if m == 0:
                        pb = 6
                        wk = (tt * 4 + tb) % 2
                        for c in range(8):
                            fw.pe.op(lambda c=c: nc.tensor.matmul(self.ps(pb, 128, 8), lhsT=XT[k][:, c, tb * 128:(tb + 1) * 128],
                                                                  rhs=WIN[:, c, 4 * D + 512:4 * D + 520], start=(c == 0), stop=(c == 7)),
                                     reads=[b_win, bXT[k]], writes=[self.bps[pb]], inc=(c == 7))
                        fw.dve.op(lambda: nc.vector.tensor_scalar(out=WIt[wk][:], in0=self.ps(pb, 128, 8), scalar1=float(8 ** -0.5 * 64 ** -0.5),
                                                                  scalar2=None, op0=ALU.mult), reads=[self.bps[pb]], writes=[bWI[wk]])
                        r0 = t0 + tb * 128
                        fw.st.op(lambda: nc.gpsimd.dma_start(out=self.WI[r0:r0 + 128, :], in_=WIt[wk][:]), reads=[bWI[wk]], writes=[self.b_WI])
                for h in range(NH):
                    fw.st.op(lambda h=h: nc.gpsimd.dma_start(out=self.Vs[h, :, tt * 4:(tt + 1) * 4, :], in_=VT[k][:, :, h * 64:(h + 1) * 64]),
                             reads=[bVT[k]], writes=[self.b_Vs])
            fw.barrier()

    def phase_c(self, li, cur, last):
        nc, fw, S = self.nc, self.fw, self.S
        first = li == 0
        src = self.x_in if first else self.xres[cur]
        bsrc = Buf() if first else self.b_xres[cur]
        dst = self.out if last else self.xres[cur ^ 1]
        bdst = Buf() if last else self.b_xres[cur ^ 1]
        with ExitStack() as es:
            sb = self.sbf(es)
            WO = sb("WO", [128, 8, D], BF16); b_wo = Buf()
            stg = [sb(f"wostg{i}", [128, D], F32) for i in range(2)]; bstg = [Buf(), Buf()]
            for c in range(8):
                k = c % 2
                fw.ld.op(lambda: nc.sync.dma_start(out=stg[k][:], in_=self.w_out[li][c * 128:(c + 1) * 128, :]), writes=[bstg[k]])
                fw.act.op(lambda: nc.scalar.copy(WO[:, c, :], stg[k][:]), reads=[bstg[k]], writes=[b_wo])
            G = sb("lnG", [128, D], F32); Bt = sb("lnB", [128, D], F32); b_gb = Buf()
            fw.ld.op(lambda: nc.sync.dma_start(out=G[:], in_=self.lng[li:li + 1, :].broadcast_to([128, D])), writes=[b_gb])
            fw.ld.op(lambda: nc.sync.dma_start(out=Bt[:], in_=self.lnb[li:li + 1, :].broadcast_to([128, D])), writes=[b_gb])
            OTt = [sb(f"pc_ot{i}", [128, 8, 512], BF16) for i in range(2)]; bOT = [Buf(), Buf()]
            XR = [sb(f"pc_x{i}", [128, D], F32) for i in range(2)]; bXR = [Buf(), Buf()]
            U = [sb(f"pc_u{i}", [128, D], F32) for i in range(2)]; bU = [Buf(), Buf()]
            XN = [sb(f"pc_xn{i}", [128, D], F32) for i in range(2)]; bXN = [Buf(), Buf()]
            SQ = sb("pc_sq", [128, D], F32); bSQ = Buf()
            ST = [sb(f"pc_st{i}", [128, 8], F32) for i in range(2)]; bST = [Buf(), Buf()]
            XTT = [sb(f"pc_xtt{i}", [128, 8, 128], BF16) for i in range(2)]; bXTT = [Buf(), Buf()]
            for tt in range(self.NT):
                kk = tt % 2
                t0 = tt * 512
                fw.ld.op(lambda: nc.sync.dma_start(out=OTt[kk][:], in_=self.OT[:, t0:t0 + 512].rearrange("(c p) t -> p c t", p=128)),
                         reads=[self.b_OT], writes=[bOT[kk]])
                for tb4 in range(4):
                    tb = tt * 4 + tb4
                    k = tb % 2
                    fw.ld.op(lambda: nc.sync.dma_start(out=XR[k][:], in_=src[tb * 128:(tb + 1) * 128, :]), reads=[bsrc], writes=[bXR[k]])
                    pbs = (4 * k, 4 * k + 1)
                    for half in range(2):
                        for c in range(8):
                            fw.pe.op(lambda c=c, half=half: nc.tensor.matmul(self.ps(pbs[half]), lhsT=OTt[kk][:, c, tb4 * 128:(tb4 + 1) * 128],
                                                                             rhs=WO[:, c, half * 512:(half + 1) * 512], start=(c == 0), stop=(c == 7)),
                                     reads=[b_wo, bOT[kk]], writes=[self.bps[pbs[half]]], inc=(c == 7))
                    st = ST[k]
                    for half in range(2):
                        hs = slice(half * 512, (half + 1) * 512)
                        fw.dve.op(lambda half=half, hs=hs: nc.vector.scalar_tensor_tensor(out=U[k][:, hs], in0=XR[k][:, hs], scalar=float(ALPHA),
                                                                                          in1=self.ps(pbs[half]), op0=ALU.mult, op1=ALU.add),
                                  reads=[bXR[k], self.bps[pbs[half]]], writes=[bU[k]])
                    fw.act.op(lambda: nc.scalar.activation(out=SQ[:], in_=U[k][:], func=AF.Copy, accum_out=st[:, 0:1]), reads=[bU[k]], writes=[bSQ, bST[k]])
                    fw.act.op(lambda: nc.scalar.activation(out=SQ[:], in_=U[k][:], func=AF.Square, accum_out=st[:, 1:2]), reads=[bU[k]], writes=[bSQ, bST[k]])
                    fw.dve.op(lambda: nc.vector.tensor_scalar(out=st[:, 2:3], in0=st[:, 0:1], scalar1=float(1.0 / D), scalar2=None, op0=ALU.mult),
                              reads=[bST[k]], writes=[bST[k]])
                    fw.dve.op(lambda: nc.vector.tensor_tensor(out=st[:, 3:4], in0=st[:, 2:3], in1=st[:, 2:3], op=ALU.mult), reads=[bST[k]], writes=[bST[k]])
                    fw.dve.op(lambda: nc.vector.scalar_tensor_tensor(out=st[:, 4:5], in0=st[:, 1:2], scalar=float(1.0 / D), in1=st[:, 3:4],
                                                                     op0=ALU.mult, op1=ALU.subtract), reads=[bST[k]], writes=[bST[k]])
                    fw.act.op(lambda: nc.scalar.activation(out=st[:, 5:6], in_=st[:, 4:5], func=AF.Sqrt, bias=self.cst[:, 0:1], scale=1.0),
                              reads=[bST[k], self.b_cst], writes=[bST[k]])
                    fw.dve.op(lambda: nc.vector.reciprocal(st[:, 5:6], st[:, 5:6]), reads=[bST[k]], writes=[bST[k]])
                    fw.dve.op(lambda: nc.vector.tensor_scalar(out=XN[k][:], in0=U[k][:], scalar1=st[:, 2:3], scalar2=st[:, 5:6], op0=ALU.subtract, op1=ALU.mult),
                              reads=[bU[k], bST[k]], writes=[bXN[k]])
                    fw.pool.op(lambda: nc.gpsimd.tensor_tensor(out=XN[k][:], in0=XN[k][:], in1=G[:], op=ALU.mult), reads=[bXN[k], b_gb], writes=[bXN[k]])
                    fw.pool.op(lambda: nc.gpsimd.tensor_tensor(out=XN[k][:], in0=XN[k][:], in1=Bt[:], op=ALU.add), reads=[bXN[k], b_gb], writes=[bXN[k]])
                    fw.st.op(lambda: nc.gpsimd.dma_start(out=dst[tb * 128:(tb + 1) * 128, :], in_=XN[k][:]), reads=[bXN[k]], writes=[bdst])
                    if not last:
                        self.emit_transpose_store(XN[k], bXN[k], tb, XTT[k], bXTT[k], (4 * k + 2, 4 * k + 3))
            fw.barrier()

    def load_head_kv(self, es_sb, tagname, nsets):
        pass

    def attn_diff(self, L):
        nc, fw, S, NB, NT = self.nc, self.fw, self.S, self.NB, self.NT
        lambda_init = 0.8 - 0.6 * math.exp(-0.3 * L)
        with ExitStack() as es:
            sb = self.sbf(es)
            lv = sb("lv", [128, 4, 64], F32); b_lv = Buf()
            fw.ld.op(lambda: nc.sync.dma_start(out=lv[:].rearrange("p a b -> p (a b)"),
                                               in_=self.lam[0:1, :].broadcast_to([128, 256])), writes=[b_lv])
            lt = sb("lt", [128, 8], F32); b_lt = Buf()
            pr = sb("lpr", [128, 2, 64], F32)
            fw.dve.op(lambda: nc.vector.tensor_tensor(out=pr[:, 0, :], in0=lv[:, 0, :], in1=lv[:, 1, :], op=ALU.mult), reads=[b_lv], writes=[b_lt])
            fw.dve.op(lambda: nc.vector.tensor_tensor(out=pr[:, 1, :], in0=lv[:, 2, :], in1=lv[:, 3, :], op=ALU.mult), reads=[b_lv], writes=[b_lt])
            fw.dve.op(lambda: nc.vector.reduce_sum(out=lt[:, 0:1], in_=pr[:, 0, :], axis=mybir.AxisListType.X), reads=[b_lt], writes=[b_lt])
            fw.dve.op(lambda: nc.vector.reduce_sum(out=lt[:, 1:2], in_=pr[:, 1, :], axis=mybir.AxisListType.X), reads=[b_lt], writes=[b_lt])
            fw.act.op(lambda: nc.scalar.activation(out=lt[:, 2:4], in_=lt[:, 0:2], func=AF.Exp), reads=[b_lt], writes=[b_lt])
            fw.dve.op(lambda: nc.vector.tensor_tensor(out=lt[:, 4:5], in0=lt[:, 3:4], in1=lt[:, 2:3], op=ALU.subtract), reads=[b_lt], writes=[b_lt])
            fw.dve.op(lambda: nc.vector.tensor_scalar(out=lt[:, 5:6], in0=lt[:, 4:5], scalar1=float(-lambda_init), scalar2=None, op0=ALU.add),
                      reads=[b_lt], writes=[b_lt])
            sg = sb("sg", [128, 2], F32); b_sg = Buf()
            fw.ld.op(lambda: nc.sync.dma_start(out=sg[:, 0:1], in_=self.subg[:, :]), writes=[b_sg])
            fw.dve.op(lambda: nc.vector.tensor_scalar(out=sg[:, 1:2], in0=sg[:, 0:1], scalar1=float(1.0 - lambda_init), scalar2=None, op0=ALU.mult),
                      reads=[b_sg], writes=[b_sg])
            ones_b = sb("ones_b", [128, 128], BF16); ones_f = sb("ones_f", [128, 128], F32); b_ones = Buf()
            fw.pool.op(lambda: nc.gpsimd.memset(ones_b[:], 1.0), writes=[b_ones])
            fw.pool.op(lambda: nc.gpsimd.memset(ones_f[:], 1.0), writes=[b_ones])
            CM = [sb(f"cm{j}", [128, 512], BF16) for j in range(4)]; b_cm = Buf()
            for j in range(4):
                fw.pool.op(lambda j=j: nc.gpsimd.memset(CM[j][:], 1.0), writes=[b_cm])
                if j > 0:
                    fw.pool.op(lambda j=j: nc.gpsimd.memset(CM[j][:, 0:j * 128], 0.0), writes=[b_cm])
                fw.pool.op(lambda j=j: nc.gpsimd.memset(CM[j][64:128, j * 128:j * 128 + 64], 0.0), writes=[b_cm])
            KTt = [sb(f"df_k{i}", [64, 2, S], BF16) for i in range(2)]; bK = [Buf(), Buf()]
            Vt = [sb(f"df_v{i}", [128, NB, 128], BF16) for i in range(2)]; bV = [Buf(), Buf()]
            Qt = [sb(f"df_q{i}", [64, 2, 512], BF16) for i in range(3)]; bQ = [Buf() for _ in range(3)]
            Gt = [sb(f"df_g{i}", [128, 512], BF16) for i in range(3)]; bG = [Buf() for _ in range(3)]
            P = [sb(f"df_p{i}", [128, 512], BF16) for i in range(4)]; bP = [Buf() for _ in range(4)]
            PF = [sb(f"df_pf{i}", [128, 512], F32) for i in range(2)]; bPF = [Buf(), Buf()]
            R0 = sb("df_r0", [128, 512], F32); R1 = sb("df_r1", [128, 512], F32); bR = [Buf(), Buf()]
            O0 = sb("df_o0", [128, 512], F32); O1 = sb("df_o1", [128, 512], F32); bO = [Buf(), Buf()]
            SQ = sb("df_sq", [128, 512], F32); bSQ = Buf()
            RS = sb("df_rs", [128, 512], F32); bRS = Buf()
            OG = [sb(f"df_og{i}", [128, 512], BF16) for i in range(2)]; bOG = [Buf(), Buf()]
            sctr = [0]
            pctr = [0]
            items = []
            loaders = []
            for hd in range(8):
                hk = hd % 2

                def load_head(hd=hd, hk=hk):
                    for mm in range(2):
                        fw.ld.op(lambda mm=mm: nc.sync.dma_start(out=KTt[hk][:, mm, :], in_=self.KT[(2 * hd + mm) * 64:(2 * hd + mm + 1) * 64, :]),
                                 reads=[self.b_KT], writes=[bK[hk]])
                        fw.ld.op(lambda mm=mm: nc.sync.dma_start(out=Vt[hk][:, :, mm * 64:(mm + 1) * 64], in_=self.Vs[2 * hd + mm]),
                                 reads=[self.b_Vs], writes=[bV[hk]])
                for tt in range(NT):
                    qk = (hd * NT + tt) % 3
                    t0 = tt * 512
                    nkb = 4 * tt + 4

                    def load_q(hd=hd, qk=qk, t0=t0, tt=tt, hk=hk, lh=load_head):
                        if tt == 0:
                            lh()
                        for mm in range(2):
                            fw.ld.op(lambda mm=mm: nc.sync.dma_start(out=Qt[qk][:, mm, :], in_=self.QT[(2 * hd + mm) * 64:(2 * hd + mm + 1) * 64, t0:t0 + 512]),
                                     reads=[self.b_QT], writes=[bQ[qk]])
                        fw.ld.op(lambda: nc.sync.dma_start(out=Gt[qk][:], in_=self.GT[hd * 128:(hd + 1) * 128, t0:t0 + 512]),
                                 reads=[self.b_GT], writes=[bG[qk]])
                    for kb in range(nkb):
                        j = kb - 4 * tt
                        sbk = []
                        pidx = []
                        for mm in range(2):
                            sbk.append(sctr[0] % 3); sctr[0] += 1
                            pidx.append(pctr[0] % 4); pctr[0] += 1

                        if kb == 0:
                            loaders.append(load_q)
                        tidx = len(loaders) - 1

                        def st0(hk=hk, qk=qk, kb=kb, sbk=sbk, first=(kb == 0), tidx=tidx):
                            if first:
                                if tidx == 0:
                                    loaders[0]()
                                if tidx + 1 < len(loaders):
                                    loaders[tidx + 1]()
                            for mm in range(2):
                                fw.pe.op(lambda mm=mm: nc.tensor.matmul(self.ps(sbk[mm]), lhsT=KTt[hk][:, mm, kb * 128:(kb + 1) * 128], rhs=Qt[qk][:, mm, :],
                                                                        start=True, stop=True),
                                         reads=[bK[hk], bQ[qk]], writes=[self.bps[sbk[mm]]])

                        def st1(sbk=sbk, pidx=pidx, j=j):
                            for mm in range(2):
                                if j >= 0:
                                    fw.act.op(lambda mm=mm: nc.scalar.activation(out=PF[mm][:], in_=self.ps(sbk[mm]), func=AF.Exp, scale=0.125),
                                              reads=[self.bps[sbk[mm]]], writes=[bPF[mm]])
                                    fw.dve.op(lambda mm=mm: nc.vector.tensor_tensor(out=P[pidx[mm]][:], in0=PF[mm][:], in1=CM[j][:], op=ALU.mult),
                                              reads=[bPF[mm], b_cm], writes=[bP[pidx[mm]]])
                                else:
                                    fw.act.op(lambda mm=mm: nc.scalar.activation(out=P[pidx[mm]][:], in_=self.ps(sbk[mm]), func=AF.Exp, scale=0.125),
                                              reads=[self.bps[sbk[mm]]], writes=[bP[pidx[mm]]])

                        def st2(hk=hk, kb=kb, pidx=pidx, nkb=nkb, hd=hd, tt=tt, qk=qk, t0=t0):
                            for mm in range(2):
                                fw.pe.op(lambda mm=mm: nc.tensor.matmul(self.ps(3 + mm), lhsT=Vt[hk][:, kb, :], rhs=P[pidx[mm]][:],
                                                                        start=(kb == 0), stop=(kb == nkb - 1)),
                                         reads=[bV[hk], bP[pidx[mm]]], writes=[self.bps[3 + mm]])
                                fw.pe.op(lambda mm=mm: nc.tensor.matmul(self.ps(5 + mm), lhsT=ones_b[:], rhs=P[pidx[mm]][:],
                                                                        start=(kb == 0), stop=(kb == nkb - 1)),
                                         reads=[b_ones, bP[pidx[mm]]], writes=[self.bps[5 + mm]])
                            if kb == nkb - 1:
                                ok = (hd * NT + tt) % 2
                                fw.dve.op(lambda: nc.vector.reciprocal(R0[:], self.ps(5)), reads=[self.bps[5]], writes=[bR[0]])
                                fw.dve.op(lambda: nc.vector.reciprocal(R1[:], self.ps(6)), reads=[self.bps[6]], writes=[bR[1]])
                                fw.dve.op(lambda: nc.vector.tensor_tensor(out=O0[:], in0=self.ps(3), in1=R0[:], op=ALU.mult), reads=[self.bps[3], bR[0]], writes=[bO[0]])
                                fw.dve.op(lambda: nc.vector.tensor_tensor(out=O1[:], in0=self.ps(4), in1=R1[:], op=ALU.mult), reads=[self.bps[4], bR[1]], writes=[bO[1]])
                                if hd == 0 and tt == 0:
                                    self.dump(0, O0[:], bO[0]); self.dump(1, O1[:], bO[1]); self.dump(2, R0[:], bR[0])
                                fw.dve.op(lambda: nc.vector.scalar_tensor_tensor(out=O0[:], in0=O1[:], scalar=lt[:, 5:6], in1=O0[:], op0=ALU.mult, op1=ALU.add),
                                          reads=[bO[0], bO[1], b_lt], writes=[bO[0]])
                                if hd == 0 and tt == 0:
                                    self.dump(3, O0[:], bO[0])
                                fw.act.op(lambda: nc.scalar.activation(out=SQ[:], in_=O0[:], func=AF.Square), reads=[bO[0]], writes=[bSQ])
                                fw.pe.op(lambda: nc.tensor.matmul(self.ps(7), lhsT=ones_f[:], rhs=SQ[:], start=True, stop=True),
                                         reads=[b_ones, bSQ], writes=[self.bps[7]])
                                fw.act.op(lambda: nc.scalar.activation(out=RS[:], in_=self.ps(7), func=AF.Sqrt, bias=self.cst[:, 0:1], scale=float(1.0 / 128)),
                                          reads=[self.bps[7], self.b_cst], writes=[bRS])
                                fw.dve.op(lambda: nc.vector.reciprocal(RS[:], RS[:]), reads=[bRS], writes=[bRS])
                                if hd == 0 and tt == 0:
                                    self.dump(4, RS[:], bRS)
                                fw.dve.op(lambda: nc.vector.scalar_tensor_tensor(out=O0[:], in0=O0[:], scalar=sg[:, 1:2], in1=RS[:], op0=ALU.mult, op1=ALU.mult),
                                          reads=[bO[0], bRS, b_sg], writes=[bO[0]])
                                fw.pool.op(lambda: nc.gpsimd.tensor_tensor(out=OG[ok][:], in0=O0[:], in1=Gt[qk][:], op=ALU.mult), reads=[bO[0], bG[qk]], writes=[bOG[ok]])
                                fw.st.op(lambda: nc.gpsimd.dma_start(out=self.OT[hd * 128:(hd + 1) * 128, t0:t0 + 512], in_=OG[ok][:]),
                                         reads=[bOG[ok]], writes=[self.b_OT])
                        items.append([st0, st1, st2])
            pipeline(items, 3)
            fw.barrier()

    def attn_sb(self):
        nc, fw, S, NB, NT = self.nc, self.fw, self.S, self.NB, self.NT
        with ExitStack() as es:
            sb = self.sbf(es)
            cb = sb("sb_cb", [128, 512], BF16); b_c = Buf()
            fw.pool.op(lambda: nc.gpsimd.memset(cb[:], 1.0), writes=[b_c])
            BEF = [sb(f"sb_bef{j}", [128, 512], BF16) for j in range(4)]
            for j in range(4):
                fw.pool.op(lambda j=j: nc.gpsimd.affine_select(out=BEF[j][:], in_=cb[:], pattern=[[1, 512]], compare_op=ALU.is_ge, fill=0.0,
                                                               base=-128 * j - 1, channel_multiplier=-1), reads=[b_c], writes=[b_c])
            n8 = sb("sb_n8", [128, 128], BF16); tri = sb("sb_tri", [128, 128], BF16)
            fw.pool.op(lambda: nc.gpsimd.memset(n8[:], -8.0), writes=[b_c])
            fw.pool.op(lambda: nc.gpsimd.affine_select(out=tri[:], in_=n8[:], pattern=[[-1, 128]], compare_op=ALU.is_ge, fill=0.0,
                                                       base=0, channel_multiplier=1), reads=[b_c], writes=[b_c])
            one1 = sb("sb_one", [128, 1], F32)
            fw.pool.op(lambda: nc.gpsimd.memset(one1[:], 1.0), writes=[b_c])
            KTt = [sb(f"sb_k{i}", [64, S], BF16) for i in range(2)]; bK = [Buf(), Buf()]
            Vt = [sb(f"sb_v{i}", [128, NB, 64], BF16) for i in range(2)]; bV = [Buf(), Buf()]
            Qt = [sb(f"sb_q{i}", [64, 512], BF16) for i in range(3)]; bQ = [Buf() for _ in range(3)]
            Gt = [sb(f"sb_g{i}", [64, 512], BF16) for i in range(3)]; bG = [Buf() for _ in range(3)]
            E1 = [sb(f"sb_e{i}", [128, 512], F32) for i in range(2)]; bE = [Buf(), Buf()]
            SPf = sb("sb_spf", [128, 512], F32); bSPf = Buf()
            SP = [sb(f"sb_sp{i}", [128, 512], BF16) for i in range(3)]; bSP = [Buf() for _ in range(3)]
            AC = [sb(f"sb_ac{i}", [128, 512], BF16) for i in range(3)]; bAC = [Buf() for _ in range(3)]
            A = [sb(f"sb_a{i}", [128, 512], BF16) for i in range(3)]; bA = [Buf() for _ in range(3)]
            Af = sb("sb_af", [128, 512], F32); bAf = Buf()
            OG = [sb(f"sb_og{i}", [64, 512], BF16) for i in range(2)]; bOG = [Buf(), Buf()]
            items = []
            loaders = []
            ic = 0
            tile_i = 0
            for h in range(NH):
                hk = h % 2

                def load_head(h=h, hk=hk):
                    fw.ld.op(lambda: nc.sync.dma_start(out=KTt[hk][:], in_=self.KT[h * 64:(h + 1) * 64, :]), reads=[self.b_KT], writes=[bK[hk]])
                    fw.ld.op(lambda: nc.sync.dma_start(out=Vt[hk][:], in_=self.Vs[h]), reads=[self.b_Vs], writes=[bV[hk]])
                for tt in range(NT):
                    qk = tile_i % 3
                    ob = 5 + tile_i % 2
                    ogk = tile_i % 2
                    tile_i += 1
                    t0 = tt * 512
                    hi = 4 * tt + 3
                    lo = max(0, 4 * tt - SB_WIN)
                    kbs = list(range(hi, lo - 1, -1))

                    def load_q(h=h, qk=qk, t0=t0, tt=tt, lh=load_head):
                        if tt == 0:
                            lh()
                        fw.ld.op(lambda: nc.sync.dma_start(out=Qt[qk][:], in_=self.QT[h * 64:(h + 1) * 64, t0:t0 + 512]), reads=[self.b_QT], writes=[bQ[qk]])
                        fw.ld.op(lambda: nc.sync.dma_start(out=Gt[qk][:], in_=self.GT[h * 64:(h + 1) * 64, t0:t0 + 512]), reads=[self.b_GT], writes=[bG[qk]])
                    loaders.append(load_q)
                    tidx = len(loaders) - 1
                    prev_acc = None
                    for n, kb in enumerate(kbs):
                        j = kb - 4 * tt
                        diag = j >= 0
                        sbank = ic % 5
                        spk = ic % 3
                        ek = ic % 2
                        ic += 1
                        first = n == 0
                        lastk = n == len(kbs) - 1
                        pacc = prev_acc
                        if first:
                            acc_out = (SP[spk], bSP[spk])
                        else:
                            acc_out = (AC[spk], bAC[spk])
                        prev_acc = acc_out

                        def st0(hk=hk, qk=qk, kb=kb, sbank=sbank, first=first, tidx=tidx):
                            if first:
                                if tidx == 0:
                                    loaders[0]()
                                if tidx + 1 < len(loaders):
                                    loaders[tidx + 1]()
                            fw.pe.op(lambda: nc.tensor.matmul(self.ps(sbank), lhsT=KTt[hk][:, kb * 128:(kb + 1) * 128], rhs=Qt[qk][:], start=True, stop=False),
                                     reads=[bK[hk], bQ[qk]], writes=[self.bps[sbank]])

                        def st1(sbank=sbank, spk=spk, ek=ek, diag=diag, j=j, first=first, pacc=pacc, acc_out=acc_out):
                            fw.act.op(lambda: nc.scalar.activation(out=E1[ek][:], in_=self.ps(sbank), func=AF.Exp, scale=0.125),
                                      reads=[self.bps[sbank]], writes=[bE[ek]])
                            if diag:
                                fw.act.op(lambda: nc.scalar.activation(out=SPf[:], in_=E1[ek][:], func=AF.Ln, bias=one1[:], scale=1.0),
                                          reads=[bE[ek], b_c], writes=[bSPf])
                                fw.dve.op(lambda: nc.vector.tensor_tensor(out=SP[spk][:], in0=SPf[:], in1=BEF[j][:], op=ALU.mult),
                                          reads=[bSPf, b_c], writes=[bSP[spk]])
                            else:
                                fw.act.op(lambda: nc.scalar.activation(out=SP[spk][:], in_=E1[ek][:], func=AF.Ln, bias=one1[:], scale=1.0),
                                          reads=[bE[ek], b_c], writes=[bSP[spk]])
                            if not first:
                                fw.pool.op(lambda: nc.gpsimd.tensor_tensor(out=acc_out[0][:], in0=pacc[0][:], in1=SP[spk][:], op=ALU.add),
                                           reads=[pacc[1], bSP[spk]], writes=[acc_out[1]])

                        def st2(sbank=sbank, spk=spk, first=first, pacc=pacc):
                            fw.pe.op(lambda: nc.tensor.matmul(self.ps(sbank), lhsT=tri[:], rhs=SP[spk][:], start=False, stop=first),
                                     reads=[b_c, bSP[spk]], writes=[self.bps[sbank]])
                            if not first:
                                fw.pe.op(lambda: nc.tensor.matmul(self.ps(sbank), lhsT=n8[:], rhs=pacc[0][:], start=False, stop=True),
                                         reads=[b_c, pacc[1]], writes=[self.bps[sbank]])

                        def st3(sbank=sbank, spk=spk, diag=diag, j=j):
                            if diag:
                                fw.act.op(lambda: nc.scalar.activation(out=Af[:], in_=self.ps(sbank), func=AF.Exp, scale=0.125),
                                          reads=[self.bps[sbank]], writes=[bAf])
                                fw.dve.op(lambda: nc.vector.tensor_tensor(out=A[spk][:], in0=Af[:], in1=BEF[j][:], op=ALU.mult),
                                          reads=[bAf, b_c], writes=[bA[spk]])
                            else:
                                fw.act.op(lambda: nc.scalar.activation(out=A[spk][:], in_=self.ps(sbank), func=AF.Exp, scale=0.125),
                                          reads=[self.bps[sbank]], writes=[bA[spk]])

                        def st4(hk=hk, kb=kb, spk=spk, first=first, lastk=lastk, ob=ob, ogk=ogk, qk=qk, h=h, t0=t0):
                            fw.pe.op(lambda: nc.tensor.matmul(self.ps(ob, 64), lhsT=Vt[hk][:, kb, :], rhs=A[spk][:], start=first, stop=lastk),
                                     reads=[bV[hk], bA[spk]], writes=[self.bps[ob]])
                            if lastk:
                                fw.dve.op(lambda: nc.vector.tensor_tensor(out=OG[ogk][:], in0=self.ps(ob, 64), in1=Gt[qk][:], op=ALU.mult),
                                          reads=[self.bps[ob], bG[qk]], writes=[bOG[ogk]])
                                fw.st.op(lambda: nc.gpsimd.dma_start(out=self.OT[h * 64:(h + 1) * 64, t0:t0 + 512], in_=OG[ogk][:]),
                                         reads=[bOG[ogk]], writes=[self.b_OT])
                        items.append([st0, st1, st2, st3, st4])
            pipeline(items, 5)
            fw.barrier()

    def attn_dsa(self):
        nc, fw, S, NB, NT = self.nc, self.fw, self.S, self.NB, self.NT
        with ExitStack() as es:
            sb = self.sbf(es)
            b_c = Buf()
            dmask = sb("ds_dm", [128, 128], F32)
            fw.pool.op(lambda: nc.gpsimd.memset(dmask[:], 0.0), writes=[b_c])
            fw.pool.op(lambda: nc.gpsimd.memset(dmask[0:64, 64:128], -1e30), writes=[b_c])
            ones_b = sb("ds_ones", [128, 64], BF16)
            fw.pool.op(lambda: nc.gpsimd.memset(ones_b[:], 1.0), writes=[b_c])
            kiT = sb("ds_ki", [64, S], BF16); b_ki = Buf()
            fw.ld.op(lambda: nc.sync.dma_start(out=kiT[:], in_=self.KT[D:D + 64, :]), reads=[self.b_KT], writes=[b_ki])
            SC = sb("ds_sc", [128, S], F32); bSC = Buf()
            JK = sb("ds_jk", [128, S], BF16); bJK = Buf()
            MT = sb("ds_mt", [128, NB, 512], BF16); bMT = Buf()
            QI = [sb(f"ds_qi{i}", [64, 8, 128], BF16) for i in range(2)]; bQI = [Buf(), Buf()]
            WQ = [sb(f"ds_wq{i}", [128, 8], F32) for i in range(2)]; bWQ = [Buf(), Buf()]
            RL = [sb(f"ds_rl{i}", [128, 512], F32) for i in range(3)]; bRL = [Buf() for _ in range(3)]
            BS = sb("ds_bs", [128, 8], F32); bBS = Buf()
            MK = [sb(f"ds_mk{i}", [128, 512], F32) for i in range(2)]; bMK = [Buf(), Buf()]
            KTt = sb("ds_k", [64, S], BF16); bK = Buf()
            Vt = [sb(f"ds_v{i}", [128, NB, 64], BF16) for i in range(2)]; bV = [Buf(), Buf()]
            Qt = [sb(f"ds_q{i}", [64, 512], BF16) for i in range(3)]; bQ = [Buf() for _ in range(3)]
            Gt = [sb(f"ds_g{i}", [64, 512], BF16) for i in range(3)]; bG = [Buf() for _ in range(3)]
            PF = [sb(f"ds_pf{i}", [128, 512], BF16) for i in range(2)]; bPF = [Buf(), Buf()]
            P = [sb(f"ds_p{i}", [128, 512], BF16) for i in range(3)]; bP = [Buf() for _ in range(3)]
            Rr = sb("ds_r", [64, 512], F32); bRr = Buf()
            O1 = sb("ds_o1", [64, 512], F32); bO1 = Buf()
            OG = [sb(f"ds_og{i}", [64, 512], BF16) for i in range(2)]; bOG = [Buf(), Buf()]
            print("sbuf remaining after dsa alloc", nc.sbuf_bytes_remaining() if callable(getattr(nc, "sbuf_bytes_remaining", None)) else "?")
            rlc = 0
            qic = 0
            psc = 0
            tile_i = 0
            for tt in range(NT):
                t0 = tt * 512
                for i4 in range(4):
                    qb = 4 * tt + i4
                    nk = qb + 1
                    ncols_tot = nk * 128
                    qk = qic % 2
                    qic += 1
                    fw.ld.op(lambda: nc.sync.dma_start(out=QI[qk][:], in_=self.QT[D:D + 512, qb * 128:(qb + 1) * 128].rearrange("(h d) t -> d h t", d=64)),
                             reads=[self.b_QT], writes=[bQI[qk]])
                    fw.ld.op(lambda: nc.sync.dma_start(out=WQ[qk][:], in_=self.WI[qb * 128:(qb + 1) * 128, :]), reads=[self.b_WI], writes=[bWQ[qk]])
                    for c0 in range(0, ncols_tot, 512):
                        ncol = min(512, ncols_tot - c0)
                        for hh in range(8):
                            pb = psc % 3
                            psc += 1
                            rk = rlc % 3
                            rlc += 1
                            fw.pe.op(lambda: nc.tensor.matmul(self.ps(pb, 128, ncol), lhsT=QI[qk][:, hh, :], rhs=kiT[:, c0:c0 + ncol], start=True, stop=True),
                                     reads=[bQI[qk], b_ki], writes=[self.bps[pb]])
                            fw.act.op(lambda: nc.scalar.activation(out=RL[rk][:, 0:ncol], in_=self.ps(pb, 128, ncol), func=AF.Relu),
                                      reads=[self.bps[pb]], writes=[bRL[rk]])
                            if hh == 0:
                                fw.dve.op(lambda: nc.vector.tensor_scalar(out=SC[:, c0:c0 + ncol], in0=RL[rk][:, 0:ncol], scalar1=WQ[qk][:, 0:1], scalar2=None,
                                                                          op0=ALU.mult), reads=[bRL[rk], bWQ[qk]], writes=[bSC])
                            else:
                                fw.dve.op(lambda: nc.vector.scalar_tensor_tensor(out=SC[:, c0:c0 + ncol], in0=RL[rk][:, 0:ncol], scalar=WQ[qk][:, hh:hh + 1],
                                                                                 in1=SC[:, c0:c0 + ncol], op0=ALU.mult, op1=ALU.add),
                                          reads=[bRL[rk], bWQ[qk], bSC], writes=[bSC])
                    dsl = slice(qb * 128, (qb + 1) * 128)
                    fw.dve.op(lambda: nc.vector.tensor_tensor(out=SC[:, dsl], in0=SC[:, dsl], in1=dmask[:], op=ALU.add), reads=[bSC, b_c], writes=[bSC])
                    fw.dve.op(lambda: nc.vector.memset(BS[:, 0:1], -BIS_R), writes=[bBS])
                    fw.dve.op(lambda: nc.vector.memset(BS[:, 1:2], 0.0), writes=[bBS])
                    hstep = BIS_R
                    for it in range(BIS_ITERS):
                        fw.dve.op(lambda: nc.vector.tensor_scalar(out=JK[:, 0:ncols_tot], in0=SC[:, 0:ncols_tot], scalar1=BS[:, 1:2], scalar2=0.0,
                                                                  op0=ALU.is_ge, op1=ALU.add, accum_out=BS[:, 2:3]), reads=[bSC, bBS], writes=[bJK, bBS])
                        fw.dve.op(lambda: nc.vector.tensor_scalar(out=BS[:, 3:4], in0=BS[:, 2:3], scalar1=float(TOPK) - 0.5, scalar2=float(hstep),
                                                                  op0=ALU.is_ge, op1=ALU.mult), reads=[bBS], writes=[bBS])
                        fw.dve.op(lambda: nc.vector.tensor_tensor(out=BS[:, 0:1], in0=BS[:, 0:1], in1=BS[:, 3:4], op=ALU.add), reads=[bBS], writes=[bBS])
                        hstep = hstep / 2
                        fw.dve.op(lambda: nc.vector.tensor_scalar(out=BS[:, 1:2], in0=BS[:, 0:1], scalar1=float(hstep), scalar2=None, op0=ALU.add),
                                  reads=[bBS], writes=[bBS])
                    for c0 in range(0, ncols_tot, 512):
                        ncol = min(512, ncols_tot - c0)
                        mk = (c0 // 512) % 2
                        pb = 3 + (c0 // 512) % 2
                        fw.dve.op(lambda: nc.vector.tensor_scalar(out=MK[mk][:, 0:ncol], in0=SC[:, c0:c0 + ncol], scalar1=BS[:, 0:1], scalar2=None, op0=ALU.is_ge),
                                  reads=[bSC, bBS], writes=[bMK[mk]])
                        nb4 = ncol // 128
                        for b4 in range(nb4):
                            fw.pe.op(lambda b4=b4: nc.tensor.transpose(self.ps(pb)[:, b4 * 128:(b4 + 1) * 128], MK[mk][:, b4 * 128:(b4 + 1) * 128], self.ident[:]),
                                     reads=[bMK[mk], self.b_ident], writes=[self.bps[pb]], inc=(b4 == nb4 - 1))
                        kb0 = c0 // 128
                        fw.act.op(lambda: nc.scalar.copy(MT[:, kb0:kb0 + nb4, i4 * 128:(i4 + 1) * 128],
                                                         self.ps(pb, 128, ncol).rearrange("p (b t) -> p b t", t=128)),
                                  reads=[self.bps[pb]], writes=[bMT])
                    if i4 < 3:
                        fw.pool.op(lambda: nc.gpsimd.memset(MT[:, qb + 1:4 * tt + 4, i4 * 128:(i4 + 1) * 128], 0.0), writes=[bMT])
                nkb = 4 * tt + 4
                items = []
                loaders = []
                ic = 0
                for h in range(NH):
                    hk = h % 2
                    qk = tile_i % 3
                    ob = 5 + (tile_i % 1)
                    ogk = tile_i % 2
                    tile_i += 1

                    def load_q(h=h, qk=qk, hk=hk, nkb=nkb):
                        fw.ld.op(lambda: nc.sync.dma_start(out=KTt[:, 0:nkb * 128], in_=self.KT[h * 64:(h + 1) * 64, 0:nkb * 128]), reads=[self.b_KT], writes=[bK])
                        fw.ld.op(lambda: nc.sync.dma_start(out=Vt[hk][:, 0:nkb, :], in_=self.Vs[h, :, 0:nkb, :]), reads=[self.b_Vs], writes=[bV[hk]])
                        fw.ld.op(lambda: nc.sync.dma_start(out=Qt[qk][:], in_=self.QT[h * 64:(h + 1) * 64, t0:t0 + 512]), reads=[self.b_QT], writes=[bQ[qk]])
                        fw.ld.op(lambda: nc.sync.dma_start(out=Gt[qk][:], in_=self.GT[h * 64:(h + 1) * 64, t0:t0 + 512]), reads=[self.b_GT], writes=[bG[qk]])
                    loaders.append(load_q)
                    for kb in range(nkb):
                        sbank = ic % 3
                        pk = ic % 3
                        fk = ic % 2
                        ic += 1
                        first = kb == 0
                        lastk = kb == nkb - 1

                        def st0(h=h, qk=qk, kb=kb, sbank=sbank, first=first):
                            if first:
                                loaders[h]()
                            fw.pe.op(lambda: nc.tensor.matmul(self.ps(sbank), lhsT=KTt[:, kb * 128:(kb + 1) * 128], rhs=Qt[qk][:], start=True, stop=True),
                                     reads=[bK, bQ[qk]], writes=[self.bps[sbank]])

                        def st1(sbank=sbank, pk=pk, fk=fk, kb=kb):
                            fw.act.op(lambda: nc.scalar.activation(out=PF[fk][:], in_=self.ps(sbank), func=AF.Exp, scale=0.125),
                                      reads=[self.bps[sbank]], writes=[bPF[fk]])
                            fw.dve.op(lambda: nc.vector.tensor_tensor(out=P[pk][:], in0=PF[fk][:], in1=MT[:, kb, :], op=ALU.mult),
                                      reads=[bPF[fk], bMT], writes=[bP[pk]])

                        def st2(h=h, hk=hk, kb=kb, pk=pk, first=first, lastk=lastk, qk=qk, ogk=ogk):
                            fw.pe.op(lambda: nc.tensor.matmul(self.ps(6, 64), lhsT=Vt[hk][:, kb, :], rhs=P[pk][:], start=first, stop=lastk),
                                     reads=[bV[hk], bP[pk]], writes=[self.bps[6]])
                            fw.pe.op(lambda: nc.tensor.matmul(self.ps(7, 64), lhsT=ones_b[:], rhs=P[pk][:], start=first, stop=lastk),
                                     reads=[b_c, bP[pk]], writes=[self.bps[7]])
                            if lastk:
                                fw.dve.op(lambda: nc.vector.reciprocal(Rr[:], self.ps(7, 64)), reads=[self.bps[7]], writes=[bRr])
                                fw.dve.op(lambda: nc.vector.tensor_tensor(out=O1[:], in0=self.ps(6, 64), in1=Rr[:], op=ALU.mult), reads=[self.bps[6], bRr], writes=[bO1])
                                fw.pool.op(lambda: nc.gpsimd.tensor_tensor(out=OG[ogk][:], in0=O1[:], in1=Gt[qk][:], op=ALU.mult), reads=[bO1, bG[qk]], writes=[bOG[ogk]])
                                fw.st.op(lambda: nc.gpsimd.dma_start(out=self.OT[h * 64:(h + 1) * 64, t0:t0 + 512], in_=OG[ogk][:]),
                                         reads=[bOG[ogk]], writes=[self.b_OT])
                        items.append([st0, st1, st2])
                pipeline(items, 3)
            fw.barrier()


_LAYERS = [(i % 3, i // 3, i) for i in range(DEPTH)]


def _consts():
    p = np.arange(128)
    i = (p % 32).astype(np.float32)
    inv = (1.0 / (np.float32(10000.0) ** (2 * i / np.float32(64)))).astype(np.float32)
    sign = np.where((p % 64) < 32, -1.0, 1.0).astype(np.float32)
    return np.stack([inv, sign], axis=1).astype(np.float32), np.eye(128, dtype=np.float32)


def make_in_map(xb, layers, w):
    ropec, ident = _consts()
    im = {"x": np.ascontiguousarray(xb), "ropec": ropec, "ident": ident}
    ws_in = {0: w["w_in_a"], 1: w["w_in_b"], 2: w["w_in_c"]}
    ws_out = {0: w["w_out_a"], 1: w["w_out_b"], 2: w["w_out_c"]}
    for li, (m, j, L) in enumerate(layers):
        im[f"w_in{li}"] = np.ascontiguousarray(ws_in[m][j])
        im[f"w_out{li}"] = np.ascontiguousarray(ws_out[m][j])
    im["ln_g"] = np.ascontiguousarray(np.stack([w["ln_g"][L] for (_, _, L) in layers]))
    im["ln_b"] = np.ascontiguousarray(np.stack([w["ln_b"][L] for (_, _, L) in layers]))
    im["lam"] = np.ascontiguousarray(np.stack([w["lambda_q1"][0], w["lambda_k1"][0], w["lambda_q2"][0], w["lambda_k2"][0]]).reshape(1, 256))
    im["subg"] = np.ascontiguousarray(w["subln_g"][0].reshape(128, 1))
    return im


def kernel(**inputs):
    w = {k: np.asarray(v, dtype=np.float32) for k, v in inputs.items()}
    x = w["x"]
    B, S, _ = x.shape
    prog = Prog(S, _LAYERS)
    n = 8
    in_maps = [make_in_map(x[c % B], _LAYERS, w) for c in range(n)]
    res = run_bass_kernel_spmd(prog.nc, in_maps, core_ids=list(range(n)))
    return np.stack([res.results[b]["out"] for b in range(B)], axis=0).astype(np.float32)
```

```python
import math
from contextlib import ExitStack
import numpy as np
import concourse.bass as bass
import concourse.mybir as mybir
from concourse.bass_utils import run_bass_kernel_spmd

F32 = mybir.dt.float32
BF16 = mybir.dt.bfloat16
I32 = mybir.dt.int32
AF = mybir.ActivationFunctionType
ALU = mybir.AluOpType

D = 1024
NH = 16
HD = 64
DEPTH = 4
ALPHA = (2.0 * DEPTH) ** 0.25
LN_EPS = 1e-5
RMS_EPS = 1e-5
A_IN = 4 * D + 8 * 64 + 8 + 64
TOPK = 256
SB_WIN = 3
BIS_ITERS = 24
BIS_R = 32.0


class Buf:
    __slots__ = ("w", "r")

    def __init__(self):
        self.w = None
        self.r = {}


class Q:
    def __init__(self, fw, eng, name, is_dma, nsems, same_engine_waits=True):
        self.fw = fw
        self.eng = eng
        self.is_dma = is_dma
        self.inc = 16 if is_dma else 1
        self.sems = []
        for i in range(nsems):
            h = fw.nc.alloc_semaphore(name=f"s_{name}{i}")
            self.sems.append(len(fw.semh))
            fw.semh.append(h)
        self.cnt = [0] * nsems
        self.rr = 0
        self.waited = {}
        self.sew = same_engine_waits
        self.pending = False

    def _wait(self, sem, val):
        if self.waited.get(sem, 0) < val:
            self.eng.wait_ge(self.fw.semh[sem], val)
            self.waited[sem] = val

    def op(self, fn, reads=(), writes=(), inc=True):
        deps = {}
        for b in reads:
            if b.w is not None:
                s, v = b.w
                if deps.get(s, 0) < v:
                    deps[s] = v
        for b in writes:
            if b.w is not None:
                s, v = b.w
                if deps.get(s, 0) < v:
                    deps[s] = v
            for s, v in b.r.items():
                if deps.get(s, 0) < v:
                    deps[s] = v
        i = self.rr
        if self.is_dma:
            self.rr = (self.rr + 1) % len(self.sems)
            if self.cnt[i] > 0:
                deps[self.sems[i]] = max(deps.get(self.sems[i], 0), self.cnt[i])
        for s, v in deps.items():
            if (not self.sew) and s == self.sems[0]:
                continue
            self._wait(s, v)
        ins = fn()
        if inc:
            self.cnt[i] += self.inc
            ins.then_inc(self.fw.semh[self.sems[i]], self.inc)
            tag = (self.sems[i], self.cnt[i])
            self.pending = False
        else:
            tag = (self.sems[i], self.cnt[i] + self.inc)
            self.pending = True
        for b in reads:
            if b.r.get(tag[0], 0) < tag[1]:
                b.r[tag[0]] = tag[1]
        for b in writes:
            b.w = tag
            b.r = {}
        return tag

    def wait_all_of(self, other):
        for i, s in enumerate(other.sems):
            if other.cnt[i] > 0:
                self._wait(s, other.cnt[i])


class FW:
    def __init__(self, nc):
        self.nc = nc
        self.semh = []
        self.pe = Q(self, nc.tensor, "pe", False, 1, same_engine_waits=False)
        self.act = Q(self, nc.scalar, "act", False, 1)
        self.dve = Q(self, nc.vector, "dve", False, 1)
        self.pool = Q(self, nc.gpsimd, "pool", False, 1)
        self.ld = Q(self, nc.sync, "ld", True, 8)
        self.st = Q(self, nc.gpsimd, "st", True, 8)
        self.qs = [self.pe, self.act, self.dve, self.pool, self.ld, self.st]

    def barrier(self):
        for q in self.qs:
            assert not q.pending
        for q in self.qs:
            for o in self.qs:
                if o is not q:
                    q.wait_all_of(o)


def pipeline(items, nstages):
    n = len(items)
    for step in range(n + nstages - 1):
        for s in range(nstages - 1, -1, -1):
            i = step - s
            if 0 <= i < n and items[i][s] is not None:
                items[i][s]()


TW = 256


def own_blocks(rho):
    return (0, 3) if rho == 0 else (1, 2)


class Prog:
    def __init__(self, S, layers, ncores=8, debug=False):
        self.S = S
        self.SO = S // 2
        self.NB = S // 128
        self.NT = S // 512
        self.CW = min(1024, self.SO)
        self.NCH = self.SO // self.CW
        self.groups = [[2 * i, 2 * i + 1] for i in range(ncores // 2)]
        self.layers = layers
        self.debug = debug
        nc = self.nc = bass.Bass("TRN2", target_bir_lowering=False)
        self.fw = FW(nc)
        self.fw.cc = Q(self.fw, nc.gpsimd, "cc", False, 1)
        self.fw.qs.append(self.fw.cc)
        SO = self.SO
        dt = nc.dram_tensor
        self.x_in = dt("x", [S, D], F32, kind="ExternalInput").ap()
        self.x_own = dt("x_own", [SO, D], F32, kind="ExternalInput").ap()
        self.posrow = dt("posrow", [1, SO], F32, kind="ExternalInput").ap()
        self.poscol = dt("poscol", [SO, 1], F32, kind="ExternalInput").ap()
        self.qrel = dt("qrel", [1, TW], F32, kind="ExternalInput").ap()
        self.out = dt("out", [SO, D], F32, kind="ExternalOutput").ap()
        self.w_in = {}
        self.w_out = {}
        for li, (m, j, L) in enumerate(layers):
            width = A_IN if m == 0 else 4 * D
            self.w_in[li] = dt(f"w_in{li}", [D, width], F32, kind="ExternalInput").ap()
            self.w_out[li] = dt(f"w_out{li}", [D, D], F32, kind="ExternalInput").ap()
        self.lng = dt("ln_g", [len(layers), D], F32, kind="ExternalInput").ap()
        self.lnb = dt("ln_b", [len(layers), D], F32, kind="ExternalInput").ap()
        self.lam = dt("lam", [1, 256], F32, kind="ExternalInput").ap()
        self.subg = dt("subg", [128, 1], F32, kind="ExternalInput").ap()
        self.ropec = dt("ropec", [128, 2], F32, kind="ExternalInput").ap()
        self.ident_in = dt("ident", [128, 128], F32, kind="ExternalInput").ap()
        IK = "ExternalOutput" if debug else "Internal"
        self.xres = [dt(f"xres{i}", [SO, D], F32, kind="Internal").ap() for i in range(2)]
        self.xTf = dt("xTf", [D, S], BF16, kind="Internal").ap()
        self.xTo = [[dt(f"xTo{s_}_{c}", [D, self.CW], BF16) for c in range(self.NCH)] for s_ in range(2)]
        self.gath = [[dt(f"gath{l}_{c}", [2 * D, self.CW], BF16) for c in range(self.NCH)] for l in range(max(1, len(layers) - 1))]
        self.QT = dt("QT", [24 * 64, SO], BF16, kind=IK).ap()
        self.KT = dt("KT", [17 * 64, S], BF16, kind=IK).ap()
        self.GT = dt("GT", [D, SO], BF16, kind=IK).ap()
        self.OT = dt("OT", [D, SO], BF16, kind=IK).ap()
        self.Vs = dt("Vs", [NH, 128, self.NB, 64], BF16, kind=IK).ap()
        self.WI = dt("WI", [SO, 8], F32, kind="Internal").ap()
        self.CC = dt("CCt", [128, S], F32, kind="Internal").ap()
        self.SS = dt("SSt", [128, S], F32, kind="Internal").ap()
        self.CCo = dt("CCo", [128, SO], F32, kind="Internal").ap()
        self.SSo = dt("SSo", [128, SO], F32, kind="Internal").ap()
        if debug:
            self.dbgSC = dt("dbgSC", [128, 512], F32, kind="ExternalOutput").ap()
            self.dbgBS = dt("dbgBS", [128, 8], F32, kind="ExternalOutput").ap()
            self.dbgMT = dt("dbgMT", [128, 4, TW], BF16, kind="ExternalOutput").ap()
            self.dbgMT2 = dt("dbgMT2", [128, 4, TW], BF16, kind="ExternalOutput").ap()
            self.dbgP = dt("dbgP", [128, 4, TW], BF16, kind="ExternalOutput").ap()
            self.dbgO = dt("dbgO", [64, 2, TW], F32, kind="ExternalOutput").ap()
        self.b_xres = [Buf(), Buf()]
        self.b_xTf, self.b_QT, self.b_KT, self.b_GT, self.b_OT, self.b_Vs, self.b_WI, self.b_tab = (Buf() for _ in range(8))
        self.b_xTo = [Buf(), Buf()]
        self.b_gath = [Buf() for _ in self.gath]
        self.psall = nc.alloc_psum_tensor("psall", [128, 4096], F32)
        self.bps = [Buf() for _ in range(8)]
        self.build()

    def sbf(self, es):
        def f(n, sh, d):
            self._uid = getattr(self, "_uid", 0) + 1
            return es.enter_context(self.nc.sbuf_tensor(f"{n}_u{self._uid}", sh, d))
        return f

    def ps(self, i, parts=128, n=512):
        return self.psall[0:parts, i * 512:i * 512 + n]

    def build(self):
        nc, fw = self.nc, self.fw
        with ExitStack() as es:
            sb = self.sbf(es)
            self.ident = sb("identsb", [128, 128], F32)
            self.b_ident = Buf()
            fw.ld.op(lambda: nc.sync.dma_start(out=self.ident[:], in_=self.ident_in[:, :]), writes=[self.b_ident])
            self.cst = sb("cst", [128, 4], F32)
            self.b_cst = Buf()
            fw.pool.op(lambda: nc.gpsimd.memset(self.cst[:, 0:1], float(LN_EPS)), writes=[self.b_cst])
            fw.pool.op(lambda: nc.gpsimd.iota(self.cst[:, 1:2], pattern=[[0, 1]], base=0, channel_multiplier=1,
                                              allow_small_or_imprecise_dtypes=True), writes=[self.b_cst])
            fw.pool.op(lambda: nc.gpsimd.memset(self.cst[0:64, 2:3], 0.0), writes=[self.b_cst])
            fw.pool.op(lambda: nc.gpsimd.memset(self.cst[64:128, 2:3], 64.0), writes=[self.b_cst])
            fw.pool.op(lambda: nc.gpsimd.memset(self.cst[:, 3:4], 1.0), writes=[self.b_cst])
            self.QR = sb("qrelsb", [128, TW], F32)
            fw.ld.op(lambda: nc.sync.dma_start(out=self.QR[:], in_=self.qrel[0:1, :].broadcast_to([128, TW])), writes=[self.b_cst])
            self.rope_tables(self.CC, self.SS, self.S, None)
            self.rope_tables(self.CCo, self.SSo, self.SO, self.posrow)
            self.transpose_input()
            cur = 0
            for li, (m, j, L) in enumerate(self.layers):
                last = li == len(self.layers) - 1
                self.phase_a(li, m)
                if m == 0:
                    self.attn_dsa()
                elif m == 1:
                    self.attn_sb()
                else:
                    self.attn_diff(L)
                self.phase_c(li, cur, last)
                if not last:
                    self.exchange(li)
                cur ^= 1
            fw.barrier()

    def exchange(self, li):
        nc, fw = self.nc, self.fw
        sset = (li + 1) % 2
        for ch in range(self.NCH):
            fw.cc.op(lambda ch=ch: nc.gpsimd.collective_compute("AllGather", ALU.bypass, replica_groups=self.groups,
                                                                ins=[self.xTo[sset][ch].ap().opt()], outs=[self.gath[li][ch].ap().opt()]),
                     reads=[self.b_xTo[sset]], writes=[self.b_gath[li]])
        fw.barrier()

    def xt_all_src(self, li, tt, c):
        if li == 0:
            g = 4 * tt + c
            return self.xTf[:, g * 128:(g + 1) * 128], self.b_xTf
        rho = 0 if c in (0, 3) else 1
        lb = 2 * tt + (0 if c in (0, 1) else 1)
        ch, col = divmod(lb * 128, self.CW)
        return self.gath[li - 1][ch].ap()[rho * D:(rho + 1) * D, col:col + 128], self.b_gath[li - 1]

    def rope_tables(self, CCd, SSd, n, posrow):
        nc, fw = self.nc, self.fw
        with ExitStack() as es:
            sb = self.sbf(es)
            rc = sb("rc", [128, 2], F32); brc = Buf()
            fw.ld.op(lambda: nc.sync.dma_start(out=rc[:], in_=self.ropec[:, :]), writes=[brc])
            C1 = 6.28125
            C2 = 2 * math.pi - C1
            pools = {}
            for nm, dtp in [("it", F32), ("ang", F32), ("a2", F32), ("ki", I32), ("kf", F32), ("r0", F32), ("r1", F32)]:
                pools[nm] = [(sb(f"rt_{nm}{q}", [128, 512], dtp), Buf()) for q in range(2)]
            for t0 in range(0, n, 512):
                sl = (t0 // 512) % 2
                it, b_it = pools["it"][sl]
                ang, b_ang = pools["ang"][sl]
                for which in range(2):
                    a2, b_a2 = pools["a2"][sl]
                    ki, b_ki = pools["ki"][sl]
                    kf, b_kf = pools["kf"][sl]
                    r, b_r = pools["r0" if which == 0 else "r1"][sl]
                    if which == 0:
                        if posrow is None:
                            fw.pool.op(lambda: nc.gpsimd.iota(it[:], pattern=[[1, 512]], base=t0, channel_multiplier=0,
                                                              allow_small_or_imprecise_dtypes=True), writes=[b_it])
                        else:
                            fw.ld.op(lambda: nc.sync.dma_start(out=it[:], in_=posrow[0:1, t0:t0 + 512].broadcast_to([128, 512])), writes=[b_it])
                        fw.dve.op(lambda: nc.vector.tensor_scalar(out=ang[:], in0=it[:], scalar1=rc[:, 0:1], scalar2=None,
                                                                  op0=ALU.mult), reads=[b_it, brc], writes=[b_ang])
                        src = ang
                    else:
                        fw.dve.op(lambda: nc.vector.tensor_scalar(out=a2[:], in0=ang[:], scalar1=float(math.pi / 2), scalar2=None,
                                                                  op0=ALU.add), reads=[b_ang], writes=[b_a2])
                        src = a2
                    bsrc = b_ang if which == 0 else b_a2
                    fw.dve.op(lambda: nc.vector.tensor_scalar(out=ki[:], in0=src[:], scalar1=float(1 / (2 * math.pi)), scalar2=None,
                                                              op0=ALU.mult), reads=[bsrc], writes=[b_ki])
                    fw.dve.op(lambda: nc.vector.tensor_copy(kf[:], ki[:]), reads=[b_ki], writes=[b_kf])
                    fw.dve.op(lambda: nc.vector.scalar_tensor_tensor(out=r[:], in0=kf[:], scalar=-C1, in1=src[:], op0=ALU.mult,
                                                                     op1=ALU.add), reads=[b_kf, bsrc], writes=[b_r])
                    fw.dve.op(lambda: nc.vector.scalar_tensor_tensor(out=r[:], in0=kf[:], scalar=-C2, in1=r[:], op0=ALU.mult,
                                                                     op1=ALU.add), reads=[b_kf, b_r], writes=[b_r])
                    fw.dve.op(lambda: nc.vector.tensor_scalar(out=r[:], in0=r[:], scalar1=float(math.pi), scalar2=float(-math.pi),
                                                              op0=ALU.min, op1=ALU.max), reads=[b_r], writes=[b_r])
                    fw.act.op(lambda: nc.scalar.activation(out=r[:], in_=r[:], func=AF.Sin), reads=[b_r], writes=[b_r])
                    if which == 0:
                        fw.dve.op(lambda: nc.vector.tensor_scalar(out=r[:], in0=r[:], scalar1=rc[:, 1:2], scalar2=None,
                                                                  op0=ALU.mult), reads=[b_r, brc], writes=[b_r])
                        fw.st.op(lambda: nc.gpsimd.dma_start(out=SSd[:, t0:t0 + 512], in_=r[:]), reads=[b_r], writes=[self.b_tab])
                    else:
                        fw.st.op(lambda: nc.gpsimd.dma_start(out=CCd[:, t0:t0 + 512], in_=r[:]), reads=[b_r], writes=[self.b_tab])
            fw.barrier()

    def emit_transpose_store(self, xn, b_xn, dst_ap, b_dst, xtt, b_xtt, pbank):
        nc, fw = self.nc, self.fw
        for half in range(2):
            bank = pbank[half]
            for c4 in range(4):
                c = half * 4 + c4
                fw.pe.op(lambda c=c, c4=c4: nc.tensor.transpose(self.ps(bank)[:, c4 * 128:(c4 + 1) * 128], xn[:, c * 128:(c + 1) * 128],
                                                                self.ident[:]),
                         reads=[b_xn, self.b_ident], writes=[self.bps[bank]], inc=(c4 == 3))
            fw.act.op(lambda half=half: nc.scalar.copy(xtt[:, half * 4:(half + 1) * 4, :],
                                                        self.ps(bank).rearrange("p (c t) -> p c t", c=4)),
                      reads=[self.bps[bank]], writes=[b_xtt])
        fw.st.op(lambda: nc.gpsimd.dma_start(out=dst_ap.rearrange("(c p) t -> p c t", p=128), in_=xtt[:]),
                 reads=[b_xtt], writes=[b_dst])

    def xto_dst(self, sset, lb):
        ch, col = divmod(lb * 128, self.CW)
        return self.xTo[sset][ch].ap()[:, col:col + 128]

    def transpose_input(self):
        nc, fw = self.nc, self.fw
        with ExitStack() as es:
            sb = self.sbf(es)
            xt = [sb(f"ti_x{i}", [128, D], F32) for i in range(2)]
            bx = [Buf(), Buf()]
            xtt = [sb(f"ti_xt{i}", [128, 8, 128], BF16) for i in range(2)]
            bxtt = [Buf(), Buf()]
            n = 0
            for tb in range(self.NB):
                k = n % 2
                n += 1
                fw.ld.op(lambda: nc.sync.dma_start(out=xt[k][:], in_=self.x_in[tb * 128:(tb + 1) * 128, :]), writes=[bx[k]])
                self.emit_transpose_store(xt[k], bx[k], self.xTf[:, tb * 128:(tb + 1) * 128], self.b_xTf, xtt[k], bxtt[k], (2 * k, 2 * k + 1))
            for lb in range(self.SO // 128):
                k = n % 2
                n += 1
                fw.ld.op(lambda: nc.sync.dma_start(out=xt[k][:], in_=self.x_own[lb * 128:(lb + 1) * 128, :]), writes=[bx[k]])
                self.emit_transpose_store(xt[k], bx[k], self.xto_dst(0, lb), self.b_xTo[0], xtt[k], bxtt[k], (2 * k, 2 * k + 1))
            fw.barrier()

    def phase_a(self, li, m):
        nc, fw, S = self.nc, self.fw, self.S
        width = A_IN if m == 0 else 4 * D
        rope = m in (0, 2)
        rot_blocks = []
        if rope:
            rot_blocks = [(0, 16), (D, 16)]
            if m == 0:
                rot_blocks += [(4 * D, 8), (4 * D + 520, 1)]
        nrot = sum(nh for _, nh in rot_blocks) * 64
        sset = li % 2
        with ExitStack() as es:
            sb = self.sbf(es)
            WIN = sb("WIN", [128, 8, width], BF16); b_win = Buf()
            WROT = sb("WROT", [128, 8, max(nrot, 64)], BF16)
            stg0 = sb("wstg0", [128, width], F32)
            stg = [stg0, stg0]
            bstg0 = Buf()
            bstg = [bstg0, bstg0]
            rot_off = {}
            off = 0
            for col0, nh in rot_blocks:
                rot_off[col0] = off
                off += nh * 64
            for c in range(8):
                k = c % 2
                fw.ld.op(lambda: nc.sync.dma_start(out=stg[k][:], in_=self.w_in[li][c * 128:(c + 1) * 128, :]), writes=[bstg[k]])
                h2 = width // 2
                fw.act.op(lambda: nc.scalar.copy(WIN[:, c, 0:h2], stg[k][:, 0:h2]), reads=[bstg[k]], writes=[b_win])
                fw.dve.op(lambda: nc.vector.tensor_copy(WIN[:, c, h2:width], stg[k][:, h2:width]), reads=[bstg[k]], writes=[b_win])
                for col0, nh in rot_blocks:
                    ro = rot_off[col0]
                    src = stg[k][:, col0:col0 + nh * 64].rearrange("p (h t i) -> p h t i", t=2, i=32)
                    dst = WROT[:, c, ro:ro + nh * 64].rearrange("p (h t i) -> p h t i", t=2, i=32)
                    fw.pool.op(lambda: nc.gpsimd.tensor_copy(dst[:, :, 0, :], src[:, :, 1, :]), reads=[bstg[k]], writes=[b_win])
                    fw.pool.op(lambda: nc.gpsimd.tensor_copy(dst[:, :, 1, :], src[:, :, 0, :]), reads=[bstg[k]], writes=[b_win])
            kgroups = [(self.KT, self.b_KT, g * 128, D + g * 128, 128, rope, None) for g in range(8)]
            if m == 0:
                kgroups.append((self.KT, self.b_KT, D, 4 * D + 520, 64, True, None))
            qgroups = [(self.QT, self.b_QT, g * 128, g * 128, 128, rope, None) for g in range(8)]
            qgroups += [(self.GT, self.b_GT, g * 128, 3 * D + g * 128, 128, False, "silu") for g in range(8)]
            if m == 0:
                qgroups += [(self.QT, self.b_QT, D + g * 128, 4 * D + g * 128, 128, True, None) for g in range(4)]

            def rotcol(col):
                for col0, nh in rot_blocks:
                    if col0 <= col < col0 + nh * 64:
                        return rot_off[col0] + (col - col0)
                raise AssertionError

            XT = [sb(f"pa_xt{i}", [128, 8, 512], BF16) for i in range(2)]; bXT = [Buf(), Buf()]
            XO = [sb(f"pa_xo{i}", [128, 8, TW], BF16) for i in range(2)]; bXO = [Buf(), Buf()]
            CCt = [sb(f"pa_cc{i}", [128, 512], F32) for i in range(2)]; bCC = [Buf(), Buf()]
            SSt = [sb(f"pa_ss{i}", [128, 512], F32) for i in range(2)]; bSS = [Buf(), Buf()]
            CCq = [sb(f"pa_ccq{i}", [128, TW], F32) for i in range(2)]; bCCq = [Buf(), Buf()]
            SSq = [sb(f"pa_ssq{i}", [128, TW], F32) for i in range(2)]; bSSq = [Buf(), Buf()]
            T1 = [sb(f"pa_t1{i}", [128, 512], F32) for i in range(2)]; bT1 = [Buf(), Buf()]
            T2 = [sb(f"pa_t2{i}", [128, 512], F32) for i in range(2)]; bT2 = [Buf(), Buf()]
            OB = [sb(f"pa_ob{i}", [128, 512], BF16) for i in range(3)]; bOB = [Buf() for _ in range(3)]
            VT0 = sb("pa_vt0", [128, 4, D], BF16); VT = [VT0, VT0]; bVT0 = Buf(); bVT = [bVT0, bVT0]
            WIt = [sb(f"pa_wi{i}", [128, 8], F32) for i in range(2)]; bWI = [Buf(), Buf()]
            gi = [0]

            def do_group(grp, xin, bxin, N, t0, cc, bcc, ss, bss):
                (dst, bdst, row0, col0, M, rp, act) = grp
                pb = (gi[0] % 2) * 2
                ob = gi[0] % 3
                tk = gi[0] % 2
                gi[0] += 1
                for c in range(8):
                    fw.pe.op(lambda c=c: nc.tensor.matmul(self.ps(pb, M, N), lhsT=WIN[:, c, col0:col0 + M], rhs=xin[:, c, :],
                                                          start=(c == 0), stop=(c == 7)),
                             reads=[b_win, bxin], writes=[self.bps[pb]], inc=(c == 7))
                if rp:
                    rc0 = rotcol(col0)
                    for c in range(8):
                        fw.pe.op(lambda c=c: nc.tensor.matmul(self.ps(pb + 1, M, N), lhsT=WROT[:, c, rc0:rc0 + M], rhs=xin[:, c, :],
                                                              start=(c == 0), stop=(c == 7)),
                                 reads=[b_win, bxin], writes=[self.bps[pb + 1]], inc=(c == 7))
                    fw.dve.op(lambda: nc.vector.tensor_tensor(out=T1[tk][0:M, 0:N], in0=self.ps(pb, M, N), in1=cc[0:M, :], op=ALU.mult),
                              reads=[self.bps[pb], bcc], writes=[bT1[tk]])
                    fw.dve.op(lambda: nc.vector.tensor_tensor(out=T2[tk][0:M, 0:N], in0=self.ps(pb + 1, M, N), in1=ss[0:M, :], op=ALU.mult),
                              reads=[self.bps[pb + 1], bss], writes=[bT2[tk]])
                    fw.pool.op(lambda: nc.gpsimd.tensor_tensor(out=OB[ob][0:M, 0:N], in0=T1[tk][0:M, 0:N], in1=T2[tk][0:M, 0:N], op=ALU.add),
                               reads=[bT1[tk], bT2[tk]], writes=[bOB[ob]])
                elif act == "silu":
                    fw.act.op(lambda: nc.scalar.activation(out=OB[ob][0:M, 0:N], in_=self.ps(pb, M, N), func=AF.Silu),
                              reads=[self.bps[pb]], writes=[bOB[ob]])
                else:
                    fw.act.op(lambda: nc.scalar.copy(OB[ob][0:M, 0:N], self.ps(pb, M, N)), reads=[self.bps[pb]], writes=[bOB[ob]])
                fw.st.op(lambda: nc.gpsimd.dma_start(out=dst[row0:row0 + M, t0:t0 + N], in_=OB[ob][0:M, 0:N]),
                         reads=[bOB[ob]], writes=[bdst])

            for tt in range(self.NT):
                k = tt % 2
                t0 = tt * 512
                for c in range(4):
                    src, bsrc = self.xt_all_src(li, tt, c)
                    fw.ld.op(lambda: nc.sync.dma_start(out=XT[k][:, :, c * 128:(c + 1) * 128], in_=src.rearrange("(c p) t -> p c t", p=128)),
                             reads=[bsrc], writes=[bXT[k]])
                l0 = tt * TW
                ch, col = divmod(l0, self.CW)
                fw.ld.op(lambda: nc.sync.dma_start(out=XO[k][:], in_=self.xTo[sset][ch].ap()[:, col:col + TW].rearrange("(c p) t -> p c t", p=128)),
                         reads=[self.b_xTo[sset]], writes=[bXO[k]])
                if rope:
                    fw.ld.op(lambda: nc.sync.dma_start(out=CCt[k][:], in_=self.CC[:, t0:t0 + 512]), reads=[self.b_tab], writes=[bCC[k]])
                    fw.ld.op(lambda: nc.sync.dma_start(out=SSt[k][:], in_=self.SS[:, t0:t0 + 512]), reads=[self.b_tab], writes=[bSS[k]])
                    fw.ld.op(lambda: nc.sync.dma_start(out=CCq[k][:], in_=self.CCo[:, l0:l0 + TW]), reads=[self.b_tab], writes=[bCCq[k]])
                    fw.ld.op(lambda: nc.sync.dma_start(out=SSq[k][:], in_=self.SSo[:, l0:l0 + TW]), reads=[self.b_tab], writes=[bSSq[k]])
                for grp in kgroups:
                    do_group(grp, XT[k], bXT[k], 512, t0, CCt[k], bCC[k], SSt[k], bSS[k])
                for grp in qgroups:
                    do_group(grp, XO[k], bXO[k], TW, l0, CCq[k], bCCq[k], SSq[k], bSSq[k])
                for tb in range(4):
                    for half in range(2):
                        pb = 4 + half
                        for c in range(8):
                            fw.pe.op(lambda c=c: nc.tensor.matmul(self.ps(pb), lhsT=XT[k][:, c, tb * 128:(tb + 1) * 128],
                                                                  rhs=WIN[:, c, 2 * D + half * 512:2 * D + (half + 1) * 512],
                                                                  start=(c == 0), stop=(c == 7)),
                                     reads=[b_win, bXT[k]], writes=[self.bps[pb]], inc=(c == 7))
                        if half == 0:
                            fw.act.op(lambda: nc.scalar.copy(VT[k][:, tb, 0:512], self.ps(pb)), reads=[self.bps[pb]], writes=[bVT[k]])
                        else:
                            fw.dve.op(lambda: nc.vector.tensor_copy(VT[k][:, tb, 512:1024], self.ps(pb)), reads=[self.bps[pb]], writes=[bVT[k]])
                if m == 0:
                    for tb in range(2):
                        pb = 6
                        wk = (tt * 2 + tb) % 2
                        for c in range(8):
                            fw.pe.op(lambda c=c: nc.tensor.matmul(self.ps(pb, 128, 8), lhsT=XO[k][:, c, tb * 128:(tb + 1) * 128],
                                                                  rhs=WIN[:, c, 4 * D + 512:4 * D + 520], start=(c == 0), stop=(c == 7)),
                                     reads=[b_win, bXO[k]], writes=[self.bps[pb]], inc=(c == 7))
                        fw.dve.op(lambda: nc.vector.tensor_scalar(out=WIt[wk][:], in0=self.ps(pb, 128, 8), scalar1=float(8 ** -0.5 * 64 ** -0.5),
                                                                  scalar2=None, op0=ALU.mult), reads=[self.bps[pb]], writes=[bWI[wk]])
                        r0 = l0 + tb * 128
                        fw.st.op(lambda: nc.gpsimd.dma_start(out=self.WI[r0:r0 + 128, :], in_=WIt[wk][:]), reads=[bWI[wk]], writes=[self.b_WI])
                for h in range(NH):
                    fw.st.op(lambda h=h: nc.gpsimd.dma_start(out=self.Vs[h, :, tt * 4:(tt + 1) * 4, :], in_=VT[k][:, :, h * 64:(h + 1) * 64]),
                             reads=[bVT[k]], writes=[self.b_Vs])
            fw.barrier()

    def phase_c(self, li, cur, last):
        nc, fw = self.nc, self.fw
        first = li == 0
        src = self.x_own if first else self.xres[cur]
        bsrc = Buf() if first else self.b_xres[cur]
        dst = self.out if last else self.xres[cur ^ 1]
        bdst = Buf() if last else self.b_xres[cur ^ 1]
        wset = (li + 1) % 2
        with ExitStack() as es:
            sb = self.sbf(es)
            WO = sb("WO", [128, 8, D], BF16); b_wo = Buf()
            stg = [sb(f"wostg{i}", [128, D], F32) for i in range(2)]; bstg = [Buf(), Buf()]
            for c in range(8):
                k = c % 2
                fw.ld.op(lambda: nc.sync.dma_start(out=stg[k][:], in_=self.w_out[li][c * 128:(c + 1) * 128, :]), writes=[bstg[k]])
                fw.act.op(lambda: nc.scalar.copy(WO[:, c, :], stg[k][:]), reads=[bstg[k]], writes=[b_wo])
            G = sb("lnG", [128, D], F32); Bt = sb("lnB", [128, D], F32); b_gb = Buf()
            fw.ld.op(lambda: nc.sync.dma_start(out=G[:], in_=self.lng[li:li + 1, :].broadcast_to([128, D])), writes=[b_gb])
            fw.ld.op(lambda: nc.sync.dma_start(out=Bt[:], in_=self.lnb[li:li + 1, :].broadcast_to([128, D])), writes=[b_gb])
            OTt = [sb(f"pc_ot{i}", [128, 8, 512], BF16) for i in range(2)]; bOT = [Buf(), Buf()]
            XR = [sb(f"pc_x{i}", [128, D], F32) for i in range(2)]; bXR = [Buf(), Buf()]
            U = [sb(f"pc_u{i}", [128, D], F32) for i in range(2)]; bU = [Buf(), Buf()]
            XN = [sb(f"pc_xn{i}", [128, D], F32) for i in range(2)]; bXN = [Buf(), Buf()]
            SQ = sb("pc_sq", [128, D], F32); bSQ = Buf()
            ST = [sb(f"pc_st{i}", [128, 8], F32) for i in range(2)]; bST = [Buf(), Buf()]
            XTT = [sb(f"pc_xtt{i}", [128, 8, 128], BF16) for i in range(2)]; bXTT = [Buf(), Buf()]
            for tt in range(self.SO // 512):
                kk = tt % 2
                t0 = tt * 512
                fw.ld.op(lambda: nc.sync.dma_start(out=OTt[kk][:], in_=self.OT[:, t0:t0 + 512].rearrange("(c p) t -> p c t", p=128)),
                         reads=[self.b_OT], writes=[bOT[kk]])
                for tb4 in range(4):
                    tb = tt * 4 + tb4
                    k = tb % 2
                    fw.ld.op(lambda: nc.sync.dma_start(out=XR[k][:], in_=src[tb * 128:(tb + 1) * 128, :]), reads=[bsrc], writes=[bXR[k]])
                    pbs = (4 * k, 4 * k + 1)
                    for half in range(2):
                        for c in range(8):
                            fw.pe.op(lambda c=c, half=half: nc.tensor.matmul(self.ps(pbs[half]), lhsT=OTt[kk][:, c, tb4 * 128:(tb4 + 1) * 128],
                                                                             rhs=WO[:, c, half * 512:(half + 1) * 512], start=(c == 0), stop=(c == 7)),
                                     reads=[b_wo, bOT[kk]], writes=[self.bps[pbs[half]]], inc=(c == 7))
                    st = ST[k]
                    for half in range(2):
                        hs = slice(half * 512, (half + 1) * 512)
                        fw.dve.op(lambda half=half, hs=hs: nc.vector.scalar_tensor_tensor(out=U[k][:, hs], in0=XR[k][:, hs], scalar=float(ALPHA),
                                                                                          in1=self.ps(pbs[half]), op0=ALU.mult, op1=ALU.add),
                                  reads=[bXR[k], self.bps[pbs[half]]], writes=[bU[k]])
                    fw.act.op(lambda: nc.scalar.activation(out=SQ[:], in_=U[k][:], func=AF.Copy, accum_out=st[:, 0:1]), reads=[bU[k]], writes=[bSQ, bST[k]])
                    fw.act.op(lambda: nc.scalar.activation(out=SQ[:], in_=U[k][:], func=AF.Square, accum_out=st[:, 1:2]), reads=[bU[k]], writes=[bSQ, bST[k]])
                    fw.dve.op(lambda: nc.vector.tensor_scalar(out=st[:, 2:3], in0=st[:, 0:1], scalar1=float(1.0 / D), scalar2=None, op0=ALU.mult),
                              reads=[bST[k]], writes=[bST[k]])
                    fw.dve.op(lambda: nc.vector.tensor_tensor(out=st[:, 3:4], in0=st[:, 2:3], in1=st[:, 2:3], op=ALU.mult), reads=[bST[k]], writes=[bST[k]])
                    fw.dve.op(lambda: nc.vector.scalar_tensor_tensor(out=st[:, 4:5], in0=st[:, 1:2], scalar=float(1.0 / D), in1=st[:, 3:4],
                                                                     op0=ALU.mult, op1=ALU.subtract), reads=[bST[k]], writes=[bST[k]])
                    fw.act.op(lambda: nc.scalar.activation(out=st[:, 5:6], in_=st[:, 4:5], func=AF.Sqrt, bias=self.cst[:, 0:1], scale=1.0),
                              reads=[bST[k], self.b_cst], writes=[bST[k]])
                    fw.dve.op(lambda: nc.vector.reciprocal(st[:, 5:6], st[:, 5:6]), reads=[bST[k]], writes=[bST[k]])
                    fw.dve.op(lambda: nc.vector.tensor_scalar(out=XN[k][:], in0=U[k][:], scalar1=st[:, 2:3], scalar2=st[:, 5:6], op0=ALU.subtract, op1=ALU.mult),
                              reads=[bU[k], bST[k]], writes=[bXN[k]])
                    fw.pool.op(lambda: nc.gpsimd.tensor_tensor(out=XN[k][:], in0=XN[k][:], in1=G[:], op=ALU.mult), reads=[bXN[k], b_gb], writes=[bXN[k]])
                    fw.pool.op(lambda: nc.gpsimd.tensor_tensor(out=XN[k][:], in0=XN[k][:], in1=Bt[:], op=ALU.add), reads=[bXN[k], b_gb], writes=[bXN[k]])
                    fw.st.op(lambda: nc.gpsimd.dma_start(out=dst[tb * 128:(tb + 1) * 128, :], in_=XN[k][:]), reads=[bXN[k]], writes=[bdst])
                    if not last:
                        self.emit_transpose_store(XN[k], bXN[k], self.xto_dst(wset, tb), self.b_xTo[wset], XTT[k], bXTT[k], (4 * k + 2, 4 * k + 3))
            fw.barrier()

    def build_masks(self, sb, kind):
        nc, fw = self.nc, self.fw
        Ms = [sb(f"mask_{kind}{j}", [128, TW], BF16) for j in range(4)]
        b = Buf()
        for j in range(4):
            if kind == "before":
                fw.dve.op(lambda j=j: nc.vector.tensor_scalar(out=Ms[j][:], in0=self.QR[:], scalar1=float(-128 * j), scalar2=self.cst[:, 1:2],
                                                              op0=ALU.add, op1=ALU.is_gt), reads=[self.b_cst], writes=[b])
            else:
                fw.dve.op(lambda j=j: nc.vector.tensor_scalar(out=Ms[j][:], in0=self.QR[:], scalar1=float(-128 * j), scalar2=self.cst[:, 2:3],
                                                              op0=ALU.add, op1=ALU.is_ge), reads=[self.b_cst], writes=[b])
        return Ms, b

    def attn_diff(self, L):
        nc, fw, S, NB, NT = self.nc, self.fw, self.S, self.NB, self.NT
        lambda_init = 0.8 - 0.6 * math.exp(-0.3 * L)
        with ExitStack() as es:
            sb = self.sbf(es)
            lv = sb("lv", [128, 4, 64], F32); b_lv = Buf()
            fw.ld.op(lambda: nc.sync.dma_start(out=lv[:].rearrange("p a b -> p (a b)"), in_=self.lam[0:1, :].broadcast_to([128, 256])), writes=[b_lv])
            lt = sb("lt", [128, 8], F32); b_lt = Buf()
            pr = sb("lpr", [128, 2, 64], F32)
            fw.dve.op(lambda: nc.vector.tensor_tensor(out=pr[:, 0, :], in0=lv[:, 0, :], in1=lv[:, 1, :], op=ALU.mult), reads=[b_lv], writes=[b_lt])
            fw.dve.op(lambda: nc.vector.tensor_tensor(out=pr[:, 1, :], in0=lv[:, 2, :], in1=lv[:, 3, :], op=ALU.mult), reads=[b_lv], writes=[b_lt])
            fw.dve.op(lambda: nc.vector.reduce_sum(out=lt[:, 0:1], in_=pr[:, 0, :], axis=mybir.AxisListType.X), reads=[b_lt], writes=[b_lt])
            fw.dve.op(lambda: nc.vector.reduce_sum(out=lt[:, 1:2], in_=pr[:, 1, :], axis=mybir.AxisListType.X), reads=[b_lt], writes=[b_lt])
            fw.act.op(lambda: nc.scalar.activation(out=lt[:, 2:4], in_=lt[:, 0:2], func=AF.Exp), reads=[b_lt], writes=[b_lt])
            fw.dve.op(lambda: nc.vector.tensor_tensor(out=lt[:, 4:5], in0=lt[:, 3:4], in1=lt[:, 2:3], op=ALU.subtract), reads=[b_lt], writes=[b_lt])
            fw.dve.op(lambda: nc.vector.tensor_scalar(out=lt[:, 5:6], in0=lt[:, 4:5], scalar1=float(-lambda_init), scalar2=None, op0=ALU.add),
                      reads=[b_lt], writes=[b_lt])
            sg = sb("sg", [128, 2], F32); b_sg = Buf()
            fw.ld.op(lambda: nc.sync.dma_start(out=sg[:, 0:1], in_=self.subg[:, :]), writes=[b_sg])
            fw.dve.op(lambda: nc.vector.tensor_scalar(out=sg[:, 1:2], in0=sg[:, 0:1], scalar1=float(1.0 - lambda_init), scalar2=None, op0=ALU.mult),
                      reads=[b_sg], writes=[b_sg])
            ones_b = sb("ones_b", [128, 128], BF16); ones_f = sb("ones_f", [128, 128], F32); b_ones = Buf()
            fw.pool.op(lambda: nc.gpsimd.memset(ones_b[:], 1.0), writes=[b_ones])
            fw.pool.op(lambda: nc.gpsimd.memset(ones_f[:], 1.0), writes=[b_ones])
            CM, b_cm = self.build_masks(sb, "chunk")
            KTt = [sb(f"df_k{i}", [64, 2, S], BF16) for i in range(2)]; bK = [Buf(), Buf()]
            Vt = [sb(f"df_v{i}", [128, NB, 128], BF16) for i in range(2)]; bV = [Buf(), Buf()]
            Qt = [sb(f"df_q{i}", [64, 2, TW], BF16) for i in range(3)]; bQ = [Buf() for _ in range(3)]
            Gt = [sb(f"df_g{i}", [128, TW], BF16) for i in range(3)]; bG = [Buf() for _ in range(3)]
            P = [sb(f"df_p{i}", [128, TW], BF16) for i in range(4)]; bP = [Buf() for _ in range(4)]
            PF = [sb(f"df_pf{i}", [128, TW], F32) for i in range(2)]; bPF = [Buf(), Buf()]
            R0 = sb("df_r0", [128, TW], F32); R1 = sb("df_r1", [128, TW], F32); bR = [Buf(), Buf()]
            O0 = sb("df_o0", [128, TW], F32); O1 = sb("df_o1", [128, TW], F32); bO = [Buf(), Buf()]
            SQ = sb("df_sq", [128, TW], F32); bSQ = Buf()
            RS = sb("df_rs", [128, TW], F32); bRS = Buf()
            OG = [sb(f"df_og{i}", [128, TW], BF16) for i in range(2)]; bOG = [Buf(), Buf()]
            N = TW
            sctr = [0]
            pctr = [0]
            items = []
            loaders = []
            for hd in range(8):
                hk = hd % 2

                def load_head(hd=hd, hk=hk):
                    for mm in range(2):
                        fw.ld.op(lambda mm=mm: nc.sync.dma_start(out=KTt[hk][:, mm, :], in_=self.KT[(2 * hd + mm) * 64:(2 * hd + mm + 1) * 64, :]),
                                 reads=[self.b_KT], writes=[bK[hk]])
                        fw.ld.op(lambda mm=mm: nc.sync.dma_start(out=Vt[hk][:, :, mm * 64:(mm + 1) * 64], in_=self.Vs[2 * hd + mm]),
                                 reads=[self.b_Vs], writes=[bV[hk]])
                for tt in range(NT):
                    qk = (hd * NT + tt) % 3
                    t0 = tt * TW
                    nkb = 4 * tt + 4

                    def load_q(hd=hd, qk=qk, t0=t0, tt=tt, hk=hk, lh=load_head):
                        if tt == 0:
                            lh()
                        for mm in range(2):
                            fw.ld.op(lambda mm=mm: nc.sync.dma_start(out=Qt[qk][:, mm, :], in_=self.QT[(2 * hd + mm) * 64:(2 * hd + mm + 1) * 64, t0:t0 + TW]),
                                     reads=[self.b_QT], writes=[bQ[qk]])
                        fw.ld.op(lambda: nc.sync.dma_start(out=Gt[qk][:], in_=self.GT[hd * 128:(hd + 1) * 128, t0:t0 + TW]),
                                 reads=[self.b_GT], writes=[bG[qk]])
                    for kb in range(nkb):
                        j = kb - 4 * tt
                        sbk = []
                        pidx = []
                        for mm in range(2):
                            sbk.append(sctr[0] % 3); sctr[0] += 1
                            pidx.append(pctr[0] % 4); pctr[0] += 1
                        if kb == 0:
                            loaders.append(load_q)
                        tidx = len(loaders) - 1

                        def st0(hk=hk, qk=qk, kb=kb, sbk=sbk, first=(kb == 0), tidx=tidx):
                            if first:
                                if tidx == 0:
                                    loaders[0]()
                                if tidx + 1 < len(loaders):
                                    loaders[tidx + 1]()
                            for mm in range(2):
                                fw.pe.op(lambda mm=mm: nc.tensor.matmul(self.ps(sbk[mm], 128, N), lhsT=KTt[hk][:, mm, kb * 128:(kb + 1) * 128], rhs=Qt[qk][:, mm, :],
                                                                        start=True, stop=True),
                                         reads=[bK[hk], bQ[qk]], writes=[self.bps[sbk[mm]]])

                        def st1(sbk=sbk, pidx=pidx, j=j):
                            for mm in range(2):
                                if j >= 0:
                                    fw.act.op(lambda mm=mm: nc.scalar.activation(out=PF[mm][:], in_=self.ps(sbk[mm], 128, N), func=AF.Exp, scale=0.125),
                                              reads=[self.bps[sbk[mm]]], writes=[bPF[mm]])
                                    fw.dve.op(lambda mm=mm: nc.vector.tensor_tensor(out=P[pidx[mm]][:], in0=PF[mm][:], in1=CM[j][:], op=ALU.mult),
                                              reads=[bPF[mm], b_cm], writes=[bP[pidx[mm]]])
                                else:
                                    fw.act.op(lambda mm=mm: nc.scalar.activation(out=P[pidx[mm]][:], in_=self.ps(sbk[mm], 128, N), func=AF.Exp, scale=0.125),
                                              reads=[self.bps[sbk[mm]]], writes=[bP[pidx[mm]]])

                        def st2(hk=hk, kb=kb, pidx=pidx, nkb=nkb, hd=hd, tt=tt, qk=qk, t0=t0):
                            for mm in range(2):
                                fw.pe.op(lambda mm=mm: nc.tensor.matmul(self.ps(3 + mm, 128, N), lhsT=Vt[hk][:, kb, :], rhs=P[pidx[mm]][:],
                                                                        start=(kb == 0), stop=(kb == nkb - 1)),
                                         reads=[bV[hk], bP[pidx[mm]]], writes=[self.bps[3 + mm]])
                                fw.pe.op(lambda mm=mm: nc.tensor.matmul(self.ps(5 + mm, 128, N), lhsT=ones_b[:], rhs=P[pidx[mm]][:],
                                                                        start=(kb == 0), stop=(kb == nkb - 1)),
                                         reads=[b_ones, bP[pidx[mm]]], writes=[self.bps[5 + mm]])
                            if kb == nkb - 1:
                                ok = (hd * NT + tt) % 2
                                fw.dve.op(lambda: nc.vector.reciprocal(R0[:], self.ps(5, 128, N)), reads=[self.bps[5]], writes=[bR[0]])
                                fw.dve.op(lambda: nc.vector.reciprocal(R1[:], self.ps(6, 128, N)), reads=[self.bps[6]], writes=[bR[1]])
                                fw.dve.op(lambda: nc.vector.tensor_tensor(out=O0[:], in0=self.ps(3, 128, N), in1=R0[:], op=ALU.mult), reads=[self.bps[3], bR[0]], writes=[bO[0]])
                                fw.dve.op(lambda: nc.vector.tensor_tensor(out=O1[:], in0=self.ps(4, 128, N), in1=R1[:], op=ALU.mult), reads=[self.bps[4], bR[1]], writes=[bO[1]])
                                fw.dve.op(lambda: nc.vector.scalar_tensor_tensor(out=O0[:], in0=O1[:], scalar=lt[:, 5:6], in1=O0[:], op0=ALU.mult, op1=ALU.add),
                                          reads=[bO[0], bO[1], b_lt], writes=[bO[0]])
                                fw.act.op(lambda: nc.scalar.activation(out=SQ[:], in_=O0[:], func=AF.Square), reads=[bO[0]], writes=[bSQ])
                                fw.pe.op(lambda: nc.tensor.matmul(self.ps(7, 128, N), lhsT=ones_f[:], rhs=SQ[:], start=True, stop=True),
                                         reads=[b_ones, bSQ], writes=[self.bps[7]])
                                fw.act.op(lambda: nc.scalar.activation(out=RS[:], in_=self.ps(7, 128, N), func=AF.Sqrt, bias=self.cst[:, 0:1], scale=float(1.0 / 128)),
                                          reads=[self.bps[7], self.b_cst], writes=[bRS])
                                fw.dve.op(lambda: nc.vector.reciprocal(RS[:], RS[:]), reads=[bRS], writes=[bRS])
                                fw.dve.op(lambda: nc.vector.scalar_tensor_tensor(out=O0[:], in0=O0[:], scalar=sg[:, 1:2], in1=RS[:], op0=ALU.mult, op1=ALU.mult),
                                          reads=[bO[0], bRS, b_sg], writes=[bO[0]])
                                fw.pool.op(lambda: nc.gpsimd.tensor_tensor(out=OG[ok][:], in0=O0[:], in1=Gt[qk][:], op=ALU.mult), reads=[bO[0], bG[qk]], writes=[bOG[ok]])
                                fw.st.op(lambda: nc.gpsimd.dma_start(out=self.OT[hd * 128:(hd + 1) * 128, t0:t0 + TW], in_=OG[ok][:]),
                                         reads=[bOG[ok]], writes=[self.b_OT])
                        items.append([st0, st1, st2])
            pipeline(items, 3)
            fw.barrier()

    def attn_sb(self):
        nc, fw, S, NB, NT = self.nc, self.fw, self.S, self.NB, self.NT
        N = TW
        with ExitStack() as es:
            sb = self.sbf(es)
            b_c = Buf()
            BEF, b_bef = self.build_masks(sb, "before")
            n8 = sb("sb_n8", [128, 128], BF16); tri = sb("sb_tri", [128, 128], BF16)
            fw.pool.op(lambda: nc.gpsimd.memset(n8[:], -8.0), writes=[b_c])
            fw.pool.op(lambda: nc.gpsimd.affine_select(out=tri[:], in_=n8[:], pattern=[[-1, 128]], compare_op=ALU.is_ge, fill=0.0,
                                                       base=0, channel_multiplier=1), reads=[b_c], writes=[b_c])
            one1 = self.cst[:, 3:4]
            KTt = [sb(f"sb_k{i}", [64, S], BF16) for i in range(2)]; bK = [Buf(), Buf()]
            Vt = [sb(f"sb_v{i}", [128, NB, 64], BF16) for i in range(2)]; bV = [Buf(), Buf()]
            Qt = [sb(f"sb_q{i}", [64, N], BF16) for i in range(3)]; bQ = [Buf() for _ in range(3)]
            Gt = [sb(f"sb_g{i}", [64, N], BF16) for i in range(3)]; bG = [Buf() for _ in range(3)]
            E1 = [sb(f"sb_e{i}", [128, N], F32) for i in range(2)]; bE = [Buf(), Buf()]
            SPf = sb("sb_spf", [128, N], F32); bSPf = Buf()
            SP = [sb(f"sb_sp{i}", [128, N], BF16) for i in range(3)]; bSP = [Buf() for _ in range(3)]
            AC = [sb(f"sb_ac{i}", [128, N], BF16) for i in range(3)]; bAC = [Buf() for _ in range(3)]
            A = [sb(f"sb_a{i}", [128, N], BF16) for i in range(3)]; bA = [Buf() for _ in range(3)]
            Af = sb("sb_af", [128, N], F32); bAf = Buf()
            OG = [sb(f"sb_og{i}", [64, N], BF16) for i in range(2)]; bOG = [Buf(), Buf()]
            items = []
            loaders = []
            ic = 0
            tile_i = 0
            for h in range(NH):
                hk = h % 2

                def load_head(h=h, hk=hk):
                    fw.ld.op(lambda: nc.sync.dma_start(out=KTt[hk][:], in_=self.KT[h * 64:(h + 1) * 64, :]), reads=[self.b_KT], writes=[bK[hk]])
                    fw.ld.op(lambda: nc.sync.dma_start(out=Vt[hk][:], in_=self.Vs[h]), reads=[self.b_Vs], writes=[bV[hk]])
                for tt in range(NT):
                    qk = tile_i % 3
                    ob = 5 + tile_i % 2
                    ogk = tile_i % 2
                    tile_i += 1
                    t0 = tt * TW
                    hi = 4 * tt + 3
                    lo = max(0, 4 * tt - SB_WIN)
                    kbs = list(range(hi, lo - 1, -1))

                    def load_q(h=h, qk=qk, t0=t0, tt=tt, lh=load_head):
                        if tt == 0:
                            lh()
                        fw.ld.op(lambda: nc.sync.dma_start(out=Qt[qk][:], in_=self.QT[h * 64:(h + 1) * 64, t0:t0 + N]), reads=[self.b_QT], writes=[bQ[qk]])
                        fw.ld.op(lambda: nc.sync.dma_start(out=Gt[qk][:], in_=self.GT[h * 64:(h + 1) * 64, t0:t0 + N]), reads=[self.b_GT], writes=[bG[qk]])
                    loaders.append(load_q)
                    tidx = len(loaders) - 1
                    prev_acc = None
                    for n, kb in enumerate(kbs):
                        j = kb - 4 * tt
                        diag = j >= 0
                        sbank = ic % 5
                        spk = ic % 3
                        ek = ic % 2
                        ic += 1
                        first = n == 0
                        lastk = n == len(kbs) - 1
                        pacc = prev_acc
                        acc_out = (SP[spk], bSP[spk]) if first else (AC[spk], bAC[spk])
                        prev_acc = acc_out

                        def st0(hk=hk, qk=qk, kb=kb, sbank=sbank, first=first, tidx=tidx):
                            if first:
                                if tidx == 0:
                                    loaders[0]()
                                if tidx + 1 < len(loaders):
                                    loaders[tidx + 1]()
                            fw.pe.op(lambda: nc.tensor.matmul(self.ps(sbank, 128, N), lhsT=KTt[hk][:, kb * 128:(kb + 1) * 128], rhs=Qt[qk][:], start=True, stop=False),
                                     reads=[bK[hk], bQ[qk]], writes=[self.bps[sbank]])

                        def st1(sbank=sbank, spk=spk, ek=ek, diag=diag, j=j, first=first, pacc=pacc, acc_out=acc_out):
                            fw.act.op(lambda: nc.scalar.activation(out=E1[ek][:], in_=self.ps(sbank, 128, N), func=AF.Exp, scale=0.125),
                                      reads=[self.bps[sbank]], writes=[bE[ek]])
                            if diag:
                                fw.act.op(lambda: nc.scalar.activation(out=SPf[:], in_=E1[ek][:], func=AF.Ln, bias=one1, scale=1.0),
                                          reads=[bE[ek], self.b_cst], writes=[bSPf])
                                fw.dve.op(lambda: nc.vector.tensor_tensor(out=SP[spk][:], in0=SPf[:], in1=BEF[j][:], op=ALU.mult),
                                          reads=[bSPf, b_bef], writes=[bSP[spk]])
                            else:
                                fw.act.op(lambda: nc.scalar.activation(out=SP[spk][:], in_=E1[ek][:], func=AF.Ln, bias=one1, scale=1.0),
                                          reads=[bE[ek], self.b_cst], writes=[bSP[spk]])
                            if not first:
                                fw.pool.op(lambda: nc.gpsimd.tensor_tensor(out=acc_out[0][:], in0=pacc[0][:], in1=SP[spk][:], op=ALU.add),
                                           reads=[pacc[1], bSP[spk]], writes=[acc_out[1]])

                        def st2(sbank=sbank, spk=spk, first=first, pacc=pacc):
                            fw.pe.op(lambda: nc.tensor.matmul(self.ps(sbank, 128, N), lhsT=tri[:], rhs=SP[spk][:], start=False, stop=first),
                                     reads=[b_c, bSP[spk]], writes=[self.bps[sbank]])
                            if not first:
                                fw.pe.op(lambda: nc.tensor.matmul(self.ps(sbank, 128, N), lhsT=n8[:], rhs=pacc[0][:], start=False, stop=True),
                                         reads=[b_c, pacc[1]], writes=[self.bps[sbank]])

                        def st3(sbank=sbank, spk=spk, diag=diag, j=j):
                            if diag:
                                fw.act.op(lambda: nc.scalar.activation(out=Af[:], in_=self.ps(sbank, 128, N), func=AF.Exp, scale=0.125),
                                          reads=[self.bps[sbank]], writes=[bAf])
                                fw.dve.op(lambda: nc.vector.tensor_tensor(out=A[spk][:], in0=Af[:], in1=BEF[j][:], op=ALU.mult),
                                          reads=[bAf, b_bef], writes=[bA[spk]])
                            else:
                                fw.act.op(lambda: nc.scalar.activation(out=A[spk][:], in_=self.ps(sbank, 128, N), func=AF.Exp, scale=0.125),
                                          reads=[self.bps[sbank]], writes=[bA[spk]])

                        def st4(hk=hk, kb=kb, spk=spk, first=first, lastk=lastk, ob=ob, ogk=ogk, qk=qk, h=h, t0=t0):
                            fw.pe.op(lambda: nc.tensor.matmul(self.ps(ob, 64, N), lhsT=Vt[hk][:, kb, :], rhs=A[spk][:], start=first, stop=lastk),
                                     reads=[bV[hk], bA[spk]], writes=[self.bps[ob]])
                            if lastk:
                                fw.dve.op(lambda: nc.vector.tensor_tensor(out=OG[ogk][:], in0=self.ps(ob, 64, N), in1=Gt[qk][:], op=ALU.mult),
                                          reads=[self.bps[ob], bG[qk]], writes=[bOG[ogk]])
                                fw.st.op(lambda: nc.gpsimd.dma_start(out=self.OT[h * 64:(h + 1) * 64, t0:t0 + N], in_=OG[ogk][:]),
                                         reads=[bOG[ogk]], writes=[self.b_OT])
                        items.append([st0, st1, st2, st3, st4])
            pipeline(items, 5)
            fw.barrier()

    def attn_dsa(self):
        nc, fw, S, NB, NT = self.nc, self.fw, self.S, self.NB, self.NT
        N = TW
        with ExitStack() as es:
            sb = self.sbf(es)
            b_c = Buf()
            kcb = sb("ds_kcb", [128, 128], F32)
            fw.pool.op(lambda: nc.gpsimd.memset(kcb[:, 0:64], 0.0), writes=[b_c])
            fw.pool.op(lambda: nc.gpsimd.memset(kcb[:, 64:128], 64.0), writes=[b_c])
            ones_b = sb("ds_ones", [128, 64], BF16)
            fw.pool.op(lambda: nc.gpsimd.memset(ones_b[:], 1.0), writes=[b_c])
            kiT = sb("ds_ki", [64, S], BF16); b_ki = Buf()
            fw.ld.op(lambda: nc.sync.dma_start(out=kiT[:], in_=self.KT[D:D + 64, :]), reads=[self.b_KT], writes=[b_ki])
            SC = sb("ds_sc", [128, S], F32); bSC = Buf()
            JK = sb("ds_jk", [128, S], BF16); bJK = Buf()
            MT = sb("ds_mt", [128, NB, N], BF16); bMT = Buf()
            QI = [sb(f"ds_qi{i}", [64, 8, 128], BF16) for i in range(2)]; bQI = [Buf(), Buf()]
            WQ = [sb(f"ds_wq{i}", [128, 8], F32) for i in range(2)]; bWQ = [Buf(), Buf()]
            PS_ = [sb(f"ds_pos{i}", [128, 2], F32) for i in range(2)]; bPS = [Buf(), Buf()]
            PM = sb("ds_pm", [128, 128], F32); bPM = Buf()
            RL = [sb(f"ds_rl{i}", [128, 512], F32) for i in range(3)]; bRL = [Buf() for _ in range(3)]
            BS = sb("ds_bs", [128, 8], F32); bBS = Buf()
            MK = [sb(f"ds_mk{i}", [128, 512], F32) for i in range(2)]; bMK = [Buf(), Buf()]
            KTt = [sb(f"ds_k{i}", [64, S], BF16) for i in range(2)]; bK = [Buf(), Buf()]
            Vt = [sb(f"ds_v{i}", [128, NB, 64], BF16) for i in range(3)]; bV = [Buf() for _ in range(3)]
            Qt = [sb(f"ds_q{i}", [64, N], BF16) for i in range(3)]; bQ = [Buf() for _ in range(3)]
            Gt = [sb(f"ds_g{i}", [64, N], BF16) for i in range(3)]; bG = [Buf() for _ in range(3)]
            PF = [sb(f"ds_pf{i}", [128, N], BF16) for i in range(2)]; bPF = [Buf(), Buf()]
            P = [sb(f"ds_p{i}", [128, N], BF16) for i in range(3)]; bP = [Buf() for _ in range(3)]
            Rr = sb("ds_r", [64, N], F32); bRr = Buf()
            O1 = sb("ds_o1", [64, N], F32); bO1 = Buf()
            OG = [sb(f"ds_og{i}", [64, N], BF16) for i in range(2)]; bOG = [Buf(), Buf()]
            rlc = 0
            qic = 0
            psc = 0
            tile_i = 0
            for tt in range(NT):
                t0 = tt * N
                for i2 in range(2):
                    lb = 2 * tt + i2
                    nk = 4 * tt + (2 if i2 == 0 else 4)
                    ncols_tot = nk * 128
                    qk = qic % 2
                    qic += 1
                    fw.ld.op(lambda: nc.sync.dma_start(out=QI[qk][:], in_=self.QT[D:D + 512, lb * 128:(lb + 1) * 128].rearrange("(h d) t -> d h t", d=64)),
                             reads=[self.b_QT], writes=[bQI[qk]])
                    fw.ld.op(lambda: nc.sync.dma_start(out=WQ[qk][:], in_=self.WI[lb * 128:(lb + 1) * 128, :]), reads=[self.b_WI], writes=[bWQ[qk]])
                    fw.ld.op(lambda: nc.sync.dma_start(out=PS_[qk][:, 0:1], in_=self.poscol[lb * 128:(lb + 1) * 128, :]), writes=[bPS[qk]])
                    for c0 in range(0, ncols_tot, 512):
                        ncol = min(512, ncols_tot - c0)
                        for hh in range(8):
                            pb = psc % 3
                            psc += 1
                            rk = rlc % 3
                            rlc += 1
                            fw.pe.op(lambda: nc.tensor.matmul(self.ps(pb, 128, ncol), lhsT=QI[qk][:, hh, :], rhs=kiT[:, c0:c0 + ncol], start=True, stop=True),
                                     reads=[bQI[qk], b_ki], writes=[self.bps[pb]])
                            fw.act.op(lambda: nc.scalar.activation(out=RL[rk][:, 0:ncol], in_=self.ps(pb, 128, ncol), func=AF.Relu),
                                      reads=[self.bps[pb]], writes=[bRL[rk]])
                            if hh == 0:
                                fw.dve.op(lambda: nc.vector.tensor_scalar(out=SC[:, c0:c0 + ncol], in0=RL[rk][:, 0:ncol], scalar1=WQ[qk][:, 0:1], scalar2=None,
                                                                          op0=ALU.mult), reads=[bRL[rk], bWQ[qk]], writes=[bSC])
                            else:
                                fw.dve.op(lambda: nc.vector.scalar_tensor_tensor(out=SC[:, c0:c0 + ncol], in0=RL[rk][:, 0:ncol], scalar=WQ[qk][:, hh:hh + 1],
                                                                                 in1=SC[:, c0:c0 + ncol], op0=ALU.mult, op1=ALU.add),
                                          reads=[bRL[rk], bWQ[qk], bSC], writes=[bSC])
                    for kb in range(4 * tt, nk):
                        dsl = slice(kb * 128, (kb + 1) * 128)
                        fw.dve.op(lambda kb=kb: nc.vector.tensor_scalar(out=PS_[qk][:, 1:2], in0=PS_[qk][:, 0:1], scalar1=float(-128 * kb), scalar2=None, op0=ALU.add),
                                  reads=[bPS[qk]], writes=[bPS[qk]])
                        fw.dve.op(lambda: nc.vector.tensor_scalar(out=PM[:], in0=kcb[:], scalar1=PS_[qk][:, 1:2], scalar2=-1e30, op0=ALU.is_gt, op1=ALU.mult),
                                  reads=[b_c, bPS[qk]], writes=[bPM])
                        fw.dve.op(lambda dsl=dsl: nc.vector.tensor_tensor(out=SC[:, dsl], in0=SC[:, dsl], in1=PM[:], op=ALU.add), reads=[bSC, bPM], writes=[bSC])
                    fw.dve.op(lambda: nc.vector.memset(BS[:, 0:1], -BIS_R), writes=[bBS])
                    fw.dve.op(lambda: nc.vector.memset(BS[:, 1:2], 0.0), writes=[bBS])
                    hstep = BIS_R
                    for it in range(BIS_ITERS):
                        fw.dve.op(lambda: nc.vector.tensor_scalar(out=JK[:, 0:ncols_tot], in0=SC[:, 0:ncols_tot], scalar1=BS[:, 1:2], scalar2=0.0,
                                                                  op0=ALU.is_ge, op1=ALU.add, accum_out=BS[:, 2:3]), reads=[bSC, bBS], writes=[bJK, bBS])
                        fw.dve.op(lambda: nc.vector.tensor_scalar(out=BS[:, 3:4], in0=BS[:, 2:3], scalar1=float(TOPK) - 0.5, scalar2=float(hstep),
                                                                  op0=ALU.is_ge, op1=ALU.mult), reads=[bBS], writes=[bBS])
                        fw.dve.op(lambda: nc.vector.tensor_tensor(out=BS[:, 0:1], in0=BS[:, 0:1], in1=BS[:, 3:4], op=ALU.add), reads=[bBS], writes=[bBS])
                        hstep = hstep / 2
                        fw.dve.op(lambda: nc.vector.tensor_scalar(out=BS[:, 1:2], in0=BS[:, 0:1], scalar1=float(hstep), scalar2=None, op0=ALU.add),
                                  reads=[bBS], writes=[bBS])
                    for c0 in range(0, ncols_tot, 512):
                        ncol = min(512, ncols_tot - c0)
                        mk = (c0 // 512) % 2
                        pb = 3 + (c0 // 512) % 2
                        fw.dve.op(lambda: nc.vector.tensor_scalar(out=MK[mk][:, 0:ncol], in0=SC[:, c0:c0 + ncol], scalar1=BS[:, 0:1], scalar2=None, op0=ALU.is_ge),
                                  reads=[bSC, bBS], writes=[bMK[mk]])
                        nb4 = ncol // 128
                        for b4 in range(nb4):
                            fw.pe.op(lambda b4=b4: nc.tensor.transpose(self.ps(pb)[:, b4 * 128:(b4 + 1) * 128], MK[mk][:, b4 * 128:(b4 + 1) * 128], self.ident[:]),
                                     reads=[bMK[mk], self.b_ident], writes=[self.bps[pb]], inc=(b4 == nb4 - 1))
                        kb0 = c0 // 128
                        fw.act.op(lambda: nc.scalar.copy(MT[:, kb0:kb0 + nb4, i2 * 128:(i2 + 1) * 128],
                                                         self.ps(pb, 128, ncol).rearrange("p (b t) -> p b t", t=128)),
                                  reads=[self.bps[pb]], writes=[bMT])
                    if nk < 4 * tt + 4:
                        fw.pool.op(lambda: nc.gpsimd.memset(MT[:, nk:4 * tt + 4, i2 * 128:(i2 + 1) * 128], 0.0), writes=[bMT])
                    if self.debug and tt == 0 and i2 == 1:
                        fw.st.op(lambda: nc.gpsimd.dma_start(out=self.dbgSC[:, :], in_=SC[:, 0:512]), reads=[bSC])
                        fw.st.op(lambda: nc.gpsimd.dma_start(out=self.dbgBS[:, :], in_=BS[:]), reads=[bBS])
                        fw.st.op(lambda: nc.gpsimd.dma_start(out=self.dbgMT[:, :, :], in_=MT[:, 0:4, :]), reads=[bMT])
                nkb = 4 * tt + 4
                items = []
                loaders = []
                ic = 0
                for h in range(NH):
                    hk = h % 2
                    qk = tile_i % 3
                    ogk = tile_i % 2
                    tile_i += 1

                    vk = h % 3

                    def load_q(h=h, qk=qk, hk=hk, vk=vk, nkb=nkb):
                        fw.ld.op(lambda: nc.sync.dma_start(out=KTt[hk][:, 0:nkb * 128], in_=self.KT[h * 64:(h + 1) * 64, 0:nkb * 128]), reads=[self.b_KT], writes=[bK[hk]])
                        fw.ld.op(lambda: nc.sync.dma_start(out=Vt[vk][:, 0:nkb, :], in_=self.Vs[h, :, 0:nkb, :]), reads=[self.b_Vs], writes=[bV[vk]])
                        fw.ld.op(lambda: nc.sync.dma_start(out=Qt[qk][:], in_=self.QT[h * 64:(h + 1) * 64, t0:t0 + N]), reads=[self.b_QT], writes=[bQ[qk]])
                        fw.ld.op(lambda: nc.sync.dma_start(out=Gt[qk][:], in_=self.GT[h * 64:(h + 1) * 64, t0:t0 + N]), reads=[self.b_GT], writes=[bG[qk]])
                    loaders.append(load_q)
                    for kb in range(nkb):
                        sbank = ic % 3
                        pk = ic % 3
                        fk = ic % 2
                        ic += 1
                        first = kb == 0
                        lastk = kb == nkb - 1

                        def st0(h=h, hk=hk, qk=qk, kb=kb, sbank=sbank, first=first):
                            if first:
                                if h == 0:
                                    loaders[0]()
                                if h + 1 < NH:
                                    loaders[h + 1]()
                            fw.pe.op(lambda: nc.tensor.matmul(self.ps(sbank, 128, N), lhsT=KTt[hk][:, kb * 128:(kb + 1) * 128], rhs=Qt[qk][:], start=True, stop=True),
                                     reads=[bK[hk], bQ[qk]], writes=[self.bps[sbank]])

                        def st1(sbank=sbank, pk=pk, fk=fk, kb=kb):
                            fw.act.op(lambda: nc.scalar.activation(out=PF[fk][:], in_=self.ps(sbank, 128, N), func=AF.Exp, scale=0.125),
                                      reads=[self.bps[sbank]], writes=[bPF[fk]])
                            fw.dve.op(lambda: nc.vector.tensor_tensor(out=P[pk][:], in0=PF[fk][:], in1=MT[:, kb, :], op=ALU.mult),
                                      reads=[bPF[fk], bMT], writes=[bP[pk]])

                        def st2(h=h, vk=vk, kb=kb, pk=pk, first=first, lastk=lastk, qk=qk, ogk=ogk):
                            fw.pe.op(lambda: nc.tensor.matmul(self.ps(6, 64, N), lhsT=Vt[vk][:, kb, :], rhs=P[pk][:], start=first, stop=lastk),
                                     reads=[bV[vk], bP[pk]], writes=[self.bps[6]])
                            fw.pe.op(lambda: nc.tensor.matmul(self.ps(7, 64, N), lhsT=ones_b[:], rhs=P[pk][:], start=first, stop=lastk),
                                     reads=[b_c, bP[pk]], writes=[self.bps[7]])
                            if self.debug and tt == 0 and h == 0:
                                fw.st.op(lambda: nc.gpsimd.dma_start(out=self.dbgP[:, kb, :], in_=P[pk][:]), reads=[bP[pk]])
                            if lastk:
                                fw.dve.op(lambda: nc.vector.reciprocal(Rr[:], self.ps(7, 64, N)), reads=[self.bps[7]], writes=[bRr])
                                fw.dve.op(lambda: nc.vector.tensor_tensor(out=O1[:], in0=self.ps(6, 64, N), in1=Rr[:], op=ALU.mult), reads=[self.bps[6], bRr], writes=[bO1])
                                if self.debug and tt == 0 and h == 0:
                                    fw.st.op(lambda: nc.gpsimd.dma_start(out=self.dbgMT2[:, :, :], in_=MT[:, 0:4, :]), reads=[bMT])
                                    fw.st.op(lambda: nc.gpsimd.dma_start(out=self.dbgO[:, 0, :], in_=O1[:]), reads=[bO1])
                                    fw.st.op(lambda: nc.gpsimd.dma_start(out=self.dbgO[:, 1, :], in_=Rr[:]), reads=[bRr])
                                fw.pool.op(lambda: nc.gpsimd.tensor_tensor(out=OG[ogk][:], in0=O1[:], in1=Gt[qk][:], op=ALU.mult), reads=[bO1, bG[qk]], writes=[bOG[ogk]])
                                fw.st.op(lambda: nc.gpsimd.dma_start(out=self.OT[h * 64:(h + 1) * 64, t0:t0 + N], in_=OG[ogk][:]),
                                         reads=[bOG[ogk]], writes=[self.b_OT])
                        items.append([st0, st1, st2])
                pipeline(items, 3)
            fw.barrier()


_LAYERS = [(i % 3, i // 3, i) for i in range(DEPTH)]


def _consts():
    p = np.arange(128)
    i = (p % 32).astype(np.float32)
    inv = (1.0 / (np.float32(10000.0) ** (2 * i / np.float32(64)))).astype(np.float32)
    sign = np.where((p % 64) < 32, -1.0, 1.0).astype(np.float32)
    return np.stack([inv, sign], axis=1).astype(np.float32), np.eye(128, dtype=np.float32)


def own_rows(S, rho):
    idx = []
    for tt in range(S // 512):
        for c in own_blocks(rho):
            g = 4 * tt + c
            idx.append(np.arange(g * 128, (g + 1) * 128))
    return np.concatenate(idx)


def make_in_map(xb, rho, layers, w):
    S = xb.shape[0]
    ropec, ident = _consts()
    rows = own_rows(S, rho)
    im = {"x": np.ascontiguousarray(xb), "x_own": np.ascontiguousarray(xb[rows]), "ropec": ropec, "ident": ident}
    im["posrow"] = rows.astype(np.float32).reshape(1, -1)
    im["poscol"] = rows.astype(np.float32).reshape(-1, 1)
    im["qrel"] = (rows[:TW] % 512).astype(np.float32).reshape(1, TW)
    ws_in = {0: w["w_in_a"], 1: w["w_in_b"], 2: w["w_in_c"]}
    ws_out = {0: w["w_out_a"], 1: w["w_out_b"], 2: w["w_out_c"]}
    for li, (m, j, L) in enumerate(layers):
        im[f"w_in{li}"] = np.ascontiguousarray(ws_in[m][j])
        im[f"w_out{li}"] = np.ascontiguousarray(ws_out[m][j])
    im["ln_g"] = np.ascontiguousarray(np.stack([w["ln_g"][L] for (_, _, L) in layers]))
    im["ln_b"] = np.ascontiguousarray(np.stack([w["ln_b"][L] for (_, _, L) in layers]))
    im["lam"] = np.ascontiguousarray(np.stack([w["lambda_q1"][0], w["lambda_k1"][0], w["lambda_q2"][0], w["lambda_k2"][0]]).reshape(1, 256))
    im["subg"] = np.ascontiguousarray(w["subln_g"][0].reshape(128, 1))
    return im


def run_layers(x, w, layers, ncores):
    B, S, _ = x.shape
    assert ncores == 2 * B
    prog = Prog(S, layers, ncores=ncores)
    in_maps = [make_in_map(x[c // 2], c % 2, layers, w) for c in range(ncores)]
    res = run_bass_kernel_spmd(prog.nc, in_maps, core_ids=list(range(ncores)))
    out = np.empty((B, S, D), np.float32)
    for c in range(ncores):
        out[c // 2][own_rows(S, c % 2)] = res.results[c]["out"]
    return out


def kernel(**inputs):
    w = {k: np.asarray(v, dtype=np.float32) for k, v in inputs.items()}
    return run_layers(w["x"], w, _LAYERS, 8)
```

```python
import math
from contextlib import ExitStack
import numpy as np
import concourse.bass as bass
import concourse.mybir as mybir
from concourse.bass_utils import run_bass_kernel_spmd

F32 = mybir.dt.float32
BF16 = mybir.dt.bfloat16
I32 = mybir.dt.int32
AF = mybir.ActivationFunctionType
ALU = mybir.AluOpType

D = 1024
NH = 16
HD = 64
DEPTH = 4
ALPHA = (2.0 * DEPTH) ** 0.25
LN_EPS = 1e-5
RMS_EPS = 1e-5
A_IN = 4 * D + 8 * 64 + 8 + 64
TOPK = 256
SB_WIN = 3
BIS_ITERS = 22
BIS_R = 16.0


class Buf:
    __slots__ = ("w", "r")

    def __init__(self):
        self.w = None
        self.r = {}


class Q:
    def __init__(self, fw, eng, name, is_dma, nsems, same_engine_waits=True):
        self.fw = fw
        self.eng = eng
        self.is_dma = is_dma
        self.inc = 16 if is_dma else 1
        self.sems = []
        for i in range(nsems):
            h = fw.nc.alloc_semaphore(name=f"s_{name}{i}")
            self.sems.append(len(fw.semh))
            fw.semh.append(h)
        self.cnt = [0] * nsems
        self.rr = 0
        self.waited = {}
        self.sew = same_engine_waits
        self.pending = False

    def _wait(self, sem, val):
        if self.waited.get(sem, 0) < val:
            self.eng.wait_ge(self.fw.semh[sem], val)
            self.waited[sem] = val

    def op(self, fn, reads=(), writes=(), inc=True):
        deps = {}
        for b in reads:
            if b.w is not None:
                s, v = b.w
                if deps.get(s, 0) < v:
                    deps[s] = v
        for b in writes:
            if b.w is not None:
                s, v = b.w
                if deps.get(s, 0) < v:
                    deps[s] = v
            for s, v in b.r.items():
                if deps.get(s, 0) < v:
                    deps[s] = v
        i = self.rr
        if self.is_dma:
            self.rr = (self.rr + 1) % len(self.sems)
            if self.cnt[i] > 0:
                deps[self.sems[i]] = max(deps.get(self.sems[i], 0), self.cnt[i])
        for s, v in deps.items():
            if (not self.sew) and s == self.sems[0]:
                continue
            self._wait(s, v)
        ins = fn()
        if inc:
            self.cnt[i] += self.inc
            ins.then_inc(self.fw.semh[self.sems[i]], self.inc)
            tag = (self.sems[i], self.cnt[i])
            self.pending = False
        else:
            tag = (self.sems[i], self.cnt[i] + self.inc)
            self.pending = True
        for b in reads:
            if b.r.get(tag[0], 0) < tag[1]:
                b.r[tag[0]] = tag[1]
        for b in writes:
            b.w = tag
            b.r = {}
        return tag

    def wait_all_of(self, other):
        for i, s in enumerate(other.sems):
            if other.cnt[i] > 0:
                self._wait(s, other.cnt[i])


class FW:
    def __init__(self, nc):
        self.nc = nc
        self.semh = []
        self.pe = Q(self, nc.tensor, "pe", False, 1, same_engine_waits=False)
        self.act = Q(self, nc.scalar, "act", False, 1)
        self.dve = Q(self, nc.vector, "dve", False, 1)
        self.pool = Q(self, nc.gpsimd, "pool", False, 1)
        self.ld = Q(self, nc.sync, "ld", True, 8)
        self.st = Q(self, nc.gpsimd, "st", True, 8)
        self.qs = [self.pe, self.act, self.dve, self.pool, self.ld, self.st]

    def barrier(self):
        for q in self.qs:
            assert not q.pending
        for q in self.qs:
            for o in self.qs:
                if o is not q:
                    q.wait_all_of(o)


def pipeline(items, nstages):
    n = len(items)
    for step in range(n + nstages - 1):
        for s in range(nstages - 1, -1, -1):
            i = step - s
            if 0 <= i < n and items[i][s] is not None:
                items[i][s]()


TW = 256


def own_blocks(rho):
    return (0, 3) if rho == 0 else (1, 2)


class Prog:
    def __init__(self, S, layers, ncores=8, debug=False):
        self.S = S
        self.SO = S // 2
        self.NB = S // 128
        self.NT = S // 512
        self.CW = min(1024, self.SO)
        self.NCH = self.SO // self.CW
        self.groups = [[2 * i, 2 * i + 1] for i in range(ncores // 2)]
        self.layers = layers
        self.debug = debug
        nc = self.nc = bass.Bass("TRN2", target_bir_lowering=False)
        self.fw = FW(nc)
        self.fw.cc = Q(self.fw, nc.gpsimd, "cc", False, 1)
        self.fw.qs.append(self.fw.cc)
        SO = self.SO
        dt = nc.dram_tensor
        self.x_in = dt("x", [S, D], F32, kind="ExternalInput").ap()
        self.x_own = dt("x_own", [SO, D], F32, kind="ExternalInput").ap()
        self.posrow = dt("posrow", [1, SO], F32, kind="ExternalInput").ap()
        self.poscol = dt("poscol", [SO, 1], F32, kind="ExternalInput").ap()
        self.qrel = dt("qrel", [1, TW], F32, kind="ExternalInput").ap()
        self.out = dt("out", [SO, D], F32, kind="ExternalOutput").ap()
        self.w_in = {}
        self.w_out = {}
        for li, (m, j, L) in enumerate(layers):
            width = A_IN if m == 0 else 4 * D
            self.w_in[li] = dt(f"w_in{li}", [D, width], F32, kind="ExternalInput").ap()
            self.w_out[li] = dt(f"w_out{li}", [D, D], F32, kind="ExternalInput").ap()
        self.lng = dt("ln_g", [len(layers), D], F32, kind="ExternalInput").ap()
        self.lnb = dt("ln_b", [len(layers), D], F32, kind="ExternalInput").ap()
        self.lam = dt("lam", [1, 256], F32, kind="ExternalInput").ap()
        self.subg = dt("subg", [128, 1], F32, kind="ExternalInput").ap()
        self.ropec = dt("ropec", [128, 2], F32, kind="ExternalInput").ap()
        self.ident_in = dt("ident", [128, 128], F32, kind="ExternalInput").ap()
        IK = "ExternalOutput" if debug else "Internal"
        self.xres = [dt(f"xres{i}", [SO, D], F32, kind="Internal").ap() for i in range(2)]
        self.xTf = dt("xTf", [D, S], BF16, kind="Internal").ap()
        self.xTo = [[dt(f"xTo{s_}_{c}", [D, self.CW], BF16) for c in range(self.NCH)] for s_ in range(2)]
        self.gath = [[dt(f"gath{l}_{c}", [2 * D, self.CW], BF16) for c in range(self.NCH)] for l in range(max(1, len(layers) - 1))]
        self.QT = dt("QT", [24 * 64, SO], BF16, kind=IK).ap()
        self.KT = dt("KT", [17 * 64, S], BF16, kind=IK).ap()
        self.GT = dt("GT", [D, SO], BF16, kind=IK).ap()
        self.OT = dt("OT", [D, SO], BF16, kind=IK).ap()
        self.Vs = dt("Vs", [NH, 128, self.NB, 64], BF16, kind=IK).ap()
        self.WI = dt("WI", [SO, 8], F32, kind="Internal").ap()
        self.CC = dt("CCt", [128, S], F32, kind="Internal").ap()
        self.SS = dt("SSt", [128, S], F32, kind="Internal").ap()
        self.CCo = dt("CCo", [128, SO], F32, kind="Internal").ap()
        self.SSo = dt("SSo", [128, SO], F32, kind="Internal").ap()
        if debug:
            self.dbgSC = dt("dbgSC", [128, 512], F32, kind="ExternalOutput").ap()
            self.dbgBS = dt("dbgBS", [128, 8], F32, kind="ExternalOutput").ap()
            self.dbgMT = dt("dbgMT", [128, 4, TW], BF16, kind="ExternalOutput").ap()
            self.dbgMT2 = dt("dbgMT2", [128, 4, TW], BF16, kind="ExternalOutput").ap()
            self.dbgP = dt("dbgP", [128, 4, TW], BF16, kind="ExternalOutput").ap()
            self.dbgO = dt("dbgO", [64, 2, TW], F32, kind="ExternalOutput").ap()
        self.b_xres = [Buf(), Buf()]
        self.b_xTf, self.b_QT, self.b_KT, self.b_GT, self.b_OT, self.b_Vs, self.b_WI, self.b_tab = (Buf() for _ in range(8))
        self.b_xTo = [Buf(), Buf()]
        self.b_gath = [Buf() for _ in self.gath]
        self.psall = nc.alloc_psum_tensor("psall", [128, 4096], F32)
        self.bps = [Buf() for _ in range(8)]
        self.build()

    def sbf(self, es):
        def f(n, sh, d):
            self._uid = getattr(self, "_uid", 0) + 1
            return es.enter_context(self.nc.sbuf_tensor(f"{n}_u{self._uid}", sh, d))
        return f

    def ps(self, i, parts=128, n=512):
        return self.psall[0:parts, i * 512:i * 512 + n]

    def build(self):
        nc, fw = self.nc, self.fw
        with ExitStack() as es:
            sb = self.sbf(es)
            self.ident = sb("identsb", [128, 128], F32)
            self.b_ident = Buf()
            fw.ld.op(lambda: nc.sync.dma_start(out=self.ident[:], in_=self.ident_in[:, :]), writes=[self.b_ident])
            self.cst = sb("cst", [128, 4], F32)
            self.b_cst = Buf()
            fw.pool.op(lambda: nc.gpsimd.memset(self.cst[:, 0:1], float(LN_EPS)), writes=[self.b_cst])
            fw.pool.op(lambda: nc.gpsimd.iota(self.cst[:, 1:2], pattern=[[0, 1]], base=0, channel_multiplier=1,
                                              allow_small_or_imprecise_dtypes=True), writes=[self.b_cst])
            fw.pool.op(lambda: nc.gpsimd.memset(self.cst[0:64, 2:3], 0.0), writes=[self.b_cst])
            fw.pool.op(lambda: nc.gpsimd.memset(self.cst[64:128, 2:3], 64.0), writes=[self.b_cst])
            fw.pool.op(lambda: nc.gpsimd.memset(self.cst[:, 3:4], 1.0), writes=[self.b_cst])
            self.QR = sb("qrelsb", [128, TW], F32)
            fw.ld.op(lambda: nc.sync.dma_start(out=self.QR[:], in_=self.qrel[0:1, :].broadcast_to([128, TW])), writes=[self.b_cst])
            self.rope_tables(self.CC, self.SS, self.S, None)
            self.rope_tables(self.CCo, self.SSo, self.SO, self.posrow)
            self.transpose_input()
            cur = 0
            for li, (m, j, L) in enumerate(self.layers):
                last = li == len(self.layers) - 1
                self.phase_a(li, m)
                if m == 0:
                    self.attn_dsa()
                elif m == 1:
                    self.attn_sb()
                else:
                    self.attn_diff(L)
                self.phase_c(li, cur, last)
                if not last:
                    self.exchange(li)
                cur ^= 1
            fw.barrier()

    def exchange(self, li):
        nc, fw = self.nc, self.fw
        sset = (li + 1) % 2
        for ch in range(self.NCH):
            fw.cc.op(lambda ch=ch: nc.gpsimd.collective_compute("AllGather", ALU.bypass, replica_groups=self.groups,
                                                                ins=[self.xTo[sset][ch].ap().opt()], outs=[self.gath[li][ch].ap().opt()]),
                     reads=[self.b_xTo[sset]], writes=[self.b_gath[li]])
        fw.barrier()

    def xt_all_src(self, li, tt, c):
        if li == 0:
            g = 4 * tt + c
            return self.xTf[:, g * 128:(g + 1) * 128], self.b_xTf
        rho = 0 if c in (0, 3) else 1
        lb = 2 * tt + (0 if c in (0, 1) else 1)
        ch, col = divmod(lb * 128, self.CW)
        return self.gath[li - 1][ch].ap()[rho * D:(rho + 1) * D, col:col + 128], self.b_gath[li - 1]

    def rope_tables(self, CCd, SSd, n, posrow):
        nc, fw = self.nc, self.fw
        with ExitStack() as es:
            sb = self.sbf(es)
            rc = sb("rc", [128, 2], F32); brc = Buf()
            fw.ld.op(lambda: nc.sync.dma_start(out=rc[:], in_=self.ropec[:, :]), writes=[brc])
            C1 = 6.28125
            C2 = 2 * math.pi - C1
            pools = {}
            for nm, dtp in [("it", F32), ("ang", F32), ("a2", F32), ("ki", I32), ("kf", F32), ("r0", F32), ("r1", F32)]:
                pools[nm] = [(sb(f"rt_{nm}{q}", [128, 512], dtp), Buf()) for q in range(2)]
            for t0 in range(0, n, 512):
                sl = (t0 // 512) % 2
                it, b_it = pools["it"][sl]
                ang, b_ang = pools["ang"][sl]
                for which in range(2):
                    a2, b_a2 = pools["a2"][sl]
                    ki, b_ki = pools["ki"][sl]
                    kf, b_kf = pools["kf"][sl]
                    r, b_r = pools["r0" if which == 0 else "r1"][sl]
                    if which == 0:
                        if posrow is None:
                            fw.pool.op(lambda: nc.gpsimd.iota(it[:], pattern=[[1, 512]], base=t0, channel_multiplier=0,
                                                              allow_small_or_imprecise_dtypes=True), writes=[b_it])
                        else:
                            fw.ld.op(lambda: nc.sync.dma_start(out=it[:], in_=posrow[0:1, t0:t0 + 512].broadcast_to([128, 512])), writes=[b_it])
                        fw.dve.op(lambda: nc.vector.tensor_scalar(out=ang[:], in0=it[:], scalar1=rc[:, 0:1], scalar2=None,
                                                                  op0=ALU.mult), reads=[b_it, brc], writes=[b_ang])
                        src = ang
                    else:
                        fw.dve.op(lambda: nc.vector.tensor_scalar(out=a2[:], in0=ang[:], scalar1=float(math.pi / 2), scalar2=None,
                                                                  op0=ALU.add), reads=[b_ang], writes=[b_a2])
                        src = a2
                    bsrc = b_ang if which == 0 else b_a2
                    fw.dve.op(lambda: nc.vector.tensor_scalar(out=ki[:], in0=src[:], scalar1=float(1 / (2 * math.pi)), scalar2=None,
                                                              op0=ALU.mult), reads=[bsrc], writes=[b_ki])
                    fw.dve.op(lambda: nc.vector.tensor_copy(kf[:], ki[:]), reads=[b_ki], writes=[b_kf])
                    fw.dve.op(lambda: nc.vector.scalar_tensor_tensor(out=r[:], in0=kf[:], scalar=-C1, in1=src[:], op0=ALU.mult,
                                                                     op1=ALU.add), reads=[b_kf, bsrc], writes=[b_r])
                    fw.dve.op(lambda: nc.vector.scalar_tensor_tensor(out=r[:], in0=kf[:], scalar=-C2, in1=r[:], op0=ALU.mult,
                                                                     op1=ALU.add), reads=[b_kf, b_r], writes=[b_r])
                    fw.dve.op(lambda: nc.vector.tensor_scalar(out=r[:], in0=r[:], scalar1=float(math.pi), scalar2=float(-math.pi),
                                                              op0=ALU.min, op1=ALU.max), reads=[b_r], writes=[b_r])
                    fw.act.op(lambda: nc.scalar.activation(out=r[:], in_=r[:], func=AF.Sin), reads=[b_r], writes=[b_r])
                    if which == 0:
                        fw.dve.op(lambda: nc.vector.tensor_scalar(out=r[:], in0=r[:], scalar1=rc[:, 1:2], scalar2=None,
                                                                  op0=ALU.mult), reads=[b_r, brc], writes=[b_r])
                        fw.st.op(lambda: nc.gpsimd.dma_start(out=SSd[:, t0:t0 + 512], in_=r[:]), reads=[b_r], writes=[self.b_tab])
                    else:
                        fw.st.op(lambda: nc.gpsimd.dma_start(out=CCd[:, t0:t0 + 512], in_=r[:]), reads=[b_r], writes=[self.b_tab])
            fw.barrier()

    def emit_transpose_store(self, xn, b_xn, dst_ap, b_dst, xtt, b_xtt, pbank):
        nc, fw = self.nc, self.fw
        for half in range(2):
            bank = pbank[half]
            for c4 in range(4):
                c = half * 4 + c4
                fw.pe.op(lambda c=c, c4=c4: nc.tensor.transpose(self.ps(bank)[:, c4 * 128:(c4 + 1) * 128], xn[:, c * 128:(c + 1) * 128],
                                                                self.ident[:]),
                         reads=[b_xn, self.b_ident], writes=[self.bps[bank]], inc=(c4 == 3))
            fw.act.op(lambda half=half: nc.scalar.copy(xtt[:, half * 4:(half + 1) * 4, :],
                                                        self.ps(bank).rearrange("p (c t) -> p c t", c=4)),
                      reads=[self.bps[bank]], writes=[b_xtt])
        fw.st.op(lambda: nc.gpsimd.dma_start(out=dst_ap.rearrange("(c p) t -> p c t", p=128), in_=xtt[:]),
                 reads=[b_xtt], writes=[b_dst])

    def xto_dst(self, sset, lb):
        ch, col = divmod(lb * 128, self.CW)
        return self.xTo[sset][ch].ap()[:, col:col + 128]

    def transpose_input(self):
        nc, fw = self.nc, self.fw
        with ExitStack() as es:
            sb = self.sbf(es)
            xt = [sb(f"ti_x{i}", [128, D], F32) for i in range(2)]
            bx = [Buf(), Buf()]
            xtt = [sb(f"ti_xt{i}", [128, 8, 128], BF16) for i in range(2)]
            bxtt = [Buf(), Buf()]
            n = 0
            for tb in range(self.NB):
                k = n % 2
                n += 1
                fw.ld.op(lambda: nc.sync.dma_start(out=xt[k][:], in_=self.x_in[tb * 128:(tb + 1) * 128, :]), writes=[bx[k]])
                self.emit_transpose_store(xt[k], bx[k], self.xTf[:, tb * 128:(tb + 1) * 128], self.b_xTf, xtt[k], bxtt[k], (2 * k, 2 * k + 1))
            for lb in range(self.SO // 128):
                k = n % 2
                n += 1
                fw.ld.op(lambda: nc.sync.dma_start(out=xt[k][:], in_=self.x_own[lb * 128:(lb + 1) * 128, :]), writes=[bx[k]])
                self.emit_transpose_store(xt[k], bx[k], self.xto_dst(0, lb), self.b_xTo[0], xtt[k], bxtt[k], (2 * k, 2 * k + 1))
            fw.barrier()

    def phase_a(self, li, m):
        nc, fw, S = self.nc, self.fw, self.S
        width = A_IN if m == 0 else 4 * D
        rope = m in (0, 2)
        rot_blocks = []
        if rope:
            rot_blocks = [(0, 16), (D, 16)]
            if m == 0:
                rot_blocks += [(4 * D, 8), (4 * D + 520, 1)]
        nrot = sum(nh for _, nh in rot_blocks) * 64
        sset = li % 2
        with ExitStack() as es:
            sb = self.sbf(es)
            WIN = sb("WIN", [128, 8, width], BF16); b_win = Buf()
            WROT = sb("WROT", [128, 8, max(nrot, 64)], BF16)
            stg0 = sb("wstg0", [128, width], F32)
            stg = [stg0, stg0]
            bstg0 = Buf()
            bstg = [bstg0, bstg0]
            rot_off = {}
            off = 0
            for col0, nh in rot_blocks:
                rot_off[col0] = off
                off += nh * 64
            for c in range(8):
                k = c % 2
                fw.ld.op(lambda: nc.sync.dma_start(out=stg[k][:], in_=self.w_in[li][c * 128:(c + 1) * 128, :]), writes=[bstg[k]])
                h2 = width // 2
                fw.act.op(lambda: nc.scalar.copy(WIN[:, c, 0:h2], stg[k][:, 0:h2]), reads=[bstg[k]], writes=[b_win])
                fw.dve.op(lambda: nc.vector.tensor_copy(WIN[:, c, h2:width], stg[k][:, h2:width]), reads=[bstg[k]], writes=[b_win])
                for col0, nh in rot_blocks:
                    ro = rot_off[col0]
                    src = stg[k][:, col0:col0 + nh * 64].rearrange("p (h t i) -> p h t i", t=2, i=32)
                    dst = WROT[:, c, ro:ro + nh * 64].rearrange("p (h t i) -> p h t i", t=2, i=32)
                    fw.pool.op(lambda: nc.gpsimd.tensor_copy(dst[:, :, 0, :], src[:, :, 1, :]), reads=[bstg[k]], writes=[b_win])
                    fw.pool.op(lambda: nc.gpsimd.tensor_copy(dst[:, :, 1, :], src[:, :, 0, :]), reads=[bstg[k]], writes=[b_win])
            kgroups = [(self.KT, self.b_KT, g * 128, D + g * 128, 128, rope, None) for g in range(8)]
            if m == 0:
                kgroups.append((self.KT, self.b_KT, D, 4 * D + 520, 64, True, None))
            qgroups = [(self.QT, self.b_QT, g * 128, g * 128, 128, rope, None) for g in range(8)]
            qgroups += [(self.GT, self.b_GT, g * 128, 3 * D + g * 128, 128, False, "silu") for g in range(8)]
            if m == 0:
                qgroups += [(self.QT, self.b_QT, D + g * 128, 4 * D + g * 128, 128, True, None) for g in range(4)]

            def rotcol(col):
                for col0, nh in rot_blocks:
                    if col0 <= col < col0 + nh * 64:
                        return rot_off[col0] + (col - col0)
                raise AssertionError

            XT = [sb(f"pa_xt{i}", [128, 8, 512], BF16) for i in range(2)]; bXT = [Buf(), Buf()]
            XO = [sb(f"pa_xo{i}", [128, 8, TW], BF16) for i in range(2)]; bXO = [Buf(), Buf()]
            CCt = [sb(f"pa_cc{i}", [128, 512], F32) for i in range(2)]; bCC = [Buf(), Buf()]
            SSt = [sb(f"pa_ss{i}", [128, 512], F32) for i in range(2)]; bSS = [Buf(), Buf()]
            CCq = [sb(f"pa_ccq{i}", [128, TW], F32) for i in range(2)]; bCCq = [Buf(), Buf()]
            SSq = [sb(f"pa_ssq{i}", [128, TW], F32) for i in range(2)]; bSSq = [Buf(), Buf()]
            T1 = [sb(f"pa_t1{i}", [128, 512], F32) for i in range(2)]; bT1 = [Buf(), Buf()]
            T2 = [sb(f"pa_t2{i}", [128, 512], F32) for i in range(2)]; bT2 = [Buf(), Buf()]
            OB = [sb(f"pa_ob{i}", [128, 512], BF16) for i in range(3)]; bOB = [Buf() for _ in range(3)]
            VT0 = sb("pa_vt0", [128, 4, D], BF16); VT = [VT0, VT0]; bVT0 = Buf(); bVT = [bVT0, bVT0]
            WIt = [sb(f"pa_wi{i}", [128, 8], F32) for i in range(2)]; bWI = [Buf(), Buf()]
            gi = [0]

            def do_group(grp, xin, bxin, N, t0, cc, bcc, ss, bss):
                (dst, bdst, row0, col0, M, rp, act) = grp
                pb = (gi[0] % 2) * 2
                ob = gi[0] % 3
                tk = gi[0] % 2
                gi[0] += 1
                for c in range(8):
                    fw.pe.op(lambda c=c: nc.tensor.matmul(self.ps(pb, M, N), lhsT=WIN[:, c, col0:col0 + M], rhs=xin[:, c, :],
                                                          start=(c == 0), stop=(c == 7)),
                             reads=[b_win, bxin], writes=[self.bps[pb]], inc=(c == 7))
                if rp:
                    rc0 = rotcol(col0)
                    for c in range(8):
                        fw.pe.op(lambda c=c: nc.tensor.matmul(self.ps(pb + 1, M, N), lhsT=WROT[:, c, rc0:rc0 + M], rhs=xin[:, c, :],
                                                              start=(c == 0), stop=(c == 7)),
                                 reads=[b_win, bxin], writes=[self.bps[pb + 1]], inc=(c == 7))
                    fw.dve.op(lambda: nc.vector.tensor_tensor(out=T1[tk][0:M, 0:N], in0=self.ps(pb, M, N), in1=cc[0:M, :], op=ALU.mult),
                              reads=[self.bps[pb], bcc], writes=[bT1[tk]])
                    fw.dve.op(lambda: nc.vector.tensor_tensor(out=T2[tk][0:M, 0:N], in0=self.ps(pb + 1, M, N), in1=ss[0:M, :], op=ALU.mult),
                              reads=[self.bps[pb + 1], bss], writes=[bT2[tk]])
                    fw.pool.op(lambda: nc.gpsimd.tensor_tensor(out=OB[ob][0:M, 0:N], in0=T1[tk][0:M, 0:N], in1=T2[tk][0:M, 0:N], op=ALU.add),
                               reads=[bT1[tk], bT2[tk]], writes=[bOB[ob]])
                elif act == "silu":
                    fw.act.op(lambda: nc.scalar.activation(out=OB[ob][0:M, 0:N], in_=self.ps(pb, M, N), func=AF.Silu),
                              reads=[self.bps[pb]], writes=[bOB[ob]])
                else:
                    fw.act.op(lambda: nc.scalar.copy(OB[ob][0:M, 0:N], self.ps(pb, M, N)), reads=[self.bps[pb]], writes=[bOB[ob]])
                fw.st.op(lambda: nc.gpsimd.dma_start(out=dst[row0:row0 + M, t0:t0 + N], in_=OB[ob][0:M, 0:N]),
                         reads=[bOB[ob]], writes=[bdst])

            for tt in range(self.NT):
                k = tt % 2
                t0 = tt * 512
                for c in range(4):
                    src, bsrc = self.xt_all_src(li, tt, c)
                    fw.ld.op(lambda: nc.sync.dma_start(out=XT[k][:, :, c * 128:(c + 1) * 128], in_=src.rearrange("(c p) t -> p c t", p=128)),
                             reads=[bsrc], writes=[bXT[k]])
                l0 = tt * TW
                ch, col = divmod(l0, self.CW)
                fw.ld.op(lambda: nc.sync.dma_start(out=XO[k][:], in_=self.xTo[sset][ch].ap()[:, col:col + TW].rearrange("(c p) t -> p c t", p=128)),
                         reads=[self.b_xTo[sset]], writes=[bXO[k]])
                if rope:
                    fw.ld.op(lambda: nc.sync.dma_start(out=CCt[k][:], in_=self.CC[:, t0:t0 + 512]), reads=[self.b_tab], writes=[bCC[k]])
                    fw.ld.op(lambda: nc.sync.dma_start(out=SSt[k][:], in_=self.SS[:, t0:t0 + 512]), reads=[self.b_tab], writes=[bSS[k]])
                    fw.ld.op(lambda: nc.sync.dma_start(out=CCq[k][:], in_=self.CCo[:, l0:l0 + TW]), reads=[self.b_tab], writes=[bCCq[k]])
                    fw.ld.op(lambda: nc.sync.dma_start(out=SSq[k][:], in_=self.SSo[:, l0:l0 + TW]), reads=[self.b_tab], writes=[bSSq[k]])
                for grp in kgroups:
                    do_group(grp, XT[k], bXT[k], 512, t0, CCt[k], bCC[k], SSt[k], bSS[k])
                for grp in qgroups:
                    do_group(grp, XO[k], bXO[k], TW, l0, CCq[k], bCCq[k], SSq[k], bSSq[k])
                for tb in range(4):
                    for half in range(2):
                        pb = 4 + half
                        for c in range(8):
                            fw.pe.op(lambda c=c: nc.tensor.matmul(self.ps(pb), lhsT=XT[k][:, c, tb * 128:(tb + 1) * 128],
                                                                  rhs=WIN[:, c, 2 * D + half * 512:2 * D + (half + 1) * 512],
                                                                  start=(c == 0), stop=(c == 7)),
                                     reads=[b_win, bXT[k]], writes=[self.bps[pb]], inc=(c == 7))
                        if half == 0:
                            fw.act.op(lambda: nc.scalar.copy(VT[k][:, tb, 0:512], self.ps(pb)), reads=[self.bps[pb]], writes=[bVT[k]])
                        else:
                            fw.dve.op(lambda: nc.vector.tensor_copy(VT[k][:, tb, 512:1024], self.ps(pb)), reads=[self.bps[pb]], writes=[bVT[k]])
                if m == 0:
                    for tb in range(2):
                        pb = 6
                        wk = (tt * 2 + tb) % 2
                        for c in range(8):
                            fw.pe.op(lambda c=c: nc.tensor.matmul(self.ps(pb, 128, 8), lhsT=XO[k][:, c, tb * 128:(tb + 1) * 128],
                                                                  rhs=WIN[:, c, 4 * D + 512:4 * D + 520], start=(c == 0), stop=(c == 7)),
                                     reads=[b_win, bXO[k]], writes=[self.bps[pb]], inc=(c == 7))
                        fw.dve.op(lambda: nc.vector.tensor_scalar(out=WIt[wk][:], in0=self.ps(pb, 128, 8), scalar1=float(8 ** -0.5 * 64 ** -0.5),
                                                                  scalar2=None, op0=ALU.mult), reads=[self.bps[pb]], writes=[bWI[wk]])
                        r0 = l0 + tb * 128
                        fw.st.op(lambda: nc.gpsimd.dma_start(out=self.WI[r0:r0 + 128, :], in_=WIt[wk][:]), reads=[bWI[wk]], writes=[self.b_WI])
                for h in range(NH):
                    fw.st.op(lambda h=h: nc.gpsimd.dma_start(out=self.Vs[h, :, tt * 4:(tt + 1) * 4, :], in_=VT[k][:, :, h * 64:(h + 1) * 64]),
                             reads=[bVT[k]], writes=[self.b_Vs])
            fw.barrier()

    def phase_c(self, li, cur, last):
        nc, fw = self.nc, self.fw
        first = li == 0
        src = self.x_own if first else self.xres[cur]
        bsrc = Buf() if first else self.b_xres[cur]
        dst = self.out if last else self.xres[cur ^ 1]
        bdst = Buf() if last else self.b_xres[cur ^ 1]
        wset = (li + 1) % 2
        with ExitStack() as es:
            sb = self.sbf(es)
            WO = sb("WO", [128, 8, D], BF16); b_wo = Buf()
            stg = [sb(f"wostg{i}", [128, D], F32) for i in range(2)]; bstg = [Buf(), Buf()]
            for c in range(8):
                k = c % 2
                fw.ld.op(lambda: nc.sync.dma_start(out=stg[k][:], in_=self.w_out[li][c * 128:(c + 1) * 128, :]), writes=[bstg[k]])
                fw.act.op(lambda: nc.scalar.copy(WO[:, c, :], stg[k][:]), reads=[bstg[k]], writes=[b_wo])
            G = sb("lnG", [128, D], F32); Bt = sb("lnB", [128, D], F32); b_gb = Buf()
            fw.ld.op(lambda: nc.sync.dma_start(out=G[:], in_=self.lng[li:li + 1, :].broadcast_to([128, D])), writes=[b_gb])
            fw.ld.op(lambda: nc.sync.dma_start(out=Bt[:], in_=self.lnb[li:li + 1, :].broadcast_to([128, D])), writes=[b_gb])
            OTt = [sb(f"pc_ot{i}", [128, 8, 512], BF16) for i in range(2)]; bOT = [Buf(), Buf()]
            XR = [sb(f"pc_x{i}", [128, D], F32) for i in range(2)]; bXR = [Buf(), Buf()]
            U = [sb(f"pc_u{i}", [128, D], F32) for i in range(2)]; bU = [Buf(), Buf()]
            XN = [sb(f"pc_xn{i}", [128, D], F32) for i in range(2)]; bXN = [Buf(), Buf()]
            SQ = sb("pc_sq", [128, D], F32); bSQ = Buf()
            ST = [sb(f"pc_st{i}", [128, 8], F32) for i in range(2)]; bST = [Buf(), Buf()]
            XTT = [sb(f"pc_xtt{i}", [128, 8, 128], BF16) for i in range(2)]; bXTT = [Buf(), Buf()]
            for tt in range(self.SO // 512):
                kk = tt % 2
                t0 = tt * 512
                fw.ld.op(lambda: nc.sync.dma_start(out=OTt[kk][:], in_=self.OT[:, t0:t0 + 512].rearrange("(c p) t -> p c t", p=128)),
                         reads=[self.b_OT], writes=[bOT[kk]])
                for tb4 in range(4):
                    tb = tt * 4 + tb4
                    k = tb % 2
                    fw.ld.op(lambda: nc.sync.dma_start(out=XR[k][:], in_=src[tb * 128:(tb + 1) * 128, :]), reads=[bsrc], writes=[bXR[k]])
                    pbs = (4 * k, 4 * k + 1)
                    for half in range(2):
                        for c in range(8):
                            fw.pe.op(lambda c=c, half=half: nc.tensor.matmul(self.ps(pbs[half]), lhsT=OTt[kk][:, c, tb4 * 128:(tb4 + 1) * 128],
                                                                             rhs=WO[:, c, half * 512:(half + 1) * 512], start=(c == 0), stop=(c == 7)),
                                     reads=[b_wo, bOT[kk]], writes=[self.bps[pbs[half]]], inc=(c == 7))
                    st = ST[k]
                    for half in range(2):
                        hs = slice(half * 512, (half + 1) * 512)
                        fw.dve.op(lambda half=half, hs=hs: nc.vector.scalar_tensor_tensor(out=U[k][:, hs], in0=XR[k][:, hs], scalar=float(ALPHA),
                                                                                          in1=self.ps(pbs[half]), op0=ALU.mult, op1=ALU.add),
                                  reads=[bXR[k], self.bps[pbs[half]]], writes=[bU[k]])
                    fw.act.op(lambda: nc.scalar.activation(out=SQ[:], in_=U[k][:], func=AF.Copy, accum_out=st[:, 0:1]), reads=[bU[k]], writes=[bSQ, bST[k]])
                    fw.act.op(lambda: nc.scalar.activation(out=SQ[:], in_=U[k][:], func=AF.Square, accum_out=st[:, 1:2]), reads=[bU[k]], writes=[bSQ, bST[k]])
                    fw.dve.op(lambda: nc.vector.tensor_scalar(out=st[:, 2:3], in0=st[:, 0:1], scalar1=float(1.0 / D), scalar2=None, op0=ALU.mult),
                              reads=[bST[k]], writes=[bST[k]])
                    fw.dve.op(lambda: nc.vector.tensor_tensor(out=st[:, 3:4], in0=st[:, 2:3], in1=st[:, 2:3], op=ALU.mult), reads=[bST[k]], writes=[bST[k]])
                    fw.dve.op(lambda: nc.vector.scalar_tensor_tensor(out=st[:, 4:5], in0=st[:, 1:2], scalar=float(1.0 / D), in1=st[:, 3:4],
                                                                     op0=ALU.mult, op1=ALU.subtract), reads=[bST[k]], writes=[bST[k]])
                    fw.act.op(lambda: nc.scalar.activation(out=st[:, 5:6], in_=st[:, 4:5], func=AF.Sqrt, bias=self.cst[:, 0:1], scale=1.0),
                              reads=[bST[k], self.b_cst], writes=[bST[k]])
                    fw.dve.op(lambda: nc.vector.reciprocal(st[:, 5:6], st[:, 5:6]), reads=[bST[k]], writes=[bST[k]])
                    fw.dve.op(lambda: nc.vector.tensor_scalar(out=XN[k][:], in0=U[k][:], scalar1=st[:, 2:3], scalar2=st[:, 5:6], op0=ALU.subtract, op1=ALU.mult),
                              reads=[bU[k], bST[k]], writes=[bXN[k]])
                    fw.pool.op(lambda: nc.gpsimd.tensor_tensor(out=XN[k][:], in0=XN[k][:], in1=G[:], op=ALU.mult), reads=[bXN[k], b_gb], writes=[bXN[k]])
                    fw.pool.op(lambda: nc.gpsimd.tensor_tensor(out=XN[k][:], in0=XN[k][:], in1=Bt[:], op=ALU.add), reads=[bXN[k], b_gb], writes=[bXN[k]])
                    fw.st.op(lambda: nc.gpsimd.dma_start(out=dst[tb * 128:(tb + 1) * 128, :], in_=XN[k][:]), reads=[bXN[k]], writes=[bdst])
                    if not last:
                        self.emit_transpose_store(XN[k], bXN[k], self.xto_dst(wset, tb), self.b_xTo[wset], XTT[k], bXTT[k], (4 * k + 2, 4 * k + 3))
            fw.barrier()

    def build_masks(self, sb, kind):
        nc, fw = self.nc, self.fw
        Ms = [sb(f"mask_{kind}{j}", [128, TW], BF16) for j in range(4)]
        b = Buf()
        for j in range(4):
            if kind == "before":
                fw.dve.op(lambda j=j: nc.vector.tensor_scalar(out=Ms[j][:], in0=self.QR[:], scalar1=float(-128 * j), scalar2=self.cst[:, 1:2],
                                                              op0=ALU.add, op1=ALU.is_gt), reads=[self.b_cst], writes=[b])
            else:
                fw.dve.op(lambda j=j: nc.vector.tensor_scalar(out=Ms[j][:], in0=self.QR[:], scalar1=float(-128 * j), scalar2=self.cst[:, 2:3],
                                                              op0=ALU.add, op1=ALU.is_ge), reads=[self.b_cst], writes=[b])
        return Ms, b

    def attn_diff(self, L):
        nc, fw, S, NB, NT = self.nc, self.fw, self.S, self.NB, self.NT
        lambda_init = 0.8 - 0.6 * math.exp(-0.3 * L)
        with ExitStack() as es:
            sb = self.sbf(es)
            lv = sb("lv", [128, 4, 64], F32); b_lv = Buf()
            fw.ld.op(lambda: nc.sync.dma_start(out=lv[:].rearrange("p a b -> p (a b)"), in_=self.lam[0:1, :].broadcast_to([128, 256])), writes=[b_lv])
            lt = sb("lt", [128, 8], F32); b_lt = Buf()
            pr = sb("lpr", [128, 2, 64], F32)
            fw.dve.op(lambda: nc.vector.tensor_tensor(out=pr[:, 0, :], in0=lv[:, 0, :], in1=lv[:, 1, :], op=ALU.mult), reads=[b_lv], writes=[b_lt])
            fw.dve.op(lambda: nc.vector.tensor_tensor(out=pr[:, 1, :], in0=lv[:, 2, :], in1=lv[:, 3, :], op=ALU.mult), reads=[b_lv], writes=[b_lt])
            fw.dve.op(lambda: nc.vector.reduce_sum(out=lt[:, 0:1], in_=pr[:, 0, :], axis=mybir.AxisListType.X), reads=[b_lt], writes=[b_lt])
            fw.dve.op(lambda: nc.vector.reduce_sum(out=lt[:, 1:2], in_=pr[:, 1, :], axis=mybir.AxisListType.X), reads=[b_lt], writes=[b_lt])
            fw.act.op(lambda: nc.scalar.activation(out=lt[:, 2:4], in_=lt[:, 0:2], func=AF.Exp), reads=[b_lt], writes=[b_lt])
            fw.dve.op(lambda: nc.vector.tensor_tensor(out=lt[:, 4:5], in0=lt[:, 3:4], in1=lt[:, 2:3], op=ALU.subtract), reads=[b_lt], writes=[b_lt])
            fw.dve.op(lambda: nc.vector.tensor_scalar(out=lt[:, 5:6], in0=lt[:, 4:5], scalar1=float(-lambda_init), scalar2=None, op0=ALU.add),
                      reads=[b_lt], writes=[b_lt])
            sg = sb("sg", [128, 2], F32); b_sg = Buf()
            fw.ld.op(lambda: nc.sync.dma_start(out=sg[:, 0:1], in_=self.subg[:, :]), writes=[b_sg])
            fw.dve.op(lambda: nc.vector.tensor_scalar(out=sg[:, 1:2], in0=sg[:, 0:1], scalar1=float(1.0 - lambda_init), scalar2=None, op0=ALU.mult),
                      reads=[b_sg], writes=[b_sg])
            ones_b = sb("ones_b", [128, 128], BF16); ones_f = sb("ones_f", [128, 128], F32); b_ones = Buf()
            fw.pool.op(lambda: nc.gpsimd.memset(ones_b[:], 1.0), writes=[b_ones])
            fw.pool.op(lambda: nc.gpsimd.memset(ones_f[:], 1.0), writes=[b_ones])
            CM, b_cm = self.build_masks(sb, "chunk")
            KTt = [sb(f"df_k{i}", [64, 2, S], BF16) for i in range(2)]; bK = [Buf(), Buf()]
            Vt = [sb(f"df_v{i}", [128, NB, 128], BF16) for i in range(2)]; bV = [Buf(), Buf()]
            Qt = [sb(f"df_q{i}", [64, 2, TW], BF16) for i in range(3)]; bQ = [Buf() for _ in range(3)]
            Gt = [sb(f"df_g{i}", [128, TW], BF16) for i in range(3)]; bG = [Buf() for _ in range(3)]
            P = [sb(f"df_p{i}", [128, TW], BF16) for i in range(4)]; bP = [Buf() for _ in range(4)]
            PF = [sb(f"df_pf{i}", [128, TW], F32) for i in range(2)]; bPF = [Buf(), Buf()]
            R0 = sb("df_r0", [128, TW], F32); R1 = sb("df_r1", [128, TW], F32); bR = [Buf(), Buf()]
            O0 = sb("df_o0", [128, TW], F32); O1 = sb("df_o1", [128, TW], F32); bO = [Buf(), Buf()]
            SQ = sb("df_sq", [128, TW], F32); bSQ = Buf()
            RS = sb("df_rs", [128, TW], F32); bRS = Buf()
            OG = [sb(f"df_og{i}", [128, TW], BF16) for i in range(2)]; bOG = [Buf(), Buf()]
            N = TW
            sctr = [0]
            pctr = [0]
            items = []
            loaders = []
            for hd in range(8):
                hk = hd % 2

                def load_head(hd=hd, hk=hk):
                    for mm in range(2):
                        fw.ld.op(lambda mm=mm: nc.sync.dma_start(out=KTt[hk][:, mm, :], in_=self.KT[(2 * hd + mm) * 64:(2 * hd + mm + 1) * 64, :]),
                                 reads=[self.b_KT], writes=[bK[hk]])
                        fw.ld.op(lambda mm=mm: nc.sync.dma_start(out=Vt[hk][:, :, mm * 64:(mm + 1) * 64], in_=self.Vs[2 * hd + mm]),
                                 reads=[self.b_Vs], writes=[bV[hk]])
                for tt in range(NT):
                    qk = (hd * NT + tt) % 3
                    t0 = tt * TW
                    nkb = 4 * tt + 4

                    def load_q(hd=hd, qk=qk, t0=t0, tt=tt, hk=hk, lh=load_head):
                        if tt == 0:
                            lh()
                        for mm in range(2):
                            fw.ld.op(lambda mm=mm: nc.sync.dma_start(out=Qt[qk][:, mm, :], in_=self.QT[(2 * hd + mm) * 64:(2 * hd + mm + 1) * 64, t0:t0 + TW]),
                                     reads=[self.b_QT], writes=[bQ[qk]])
                        fw.ld.op(lambda: nc.sync.dma_start(out=Gt[qk][:], in_=self.GT[hd * 128:(hd + 1) * 128, t0:t0 + TW]),
                                 reads=[self.b_GT], writes=[bG[qk]])
                    for kb in range(nkb):
                        j = kb - 4 * tt
                        sbk = []
                        pidx = []
                        for mm in range(2):
                            sbk.append(sctr[0] % 3); sctr[0] += 1
                            pidx.append(pctr[0] % 4); pctr[0] += 1
                        if kb == 0:
                            loaders.append(load_q)
                        tidx = len(loaders) - 1

                        def st0(hk=hk, qk=qk, kb=kb, sbk=sbk, first=(kb == 0), tidx=tidx):
                            if first:
                                if tidx == 0:
                                    loaders[0]()
                                if tidx + 1 < len(loaders):
                                    loaders[tidx + 1]()
                            for mm in range(2):
                                fw.pe.op(lambda mm=mm: nc.tensor.matmul(self.ps(sbk[mm], 128, N), lhsT=KTt[hk][:, mm, kb * 128:(kb + 1) * 128], rhs=Qt[qk][:, mm, :],
                                                                        start=True, stop=True),
                                         reads=[bK[hk], bQ[qk]], writes=[self.bps[sbk[mm]]])

                        def st1(sbk=sbk, pidx=pidx, j=j):
                            for mm in range(2):
                                if j >= 0:
                                    fw.act.op(lambda mm=mm: nc.scalar.activation(out=PF[mm][:], in_=self.ps(sbk[mm], 128, N), func=AF.Exp, scale=0.125),
                                              reads=[self.bps[sbk[mm]]], writes=[bPF[mm]])
                                    fw.dve.op(lambda mm=mm: nc.vector.tensor_tensor(out=P[pidx[mm]][:], in0=PF[mm][:], in1=CM[j][:], op=ALU.mult),
                                              reads=[bPF[mm], b_cm], writes=[bP[pidx[mm]]])
                                else:
                                    fw.act.op(lambda mm=mm: nc.scalar.activation(out=P[pidx[mm]][:], in_=self.ps(sbk[mm], 128, N), func=AF.Exp, scale=0.125),
                                              reads=[self.bps[sbk[mm]]], writes=[bP[pidx[mm]]])

                        def st2(hk=hk, kb=kb, pidx=pidx, nkb=nkb, hd=hd, tt=tt, qk=qk, t0=t0):
                            for mm in range(2):
                                fw.pe.op(lambda mm=mm: nc.tensor.matmul(self.ps(3 + mm, 128, N), lhsT=Vt[hk][:, kb, :], rhs=P[pidx[mm]][:],
                                                                        start=(kb == 0), stop=(kb == nkb - 1)),
                                         reads=[bV[hk], bP[pidx[mm]]], writes=[self.bps[3 + mm]])
                                fw.pe.op(lambda mm=mm: nc.tensor.matmul(self.ps(5 + mm, 128, N), lhsT=ones_b[:], rhs=P[pidx[mm]][:],
                                                                        start=(kb == 0), stop=(kb == nkb - 1)),
                                         reads=[b_ones, bP[pidx[mm]]], writes=[self.bps[5 + mm]])
                            if kb == nkb - 1:
                                ok = (hd * NT + tt) % 2
                                fw.dve.op(lambda: nc.vector.reciprocal(R0[:], self.ps(5, 128, N)), reads=[self.bps[5]], writes=[bR[0]])
                                fw.dve.op(lambda: nc.vector.reciprocal(R1[:], self.ps(6, 128, N)), reads=[self.bps[6]], writes=[bR[1]])
                                fw.dve.op(lambda: nc.vector.tensor_tensor(out=O0[:], in0=self.ps(3, 128, N), in1=R0[:], op=ALU.mult), reads=[self.bps[3], bR[0]], writes=[bO[0]])
                                fw.dve.op(lambda: nc.vector.tensor_tensor(out=O1[:], in0=self.ps(4, 128, N), in1=R1[:], op=ALU.mult), reads=[self.bps[4], bR[1]], writes=[bO[1]])
                                fw.dve.op(lambda: nc.vector.scalar_tensor_tensor(out=O0[:], in0=O1[:], scalar=lt[:, 5:6], in1=O0[:], op0=ALU.mult, op1=ALU.add),
                                          reads=[bO[0], bO[1], b_lt], writes=[bO[0]])
                                fw.act.op(lambda: nc.scalar.activation(out=SQ[:], in_=O0[:], func=AF.Square), reads=[bO[0]], writes=[bSQ])
                                fw.pe.op(lambda: nc.tensor.matmul(self.ps(7, 128, N), lhsT=ones_f[:], rhs=SQ[:], start=True, stop=True),
                                         reads=[b_ones, bSQ], writes=[self.bps[7]])
                                fw.act.op(lambda: nc.scalar.activation(out=RS[:], in_=self.ps(7, 128, N), func=AF.Sqrt, bias=self.cst[:, 0:1], scale=float(1.0 / 128)),
                                          reads=[self.bps[7], self.b_cst], writes=[bRS])
                                fw.dve.op(lambda: nc.vector.reciprocal(RS[:], RS[:]), reads=[bRS], writes=[bRS])
                                fw.dve.op(lambda: nc.vector.scalar_tensor_tensor(out=O0[:], in0=O0[:], scalar=sg[:, 1:2], in1=RS[:], op0=ALU.mult, op1=ALU.mult),
                                          reads=[bO[0], bRS, b_sg], writes=[bO[0]])
                                fw.pool.op(lambda: nc.gpsimd.tensor_tensor(out=OG[ok][:], in0=O0[:], in1=Gt[qk][:], op=ALU.mult), reads=[bO[0], bG[qk]], writes=[bOG[ok]])
                                fw.st.op(lambda: nc.gpsimd.dma_start(out=self.OT[hd * 128:(hd + 1) * 128, t0:t0 + TW], in_=OG[ok][:]),
                                         reads=[bOG[ok]], writes=[self.b_OT])
                        items.append([st0, st1, st2])
            pipeline(items, 3)
            fw.barrier()

    def attn_sb(self):
        nc, fw, S, NB, NT = self.nc, self.fw, self.S, self.NB, self.NT
        N = TW
        with ExitStack() as es:
            sb = self.sbf(es)
            b_c = Buf()
            BEF, b_bef = self.build_masks(sb, "before")
            n8 = sb("sb_n8", [128, 128], BF16); tri = sb("sb_tri", [128, 128], BF16)
            fw.pool.op(lambda: nc.gpsimd.memset(n8[:], -8.0), writes=[b_c])
            fw.pool.op(lambda: nc.gpsimd.affine_select(out=tri[:], in_=n8[:], pattern=[[-1, 128]], compare_op=ALU.is_ge, fill=0.0,
                                                       base=0, channel_multiplier=1), reads=[b_c], writes=[b_c])
            one1 = self.cst[:, 3:4]
            KTt = [sb(f"sb_k{i}", [64, S], BF16) for i in range(2)]; bK = [Buf(), Buf()]
            Vt = [sb(f"sb_v{i}", [128, NB, 64], BF16) for i in range(2)]; bV = [Buf(), Buf()]
            Qt = [sb(f"sb_q{i}", [64, N], BF16) for i in range(3)]; bQ = [Buf() for _ in range(3)]
            Gt = [sb(f"sb_g{i}", [64, N], BF16) for i in range(3)]; bG = [Buf() for _ in range(3)]
            E1 = [sb(f"sb_e{i}", [128, N], F32) for i in range(2)]; bE = [Buf(), Buf()]
            SPf = sb("sb_spf", [128, N], F32); bSPf = Buf()
            SP = [sb(f"sb_sp{i}", [128, N], BF16) for i in range(3)]; bSP = [Buf() for _ in range(3)]
            AC = [sb(f"sb_ac{i}", [128, N], BF16) for i in range(3)]; bAC = [Buf() for _ in range(3)]
            A = [sb(f"sb_a{i}", [128, N], BF16) for i in range(3)]; bA = [Buf() for _ in range(3)]
            Af = sb("sb_af", [128, N], F32); bAf = Buf()
            OG = [sb(f"sb_og{i}", [64, N], BF16) for i in range(2)]; bOG = [Buf(), Buf()]
            items = []
            loaders = []
            ic = 0
            tile_i = 0
            for h in range(NH):
                hk = h % 2

                def load_head(h=h, hk=hk):
                    fw.ld.op(lambda: nc.sync.dma_start(out=KTt[hk][:], in_=self.KT[h * 64:(h + 1) * 64, :]), reads=[self.b_KT], writes=[bK[hk]])
                    fw.ld.op(lambda: nc.sync.dma_start(out=Vt[hk][:], in_=self.Vs[h]), reads=[self.b_Vs], writes=[bV[hk]])
                for tt in range(NT):
                    qk = tile_i % 3
                    ob = 5 + tile_i % 2
                    ogk = tile_i % 2
                    tile_i += 1
                    t0 = tt * TW
                    hi = 4 * tt + 3
                    lo = max(0, 4 * tt - SB_WIN)
                    kbs = list(range(hi, lo - 1, -1))

                    def load_q(h=h, qk=qk, t0=t0, tt=tt, lh=load_head):
                        if tt == 0:
                            lh()
                        fw.ld.op(lambda: nc.sync.dma_start(out=Qt[qk][:], in_=self.QT[h * 64:(h + 1) * 64, t0:t0 + N]), reads=[self.b_QT], writes=[bQ[qk]])
                        fw.ld.op(lambda: nc.sync.dma_start(out=Gt[qk][:], in_=self.GT[h * 64:(h + 1) * 64, t0:t0 + N]), reads=[self.b_GT], writes=[bG[qk]])
                    loaders.append(load_q)
                    tidx = len(loaders) - 1
                    prev_acc = None
                    for n, kb in enumerate(kbs):
                        j = kb - 4 * tt
                        diag = j >= 0
                        sbank = ic % 5
                        spk = ic % 3
                        ek = ic % 2
                        ic += 1
                        first = n == 0
                        lastk = n == len(kbs) - 1
                        pacc = prev_acc
                        acc_out = (SP[spk], bSP[spk]) if first else (AC[spk], bAC[spk])
                        prev_acc = acc_out

                        def st0(hk=hk, qk=qk, kb=kb, sbank=sbank, first=first, tidx=tidx):
                            if first:
                                if tidx == 0:
                                    loaders[0]()
                                if tidx + 1 < len(loaders):
                                    loaders[tidx + 1]()
                            fw.pe.op(lambda: nc.tensor.matmul(self.ps(sbank, 128, N), lhsT=KTt[hk][:, kb * 128:(kb + 1) * 128], rhs=Qt[qk][:], start=True, stop=False),
                                     reads=[bK[hk], bQ[qk]], writes=[self.bps[sbank]])

                        def st1(sbank=sbank, spk=spk, ek=ek, diag=diag, j=j, first=first, pacc=pacc, acc_out=acc_out):
                            fw.act.op(lambda: nc.scalar.activation(out=E1[ek][:], in_=self.ps(sbank, 128, N), func=AF.Exp, scale=0.125),
                                      reads=[self.bps[sbank]], writes=[bE[ek]])
                            if diag:
                                fw.act.op(lambda: nc.scalar.activation(out=SPf[:], in_=E1[ek][:], func=AF.Ln, bias=one1, scale=1.0),
                                          reads=[bE[ek], self.b_cst], writes=[bSPf])
                                fw.dve.op(lambda: nc.vector.tensor_tensor(out=SP[spk][:], in0=SPf[:], in1=BEF[j][:], op=ALU.mult),
                                          reads=[bSPf, b_bef], writes=[bSP[spk]])
                            else:
                                fw.act.op(lambda: nc.scalar.activation(out=SP[spk][:], in_=E1[ek][:], func=AF.Ln, bias=one1, scale=1.0),
                                          reads=[bE[ek], self.b_cst], writes=[bSP[spk]])
                            if not first:
                                fw.pool.op(lambda: nc.gpsimd.tensor_tensor(out=acc_out[0][:], in0=pacc[0][:], in1=SP[spk][:], op=ALU.add),
                                           reads=[pacc[1], bSP[spk]], writes=[acc_out[1]])

                        def st2(sbank=sbank, spk=spk, first=first, pacc=pacc):
                            fw.pe.op(lambda: nc.tensor.matmul(self.ps(sbank, 128, N), lhsT=tri[:], rhs=SP[spk][:], start=False, stop=first),
                                     reads=[b_c, bSP[spk]], writes=[self.bps[sbank]])
                            if not first:
                                fw.pe.op(lambda: nc.tensor.matmul(self.ps(sbank, 128, N), lhsT=n8[:], rhs=pacc[0][:], start=False, stop=True),
                                         reads=[b_c, pacc[1]], writes=[self.bps[sbank]])

                        def st3(sbank=sbank, spk=spk, diag=diag, j=j):
                            if diag:
                                fw.act.op(lambda: nc.scalar.activation(out=Af[:], in_=self.ps(sbank, 128, N), func=AF.Exp, scale=0.125),
                                          reads=[self.bps[sbank]], writes=[bAf])
                                fw.dve.op(lambda: nc.vector.tensor_tensor(out=A[spk][:], in0=Af[:], in1=BEF[j][:], op=ALU.mult),
                                          reads=[bAf, b_bef], writes=[bA[spk]])
                            else:
                                fw.act.op(lambda: nc.scalar.activation(out=A[spk][:], in_=self.ps(sbank, 128, N), func=AF.Exp, scale=0.125),
                                          reads=[self.bps[sbank]], writes=[bA[spk]])

                        def st4(hk=hk, kb=kb, spk=spk, first=first, lastk=lastk, ob=ob, ogk=ogk, qk=qk, h=h, t0=t0):
                            fw.pe.op(lambda: nc.tensor.matmul(self.ps(ob, 64, N), lhsT=Vt[hk][:, kb, :], rhs=A[spk][:], start=first, stop=lastk),
                                     reads=[bV[hk], bA[spk]], writes=[self.bps[ob]])
                            if lastk:
                                fw.dve.op(lambda: nc.vector.tensor_tensor(out=OG[ogk][:], in0=self.ps(ob, 64, N), in1=Gt[qk][:], op=ALU.mult),
                                          reads=[self.bps[ob], bG[qk]], writes=[bOG[ogk]])
                                fw.st.op(lambda: nc.gpsimd.dma_start(out=self.OT[h * 64:(h + 1) * 64, t0:t0 + N], in_=OG[ogk][:]),
                                         reads=[bOG[ogk]], writes=[self.b_OT])
                        items.append([st0, st1, st2, st3, st4])
            pipeline(items, 5)
            fw.barrier()

    def attn_dsa(self):
        nc, fw, S, NB, NT = self.nc, self.fw, self.S, self.NB, self.NT
        N = TW
        with ExitStack() as es:
            sb = self.sbf(es)
            b_c = Buf()
            kcb = sb("ds_kcb", [128, 128], F32)
            fw.pool.op(lambda: nc.gpsimd.memset(kcb[:, 0:64], 0.0), writes=[b_c])
            fw.pool.op(lambda: nc.gpsimd.memset(kcb[:, 64:128], 64.0), writes=[b_c])
            ones_b = sb("ds_ones", [128, 64], BF16)
            fw.pool.op(lambda: nc.gpsimd.memset(ones_b[:], 1.0), writes=[b_c])
            kiT = sb("ds_ki", [64, S], BF16); b_ki = Buf()
            fw.ld.op(lambda: nc.sync.dma_start(out=kiT[:], in_=self.KT[D:D + 64, :]), reads=[self.b_KT], writes=[b_ki])
            SC = sb("ds_sc", [128, S], F32); bSC = Buf()
            JK = sb("ds_jk", [128, S], BF16); bJK = Buf()
            MTs = [sb(f"ds_mt{i}", [128, NB, N], BF16) for i in range(2)]; bMTs = [Buf(), Buf()]
            QI = [sb(f"ds_qi{i}", [64, 8, 128], BF16) for i in range(2)]; bQI = [Buf(), Buf()]
            WQ = [sb(f"ds_wq{i}", [128, 8], F32) for i in range(2)]; bWQ = [Buf(), Buf()]
            PS_ = [sb(f"ds_pos{i}", [128, 2], F32) for i in range(2)]; bPS = [Buf(), Buf()]
            PM = sb("ds_pm", [128, 128], F32); bPM = Buf()
            RL = [sb(f"ds_rl{i}", [128, 512], F32) for i in range(2)]; bRL = [Buf() for _ in range(2)]
            BS = sb("ds_bs", [128, 8], F32); bBS = Buf()
            MK0 = sb("ds_mk0", [128, 512], F32); MK = [MK0, MK0]; bMK0 = Buf(); bMK = [bMK0, bMK0]
            KTt = [sb(f"ds_k{i}", [64, S], BF16) for i in range(2)]; bK = [Buf(), Buf()]
            Vt = [sb(f"ds_v{i}", [128, NB, 64], BF16) for i in range(3)]; bV = [Buf() for _ in range(3)]
            Qt = [sb(f"ds_q{i}", [64, N], BF16) for i in range(3)]; bQ = [Buf() for _ in range(3)]
            Gt = [sb(f"ds_g{i}", [64, N], BF16) for i in range(3)]; bG = [Buf() for _ in range(3)]
            PF = [sb(f"ds_pf{i}", [128, N], BF16) for i in range(2)]; bPF = [Buf(), Buf()]
            P = [sb(f"ds_p{i}", [128, N], BF16) for i in range(3)]; bP = [Buf() for _ in range(3)]
            Rr = sb("ds_r", [64, N], F32); bRr = Buf()
            O1 = sb("ds_o1", [64, N], F32); bO1 = Buf()
            OG = [sb(f"ds_og{i}", [64, N], BF16) for i in range(2)]; bOG = [Buf(), Buf()]
            ctr = {"rl": 0, "qi": 0, "ps": 0, "tile": 0}

            def nk_of(tt, i2):
                return 4 * tt + (2 if i2 == 0 else 4)

            def part1_yields(tt):
                n = 0
                for i2 in range(2):
                    nch = (nk_of(tt, i2) * 128 + 511) // 512
                    n += nch * 8 + 1 + BIS_ITERS + nch
                return n

            def part1(tt):
                MT, bMT = MTs[tt % 2], bMTs[tt % 2]
                for i2 in range(2):
                    lb = 2 * tt + i2
                    nk = nk_of(tt, i2)
                    ncols_tot = nk * 128
                    qk = ctr["qi"] % 2
                    ctr["qi"] += 1
                    fw.ld.op(lambda: nc.sync.dma_start(out=QI[qk][:], in_=self.QT[D:D + 512, lb * 128:(lb + 1) * 128].rearrange("(h d) t -> d h t", d=64)),
                             reads=[self.b_QT], writes=[bQI[qk]])
                    fw.ld.op(lambda: nc.sync.dma_start(out=WQ[qk][:], in_=self.WI[lb * 128:(lb + 1) * 128, :]), reads=[self.b_WI], writes=[bWQ[qk]])
                    fw.ld.op(lambda: nc.sync.dma_start(out=PS_[qk][:, 0:1], in_=self.poscol[lb * 128:(lb + 1) * 128, :]), writes=[bPS[qk]])
                    for c0 in range(0, ncols_tot, 512):
                        ncol = min(512, ncols_tot - c0)
                        for hh in range(8):
                            pb = 3 + ctr["ps"] % 2
                            ctr["ps"] += 1
                            rk = ctr["rl"] % 2
                            ctr["rl"] += 1
                            fw.pe.op(lambda: nc.tensor.matmul(self.ps(pb, 128, ncol), lhsT=QI[qk][:, hh, :], rhs=kiT[:, c0:c0 + ncol], start=True, stop=True),
                                     reads=[bQI[qk], b_ki], writes=[self.bps[pb]])
                            fw.act.op(lambda: nc.scalar.activation(out=RL[rk][:, 0:ncol], in_=self.ps(pb, 128, ncol), func=AF.Relu),
                                      reads=[self.bps[pb]], writes=[bRL[rk]])
                            if hh == 0:
                                fw.dve.op(lambda: nc.vector.tensor_scalar(out=SC[:, c0:c0 + ncol], in0=RL[rk][:, 0:ncol], scalar1=WQ[qk][:, 0:1], scalar2=None,
                                                                          op0=ALU.mult), reads=[bRL[rk], bWQ[qk]], writes=[bSC])
                            else:
                                fw.dve.op(lambda: nc.vector.scalar_tensor_tensor(out=SC[:, c0:c0 + ncol], in0=RL[rk][:, 0:ncol], scalar=WQ[qk][:, hh:hh + 1],
                                                                                 in1=SC[:, c0:c0 + ncol], op0=ALU.mult, op1=ALU.add),
                                          reads=[bRL[rk], bWQ[qk], bSC], writes=[bSC])
                            yield
                    for kb in range(4 * tt, nk):
                        dsl = slice(kb * 128, (kb + 1) * 128)
                        fw.dve.op(lambda kb=kb: nc.vector.tensor_scalar(out=PS_[qk][:, 1:2], in0=PS_[qk][:, 0:1], scalar1=float(-128 * kb), scalar2=None, op0=ALU.add),
                                  reads=[bPS[qk]], writes=[bPS[qk]])
                        fw.dve.op(lambda: nc.vector.tensor_scalar(out=PM[:], in0=kcb[:], scalar1=PS_[qk][:, 1:2], scalar2=-1e30, op0=ALU.is_gt, op1=ALU.mult),
                                  reads=[b_c, bPS[qk]], writes=[bPM])
                        fw.dve.op(lambda dsl=dsl: nc.vector.tensor_tensor(out=SC[:, dsl], in0=SC[:, dsl], in1=PM[:], op=ALU.add), reads=[bSC, bPM], writes=[bSC])
                    fw.dve.op(lambda: nc.vector.memset(BS[:, 0:1], -BIS_R), writes=[bBS])
                    fw.dve.op(lambda: nc.vector.memset(BS[:, 1:2], 0.0), writes=[bBS])
                    yield
                    hstep = BIS_R
                    for it in range(BIS_ITERS):
                        fw.dve.op(lambda: nc.vector.tensor_scalar(out=JK[:, 0:ncols_tot], in0=SC[:, 0:ncols_tot], scalar1=BS[:, 1:2], scalar2=0.0,
                                                                  op0=ALU.is_ge, op1=ALU.add, accum_out=BS[:, 2:3]), reads=[bSC, bBS], writes=[bJK, bBS])
                        fw.dve.op(lambda: nc.vector.tensor_scalar(out=BS[:, 3:4], in0=BS[:, 2:3], scalar1=float(TOPK) - 0.5, scalar2=float(hstep),
                                                                  op0=ALU.is_ge, op1=ALU.mult), reads=[bBS], writes=[bBS])
                        fw.dve.op(lambda: nc.vector.tensor_tensor(out=BS[:, 0:1], in0=BS[:, 0:1], in1=BS[:, 3:4], op=ALU.add), reads=[bBS], writes=[bBS])
                        hstep = hstep / 2
                        fw.dve.op(lambda: nc.vector.tensor_scalar(out=BS[:, 1:2], in0=BS[:, 0:1], scalar1=float(hstep), scalar2=None, op0=ALU.add),
                                  reads=[bBS], writes=[bBS])
                        yield
                    for c0 in range(0, ncols_tot, 512):
                        ncol = min(512, ncols_tot - c0)
                        mk = (c0 // 512) % 2
                        pb = 5
                        fw.dve.op(lambda: nc.vector.tensor_scalar(out=MK[mk][:, 0:ncol], in0=SC[:, c0:c0 + ncol], scalar1=BS[:, 0:1], scalar2=None, op0=ALU.is_ge),
                                  reads=[bSC, bBS], writes=[bMK[mk]])
                        nb4 = ncol // 128
                        for b4 in range(nb4):
                            fw.pe.op(lambda b4=b4: nc.tensor.transpose(self.ps(pb)[:, b4 * 128:(b4 + 1) * 128], MK[mk][:, b4 * 128:(b4 + 1) * 128], self.ident[:]),
                                     reads=[bMK[mk], self.b_ident], writes=[self.bps[pb]], inc=(b4 == nb4 - 1))
                        kb0 = c0 // 128
                        fw.act.op(lambda: nc.scalar.copy(MT[:, kb0:kb0 + nb4, i2 * 128:(i2 + 1) * 128],
                                                         self.ps(pb, 128, ncol).rearrange("p (b t) -> p b t", t=128)),
                                  reads=[self.bps[pb]], writes=[bMT])
                        yield
                    if nk < 4 * tt + 4:
                        fw.pool.op(lambda: nc.gpsimd.memset(MT[:, nk:4 * tt + 4, i2 * 128:(i2 + 1) * 128], 0.0), writes=[bMT])

            def part2_items(tt):
                MT, bMT = MTs[tt % 2], bMTs[tt % 2]
                t0 = tt * N
                nkb = 4 * tt + 4
                items = []
                loaders = []
                ic = 0
                for h in range(NH):
                    hk = h % 2
                    vk = h % 3
                    qk = ctr["tile"] % 3
                    ogk = ctr["tile"] % 2
                    ctr["tile"] += 1

                    def load_q(h=h, qk=qk, hk=hk, vk=vk):
                        fw.ld.op(lambda: nc.sync.dma_start(out=KTt[hk][:, 0:nkb * 128], in_=self.KT[h * 64:(h + 1) * 64, 0:nkb * 128]), reads=[self.b_KT], writes=[bK[hk]])
                        fw.ld.op(lambda: nc.sync.dma_start(out=Vt[vk][:, 0:nkb, :], in_=self.Vs[h, :, 0:nkb, :]), reads=[self.b_Vs], writes=[bV[vk]])
                        fw.ld.op(lambda: nc.sync.dma_start(out=Qt[qk][:], in_=self.QT[h * 64:(h + 1) * 64, t0:t0 + N]), reads=[self.b_QT], writes=[bQ[qk]])
                        fw.ld.op(lambda: nc.sync.dma_start(out=Gt[qk][:], in_=self.GT[h * 64:(h + 1) * 64, t0:t0 + N]), reads=[self.b_GT], writes=[bG[qk]])
                    loaders.append(load_q)
                    for kb in range(nkb):
                        sbank = ic % 3
                        pk = ic % 3
                        fk = ic % 2
                        ic += 1
                        first = kb == 0
                        lastk = kb == nkb - 1

                        def st0(h=h, hk=hk, qk=qk, kb=kb, sbank=sbank, first=first):
                            if first:
                                if h == 0:
                                    loaders[0]()
                                if h + 1 < NH:
                                    loaders[h + 1]()
                            fw.pe.op(lambda: nc.tensor.matmul(self.ps(sbank, 128, N), lhsT=KTt[hk][:, kb * 128:(kb + 1) * 128], rhs=Qt[qk][:], start=True, stop=True),
                                     reads=[bK[hk], bQ[qk]], writes=[self.bps[sbank]])

                        def st1(sbank=sbank, pk=pk, fk=fk, kb=kb):
                            fw.act.op(lambda: nc.scalar.activation(out=PF[fk][:], in_=self.ps(sbank, 128, N), func=AF.Exp, scale=0.125),
                                      reads=[self.bps[sbank]], writes=[bPF[fk]])
                            fw.pool.op(lambda: nc.gpsimd.tensor_tensor(out=P[pk][:], in0=PF[fk][:], in1=MT[:, kb, :], op=ALU.mult),
                                       reads=[bPF[fk], bMT], writes=[bP[pk]])

                        def st2(h=h, vk=vk, kb=kb, pk=pk, first=first, lastk=lastk, qk=qk, ogk=ogk):
                            fw.pe.op(lambda: nc.tensor.matmul(self.ps(6, 64, N), lhsT=Vt[vk][:, kb, :], rhs=P[pk][:], start=first, stop=lastk),
                                     reads=[bV[vk], bP[pk]], writes=[self.bps[6]])
                            fw.pe.op(lambda: nc.tensor.matmul(self.ps(7, 64, N), lhsT=ones_b[:], rhs=P[pk][:], start=first, stop=lastk),
                                     reads=[b_c, bP[pk]], writes=[self.bps[7]])
                            if lastk:
                                fw.act.op(lambda: nc.scalar.copy(O1[:], self.ps(6, 64, N)), reads=[self.bps[6]], writes=[bO1])
                                fw.dve.op(lambda: nc.vector.reciprocal(Rr[:], self.ps(7, 64, N)), reads=[self.bps[7]], writes=[bRr])
                                fw.pool.op(lambda: nc.gpsimd.tensor_tensor(out=O1[:], in0=O1[:], in1=Rr[:], op=ALU.mult), reads=[bO1, bRr], writes=[bO1])
                                fw.pool.op(lambda: nc.gpsimd.tensor_tensor(out=OG[ogk][:], in0=O1[:], in1=Gt[qk][:], op=ALU.mult), reads=[bO1, bG[qk]], writes=[bOG[ogk]])
                                fw.st.op(lambda: nc.gpsimd.dma_start(out=self.OT[h * 64:(h + 1) * 64, t0:t0 + N], in_=OG[ogk][:]),
                                         reads=[bOG[ogk]], writes=[self.b_OT])
                        items.append([st0, st1, st2])
                return items

            for _ in part1(0):
                pass
            for tt in range(NT):
                items = part2_items(tt)
                if tt + 1 < NT:
                    gen = part1(tt + 1)
                    ny = part1_yields(tt + 1)
                else:
                    gen, ny = None, 0
                nsteps = len(items) + 2
                done = 0
                n = len(items)
                for step in range(nsteps):
                    for s_ in range(2, -1, -1):
                        i_ = step - s_
                        if 0 <= i_ < n:
                            items[i_][s_]()
                    if gen is not None:
                        target = ((step + 1) * ny + nsteps - 1) // nsteps
                        while done < target:
                            next(gen, None)
                            done += 1
                if gen is not None:
                    for _ in gen:
                        pass
            fw.barrier()


_LAYERS = [(i % 3, i // 3, i) for i in range(DEPTH)]


def _consts():
    p = np.arange(128)
    i = (p % 32).astype(np.float32)
    inv = (1.0 / (np.float32(10000.0) ** (2 * i / np.float32(64)))).astype(np.float32)
    sign = np.where((p % 64) < 32, -1.0, 1.0).astype(np.float32)
    return np.stack([inv, sign], axis=1).astype(np.float32), np.eye(128, dtype=np.float32)


def own_rows(S, rho):
    idx = []
    for tt in range(S // 512):
        for c in own_blocks(rho):
            g = 4 * tt + c
            idx.append(np.arange(g * 128, (g + 1) * 128))
    return np.concatenate(idx)


def make_in_map(xb, rho, layers, w):
    S = xb.shape[0]
    ropec, ident = _consts()
    rows = own_rows(S, rho)
    im = {"x": np.ascontiguousarray(xb), "x_own": np.ascontiguousarray(xb[rows]), "ropec": ropec, "ident": ident}
    im["posrow"] = rows.astype(np.float32).reshape(1, -1)
    im["poscol"] = rows.astype(np.float32).reshape(-1, 1)
    im["qrel"] = (rows[:TW] % 512).astype(np.float32).reshape(1, TW)
    ws_in = {0: w["w_in_a"], 1: w["w_in_b"], 2: w["w_in_c"]}
    ws_out = {0: w["w_out_a"], 1: w["w_out_b"], 2: w["w_out_c"]}
    for li, (m, j, L) in enumerate(layers):
        im[f"w_in{li}"] = np.ascontiguousarray(ws_in[m][j])
        im[f"w_out{li}"] = np.ascontiguousarray(ws_out[m][j])
    im["ln_g"] = np.ascontiguousarray(np.stack([w["ln_g"][L] for (_, _, L) in layers]))
    im["ln_b"] = np.ascontiguousarray(np.stack([w["ln_b"][L] for (_, _, L) in layers]))
    im["lam"] = np.ascontiguousarray(np.stack([w["lambda_q1"][0], w["lambda_k1"][0], w["lambda_q2"][0], w["lambda_k2"][0]]).reshape(1, 256))
    im["subg"] = np.ascontiguousarray(w["subln_g"][0].reshape(128, 1))
    return im


def run_layers(x, w, layers, ncores):
    B, S, _ = x.shape
    assert ncores == 2 * B
    prog = Prog(S, layers, ncores=ncores)
    in_maps = [make_in_map(x[c // 2], c % 2, layers, w) for c in range(ncores)]
    res = run_bass_kernel_spmd(prog.nc, in_maps, core_ids=list(range(ncores)))
    out = np.empty((B, S, D), np.float32)
    for c in range(ncores):
        out[c // 2][own_rows(S, c % 2)] = res.results[c]["out"]
    return out


def kernel(**inputs):
    w = {k: np.asarray(v, dtype=np.float32) for k, v in inputs.items()}
    return run_layers(w["x"], w, _LAYERS, 8)
```

```python
import math
from contextlib import ExitStack
import numpy as np
import concourse.bass as bass
import concourse.mybir as mybir
from concourse.bass_utils import run_bass_kernel_spmd

F32 = mybir.dt.float32
BF16 = mybir.dt.bfloat16
I32 = mybir.dt.int32
AF = mybir.ActivationFunctionType
ALU = mybir.AluOpType

D = 1024
NH = 16
HD = 64
DEPTH = 4
ALPHA = (2.0 * DEPTH) ** 0.25
LN_EPS = 1e-5
RMS_EPS = 1e-5
A_IN = 4 * D + 8 * 64 + 8 + 64
TOPK = 256
SB_WIN = 3
NEGM = 30000.0
BIS_ITERS = 22
BIS_R = 16.0


class Buf:
    __slots__ = ("w", "r")

    def __init__(self):
        self.w = None
        self.r = {}


class Q:
    def __init__(self, fw, eng, name, is_dma, nsems, same_engine_waits=True):
        self.fw = fw
        self.eng = eng
        self.is_dma = is_dma
        self.inc = 16 if is_dma else 1
        self.sems = []
        for i in range(nsems):
            h = fw.nc.alloc_semaphore(name=f"s_{name}{i}")
            self.sems.append(len(fw.semh))
            fw.semh.append(h)
        self.cnt = [0] * nsems
        self.rr = 0
        self.waited = {}
        self.sew = same_engine_waits
        self.pending = False

    def _wait(self, sem, val):
        if self.waited.get(sem, 0) < val:
            self.eng.wait_ge(self.fw.semh[sem], val)
            self.waited[sem] = val

    def op(self, fn, reads=(), writes=(), inc=True):
        deps = {}
        for b in reads:
            if b.w is not None:
                s, v = b.w
                if deps.get(s, 0) < v:
                    deps[s] = v
        for b in writes:
            if b.w is not None:
                s, v = b.w
                if deps.get(s, 0) < v:
                    deps[s] = v
            for s, v in b.r.items():
                if deps.get(s, 0) < v:
                    deps[s] = v
        i = self.rr
        if self.is_dma:
            self.rr = (self.rr + 1) % len(self.sems)
            if self.cnt[i] > 0:
                deps[self.sems[i]] = max(deps.get(self.sems[i], 0), self.cnt[i])
        for s, v in deps.items():
            if (not self.sew) and s == self.sems[0]:
                continue
            self._wait(s, v)
        ins = fn()
        if inc:
            self.cnt[i] += self.inc
            ins.then_inc(self.fw.semh[self.sems[i]], self.inc)
            tag = (self.sems[i], self.cnt[i])
            self.pending = False
        else:
            tag = (self.sems[i], self.cnt[i] + self.inc)
            self.pending = True
        for b in reads:
            if b.r.get(tag[0], 0) < tag[1]:
                b.r[tag[0]] = tag[1]
        for b in writes:
            b.w = tag
            b.r = {}
        return tag

    def wait_all_of(self, other):
        for i, s in enumerate(other.sems):
            if other.cnt[i] > 0:
                self._wait(s, other.cnt[i])


class FW:
    def __init__(self, nc):
        self.nc = nc
        self.semh = []
        self.pe = Q(self, nc.tensor, "pe", False, 1, same_engine_waits=False)
        self.act = Q(self, nc.scalar, "act", False, 1)
        self.dve = Q(self, nc.vector, "dve", False, 1)
        self.pool = Q(self, nc.gpsimd, "pool", False, 1)
        self.ld = Q(self, nc.sync, "ld", True, 8)
        self.st = Q(self, nc.gpsimd, "st", True, 8)
        self.qs = [self.pe, self.act, self.dve, self.pool, self.ld, self.st]

    def barrier(self):
        for q in self.qs:
            assert not q.pending
        for q in self.qs:
            for o in self.qs:
                if o is not q:
                    q.wait_all_of(o)


def pipeline(items, nstages):
    n = len(items)
    for step in range(n + nstages - 1):
        for s in range(nstages - 1, -1, -1):
            i = step - s
            if 0 <= i < n and items[i][s] is not None:
                items[i][s]()


TW = 256


def own_blocks(rho):
    return (0, 3) if rho == 0 else (1, 2)


class Prog:
    def __init__(self, S, layers, ncores=8, debug=False):
        self.S = S
        self.SO = S // 2
        self.NB = S // 128
        self.NT = S // 512
        self.CW = min(1024, self.SO)
        self.NCH = self.SO // self.CW
        self.groups = [[2 * i, 2 * i + 1] for i in range(ncores // 2)]
        self.layers = layers
        self.debug = debug
        nc = self.nc = bass.Bass("TRN2", target_bir_lowering=False)
        self.fw = FW(nc)
        self.fw.cc = Q(self.fw, nc.gpsimd, "cc", False, 1)
        self.fw.qs.append(self.fw.cc)
        SO = self.SO
        dt = nc.dram_tensor
        self.x_in = dt("x", [S, D], F32, kind="ExternalInput").ap()
        self.x_own = dt("x_own", [SO, D], F32, kind="ExternalInput").ap()
        self.posrow = dt("posrow", [1, SO], F32, kind="ExternalInput").ap()
        self.poscol = dt("poscol", [SO, 1], F32, kind="ExternalInput").ap()
        self.qrel = dt("qrel", [1, TW], F32, kind="ExternalInput").ap()
        self.out = dt("out", [SO, D], F32, kind="ExternalOutput").ap()
        self.w_in = {}
        self.w_out = {}
        for li, (m, j, L) in enumerate(layers):
            width = A_IN if m == 0 else 4 * D
            self.w_in[li] = dt(f"w_in{li}", [D, width], F32, kind="ExternalInput").ap()
            self.w_out[li] = dt(f"w_out{li}", [D, D], F32, kind="ExternalInput").ap()
        self.lng = dt("ln_g", [len(layers), D], F32, kind="ExternalInput").ap()
        self.lnb = dt("ln_b", [len(layers), D], F32, kind="ExternalInput").ap()
        self.lam = dt("lam", [1, 256], F32, kind="ExternalInput").ap()
        self.subg = dt("subg", [128, 1], F32, kind="ExternalInput").ap()
        self.ropec = dt("ropec", [128, 2], F32, kind="ExternalInput").ap()
        self.ident_in = dt("ident", [128, 128], F32, kind="ExternalInput").ap()
        IK = "ExternalOutput" if debug else "Internal"
        self.xres = [dt(f"xres{i}", [SO, D], F32, kind="Internal").ap() for i in range(2)]
        self.xTf = dt("xTf", [D, S], BF16, kind="Internal").ap()
        self.xTo = [[dt(f"xTo{s_}_{c}", [D, self.CW], BF16) for c in range(self.NCH)] for s_ in range(2)]
        self.gath = [[dt(f"gath{l}_{c}", [2 * D, self.CW], BF16) for c in range(self.NCH)] for l in range(max(1, len(layers) - 1))]
        self.QT = dt("QT", [24 * 64, SO], BF16, kind=IK).ap()
        self.KT = dt("KT", [17 * 64, S], BF16, kind=IK).ap()
        self.GT = dt("GT", [D, SO], BF16, kind=IK).ap()
        self.OT = dt("OT", [D, SO], BF16, kind=IK).ap()
        self.Vs = dt("Vs", [NH, 128, self.NB, 64], BF16, kind=IK).ap()
        self.WI = dt("WI", [SO, 8], F32, kind="Internal").ap()
        self.CC = dt("CCt", [128, S], F32, kind="Internal").ap()
        self.SS = dt("SSt", [128, S], F32, kind="Internal").ap()
        self.CCo = dt("CCo", [128, SO], F32, kind="Internal").ap()
        self.SSo = dt("SSo", [128, SO], F32, kind="Internal").ap()
        if debug:
            self.dbgSC = dt("dbgSC", [128, 512], F32, kind="ExternalOutput").ap()
            self.dbgBS = dt("dbgBS", [128, 8], F32, kind="ExternalOutput").ap()
            self.dbgMT = dt("dbgMT", [128, 4, TW], BF16, kind="ExternalOutput").ap()
            self.dbgMT2 = dt("dbgMT2", [128, 4, TW], BF16, kind="ExternalOutput").ap()
            self.dbgP = dt("dbgP", [128, 4, TW], BF16, kind="ExternalOutput").ap()
            self.dbgO = dt("dbgO", [64, 2, TW], F32, kind="ExternalOutput").ap()
        self.b_xres = [Buf(), Buf()]
        self.b_xTf, self.b_QT, self.b_KT, self.b_GT, self.b_OT, self.b_Vs, self.b_WI, self.b_tab = (Buf() for _ in range(8))
        self.b_xTo = [Buf(), Buf()]
        self.b_gath = [Buf() for _ in self.gath]
        self.psall = nc.alloc_psum_tensor("psall", [128, 4096], F32)
        self.bps = [Buf() for _ in range(8)]
        self.build()

    def sbf(self, es):
        def f(n, sh, d):
            self._uid = getattr(self, "_uid", 0) + 1
            return es.enter_context(self.nc.sbuf_tensor(f"{n}_u{self._uid}", sh, d))
        return f

    def ps(self, i, parts=128, n=512):
        return self.psall[0:parts, i * 512:i * 512 + n]

    def build(self):
        nc, fw = self.nc, self.fw
        with ExitStack() as es:
            sb = self.sbf(es)
            self.ident = sb("identsb", [128, 128], F32)
            self.b_ident = Buf()
            fw.ld.op(lambda: nc.sync.dma_start(out=self.ident[:], in_=self.ident_in[:, :]), writes=[self.b_ident])
            self.cst = sb("cst", [128, 6], F32)
            self.identb = sb("identb", [128, 128], BF16)
            self.b_cst = Buf()
            fw.pool.op(lambda: nc.gpsimd.memset(self.cst[:, 0:1], float(LN_EPS)), writes=[self.b_cst])
            fw.pool.op(lambda: nc.gpsimd.iota(self.cst[:, 1:2], pattern=[[0, 1]], base=0, channel_multiplier=1,
                                              allow_small_or_imprecise_dtypes=True), writes=[self.b_cst])
            fw.pool.op(lambda: nc.gpsimd.memset(self.cst[0:64, 2:3], 0.0), writes=[self.b_cst])
            fw.pool.op(lambda: nc.gpsimd.memset(self.cst[64:128, 2:3], 64.0), writes=[self.b_cst])
            fw.pool.op(lambda: nc.gpsimd.memset(self.cst[:, 3:4], 1.0), writes=[self.b_cst])
            fw.pool.op(lambda: nc.gpsimd.memset(self.cst[:, 4:5], -NEGM), writes=[self.b_cst])
            fw.dve.op(lambda: nc.vector.tensor_copy(self.identb[:], self.ident[:]), reads=[self.b_ident], writes=[self.b_cst])
            self.QR = sb("qrelsb", [128, TW], F32)
            fw.ld.op(lambda: nc.sync.dma_start(out=self.QR[:], in_=self.qrel[0:1, :].broadcast_to([128, TW])), writes=[self.b_cst])
            self.rope_tables(self.CC, self.SS, self.S, None)
            self.rope_tables(self.CCo, self.SSo, self.SO, self.posrow)
            self.transpose_input()
            cur = 0
            for li, (m, j, L) in enumerate(self.layers):
                last = li == len(self.layers) - 1
                self.phase_a(li, m)
                if m == 0:
                    self.attn_dsa()
                elif m == 1:
                    self.attn_sb()
                else:
                    self.attn_diff(L)
                self.phase_c(li, cur, last)
                if not last:
                    self.exchange(li)
                cur ^= 1
            fw.barrier()

    def exchange(self, li):
        nc, fw = self.nc, self.fw
        sset = (li + 1) % 2
        for ch in range(self.NCH):
            fw.cc.op(lambda ch=ch: nc.gpsimd.collective_compute("AllGather", ALU.bypass, replica_groups=self.groups,
                                                                ins=[self.xTo[sset][ch].ap().opt()], outs=[self.gath[li][ch].ap().opt()]),
                     reads=[self.b_xTo[sset]], writes=[self.b_gath[li]])
        fw.barrier()

    def xt_all_src(self, li, tt, c):
        if li == 0:
            g = 4 * tt + c
            return self.xTf[:, g * 128:(g + 1) * 128], self.b_xTf
        rho = 0 if c in (0, 3) else 1
        lb = 2 * tt + (0 if c in (0, 1) else 1)
        ch, col = divmod(lb * 128, self.CW)
        return self.gath[li - 1][ch].ap()[rho * D:(rho + 1) * D, col:col + 128], self.b_gath[li - 1]

    def rope_tables(self, CCd, SSd, n, posrow):
        nc, fw = self.nc, self.fw
        with ExitStack() as es:
            sb = self.sbf(es)
            rc = sb("rc", [128, 2], F32); brc = Buf()
            fw.ld.op(lambda: nc.sync.dma_start(out=rc[:], in_=self.ropec[:, :]), writes=[brc])
            C1 = 6.28125
            C2 = 2 * math.pi - C1
            pools = {}
            for nm, dtp in [("it", F32), ("ang", F32), ("a2", F32), ("ki", I32), ("kf", F32), ("r0", F32), ("r1", F32)]:
                pools[nm] = [(sb(f"rt_{nm}{q}", [128, 512], dtp), Buf()) for q in range(2)]
            for t0 in range(0, n, 512):
                sl = (t0 // 512) % 2
                it, b_it = pools["it"][sl]
                ang, b_ang = pools["ang"][sl]
                for which in range(2):
                    a2, b_a2 = pools["a2"][sl]
                    ki, b_ki = pools["ki"][sl]
                    kf, b_kf = pools["kf"][sl]
                    r, b_r = pools["r0" if which == 0 else "r1"][sl]
                    if which == 0:
                        if posrow is None:
                            fw.pool.op(lambda: nc.gpsimd.iota(it[:], pattern=[[1, 512]], base=t0, channel_multiplier=0,
                                                              allow_small_or_imprecise_dtypes=True), writes=[b_it])
                        else:
                            fw.ld.op(lambda: nc.sync.dma_start(out=it[:], in_=posrow[0:1, t0:t0 + 512].broadcast_to([128, 512])), writes=[b_it])
                        fw.dve.op(lambda: nc.vector.tensor_scalar(out=ang[:], in0=it[:], scalar1=rc[:, 0:1], scalar2=None,
                                                                  op0=ALU.mult), reads=[b_it, brc], writes=[b_ang])
                        src = ang
                    else:
                        fw.dve.op(lambda: nc.vector.tensor_scalar(out=a2[:], in0=ang[:], scalar1=float(math.pi / 2), scalar2=None,
                                                                  op0=ALU.add), reads=[b_ang], writes=[b_a2])
                        src = a2
                    bsrc = b_ang if which == 0 else b_a2
                    fw.dve.op(lambda: nc.vector.tensor_scalar(out=ki[:], in0=src[:], scalar1=float(1 / (2 * math.pi)), scalar2=None,
                                                              op0=ALU.mult), reads=[bsrc], writes=[b_ki])
                    fw.dve.op(lambda: nc.vector.tensor_copy(kf[:], ki[:]), reads=[b_ki], writes=[b_kf])
                    fw.dve.op(lambda: nc.vector.scalar_tensor_tensor(out=r[:], in0=kf[:], scalar=-C1, in1=src[:], op0=ALU.mult,
                                                                     op1=ALU.add), reads=[b_kf, bsrc], writes=[b_r])
                    fw.dve.op(lambda: nc.vector.scalar_tensor_tensor(out=r[:], in0=kf[:], scalar=-C2, in1=r[:], op0=ALU.mult,
                                                                     op1=ALU.add), reads=[b_kf, b_r], writes=[b_r])
                    fw.dve.op(lambda: nc.vector.tensor_scalar(out=r[:], in0=r[:], scalar1=float(math.pi), scalar2=float(-math.pi),
                                                              op0=ALU.min, op1=ALU.max), reads=[b_r], writes=[b_r])
                    fw.act.op(lambda: nc.scalar.activation(out=r[:], in_=r[:], func=AF.Sin), reads=[b_r], writes=[b_r])
                    if which == 0:
                        fw.dve.op(lambda: nc.vector.tensor_scalar(out=r[:], in0=r[:], scalar1=rc[:, 1:2], scalar2=None,
                                                                  op0=ALU.mult), reads=[b_r, brc], writes=[b_r])
                        fw.st.op(lambda: nc.gpsimd.dma_start(out=SSd[:, t0:t0 + 512], in_=r[:]), reads=[b_r], writes=[self.b_tab])
                    else:
                        fw.st.op(lambda: nc.gpsimd.dma_start(out=CCd[:, t0:t0 + 512], in_=r[:]), reads=[b_r], writes=[self.b_tab])
            fw.barrier()

    def emit_transpose_store(self, xn, b_xn, dst_ap, b_dst, xtt, b_xtt, pbank):
        nc, fw = self.nc, self.fw
        for half in range(2):
            bank = pbank[half]
            for c4 in range(4):
                c = half * 4 + c4
                fw.pe.op(lambda c=c, c4=c4: nc.tensor.transpose(self.ps(bank)[:, c4 * 128:(c4 + 1) * 128], xn[:, c * 128:(c + 1) * 128],
                                                                self.ident[:]),
                         reads=[b_xn, self.b_ident], writes=[self.bps[bank]], inc=(c4 == 3))
            fw.act.op(lambda half=half: nc.scalar.copy(xtt[:, half * 4:(half + 1) * 4, :],
                                                        self.ps(bank).rearrange("p (c t) -> p c t", c=4)),
                      reads=[self.bps[bank]], writes=[b_xtt])
        fw.st.op(lambda: nc.gpsimd.dma_start(out=dst_ap.rearrange("(c p) t -> p c t", p=128), in_=xtt[:]),
                 reads=[b_xtt], writes=[b_dst])

    def xto_dst(self, sset, lb):
        ch, col = divmod(lb * 128, self.CW)
        return self.xTo[sset][ch].ap()[:, col:col + 128]

    def transpose_input(self):
        nc, fw = self.nc, self.fw
        with ExitStack() as es:
            sb = self.sbf(es)
            xt = [sb(f"ti_x{i}", [128, D], F32) for i in range(2)]
            bx = [Buf(), Buf()]
            xtt = [sb(f"ti_xt{i}", [128, 8, 128], BF16) for i in range(2)]
            bxtt = [Buf(), Buf()]
            n = 0
            for tb in range(self.NB):
                k = n % 2
                n += 1
                fw.ld.op(lambda: nc.sync.dma_start(out=xt[k][:], in_=self.x_in[tb * 128:(tb + 1) * 128, :]), writes=[bx[k]])
                self.emit_transpose_store(xt[k], bx[k], self.xTf[:, tb * 128:(tb + 1) * 128], self.b_xTf, xtt[k], bxtt[k], (2 * k, 2 * k + 1))
            for lb in range(self.SO // 128):
                k = n % 2
                n += 1
                fw.ld.op(lambda: nc.sync.dma_start(out=xt[k][:], in_=self.x_own[lb * 128:(lb + 1) * 128, :]), writes=[bx[k]])
                self.emit_transpose_store(xt[k], bx[k], self.xto_dst(0, lb), self.b_xTo[0], xtt[k], bxtt[k], (2 * k, 2 * k + 1))
            fw.barrier()

    def phase_a(self, li, m):
        nc, fw, S = self.nc, self.fw, self.S
        width = A_IN if m == 0 else 4 * D
        rope = m in (0, 2)
        rot_blocks = []
        if rope:
            rot_blocks = [(0, 16), (D, 16)]
            if m == 0:
                rot_blocks += [(4 * D, 8), (4 * D + 520, 1)]
        nrot = sum(nh for _, nh in rot_blocks) * 64
        sset = li % 2
        with ExitStack() as es:
            sb = self.sbf(es)
            WIN = sb("WIN", [128, 8, width], BF16); b_win = Buf()
            WROT = sb("WROT", [128, 8, max(nrot, 64)], BF16)
            stg0 = sb("wstg0", [128, width], F32)
            stg = [stg0, stg0]
            bstg0 = Buf()
            bstg = [bstg0, bstg0]
            rot_off = {}
            off = 0
            for col0, nh in rot_blocks:
                rot_off[col0] = off
                off += nh * 64
            for c in range(8):
                k = c % 2
                fw.ld.op(lambda: nc.sync.dma_start(out=stg[k][:], in_=self.w_in[li][c * 128:(c + 1) * 128, :]), writes=[bstg[k]])
                h2 = width // 2
                fw.act.op(lambda: nc.scalar.copy(WIN[:, c, 0:h2], stg[k][:, 0:h2]), reads=[bstg[k]], writes=[b_win])
                fw.dve.op(lambda: nc.vector.tensor_copy(WIN[:, c, h2:width], stg[k][:, h2:width]), reads=[bstg[k]], writes=[b_win])
                for col0, nh in rot_blocks:
                    ro = rot_off[col0]
                    src = stg[k][:, col0:col0 + nh * 64].rearrange("p (h t i) -> p h t i", t=2, i=32)
                    dst = WROT[:, c, ro:ro + nh * 64].rearrange("p (h t i) -> p h t i", t=2, i=32)
                    fw.pool.op(lambda: nc.gpsimd.tensor_copy(dst[:, :, 0, :], src[:, :, 1, :]), reads=[bstg[k]], writes=[b_win])
                    fw.pool.op(lambda: nc.gpsimd.tensor_copy(dst[:, :, 1, :], src[:, :, 0, :]), reads=[bstg[k]], writes=[b_win])
            kgroups = [(self.KT, self.b_KT, g * 128, D + g * 128, 128, rope, None) for g in range(8)]
            if m == 0:
                kgroups.append((self.KT, self.b_KT, D, 4 * D + 520, 64, True, None))
            qgroups = [(self.QT, self.b_QT, g * 128, g * 128, 128, rope, None) for g in range(8)]
            qgroups += [(self.GT, self.b_GT, g * 128, 3 * D + g * 128, 128, False, "silu") for g in range(8)]
            if m == 0:
                qgroups += [(self.QT, self.b_QT, D + g * 128, 4 * D + g * 128, 128, True, None) for g in range(4)]

            def rotcol(col):
                for col0, nh in rot_blocks:
                    if col0 <= col < col0 + nh * 64:
                        return rot_off[col0] + (col - col0)
                raise AssertionError

            XT = [sb(f"pa_xt{i}", [128, 8, 512], BF16) for i in range(2)]; bXT = [Buf(), Buf()]
            XO = [sb(f"pa_xo{i}", [128, 8, TW], BF16) for i in range(2)]; bXO = [Buf(), Buf()]
            CCt = [sb(f"pa_cc{i}", [128, 512], F32) for i in range(2)]; bCC = [Buf(), Buf()]
            SSt = [sb(f"pa_ss{i}", [128, 512], F32) for i in range(2)]; bSS = [Buf(), Buf()]
            CCq = [sb(f"pa_ccq{i}", [128, TW], F32) for i in range(2)]; bCCq = [Buf(), Buf()]
            SSq = [sb(f"pa_ssq{i}", [128, TW], F32) for i in range(2)]; bSSq = [Buf(), Buf()]
            T1 = [sb(f"pa_t1{i}", [128, 512], F32) for i in range(2)]; bT1 = [Buf(), Buf()]
            T2 = [sb(f"pa_t2{i}", [128, 512], F32) for i in range(2)]; bT2 = [Buf(), Buf()]
            OB = [sb(f"pa_ob{i}", [128, 512], BF16) for i in range(3)]; bOB = [Buf() for _ in range(3)]
            VT0 = sb("pa_vt0", [128, 4, D], BF16); VT = [VT0, VT0]; bVT0 = Buf(); bVT = [bVT0, bVT0]
            WIt = [sb(f"pa_wi{i}", [128, 8], F32) for i in range(2)]; bWI = [Buf(), Buf()]
            gi = [0]

            def do_group(grp, xin, bxin, N, t0, cc, bcc, ss, bss):
                (dst, bdst, row0, col0, M, rp, act) = grp
                pb = (gi[0] % 2) * 2
                ob = gi[0] % 3
                tk = gi[0] % 2
                gi[0] += 1
                for c in range(8):
                    fw.pe.op(lambda c=c: nc.tensor.matmul(self.ps(pb, M, N), lhsT=WIN[:, c, col0:col0 + M], rhs=xin[:, c, :],
                                                          start=(c == 0), stop=(c == 7)),
                             reads=[b_win, bxin], writes=[self.bps[pb]], inc=(c == 7))
                if rp:
                    rc0 = rotcol(col0)
                    for c in range(8):
                        fw.pe.op(lambda c=c: nc.tensor.matmul(self.ps(pb + 1, M, N), lhsT=WROT[:, c, rc0:rc0 + M], rhs=xin[:, c, :],
                                                              start=(c == 0), stop=(c == 7)),
                                 reads=[b_win, bxin], writes=[self.bps[pb + 1]], inc=(c == 7))
                    fw.dve.op(lambda: nc.vector.tensor_tensor(out=T1[tk][0:M, 0:N], in0=self.ps(pb, M, N), in1=cc[0:M, :], op=ALU.mult),
                              reads=[self.bps[pb], bcc], writes=[bT1[tk]])
                    fw.dve.op(lambda: nc.vector.tensor_tensor(out=T2[tk][0:M, 0:N], in0=self.ps(pb + 1, M, N), in1=ss[0:M, :], op=ALU.mult),
                              reads=[self.bps[pb + 1], bss], writes=[bT2[tk]])
                    fw.pool.op(lambda: nc.gpsimd.tensor_tensor(out=OB[ob][0:M, 0:N], in0=T1[tk][0:M, 0:N], in1=T2[tk][0:M, 0:N], op=ALU.add),
                               reads=[bT1[tk], bT2[tk]], writes=[bOB[ob]])
                elif act == "silu":
                    fw.act.op(lambda: nc.scalar.activation(out=OB[ob][0:M, 0:N], in_=self.ps(pb, M, N), func=AF.Silu),
                              reads=[self.bps[pb]], writes=[bOB[ob]])
                else:
                    fw.act.op(lambda: nc.scalar.copy(OB[ob][0:M, 0:N], self.ps(pb, M, N)), reads=[self.bps[pb]], writes=[bOB[ob]])
                fw.st.op(lambda: nc.gpsimd.dma_start(out=dst[row0:row0 + M, t0:t0 + N], in_=OB[ob][0:M, 0:N]),
                         reads=[bOB[ob]], writes=[bdst])

            for tt in range(self.NT):
                k = tt % 2
                t0 = tt * 512
                for c in range(4):
                    src, bsrc = self.xt_all_src(li, tt, c)
                    fw.ld.op(lambda: nc.sync.dma_start(out=XT[k][:, :, c * 128:(c + 1) * 128], in_=src.rearrange("(c p) t -> p c t", p=128)),
                             reads=[bsrc], writes=[bXT[k]])
                l0 = tt * TW
                ch, col = divmod(l0, self.CW)
                fw.ld.op(lambda: nc.sync.dma_start(out=XO[k][:], in_=self.xTo[sset][ch].ap()[:, col:col + TW].rearrange("(c p) t -> p c t", p=128)),
                         reads=[self.b_xTo[sset]], writes=[bXO[k]])
                if rope:
                    fw.ld.op(lambda: nc.sync.dma_start(out=CCt[k][:], in_=self.CC[:, t0:t0 + 512]), reads=[self.b_tab], writes=[bCC[k]])
                    fw.ld.op(lambda: nc.sync.dma_start(out=SSt[k][:], in_=self.SS[:, t0:t0 + 512]), reads=[self.b_tab], writes=[bSS[k]])
                    fw.ld.op(lambda: nc.sync.dma_start(out=CCq[k][:], in_=self.CCo[:, l0:l0 + TW]), reads=[self.b_tab], writes=[bCCq[k]])
                    fw.ld.op(lambda: nc.sync.dma_start(out=SSq[k][:], in_=self.SSo[:, l0:l0 + TW]), reads=[self.b_tab], writes=[bSSq[k]])
                for grp in kgroups:
                    do_group(grp, XT[k], bXT[k], 512, t0, CCt[k], bCC[k], SSt[k], bSS[k])
                for grp in qgroups:
                    do_group(grp, XO[k], bXO[k], TW, l0, CCq[k], bCCq[k], SSq[k], bSSq[k])
                for tb in range(4):
                    for half in range(2):
                        pb = 4 + half
                        for c in range(8):
                            fw.pe.op(lambda c=c: nc.tensor.matmul(self.ps(pb), lhsT=XT[k][:, c, tb * 128:(tb + 1) * 128],
                                                                  rhs=WIN[:, c, 2 * D + half * 512:2 * D + (half + 1) * 512],
                                                                  start=(c == 0), stop=(c == 7)),
                                     reads=[b_win, bXT[k]], writes=[self.bps[pb]], inc=(c == 7))
                        if half == 0:
                            fw.act.op(lambda: nc.scalar.copy(VT[k][:, tb, 0:512], self.ps(pb)), reads=[self.bps[pb]], writes=[bVT[k]])
                        else:
                            fw.dve.op(lambda: nc.vector.tensor_copy(VT[k][:, tb, 512:1024], self.ps(pb)), reads=[self.bps[pb]], writes=[bVT[k]])
                if m == 0:
                    for tb in range(2):
                        pb = 6
                        wk = (tt * 2 + tb) % 2
                        for c in range(8):
                            fw.pe.op(lambda c=c: nc.tensor.matmul(self.ps(pb, 128, 8), lhsT=XO[k][:, c, tb * 128:(tb + 1) * 128],
                                                                  rhs=WIN[:, c, 4 * D + 512:4 * D + 520], start=(c == 0), stop=(c == 7)),
                                     reads=[b_win, bXO[k]], writes=[self.bps[pb]], inc=(c == 7))
                        fw.dve.op(lambda: nc.vector.tensor_scalar(out=WIt[wk][:], in0=self.ps(pb, 128, 8), scalar1=float(8 ** -0.5 * 64 ** -0.5),
                                                                  scalar2=None, op0=ALU.mult), reads=[self.bps[pb]], writes=[bWI[wk]])
                        r0 = l0 + tb * 128
                        fw.st.op(lambda: nc.gpsimd.dma_start(out=self.WI[r0:r0 + 128, :], in_=WIt[wk][:]), reads=[bWI[wk]], writes=[self.b_WI])
                for h in range(NH):
                    fw.st.op(lambda h=h: nc.gpsimd.dma_start(out=self.Vs[h, :, tt * 4:(tt + 1) * 4, :], in_=VT[k][:, :, h * 64:(h + 1) * 64]),
                             reads=[bVT[k]], writes=[self.b_Vs])
            fw.barrier()

    def phase_c(self, li, cur, last):
        nc, fw = self.nc, self.fw
        first = li == 0
        src = self.x_own if first else self.xres[cur]
        bsrc = Buf() if first else self.b_xres[cur]
        dst = self.out if last else self.xres[cur ^ 1]
        bdst = Buf() if last else self.b_xres[cur ^ 1]
        wset = (li + 1) % 2
        with ExitStack() as es:
            sb = self.sbf(es)
            WO = sb("WO", [128, 8, D], BF16); b_wo = Buf()
            stg = [sb(f"wostg{i}", [128, D], F32) for i in range(2)]; bstg = [Buf(), Buf()]
            for c in range(8):
                k = c % 2
                fw.ld.op(lambda: nc.sync.dma_start(out=stg[k][:], in_=self.w_out[li][c * 128:(c + 1) * 128, :]), writes=[bstg[k]])
                fw.act.op(lambda: nc.scalar.copy(WO[:, c, :], stg[k][:]), reads=[bstg[k]], writes=[b_wo])
            G = sb("lnG", [128, D], F32); Bt = sb("lnB", [128, D], F32); b_gb = Buf()
            fw.ld.op(lambda: nc.sync.dma_start(out=G[:], in_=self.lng[li:li + 1, :].broadcast_to([128, D])), writes=[b_gb])
            fw.ld.op(lambda: nc.sync.dma_start(out=Bt[:], in_=self.lnb[li:li + 1, :].broadcast_to([128, D])), writes=[b_gb])
            OTt = [sb(f"pc_ot{i}", [128, 8, 512], BF16) for i in range(2)]; bOT = [Buf(), Buf()]
            XR = [sb(f"pc_x{i}", [128, D], F32) for i in range(2)]; bXR = [Buf(), Buf()]
            U = [sb(f"pc_u{i}", [128, D], F32) for i in range(2)]; bU = [Buf(), Buf()]
            XN = [sb(f"pc_xn{i}", [128, D], F32) for i in range(2)]; bXN = [Buf(), Buf()]
            SQ = sb("pc_sq", [128, D], F32); bSQ = Buf()
            ST = [sb(f"pc_st{i}", [128, 8], F32) for i in range(2)]; bST = [Buf(), Buf()]
            XTT = [sb(f"pc_xtt{i}", [128, 8, 128], BF16) for i in range(2)]; bXTT = [Buf(), Buf()]
            for tt in range(self.SO // 512):
                kk = tt % 2
                t0 = tt * 512
                fw.ld.op(lambda: nc.sync.dma_start(out=OTt[kk][:], in_=self.OT[:, t0:t0 + 512].rearrange("(c p) t -> p c t", p=128)),
                         reads=[self.b_OT], writes=[bOT[kk]])
                for tb4 in range(4):
                    tb = tt * 4 + tb4
                    k = tb % 2
                    fw.ld.op(lambda: nc.sync.dma_start(out=XR[k][:], in_=src[tb * 128:(tb + 1) * 128, :]), reads=[bsrc], writes=[bXR[k]])
                    pbs = (4 * k, 4 * k + 1)
                    for half in range(2):
                        for c in range(8):
                            fw.pe.op(lambda c=c, half=half: nc.tensor.matmul(self.ps(pbs[half]), lhsT=OTt[kk][:, c, tb4 * 128:(tb4 + 1) * 128],
                                                                             rhs=WO[:, c, half * 512:(half + 1) * 512], start=(c == 0), stop=(c == 7)),
                                     reads=[b_wo, bOT[kk]], writes=[self.bps[pbs[half]]], inc=(c == 7))
                    st = ST[k]
                    for half in range(2):
                        hs = slice(half * 512, (half + 1) * 512)
                        fw.dve.op(lambda half=half, hs=hs: nc.vector.scalar_tensor_tensor(out=U[k][:, hs], in0=XR[k][:, hs], scalar=float(ALPHA),
                                                                                          in1=self.ps(pbs[half]), op0=ALU.mult, op1=ALU.add),
                                  reads=[bXR[k], self.bps[pbs[half]]], writes=[bU[k]])
                    fw.act.op(lambda: nc.scalar.activation(out=SQ[:], in_=U[k][:], func=AF.Copy, accum_out=st[:, 0:1]), reads=[bU[k]], writes=[bSQ, bST[k]])
                    fw.act.op(lambda: nc.scalar.activation(out=SQ[:], in_=U[k][:], func=AF.Square, accum_out=st[:, 1:2]), reads=[bU[k]], writes=[bSQ, bST[k]])
                    fw.dve.op(lambda: nc.vector.tensor_scalar(out=st[:, 2:3], in0=st[:, 0:1], scalar1=float(1.0 / D), scalar2=None, op0=ALU.mult),
                              reads=[bST[k]], writes=[bST[k]])
                    fw.dve.op(lambda: nc.vector.tensor_tensor(out=st[:, 3:4], in0=st[:, 2:3], in1=st[:, 2:3], op=ALU.mult), reads=[bST[k]], writes=[bST[k]])
                    fw.dve.op(lambda: nc.vector.scalar_tensor_tensor(out=st[:, 4:5], in0=st[:, 1:2], scalar=float(1.0 / D), in1=st[:, 3:4],
                                                                     op0=ALU.mult, op1=ALU.subtract), reads=[bST[k]], writes=[bST[k]])
                    fw.act.op(lambda: nc.scalar.activation(out=st[:, 5:6], in_=st[:, 4:5], func=AF.Sqrt, bias=self.cst[:, 0:1], scale=1.0),
                              reads=[bST[k], self.b_cst], writes=[bST[k]])
                    fw.dve.op(lambda: nc.vector.reciprocal(st[:, 5:6], st[:, 5:6]), reads=[bST[k]], writes=[bST[k]])
                    fw.dve.op(lambda: nc.vector.tensor_scalar(out=XN[k][:], in0=U[k][:], scalar1=st[:, 2:3], scalar2=st[:, 5:6], op0=ALU.subtract, op1=ALU.mult),
                              reads=[bU[k], bST[k]], writes=[bXN[k]])
                    fw.pool.op(lambda: nc.gpsimd.tensor_tensor(out=XN[k][:], in0=XN[k][:], in1=G[:], op=ALU.mult), reads=[bXN[k], b_gb], writes=[bXN[k]])
                    fw.pool.op(lambda: nc.gpsimd.tensor_tensor(out=XN[k][:], in0=XN[k][:], in1=Bt[:], op=ALU.add), reads=[bXN[k], b_gb], writes=[bXN[k]])
                    fw.st.op(lambda: nc.gpsimd.dma_start(out=dst[tb * 128:(tb + 1) * 128, :], in_=XN[k][:]), reads=[bXN[k]], writes=[bdst])
                    if not last:
                        self.emit_transpose_store(XN[k], bXN[k], self.xto_dst(wset, tb), self.b_xTo[wset], XTT[k], bXTT[k], (4 * k + 2, 4 * k + 3))
            fw.barrier()

    def build_masks(self, sb, kind):
        nc, fw = self.nc, self.fw
        Ms = [sb(f"mask_{kind}{j}", [128, TW], BF16) for j in range(4)]
        b = Buf()
        for j in range(4):
            if kind == "before":
                fw.dve.op(lambda j=j: nc.vector.tensor_scalar(out=Ms[j][:], in0=self.QR[:], scalar1=float(-128 * j), scalar2=self.cst[:, 1:2],
                                                              op0=ALU.add, op1=ALU.is_gt), reads=[self.b_cst], writes=[b])
            else:
                fw.dve.op(lambda j=j: nc.vector.tensor_scalar(out=Ms[j][:], in0=self.QR[:], scalar1=float(-128 * j), scalar2=self.cst[:, 2:3],
                                                              op0=ALU.add, op1=ALU.is_ge), reads=[self.b_cst], writes=[b])
        return Ms, b

    def attn_diff(self, L):
        nc, fw, S, NB, NT = self.nc, self.fw, self.S, self.NB, self.NT
        lambda_init = 0.8 - 0.6 * math.exp(-0.3 * L)
        with ExitStack() as es:
            sb = self.sbf(es)
            lv = sb("lv", [128, 4, 64], F32); b_lv = Buf()
            fw.ld.op(lambda: nc.sync.dma_start(out=lv[:].rearrange("p a b -> p (a b)"), in_=self.lam[0:1, :].broadcast_to([128, 256])), writes=[b_lv])
            lt = sb("lt", [128, 8], F32); b_lt = Buf()
            pr = sb("lpr", [128, 2, 64], F32)
            fw.dve.op(lambda: nc.vector.tensor_tensor(out=pr[:, 0, :], in0=lv[:, 0, :], in1=lv[:, 1, :], op=ALU.mult), reads=[b_lv], writes=[b_lt])
            fw.dve.op(lambda: nc.vector.tensor_tensor(out=pr[:, 1, :], in0=lv[:, 2, :], in1=lv[:, 3, :], op=ALU.mult), reads=[b_lv], writes=[b_lt])
            fw.dve.op(lambda: nc.vector.reduce_sum(out=lt[:, 0:1], in_=pr[:, 0, :], axis=mybir.AxisListType.X), reads=[b_lt], writes=[b_lt])
            fw.dve.op(lambda: nc.vector.reduce_sum(out=lt[:, 1:2], in_=pr[:, 1, :], axis=mybir.AxisListType.X), reads=[b_lt], writes=[b_lt])
            fw.act.op(lambda: nc.scalar.activation(out=lt[:, 2:4], in_=lt[:, 0:2], func=AF.Exp), reads=[b_lt], writes=[b_lt])
            fw.dve.op(lambda: nc.vector.tensor_tensor(out=lt[:, 4:5], in0=lt[:, 3:4], in1=lt[:, 2:3], op=ALU.subtract), reads=[b_lt], writes=[b_lt])
            fw.dve.op(lambda: nc.vector.tensor_scalar(out=lt[:, 5:6], in0=lt[:, 4:5], scalar1=float(-lambda_init), scalar2=None, op0=ALU.add),
                      reads=[b_lt], writes=[b_lt])
            sg = sb("sg", [128, 2], F32); b_sg = Buf()
            fw.ld.op(lambda: nc.sync.dma_start(out=sg[:, 0:1], in_=self.subg[:, :]), writes=[b_sg])
            fw.dve.op(lambda: nc.vector.tensor_scalar(out=sg[:, 1:2], in0=sg[:, 0:1], scalar1=float(1.0 - lambda_init), scalar2=None, op0=ALU.mult),
                      reads=[b_sg], writes=[b_sg])
            ones_b = sb("ones_b", [128, 128], BF16); ones_f = sb("ones_f", [128, 128], F32); b_ones = Buf()
            fw.pool.op(lambda: nc.gpsimd.memset(ones_b[:], 1.0), writes=[b_ones])
            fw.pool.op(lambda: nc.gpsimd.memset(ones_f[:], 1.0), writes=[b_ones])
            CM, b_cm = self.build_masks(sb, "chunk")
            KTt = [sb(f"df_k{i}", [64, 2, S], BF16) for i in range(2)]; bK = [Buf(), Buf()]
            Vt = [sb(f"df_v{i}", [128, NB, 128], BF16) for i in range(2)]; bV = [Buf(), Buf()]
            Qt = [sb(f"df_q{i}", [64, 2, TW], BF16) for i in range(3)]; bQ = [Buf() for _ in range(3)]
            Gt = [sb(f"df_g{i}", [128, TW], BF16) for i in range(3)]; bG = [Buf() for _ in range(3)]
            P = [sb(f"df_p{i}", [128, TW], BF16) for i in range(10)]; bP = [Buf() for _ in range(10)]
            PF = [sb(f"df_pf{i}", [128, TW], F32) for i in range(2)]; bPF = [Buf(), Buf()]
            R0 = sb("df_r0", [128, TW], F32); R1 = sb("df_r1", [128, TW], F32); bR = [Buf(), Buf()]
            O0 = sb("df_o0", [128, TW], F32); O1 = sb("df_o1", [128, TW], F32); bO = [Buf(), Buf()]
            SQ = sb("df_sq", [128, TW], F32); bSQ = Buf()
            RS = sb("df_rs", [128, TW], F32); bRS = Buf()
            OG = [sb(f"df_og{i}", [128, TW], BF16) for i in range(2)]; bOG = [Buf(), Buf()]
            N = TW
            sctr = [0]
            pctr = [0]
            items = []
            loaders = []
            for hd in range(8):
                hk = hd % 2

                def load_head(hd=hd, hk=hk):
                    for mm in range(2):
                        fw.ld.op(lambda mm=mm: nc.sync.dma_start(out=KTt[hk][:, mm, :], in_=self.KT[(2 * hd + mm) * 64:(2 * hd + mm + 1) * 64, :]),
                                 reads=[self.b_KT], writes=[bK[hk]])
                        fw.ld.op(lambda mm=mm: nc.sync.dma_start(out=Vt[hk][:, :, mm * 64:(mm + 1) * 64], in_=self.Vs[2 * hd + mm]),
                                 reads=[self.b_Vs], writes=[bV[hk]])
                for tt in range(NT):
                    qk = (hd * NT + tt) % 3
                    t0 = tt * TW
                    nkb = 4 * tt + 4

                    def load_q(hd=hd, qk=qk, t0=t0, tt=tt, hk=hk, lh=load_head):
                        if tt == 0:
                            lh()
                        for mm in range(2):
                            fw.ld.op(lambda mm=mm: nc.sync.dma_start(out=Qt[qk][:, mm, :], in_=self.QT[(2 * hd + mm) * 64:(2 * hd + mm + 1) * 64, t0:t0 + TW]),
                                     reads=[self.b_QT], writes=[bQ[qk]])
                        fw.ld.op(lambda: nc.sync.dma_start(out=Gt[qk][:], in_=self.GT[hd * 128:(hd + 1) * 128, t0:t0 + TW]),
                                 reads=[self.b_GT], writes=[bG[qk]])
                    for kb in range(nkb):
                        j = kb - 4 * tt
                        sbk = []
                        pidx = []
                        for mm in range(2):
                            sbk.append(sctr[0] % 3); sctr[0] += 1
                            pidx.append(pctr[0] % 10); pctr[0] += 1
                        if kb == 0:
                            loaders.append(load_q)
                        tidx = len(loaders) - 1

                        def st0(hk=hk, qk=qk, kb=kb, sbk=sbk, first=(kb == 0), tidx=tidx):
                            if first:
                                if tidx == 0:
                                    loaders[0]()
                                if tidx + 1 < len(loaders):
                                    loaders[tidx + 1]()
                            for mm in range(2):
                                fw.pe.op(lambda mm=mm: nc.tensor.matmul(self.ps(sbk[mm], 128, N), lhsT=KTt[hk][:, mm, kb * 128:(kb + 1) * 128], rhs=Qt[qk][:, mm, :],
                                                                        start=True, stop=True),
                                         reads=[bK[hk], bQ[qk]], writes=[self.bps[sbk[mm]]], inc=(mm == 1))

                        def st1(sbk=sbk, pidx=pidx, j=j):
                            for mm in range(2):
                                if j >= 0:
                                    fw.act.op(lambda mm=mm: nc.scalar.activation(out=PF[mm][:], in_=self.ps(sbk[mm], 128, N), func=AF.Exp, scale=0.125),
                                              reads=[self.bps[sbk[mm]]], writes=[bPF[mm]])
                                    fw.dve.op(lambda mm=mm: nc.vector.tensor_tensor(out=P[pidx[mm]][:], in0=PF[mm][:], in1=CM[j][:], op=ALU.mult),
                                              reads=[bPF[mm], b_cm], writes=[bP[pidx[mm]]])
                                else:
                                    fw.act.op(lambda mm=mm: nc.scalar.activation(out=P[pidx[mm]][:], in_=self.ps(sbk[mm], 128, N), func=AF.Exp, scale=0.125),
                                              reads=[self.bps[sbk[mm]]], writes=[bP[pidx[mm]]])

                        def st2(hk=hk, kb=kb, pidx=pidx, nkb=nkb, hd=hd, tt=tt, qk=qk, t0=t0):
                            for mm in range(2):
                                fw.pe.op(lambda mm=mm: nc.tensor.matmul(self.ps(3 + mm, 128, N), lhsT=Vt[hk][:, kb, :], rhs=P[pidx[mm]][:],
                                                                        start=(kb == 0), stop=(kb == nkb - 1)),
                                         reads=[bV[hk], bP[pidx[mm]]], writes=[self.bps[3 + mm]], inc=False)
                                fw.pe.op(lambda mm=mm: nc.tensor.matmul(self.ps(5 + mm, 128, N), lhsT=ones_b[:], rhs=P[pidx[mm]][:],
                                                                        start=(kb == 0), stop=(kb == nkb - 1)),
                                         reads=[b_ones, bP[pidx[mm]]], writes=[self.bps[5 + mm]], inc=(mm == 1))
                            if kb == nkb - 1:
                                ok = (hd * NT + tt) % 2
                                fw.dve.op(lambda: nc.vector.reciprocal(R0[:], self.ps(5, 128, N)), reads=[self.bps[5]], writes=[bR[0]])
                                fw.dve.op(lambda: nc.vector.reciprocal(R1[:], self.ps(6, 128, N)), reads=[self.bps[6]], writes=[bR[1]])
                                fw.dve.op(lambda: nc.vector.tensor_tensor(out=O0[:], in0=self.ps(3, 128, N), in1=R0[:], op=ALU.mult), reads=[self.bps[3], bR[0]], writes=[bO[0]])
                                fw.dve.op(lambda: nc.vector.tensor_tensor(out=O1[:], in0=self.ps(4, 128, N), in1=R1[:], op=ALU.mult), reads=[self.bps[4], bR[1]], writes=[bO[1]])
                                fw.dve.op(lambda: nc.vector.scalar_tensor_tensor(out=O0[:], in0=O1[:], scalar=lt[:, 5:6], in1=O0[:], op0=ALU.mult, op1=ALU.add),
                                          reads=[bO[0], bO[1], b_lt], writes=[bO[0]])
                                fw.act.op(lambda: nc.scalar.activation(out=SQ[:], in_=O0[:], func=AF.Square), reads=[bO[0]], writes=[bSQ])
                                fw.pe.op(lambda: nc.tensor.matmul(self.ps(7, 128, N), lhsT=ones_f[:], rhs=SQ[:], start=True, stop=True),
                                         reads=[b_ones, bSQ], writes=[self.bps[7]])
                                fw.act.op(lambda: nc.scalar.activation(out=RS[:], in_=self.ps(7, 128, N), func=AF.Sqrt, bias=self.cst[:, 0:1], scale=float(1.0 / 128)),
                                          reads=[self.bps[7], self.b_cst], writes=[bRS])
                                fw.dve.op(lambda: nc.vector.reciprocal(RS[:], RS[:]), reads=[bRS], writes=[bRS])
                                fw.dve.op(lambda: nc.vector.scalar_tensor_tensor(out=O0[:], in0=O0[:], scalar=sg[:, 1:2], in1=RS[:], op0=ALU.mult, op1=ALU.mult),
                                          reads=[bO[0], bRS, b_sg], writes=[bO[0]])
                                fw.pool.op(lambda: nc.gpsimd.tensor_tensor(out=OG[ok][:], in0=O0[:], in1=Gt[qk][:], op=ALU.mult), reads=[bO[0], bG[qk]], writes=[bOG[ok]])
                                fw.st.op(lambda: nc.gpsimd.dma_start(out=self.OT[hd * 128:(hd + 1) * 128, t0:t0 + TW], in_=OG[ok][:]),
                                         reads=[bOG[ok]], writes=[self.b_OT])
                        items.append([st0, st1, None, None, st2])
            pipeline(items, 5)
            fw.barrier()

    def attn_sb(self):
        nc, fw, S, NB, NT = self.nc, self.fw, self.S, self.NB, self.NT
        N = TW
        with ExitStack() as es:
            sb = self.sbf(es)
            b_c = Buf()
            BEF, b_bef = self.build_masks(sb, "before")
            n8 = sb("sb_n8", [128, 128], BF16); tri = sb("sb_tri", [128, 128], BF16)
            fw.pool.op(lambda: nc.gpsimd.memset(n8[:], -8.0), writes=[b_c])
            fw.pool.op(lambda: nc.gpsimd.affine_select(out=tri[:], in_=n8[:], pattern=[[-1, 128]], compare_op=ALU.is_ge, fill=0.0,
                                                       base=0, channel_multiplier=1), reads=[b_c], writes=[b_c])
            one1 = self.cst[:, 3:4]
            KTt = [sb(f"sb_k{i}", [64, S], BF16) for i in range(2)]; bK = [Buf(), Buf()]
            Vt = [sb(f"sb_v{i}", [128, NB, 64], BF16) for i in range(2)]; bV = [Buf(), Buf()]
            Qt = [sb(f"sb_q{i}", [64, N], BF16) for i in range(4)]; bQ = [Buf() for _ in range(4)]
            Gt = [sb(f"sb_g{i}", [64, N], BF16) for i in range(4)]; bG = [Buf() for _ in range(4)]
            E1 = [sb(f"sb_e{i}", [128, N], F32) for i in range(2)]; bE = [Buf(), Buf()]
            SPf = sb("sb_spf", [128, N], F32); bSPf = Buf()
            SP = [sb(f"sb_sp{i}", [128, N], BF16) for i in range(6)]; bSP = [Buf() for _ in range(6)]
            AC = [sb(f"sb_ac{i}", [128, N], BF16) for i in range(6)]; bAC = [Buf() for _ in range(6)]
            A = [sb(f"sb_a{i}", [128, N], BF16) for i in range(6)]; bA = [Buf() for _ in range(6)]
            Af = sb("sb_af", [128, N], F32); bAf = Buf()
            OG = [sb(f"sb_og{i}", [64, N], BF16) for i in range(2)]; bOG = [Buf(), Buf()]
            items = []
            loaders = []
            ic = 0
            tile_i = 0
            for h in range(NH):
                hk = h % 2

                def load_head(h=h, hk=hk):
                    fw.ld.op(lambda: nc.sync.dma_start(out=KTt[hk][:], in_=self.KT[h * 64:(h + 1) * 64, :]), reads=[self.b_KT], writes=[bK[hk]])
                    fw.ld.op(lambda: nc.sync.dma_start(out=Vt[hk][:], in_=self.Vs[h]), reads=[self.b_Vs], writes=[bV[hk]])
                for tt in range(NT):
                    qk = tile_i % 4
                    ob = 5 + tile_i % 2
                    ogk = tile_i % 2
                    tile_i += 1
                    t0 = tt * TW
                    hi = 4 * tt + 3
                    lo = max(0, 4 * tt - SB_WIN)
                    kbs = list(range(hi, lo - 1, -1))

                    def load_q(h=h, qk=qk, t0=t0, tt=tt, lh=load_head):
                        if tt == 0:
                            lh()
                        fw.ld.op(lambda: nc.sync.dma_start(out=Qt[qk][:], in_=self.QT[h * 64:(h + 1) * 64, t0:t0 + N]), reads=[self.b_QT], writes=[bQ[qk]])
                        fw.ld.op(lambda: nc.sync.dma_start(out=Gt[qk][:], in_=self.GT[h * 64:(h + 1) * 64, t0:t0 + N]), reads=[self.b_GT], writes=[bG[qk]])
                    loaders.append(load_q)
                    tidx = len(loaders) - 1
                    prev_acc = None
                    for n, kb in enumerate(kbs):
                        j = kb - 4 * tt
                        diag = j >= 0
                        sbank = ic % 5
                        spk = ic % 6
                        ek = ic % 2
                        ic += 1
                        first = n == 0
                        lastk = n == len(kbs) - 1
                        pacc = prev_acc
                        acc_out = (SP[spk], bSP[spk]) if first else (AC[spk], bAC[spk])
                        prev_acc = acc_out

                        def st0(hk=hk, qk=qk, kb=kb, sbank=sbank, first=first, tidx=tidx):
                            if first:
                                if tidx == 0:
                                    loaders[0]()
                                if tidx + 1 < len(loaders):
                                    loaders[tidx + 1]()
                            fw.pe.op(lambda: nc.tensor.matmul(self.ps(sbank, 128, N), lhsT=KTt[hk][:, kb * 128:(kb + 1) * 128], rhs=Qt[qk][:], start=True, stop=False),
                                     reads=[bK[hk], bQ[qk]], writes=[self.bps[sbank]])

                        def st1(sbank=sbank, spk=spk, ek=ek, diag=diag, j=j, first=first, pacc=pacc, acc_out=acc_out):
                            fw.act.op(lambda: nc.scalar.activation(out=E1[ek][:], in_=self.ps(sbank, 128, N), func=AF.Exp, scale=0.125),
                                      reads=[self.bps[sbank]], writes=[bE[ek]])
                            if diag:
                                fw.act.op(lambda: nc.scalar.activation(out=SPf[:], in_=E1[ek][:], func=AF.Ln, bias=one1, scale=1.0),
                                          reads=[bE[ek], self.b_cst], writes=[bSPf])
                                fw.dve.op(lambda: nc.vector.tensor_tensor(out=SP[spk][:], in0=SPf[:], in1=BEF[j][:], op=ALU.mult),
                                          reads=[bSPf, b_bef], writes=[bSP[spk]])
                            else:
                                fw.act.op(lambda: nc.scalar.activation(out=SP[spk][:], in_=E1[ek][:], func=AF.Ln, bias=one1, scale=1.0),
                                          reads=[bE[ek], self.b_cst], writes=[bSP[spk]])
                            if not first:
                                fw.pool.op(lambda: nc.gpsimd.tensor_tensor(out=acc_out[0][:], in0=pacc[0][:], in1=SP[spk][:], op=ALU.add),
                                           reads=[pacc[1], bSP[spk]], writes=[acc_out[1]])

                        def st2(sbank=sbank, spk=spk, first=first, pacc=pacc):
                            fw.pe.op(lambda: nc.tensor.matmul(self.ps(sbank, 128, N), lhsT=tri[:], rhs=SP[spk][:], start=False, stop=first),
                                     reads=[b_c, bSP[spk]], writes=[self.bps[sbank]], inc=first)
                            if not first:
                                fw.pe.op(lambda: nc.tensor.matmul(self.ps(sbank, 128, N), lhsT=n8[:], rhs=pacc[0][:], start=False, stop=True),
                                         reads=[b_c, pacc[1]], writes=[self.bps[sbank]])

                        def st3(sbank=sbank, spk=spk, diag=diag, j=j):
                            if diag:
                                fw.act.op(lambda: nc.scalar.activation(out=Af[:], in_=self.ps(sbank, 128, N), func=AF.Exp, scale=0.125),
                                          reads=[self.bps[sbank]], writes=[bAf])
                                fw.dve.op(lambda: nc.vector.tensor_tensor(out=A[spk][:], in0=Af[:], in1=BEF[j][:], op=ALU.mult),
                                          reads=[bAf, b_bef], writes=[bA[spk]])
                            else:
                                fw.act.op(lambda: nc.scalar.activation(out=A[spk][:], in_=self.ps(sbank, 128, N), func=AF.Exp, scale=0.125),
                                          reads=[self.bps[sbank]], writes=[bA[spk]])

                        def st4(hk=hk, kb=kb, spk=spk, first=first, lastk=lastk, ob=ob, ogk=ogk, qk=qk, h=h, t0=t0):
                            fw.pe.op(lambda: nc.tensor.matmul(self.ps(ob, 64, N), lhsT=Vt[hk][:, kb, :], rhs=A[spk][:], start=first, stop=lastk),
                                     reads=[bV[hk], bA[spk]], writes=[self.bps[ob]])
                            if lastk:
                                fw.dve.op(lambda: nc.vector.tensor_tensor(out=OG[ogk][:], in0=self.ps(ob, 64, N), in1=Gt[qk][:], op=ALU.mult),
                                          reads=[self.bps[ob], bG[qk]], writes=[bOG[ogk]])
                                fw.st.op(lambda: nc.gpsimd.dma_start(out=self.OT[h * 64:(h + 1) * 64, t0:t0 + N], in_=OG[ogk][:]),
                                         reads=[bOG[ogk]], writes=[self.b_OT])
                        items.append([st0, st1, None, st2, st3, None, st4])
            pipeline(items, 7)
            fw.barrier()

    def attn_dsa(self):
        nc, fw, S, NB, NT = self.nc, self.fw, self.S, self.NB, self.NT
        N = TW
        with ExitStack() as es:
            sb = self.sbf(es)
            b_c = Buf()
            kcb = sb("ds_kcb", [128, 128], F32)
            fw.pool.op(lambda: nc.gpsimd.memset(kcb[:, 0:64], 0.0), writes=[b_c])
            fw.pool.op(lambda: nc.gpsimd.memset(kcb[:, 64:128], 64.0), writes=[b_c])
            ones_b = sb("ds_ones", [128, 64], BF16)
            fw.pool.op(lambda: nc.gpsimd.memset(ones_b[:], 1.0), writes=[b_c])
            kiT = sb("ds_ki", [64, S], BF16); b_ki = Buf()
            fw.ld.op(lambda: nc.sync.dma_start(out=kiT[:], in_=self.KT[D:D + 64, :]), reads=[self.b_KT], writes=[b_ki])
            SC = sb("ds_sc", [128, S], F32); bSC = Buf()
            JK = sb("ds_jk", [128, S], BF16); bJK = Buf()
            MTs = [sb(f"ds_mt{i}", [128, NB, N], BF16) for i in range(2)]; bMTs = [Buf(), Buf()]
            QI = [sb(f"ds_qi{i}", [64, 8, 128], BF16) for i in range(2)]; bQI = [Buf(), Buf()]
            WQ = [sb(f"ds_wq{i}", [128, 8], F32) for i in range(2)]; bWQ = [Buf(), Buf()]
            PS_ = [sb(f"ds_pos{i}", [128, 2], F32) for i in range(2)]; bPS = [Buf(), Buf()]
            PM = sb("ds_pm", [128, 128], F32); bPM = Buf()
            RL = [sb(f"ds_rl{i}", [128, 512], F32) for i in range(2)]; bRL = [Buf() for _ in range(2)]
            BS = sb("ds_bs", [128, 8], F32); bBS = Buf()
            MK0 = sb("ds_mk0", [128, 512], F32); MK = [MK0, MK0]; bMK0 = Buf(); bMK = [bMK0, bMK0]
            KTt = [sb(f"ds_k{i}", [64, S], BF16) for i in range(2)]; bK = [Buf(), Buf()]
            Vt = [sb(f"ds_v{i}", [128, NB, 65], BF16) for i in range(3)]; bV = [Buf() for _ in range(3)]
            for i_ in range(3):
                fw.pool.op(lambda i_=i_: nc.gpsimd.memset(Vt[i_][:, :, 64:65], 1.0), writes=[bV[i_]])
            onesr = sb("ds_onesr", [65, 64], F32)
            fw.pool.op(lambda: nc.gpsimd.memset(onesr[:], 1.0), writes=[b_c])
            R1 = sb("ds_r1", [65, N], F32); bR1 = Buf()
            Qt = [sb(f"ds_q{i}", [64, N], BF16) for i in range(3)]; bQ = [Buf() for _ in range(3)]
            Gt = [sb(f"ds_g{i}", [64, N], BF16) for i in range(3)]; bG = [Buf() for _ in range(3)]
            P = [sb(f"ds_p{i}", [128, N], BF16) for i in range(6)]; bP = [Buf() for _ in range(6)]
            O1 = sb("ds_o1", [64, N], F32); bO1 = Buf()
            OG = [sb(f"ds_og{i}", [64, N], BF16) for i in range(2)]; bOG = [Buf(), Buf()]
            ctr = {"rl": 0, "qi": 0, "ps": 0, "tile": 0}

            def nk_of(tt, i2):
                return 4 * tt + (2 if i2 == 0 else 4)

            def part1_yields(tt):
                n = 0
                for i2 in range(2):
                    nch = (nk_of(tt, i2) * 128 + 511) // 512
                    n += nch * 8 + 1 + BIS_ITERS + nch
                return n

            def part1(tt):
                MT, bMT = MTs[tt % 2], bMTs[tt % 2]
                for i2 in range(2):
                    lb = 2 * tt + i2
                    nk = nk_of(tt, i2)
                    ncols_tot = nk * 128
                    qk = ctr["qi"] % 2
                    ctr["qi"] += 1
                    fw.ld.op(lambda: nc.sync.dma_start(out=QI[qk][:], in_=self.QT[D:D + 512, lb * 128:(lb + 1) * 128].rearrange("(h d) t -> d h t", d=64)),
                             reads=[self.b_QT], writes=[bQI[qk]])
                    fw.ld.op(lambda: nc.sync.dma_start(out=WQ[qk][:], in_=self.WI[lb * 128:(lb + 1) * 128, :]), reads=[self.b_WI], writes=[bWQ[qk]])
                    fw.ld.op(lambda: nc.sync.dma_start(out=PS_[qk][:, 0:1], in_=self.poscol[lb * 128:(lb + 1) * 128, :]), writes=[bPS[qk]])
                    for c0 in range(0, ncols_tot, 512):
                        ncol = min(512, ncols_tot - c0)
                        for hh in range(8):
                            pb = 3 + ctr["ps"] % 2
                            ctr["ps"] += 1
                            rk = ctr["rl"] % 2
                            ctr["rl"] += 1
                            fw.pe.op(lambda: nc.tensor.matmul(self.ps(pb, 128, ncol), lhsT=QI[qk][:, hh, :], rhs=kiT[:, c0:c0 + ncol], start=True, stop=True),
                                     reads=[bQI[qk], b_ki], writes=[self.bps[pb]])
                            fw.act.op(lambda: nc.scalar.activation(out=RL[rk][:, 0:ncol], in_=self.ps(pb, 128, ncol), func=AF.Relu),
                                      reads=[self.bps[pb]], writes=[bRL[rk]])
                            if hh == 0:
                                fw.dve.op(lambda: nc.vector.tensor_scalar(out=SC[:, c0:c0 + ncol], in0=RL[rk][:, 0:ncol], scalar1=WQ[qk][:, 0:1], scalar2=None,
                                                                          op0=ALU.mult), reads=[bRL[rk], bWQ[qk]], writes=[bSC])
                            else:
                                fw.dve.op(lambda: nc.vector.scalar_tensor_tensor(out=SC[:, c0:c0 + ncol], in0=RL[rk][:, 0:ncol], scalar=WQ[qk][:, hh:hh + 1],
                                                                                 in1=SC[:, c0:c0 + ncol], op0=ALU.mult, op1=ALU.add),
                                          reads=[bRL[rk], bWQ[qk], bSC], writes=[bSC])
                            yield
                    for kb in range(4 * tt, nk):
                        dsl = slice(kb * 128, (kb + 1) * 128)
                        fw.dve.op(lambda kb=kb: nc.vector.tensor_scalar(out=PS_[qk][:, 1:2], in0=PS_[qk][:, 0:1], scalar1=float(-128 * kb), scalar2=None, op0=ALU.add),
                                  reads=[bPS[qk]], writes=[bPS[qk]])
                        fw.dve.op(lambda: nc.vector.tensor_scalar(out=PM[:], in0=kcb[:], scalar1=PS_[qk][:, 1:2], scalar2=-1e30, op0=ALU.is_gt, op1=ALU.mult),
                                  reads=[b_c, bPS[qk]], writes=[bPM])
                        fw.dve.op(lambda dsl=dsl: nc.vector.tensor_tensor(out=SC[:, dsl], in0=SC[:, dsl], in1=PM[:], op=ALU.add), reads=[bSC, bPM], writes=[bSC])
                    fw.dve.op(lambda: nc.vector.memset(BS[:, 0:1], -BIS_R), writes=[bBS])
                    fw.dve.op(lambda: nc.vector.memset(BS[:, 1:2], 0.0), writes=[bBS])
                    yield
                    hstep = BIS_R
                    for it in range(BIS_ITERS):
                        fw.dve.op(lambda: nc.vector.tensor_scalar(out=JK[:, 0:ncols_tot], in0=SC[:, 0:ncols_tot], scalar1=BS[:, 1:2], scalar2=0.0,
                                                                  op0=ALU.is_ge, op1=ALU.add, accum_out=BS[:, 2:3]), reads=[bSC, bBS], writes=[bJK, bBS])
                        fw.dve.op(lambda: nc.vector.tensor_scalar(out=BS[:, 3:4], in0=BS[:, 2:3], scalar1=float(TOPK) - 0.5, scalar2=float(hstep),
                                                                  op0=ALU.is_ge, op1=ALU.mult), reads=[bBS], writes=[bBS])
                        fw.dve.op(lambda: nc.vector.tensor_tensor(out=BS[:, 0:1], in0=BS[:, 0:1], in1=BS[:, 3:4], op=ALU.add), reads=[bBS], writes=[bBS])
                        hstep = hstep / 2
                        fw.dve.op(lambda: nc.vector.tensor_scalar(out=BS[:, 1:2], in0=BS[:, 0:1], scalar1=float(hstep), scalar2=None, op0=ALU.add),
                                  reads=[bBS], writes=[bBS])
                        yield
                    for c0 in range(0, ncols_tot, 512):
                        ncol = min(512, ncols_tot - c0)
                        mk = (c0 // 512) % 2
                        pb = 5
                        fw.dve.op(lambda: nc.vector.tensor_scalar(out=MK[mk][:, 0:ncol], in0=SC[:, c0:c0 + ncol], scalar1=BS[:, 0:1], scalar2=None, op0=ALU.is_ge),
                                  reads=[bSC, bBS], writes=[bMK[mk]])
                        nb4 = ncol // 128
                        for b4 in range(nb4):
                            fw.pe.op(lambda b4=b4: nc.tensor.transpose(self.ps(pb)[:, b4 * 128:(b4 + 1) * 128], MK[mk][:, b4 * 128:(b4 + 1) * 128], self.ident[:]),
                                     reads=[bMK[mk], self.b_ident], writes=[self.bps[pb]], inc=(b4 == nb4 - 1))
                        kb0 = c0 // 128
                        fw.act.op(lambda: nc.scalar.copy(MT[:, kb0:kb0 + nb4, i2 * 128:(i2 + 1) * 128],
                                                         self.ps(pb, 128, ncol).rearrange("p (b t) -> p b t", t=128)),
                                  reads=[self.bps[pb]], writes=[bMT])
                        yield
                    if nk < 4 * tt + 4:
                        fw.pool.op(lambda: nc.gpsimd.memset(MT[:, nk:4 * tt + 4, i2 * 128:(i2 + 1) * 128], 0.0), writes=[bMT])

            def part2_items(tt):
                MT, bMT = MTs[tt % 2], bMTs[tt % 2]
                t0 = tt * N
                nkb = 4 * tt + 4
                items = []
                loaders = []
                ic = 0
                for h in range(NH):
                    hk = h % 2
                    vk = h % 3
                    qk = ctr["tile"] % 3
                    ogk = ctr["tile"] % 2
                    ctr["tile"] += 1

                    def load_q(h=h, qk=qk, hk=hk, vk=vk):
                        fw.ld.op(lambda: nc.sync.dma_start(out=KTt[hk][:, 0:nkb * 128], in_=self.KT[h * 64:(h + 1) * 64, 0:nkb * 128]), reads=[self.b_KT], writes=[bK[hk]])
                        fw.ld.op(lambda: nc.sync.dma_start(out=Vt[vk][:, 0:nkb, 0:64], in_=self.Vs[h, :, 0:nkb, :]), reads=[self.b_Vs], writes=[bV[vk]])
                        fw.ld.op(lambda: nc.sync.dma_start(out=Qt[qk][:], in_=self.QT[h * 64:(h + 1) * 64, t0:t0 + N]), reads=[self.b_QT], writes=[bQ[qk]])
                        fw.ld.op(lambda: nc.sync.dma_start(out=Gt[qk][:], in_=self.GT[h * 64:(h + 1) * 64, t0:t0 + N]), reads=[self.b_GT], writes=[bG[qk]])
                    loaders.append(load_q)
                    for kb in range(nkb):
                        sbank = ic % 3
                        pk = ic % 6
                        ic += 1
                        first = kb == 0
                        lastk = kb == nkb - 1

                        def st0(h=h, hk=hk, qk=qk, kb=kb, sbank=sbank, first=first):
                            if first:
                                if h == 0:
                                    loaders[0]()
                                if h + 1 < NH:
                                    loaders[h + 1]()
                            fw.pe.op(lambda: nc.tensor.matmul(self.ps(sbank, 128, N), lhsT=KTt[hk][:, kb * 128:(kb + 1) * 128], rhs=Qt[qk][:], start=True, stop=True),
                                     reads=[bK[hk], bQ[qk]], writes=[self.bps[sbank]])

                        def st1(sbank=sbank, pk=pk, kb=kb, onpool=(ic % 3 == 0)):
                            fw.act.op(lambda: nc.scalar.activation(out=P[pk][:], in_=self.ps(sbank, 128, N), func=AF.Exp, scale=0.125),
                                      reads=[self.bps[sbank]], writes=[bP[pk]])
                            if onpool:
                                fw.pool.op(lambda: nc.gpsimd.tensor_tensor(out=P[pk][:], in0=P[pk][:], in1=MT[:, kb, :], op=ALU.mult),
                                           reads=[bP[pk], bMT], writes=[bP[pk]])
                            else:
                                fw.dve.op(lambda: nc.vector.tensor_tensor(out=P[pk][:], in0=P[pk][:], in1=MT[:, kb, :], op=ALU.mult),
                                          reads=[bP[pk], bMT], writes=[bP[pk]])

                        def st2(h=h, vk=vk, kb=kb, pk=pk, first=first, lastk=lastk, qk=qk, ogk=ogk):
                            fw.pe.op(lambda: nc.tensor.matmul(self.ps(6, 65, N), lhsT=Vt[vk][:, kb, :], rhs=P[pk][:], start=first, stop=lastk),
                                     reads=[bV[vk], bP[pk]], writes=[self.bps[6]])
                            if lastk:
                                fw.act.op(lambda: nc.scalar.copy(O1[:], self.ps(6, 64, N)), reads=[self.bps[6]], writes=[bO1])
                                fw.dve.op(lambda: nc.vector.reciprocal(R1[64:65, :], self.psall[64:65, 6 * 512:6 * 512 + N]), reads=[self.bps[6]], writes=[bR1])
                                fw.pe.op(lambda: nc.tensor.matmul(self.ps(7, 64, N), lhsT=onesr[64:65, :], rhs=R1[64:65, :], start=True, stop=True),
                                         reads=[b_c, bR1], writes=[self.bps[7]])
                                fw.dve.op(lambda: nc.vector.tensor_tensor(out=O1[:], in0=O1[:], in1=self.ps(7, 64, N), op=ALU.mult), reads=[bO1, self.bps[7]], writes=[bO1])
                                fw.pool.op(lambda: nc.gpsimd.tensor_tensor(out=OG[ogk][:], in0=O1[:], in1=Gt[qk][:], op=ALU.mult), reads=[bO1, bG[qk]], writes=[bOG[ogk]])
                                fw.st.op(lambda: nc.gpsimd.dma_start(out=self.OT[h * 64:(h + 1) * 64, t0:t0 + N], in_=OG[ogk][:]),
                                         reads=[bOG[ogk]], writes=[self.b_OT])
                        items.append([st0, st1, None, None, st2])
                return items

            for _ in part1(0):
                pass
            for tt in range(NT):
                items = part2_items(tt)
                if tt + 1 < NT:
                    gen = part1(tt + 1)
                    ny = part1_yields(tt + 1)
                else:
                    gen, ny = None, 0
                nsteps = len(items) + 4
                done = 0
                n = len(items)
                for step in range(nsteps):
                    for s_ in range(4, -1, -1):
                        i_ = step - s_
                        if 0 <= i_ < n and items[i_][s_] is not None:
                            items[i_][s_]()
                    if gen is not None:
                        target = ((step + 1) * ny + nsteps - 1) // nsteps
                        while done < target:
                            next(gen, None)
                            done += 1
                if gen is not None:
                    for _ in gen:
                        pass
            fw.barrier()


_LAYERS = [(i % 3, i // 3, i) for i in range(DEPTH)]


def _consts():
    p = np.arange(128)
    i = (p % 32).astype(np.float32)
    inv = (1.0 / (np.float32(10000.0) ** (2 * i / np.float32(64)))).astype(np.float32)
    sign = np.where((p % 64) < 32, -1.0, 1.0).astype(np.float32)
    return np.stack([inv, sign], axis=1).astype(np.float32), np.eye(128, dtype=np.float32)


def own_rows(S, rho):
    idx = []
    for tt in range(S // 512):
        for c in own_blocks(rho):
            g = 4 * tt + c
            idx.append(np.arange(g * 128, (g + 1) * 128))
    return np.concatenate(idx)


def make_in_map(xb, rho, layers, w):
    S = xb.shape[0]
    ropec, ident = _consts()
    rows = own_rows(S, rho)
    im = {"x": np.ascontiguousarray(xb), "x_own": np.ascontiguousarray(xb[rows]), "ropec": ropec, "ident": ident}
    im["posrow"] = rows.astype(np.float32).reshape(1, -1)
    im["poscol"] = rows.astype(np.float32).reshape(-1, 1)
    im["qrel"] = (rows[:TW] % 512).astype(np.float32).reshape(1, TW)
    ws_in = {0: w["w_in_a"], 1: w["w_in_b"], 2: w["w_in_c"]}
    ws_out = {0: w["w_out_a"], 1: w["w_out_b"], 2: w["w_out_c"]}
    for li, (m, j, L) in enumerate(layers):
        im[f"w_in{li}"] = np.ascontiguousarray(ws_in[m][j])
        im[f"w_out{li}"] = np.ascontiguousarray(ws_out[m][j])
    im["ln_g"] = np.ascontiguousarray(np.stack([w["ln_g"][L] for (_, _, L) in layers]))
    im["ln_b"] = np.ascontiguousarray(np.stack([w["ln_b"][L] for (_, _, L) in layers]))
    im["lam"] = np.ascontiguousarray(np.stack([w["lambda_q1"][0], w["lambda_k1"][0], w["lambda_q2"][0], w["lambda_k2"][0]]).reshape(1, 256))
    im["subg"] = np.ascontiguousarray(w["subln_g"][0].reshape(128, 1))
    return im


def run_layers(x, w, layers, ncores):
    B, S, _ = x.shape
    assert ncores == 2 * B
    prog = Prog(S, layers, ncores=ncores)
    in_maps = [make_in_map(x[c // 2], c % 2, layers, w) for c in range(ncores)]
    res = run_bass_kernel_spmd(prog.nc, in_maps, core_ids=list(range(ncores)))
    out = np.empty((B, S, D), np.float32)
    for c in range(ncores):
        out[c // 2][own_rows(S, c % 2)] = res.results[c]["out"]
    return out


def kernel(**inputs):
    w = {k: np.asarray(v, dtype=np.float32) for k, v in inputs.items()}
    return run_layers(w["x"], w, _LAYERS, 8)
```

```python
import math
from contextlib import ExitStack
import numpy as np
import concourse.bass as bass
import concourse.mybir as mybir
from concourse.bass_utils import run_bass_kernel_spmd

F32 = mybir.dt.float32
BF16 = mybir.dt.bfloat16
I32 = mybir.dt.int32
AF = mybir.ActivationFunctionType
ALU = mybir.AluOpType

D = 1024
NH = 16
HD = 64
DEPTH = 4
ALPHA = (2.0 * DEPTH) ** 0.25
LN_EPS = 1e-5
RMS_EPS = 1e-5
A_IN = 4 * D + 8 * 64 + 8 + 64
TOPK = 256
SB_WIN = 3
NEGM = 30000.0
BIS_ITERS = 22
BIS_R = 16.0


class Buf:
    __slots__ = ("w", "r")

    def __init__(self):
        self.w = None
        self.r = {}


class Q:
    def __init__(self, fw, eng, name, is_dma, nsems, same_engine_waits=True):
        self.fw = fw
        self.eng = eng
        self.is_dma = is_dma
        self.inc = 16 if is_dma else 1
        self.sems = []
        for i in range(nsems):
            h = fw.nc.alloc_semaphore(name=f"s_{name}{i}")
            self.sems.append(len(fw.semh))
            fw.semh.append(h)
        self.cnt = [0] * nsems
        self.rr = 0
        self.waited = {}
        self.sew = same_engine_waits
        self.pending = False

    def _wait(self, sem, val):
        if self.waited.get(sem, 0) < val:
            self.eng.wait_ge(self.fw.semh[sem], val)
            self.waited[sem] = val

    def op(self, fn, reads=(), writes=(), inc=True):
        deps = {}
        for b in reads:
            if b.w is not None:
                s, v = b.w
                if deps.get(s, 0) < v:
                    deps[s] = v
        for b in writes:
            if b.w is not None:
                s, v = b.w
                if deps.get(s, 0) < v:
                    deps[s] = v
            for s, v in b.r.items():
                if deps.get(s, 0) < v:
                    deps[s] = v
        i = self.rr
        if self.is_dma:
            self.rr = (self.rr + 1) % len(self.sems)
            if self.cnt[i] > 0:
                deps[self.sems[i]] = max(deps.get(self.sems[i], 0), self.cnt[i])
        for s, v in deps.items():
            if (not self.sew) and s == self.sems[0]:
                continue
            self._wait(s, v)
        ins = fn()
        if inc:
            self.cnt[i] += self.inc
            ins.then_inc(self.fw.semh[self.sems[i]], self.inc)
            tag = (self.sems[i], self.cnt[i])
            self.pending = False
        else:
            tag = (self.sems[i], self.cnt[i] + self.inc)
            self.pending = True
        for b in reads:
            if b.r.get(tag[0], 0) < tag[1]:
                b.r[tag[0]] = tag[1]
        for b in writes:
            b.w = tag
            b.r = {}
        return tag

    def wait_all_of(self, other):
        for i, s in enumerate(other.sems):
            if other.cnt[i] > 0:
                self._wait(s, other.cnt[i])


class FW:
    def __init__(self, nc):
        self.nc = nc
        self.semh = []
        self.pe = Q(self, nc.tensor, "pe", False, 1, same_engine_waits=False)
        self.act = Q(self, nc.scalar, "act", False, 1)
        self.dve = Q(self, nc.vector, "dve", False, 1)
        self.pool = Q(self, nc.gpsimd, "pool", False, 1)
        self.ld = Q(self, nc.sync, "ld", True, 8)
        self.st = Q(self, nc.gpsimd, "st", True, 8)
        self.qs = [self.pe, self.act, self.dve, self.pool, self.ld, self.st]

    def barrier(self):
        for q in self.qs:
            assert not q.pending
        for q in self.qs:
            for o in self.qs:
                if o is not q:
                    q.wait_all_of(o)


def pipeline(items, nstages):
    n = len(items)
    for step in range(n + nstages - 1):
        for s in range(nstages - 1, -1, -1):
            i = step - s
            if 0 <= i < n and items[i][s] is not None:
                items[i][s]()


TW2 = 512
TW = 256


def own_blocks(rho):
    return (0, 3) if rho == 0 else (1, 2)


class Prog:
    def __init__(self, S, layers, ncores=8, debug=False):
        self.S = S
        self.SO = S // 2
        self.NB = S // 128
        self.NT = S // 512
        self.CW = min(1024, self.SO)
        self.NCH = self.SO // self.CW
        self.groups = [[2 * i, 2 * i + 1] for i in range(ncores // 2)]
        self.layers = layers
        self.debug = debug
        nc = self.nc = bass.Bass("TRN2", target_bir_lowering=False)
        self.fw = FW(nc)
        self.fw.cc = Q(self.fw, nc.gpsimd, "cc", False, 1)
        self.fw.qs.append(self.fw.cc)
        SO = self.SO
        dt = nc.dram_tensor
        self.x_in = dt("x", [S, D], F32, kind="ExternalInput").ap()
        self.x_own = dt("x_own", [SO, D], F32, kind="ExternalInput").ap()
        self.posrow = dt("posrow", [1, SO], F32, kind="ExternalInput").ap()
        self.poscol = dt("poscol", [SO, 1], F32, kind="ExternalInput").ap()
        self.qrel = dt("qrel", [1, TW2], F32, kind="ExternalInput").ap()
        self.out = dt("out", [SO, D], F32, kind="ExternalOutput").ap()
        self.w_in = {}
        self.w_out = {}
        for li, (m, j, L) in enumerate(layers):
            width = A_IN if m == 0 else 4 * D
            self.w_in[li] = dt(f"w_in{li}", [D, width], F32, kind="ExternalInput").ap()
            self.w_out[li] = dt(f"w_out{li}", [D, D], F32, kind="ExternalInput").ap()
        self.lng = dt("ln_g", [len(layers), D], F32, kind="ExternalInput").ap()
        self.lnb = dt("ln_b", [len(layers), D], F32, kind="ExternalInput").ap()
        self.lam = dt("lam", [1, 256], F32, kind="ExternalInput").ap()
        self.subg = dt("subg", [128, 1], F32, kind="ExternalInput").ap()
        self.ropec = dt("ropec", [128, 2], F32, kind="ExternalInput").ap()
        self.ident_in = dt("ident", [128, 128], F32, kind="ExternalInput").ap()
        IK = "ExternalOutput" if debug else "Internal"
        self.xres = [dt(f"xres{i}", [SO, D], F32, kind="Internal").ap() for i in range(2)]
        self.xTf = dt("xTf", [D, S], BF16, kind="Internal").ap()
        self.xTo = [[dt(f"xTo{s_}_{c}", [D, self.CW], BF16) for c in range(self.NCH)] for s_ in range(2)]
        self.gath = [[dt(f"gath{l}_{c}", [2 * D, self.CW], BF16) for c in range(self.NCH)] for l in range(max(1, len(layers) - 1))]
        self.QT = dt("QT", [24 * 64, SO], BF16, kind=IK).ap()
        self.KT = dt("KT", [17 * 64, S], BF16, kind=IK).ap()
        self.GT = dt("GT", [D, SO], BF16, kind=IK).ap()
        self.OT = dt("OT", [D, SO], BF16, kind=IK).ap()
        self.Vs = dt("Vs", [NH, 128, self.NB, 64], BF16, kind=IK).ap()
        self.WI = dt("WI", [SO, 8], F32, kind="Internal").ap()
        self.CC = dt("CCt", [128, S], F32, kind="Internal").ap()
        self.SS = dt("SSt", [128, S], F32, kind="Internal").ap()
        self.CCo = dt("CCo", [128, SO], F32, kind="Internal").ap()
        self.SSo = dt("SSo", [128, SO], F32, kind="Internal").ap()
        if debug:
            self.dbgSC = dt("dbgSC", [128, 512], F32, kind="ExternalOutput").ap()
            self.dbgBS = dt("dbgBS", [128, 8], F32, kind="ExternalOutput").ap()
            self.dbgMT = dt("dbgMT", [128, 4, TW], BF16, kind="ExternalOutput").ap()
            self.dbgMT2 = dt("dbgMT2", [128, 4, TW], BF16, kind="ExternalOutput").ap()
            self.dbgP = dt("dbgP", [128, 4, TW], BF16, kind="ExternalOutput").ap()
            self.dbgO = dt("dbgO", [64, 2, TW], F32, kind="ExternalOutput").ap()
        self.b_xres = [Buf(), Buf()]
        self.b_xTf, self.b_QT, self.b_KT, self.b_GT, self.b_OT, self.b_Vs, self.b_WI, self.b_tab = (Buf() for _ in range(8))
        self.b_xTo = [Buf(), Buf()]
        self.b_gath = [Buf() for _ in self.gath]
        self.psall = nc.alloc_psum_tensor("psall", [128, 4096], F32)
        self.bps = [Buf() for _ in range(8)]
        self.build()

    def sbf(self, es):
        def f(n, sh, d):
            self._uid = getattr(self, "_uid", 0) + 1
            return es.enter_context(self.nc.sbuf_tensor(f"{n}_u{self._uid}", sh, d))
        return f

    def ps(self, i, parts=128, n=512):
        return self.psall[0:parts, i * 512:i * 512 + n]

    def build(self):
        nc, fw = self.nc, self.fw
        with ExitStack() as es:
            sb = self.sbf(es)
            self.ident = sb("identsb", [128, 128], F32)
            self.b_ident = Buf()
            fw.ld.op(lambda: nc.sync.dma_start(out=self.ident[:], in_=self.ident_in[:, :]), writes=[self.b_ident])
            self.cst = sb("cst", [128, 6], F32)
            self.identb = sb("identb", [128, 128], BF16)
            self.b_cst = Buf()
            fw.pool.op(lambda: nc.gpsimd.memset(self.cst[:, 0:1], float(LN_EPS)), writes=[self.b_cst])
            fw.pool.op(lambda: nc.gpsimd.iota(self.cst[:, 1:2], pattern=[[0, 1]], base=0, channel_multiplier=1,
                                              allow_small_or_imprecise_dtypes=True), writes=[self.b_cst])
            fw.pool.op(lambda: nc.gpsimd.memset(self.cst[0:64, 2:3], 0.0), writes=[self.b_cst])
            fw.pool.op(lambda: nc.gpsimd.memset(self.cst[64:128, 2:3], 64.0), writes=[self.b_cst])
            fw.pool.op(lambda: nc.gpsimd.memset(self.cst[:, 3:4], 1.0), writes=[self.b_cst])
            fw.pool.op(lambda: nc.gpsimd.memset(self.cst[:, 4:5], -NEGM), writes=[self.b_cst])
            fw.dve.op(lambda: nc.vector.tensor_copy(self.identb[:], self.ident[:]), reads=[self.b_ident], writes=[self.b_cst])
            self.QR2 = sb("qrelsb", [128, TW2], F32)
            fw.ld.op(lambda: nc.sync.dma_start(out=self.QR2[:], in_=self.qrel[0:1, :].broadcast_to([128, TW2])), writes=[self.b_cst])
            self.rope_tables(self.CC, self.SS, self.S, None)
            self.rope_tables(self.CCo, self.SSo, self.SO, self.posrow)
            self.transpose_input()
            cur = 0
            for li, (m, j, L) in enumerate(self.layers):
                last = li == len(self.layers) - 1
                self.phase_a(li, m)
                if m == 0:
                    self.attn_dsa()
                elif m == 1:
                    self.attn_sb()
                else:
                    self.attn_diff(L)
                self.phase_c(li, cur, last)
                if not last:
                    self.exchange(li)
                cur ^= 1
            fw.barrier()

    def exchange(self, li):
        nc, fw = self.nc, self.fw
        sset = (li + 1) % 2
        for ch in range(self.NCH):
            fw.cc.op(lambda ch=ch: nc.gpsimd.collective_compute("AllGather", ALU.bypass, replica_groups=self.groups,
                                                                ins=[self.xTo[sset][ch].ap().opt()], outs=[self.gath[li][ch].ap().opt()]),
                     reads=[self.b_xTo[sset]], writes=[self.b_gath[li]])
        fw.barrier()

    def xt_all_src(self, li, tt, c):
        if li == 0:
            g = 4 * tt + c
            return self.xTf[:, g * 128:(g + 1) * 128], self.b_xTf
        rho = 0 if c in (0, 3) else 1
        lb = 2 * tt + (0 if c in (0, 1) else 1)
        ch, col = divmod(lb * 128, self.CW)
        return self.gath[li - 1][ch].ap()[rho * D:(rho + 1) * D, col:col + 128], self.b_gath[li - 1]

    def rope_tables(self, CCd, SSd, n, posrow):
        nc, fw = self.nc, self.fw
        with ExitStack() as es:
            sb = self.sbf(es)
            rc = sb("rc", [128, 2], F32); brc = Buf()
            fw.ld.op(lambda: nc.sync.dma_start(out=rc[:], in_=self.ropec[:, :]), writes=[brc])
            C1 = 6.28125
            C2 = 2 * math.pi - C1
            pools = {}
            for nm, dtp in [("it", F32), ("ang", F32), ("a2", F32), ("ki", I32), ("kf", F32), ("r0", F32), ("r1", F32)]:
                pools[nm] = [(sb(f"rt_{nm}{q}", [128, 512], dtp), Buf()) for q in range(2)]
            for t0 in range(0, n, 512):
                sl = (t0 // 512) % 2
                it, b_it = pools["it"][sl]
                ang, b_ang = pools["ang"][sl]
                for which in range(2):
                    a2, b_a2 = pools["a2"][sl]
                    ki, b_ki = pools["ki"][sl]
                    kf, b_kf = pools["kf"][sl]
                    r, b_r = pools["r0" if which == 0 else "r1"][sl]
                    if which == 0:
                        if posrow is None:
                            fw.pool.op(lambda: nc.gpsimd.iota(it[:], pattern=[[1, 512]], base=t0, channel_multiplier=0,
                                                              allow_small_or_imprecise_dtypes=True), writes=[b_it])
                        else:
                            fw.ld.op(lambda: nc.sync.dma_start(out=it[:], in_=posrow[0:1, t0:t0 + 512].broadcast_to([128, 512])), writes=[b_it])
                        fw.dve.op(lambda: nc.vector.tensor_scalar(out=ang[:], in0=it[:], scalar1=rc[:, 0:1], scalar2=None,
                                                                  op0=ALU.mult), reads=[b_it, brc], writes=[b_ang])
                        src = ang
                    else:
                        fw.dve.op(lambda: nc.vector.tensor_scalar(out=a2[:], in0=ang[:], scalar1=float(math.pi / 2), scalar2=None,
                                                                  op0=ALU.add), reads=[b_ang], writes=[b_a2])
                        src = a2
                    bsrc = b_ang if which == 0 else b_a2
                    fw.dve.op(lambda: nc.vector.tensor_scalar(out=ki[:], in0=src[:], scalar1=float(1 / (2 * math.pi)), scalar2=None,
                                                              op0=ALU.mult), reads=[bsrc], writes=[b_ki])
                    fw.dve.op(lambda: nc.vector.tensor_copy(kf[:], ki[:]), reads=[b_ki], writes=[b_kf])
                    fw.dve.op(lambda: nc.vector.scalar_tensor_tensor(out=r[:], in0=kf[:], scalar=-C1, in1=src[:], op0=ALU.mult,
                                                                     op1=ALU.add), reads=[b_kf, bsrc], writes=[b_r])
                    fw.dve.op(lambda: nc.vector.scalar_tensor_tensor(out=r[:], in0=kf[:], scalar=-C2, in1=r[:], op0=ALU.mult,
                                                                     op1=ALU.add), reads=[b_kf, b_r], writes=[b_r])
                    fw.dve.op(lambda: nc.vector.tensor_scalar(out=r[:], in0=r[:], scalar1=float(math.pi), scalar2=float(-math.pi),
                                                              op0=ALU.min, op1=ALU.max), reads=[b_r], writes=[b_r])
                    fw.act.op(lambda: nc.scalar.activation(out=r[:], in_=r[:], func=AF.Sin), reads=[b_r], writes=[b_r])
                    if which == 0:
                        fw.dve.op(lambda: nc.vector.tensor_scalar(out=r[:], in0=r[:], scalar1=rc[:, 1:2], scalar2=None,
                                                                  op0=ALU.mult), reads=[b_r, brc], writes=[b_r])
                        fw.st.op(lambda: nc.gpsimd.dma_start(out=SSd[:, t0:t0 + 512], in_=r[:]), reads=[b_r], writes=[self.b_tab])
                    else:
                        fw.st.op(lambda: nc.gpsimd.dma_start(out=CCd[:, t0:t0 + 512], in_=r[:]), reads=[b_r], writes=[self.b_tab])
            fw.barrier()

    def emit_transpose_store(self, xn, b_xn, dst_ap, b_dst, xtt, b_xtt, pbank):
        nc, fw = self.nc, self.fw
        for half in range(2):
            bank = pbank[half]
            for c4 in range(4):
                c = half * 4 + c4
                fw.pe.op(lambda c=c, c4=c4: nc.tensor.transpose(self.ps(bank)[:, c4 * 128:(c4 + 1) * 128], xn[:, c * 128:(c + 1) * 128],
                                                                self.ident[:]),
                         reads=[b_xn, self.b_ident], writes=[self.bps[bank]], inc=(c4 == 3))
            fw.act.op(lambda half=half: nc.scalar.copy(xtt[:, half * 4:(half + 1) * 4, :],
                                                        self.ps(bank).rearrange("p (c t) -> p c t", c=4)),
                      reads=[self.bps[bank]], writes=[b_xtt])
        fw.st.op(lambda: nc.gpsimd.dma_start(out=dst_ap.rearrange("(c p) t -> p c t", p=128), in_=xtt[:]),
                 reads=[b_xtt], writes=[b_dst])

    def xto_dst(self, sset, lb):
        ch, col = divmod(lb * 128, self.CW)
        return self.xTo[sset][ch].ap()[:, col:col + 128]

    def transpose_input(self):
        nc, fw = self.nc, self.fw
        with ExitStack() as es:
            sb = self.sbf(es)
            xt = [sb(f"ti_x{i}", [128, D], F32) for i in range(2)]
            bx = [Buf(), Buf()]
            xtt = [sb(f"ti_xt{i}", [128, 8, 128], BF16) for i in range(2)]
            bxtt = [Buf(), Buf()]
            n = 0
            for tb in range(self.NB):
                k = n % 2
                n += 1
                fw.ld.op(lambda: nc.sync.dma_start(out=xt[k][:], in_=self.x_in[tb * 128:(tb + 1) * 128, :]), writes=[bx[k]])
                self.emit_transpose_store(xt[k], bx[k], self.xTf[:, tb * 128:(tb + 1) * 128], self.b_xTf, xtt[k], bxtt[k], (2 * k, 2 * k + 1))
            for lb in range(self.SO // 128):
                k = n % 2
                n += 1
                fw.ld.op(lambda: nc.sync.dma_start(out=xt[k][:], in_=self.x_own[lb * 128:(lb + 1) * 128, :]), writes=[bx[k]])
                self.emit_transpose_store(xt[k], bx[k], self.xto_dst(0, lb), self.b_xTo[0], xtt[k], bxtt[k], (2 * k, 2 * k + 1))
            fw.barrier()

    def phase_a(self, li, m):
        nc, fw, S = self.nc, self.fw, self.S
        width = A_IN if m == 0 else 4 * D
        rope = m in (0, 2)
        rot_blocks = []
        if rope:
            rot_blocks = [(0, 16), (D, 16)]
            if m == 0:
                rot_blocks += [(4 * D, 8), (4 * D + 520, 1)]
        nrot = sum(nh for _, nh in rot_blocks) * 64
        sset = li % 2
        with ExitStack() as es:
            sb = self.sbf(es)
            WIN = sb("WIN", [128, 8, width], BF16); b_win = Buf()
            WROT = sb("WROT", [128, 8, max(nrot, 64)], BF16)
            stg0 = sb("wstg0", [128, width], F32)
            stg = [stg0, stg0]
            bstg0 = Buf()
            bstg = [bstg0, bstg0]
            rot_off = {}
            off = 0
            for col0, nh in rot_blocks:
                rot_off[col0] = off
                off += nh * 64
            for c in range(8):
                k = c % 2
                fw.ld.op(lambda: nc.sync.dma_start(out=stg[k][:], in_=self.w_in[li][c * 128:(c + 1) * 128, :]), writes=[bstg[k]])
                h2 = width // 2
                fw.act.op(lambda: nc.scalar.copy(WIN[:, c, 0:h2], stg[k][:, 0:h2]), reads=[bstg[k]], writes=[b_win])
                fw.dve.op(lambda: nc.vector.tensor_copy(WIN[:, c, h2:width], stg[k][:, h2:width]), reads=[bstg[k]], writes=[b_win])
                for col0, nh in rot_blocks:
                    ro = rot_off[col0]
                    src = stg[k][:, col0:col0 + nh * 64].rearrange("p (h t i) -> p h t i", t=2, i=32)
                    dst = WROT[:, c, ro:ro + nh * 64].rearrange("p (h t i) -> p h t i", t=2, i=32)
                    fw.pool.op(lambda: nc.gpsimd.tensor_copy(dst[:, :, 0, :], src[:, :, 1, :]), reads=[bstg[k]], writes=[b_win])
                    fw.pool.op(lambda: nc.gpsimd.tensor_copy(dst[:, :, 1, :], src[:, :, 0, :]), reads=[bstg[k]], writes=[b_win])
            kgroups = [(self.KT, self.b_KT, g * 128, D + g * 128, 128, rope, None) for g in range(8)]
            if m == 0:
                kgroups.append((self.KT, self.b_KT, D, 4 * D + 520, 64, True, None))
            qgroups = [(self.QT, self.b_QT, g * 128, g * 128, 128, rope, None) for g in range(8)]
            qgroups += [(self.GT, self.b_GT, g * 128, 3 * D + g * 128, 128, False, "silu") for g in range(8)]
            if m == 0:
                qgroups += [(self.QT, self.b_QT, D + g * 128, 4 * D + g * 128, 128, True, None) for g in range(4)]

            def rotcol(col):
                for col0, nh in rot_blocks:
                    if col0 <= col < col0 + nh * 64:
                        return rot_off[col0] + (col - col0)
                raise AssertionError

            XT = [sb(f"pa_xt{i}", [128, 8, 512], BF16) for i in range(2)]; bXT = [Buf(), Buf()]
            XO = [sb(f"pa_xo{i}", [128, 8, TW], BF16) for i in range(2)]; bXO = [Buf(), Buf()]
            CCt = [sb(f"pa_cc{i}", [128, 512], F32) for i in range(2)]; bCC = [Buf(), Buf()]
            SSt = [sb(f"pa_ss{i}", [128, 512], F32) for i in range(2)]; bSS = [Buf(), Buf()]
            CCq = [sb(f"pa_ccq{i}", [128, TW], F32) for i in range(2)]; bCCq = [Buf(), Buf()]
            SSq = [sb(f"pa_ssq{i}", [128, TW], F32) for i in range(2)]; bSSq = [Buf(), Buf()]
            T1 = [sb(f"pa_t1{i}", [128, 512], F32) for i in range(2)]; bT1 = [Buf(), Buf()]
            T2 = [sb(f"pa_t2{i}", [128, 512], F32) for i in range(2)]; bT2 = [Buf(), Buf()]
            OB = [sb(f"pa_ob{i}", [128, 512], BF16) for i in range(3)]; bOB = [Buf() for _ in range(3)]
            VT0 = sb("pa_vt0", [128, 4, D], BF16); VT = [VT0, VT0]; bVT0 = Buf(); bVT = [bVT0, bVT0]
            WIt = [sb(f"pa_wi{i}", [128, 8], F32) for i in range(2)]; bWI = [Buf(), Buf()]
            gi = [0]

            def do_group(grp, xin, bxin, N, t0, cc, bcc, ss, bss):
                (dst, bdst, row0, col0, M, rp, act) = grp
                pb = (gi[0] % 2) * 2
                ob = gi[0] % 3
                tk = gi[0] % 2
                gi[0] += 1
                for c in range(8):
                    fw.pe.op(lambda c=c: nc.tensor.matmul(self.ps(pb, M, N), lhsT=WIN[:, c, col0:col0 + M], rhs=xin[:, c, :],
                                                          start=(c == 0), stop=(c == 7)),
                             reads=[b_win, bxin], writes=[self.bps[pb]], inc=(c == 7))
                if rp:
                    rc0 = rotcol(col0)
                    for c in range(8):
                        fw.pe.op(lambda c=c: nc.tensor.matmul(self.ps(pb + 1, M, N), lhsT=WROT[:, c, rc0:rc0 + M], rhs=xin[:, c, :],
                                                              start=(c == 0), stop=(c == 7)),
                                 reads=[b_win, bxin], writes=[self.bps[pb + 1]], inc=(c == 7))
                    fw.dve.op(lambda: nc.vector.tensor_tensor(out=T1[tk][0:M, 0:N], in0=self.ps(pb, M, N), in1=cc[0:M, :], op=ALU.mult),
                              reads=[self.bps[pb], bcc], writes=[bT1[tk]])
                    fw.dve.op(lambda: nc.vector.tensor_tensor(out=T2[tk][0:M, 0:N], in0=self.ps(pb + 1, M, N), in1=ss[0:M, :], op=ALU.mult),
                              reads=[self.bps[pb + 1], bss], writes=[bT2[tk]])
                    fw.pool.op(lambda: nc.gpsimd.tensor_tensor(out=OB[ob][0:M, 0:N], in0=T1[tk][0:M, 0:N], in1=T2[tk][0:M, 0:N], op=ALU.add),
                               reads=[bT1[tk], bT2[tk]], writes=[bOB[ob]])
                elif act == "silu":
                    fw.act.op(lambda: nc.scalar.activation(out=OB[ob][0:M, 0:N], in_=self.ps(pb, M, N), func=AF.Silu),
                              reads=[self.bps[pb]], writes=[bOB[ob]])
                else:
                    fw.act.op(lambda: nc.scalar.copy(OB[ob][0:M, 0:N], self.ps(pb, M, N)), reads=[self.bps[pb]], writes=[bOB[ob]])
                fw.st.op(lambda: nc.gpsimd.dma_start(out=dst[row0:row0 + M, t0:t0 + N], in_=OB[ob][0:M, 0:N]),
                         reads=[bOB[ob]], writes=[bdst])

            for tt in range(self.NT):
                k = tt % 2
                t0 = tt * 512
                for c in range(4):
                    src, bsrc = self.xt_all_src(li, tt, c)
                    fw.ld.op(lambda: nc.sync.dma_start(out=XT[k][:, :, c * 128:(c + 1) * 128], in_=src.rearrange("(c p) t -> p c t", p=128)),
                             reads=[bsrc], writes=[bXT[k]])
                l0 = tt * TW
                ch, col = divmod(l0, self.CW)
                fw.ld.op(lambda: nc.sync.dma_start(out=XO[k][:], in_=self.xTo[sset][ch].ap()[:, col:col + TW].rearrange("(c p) t -> p c t", p=128)),
                         reads=[self.b_xTo[sset]], writes=[bXO[k]])
                if rope:
                    fw.ld.op(lambda: nc.sync.dma_start(out=CCt[k][:], in_=self.CC[:, t0:t0 + 512]), reads=[self.b_tab], writes=[bCC[k]])
                    fw.ld.op(lambda: nc.sync.dma_start(out=SSt[k][:], in_=self.SS[:, t0:t0 + 512]), reads=[self.b_tab], writes=[bSS[k]])
                    fw.ld.op(lambda: nc.sync.dma_start(out=CCq[k][:], in_=self.CCo[:, l0:l0 + TW]), reads=[self.b_tab], writes=[bCCq[k]])
                    fw.ld.op(lambda: nc.sync.dma_start(out=SSq[k][:], in_=self.SSo[:, l0:l0 + TW]), reads=[self.b_tab], writes=[bSSq[k]])
                for grp in kgroups:
                    do_group(grp, XT[k], bXT[k], 512, t0, CCt[k], bCC[k], SSt[k], bSS[k])
                for grp in qgroups:
                    do_group(grp, XO[k], bXO[k], TW, l0, CCq[k], bCCq[k], SSq[k], bSSq[k])
                for tb in range(4):
                    for half in range(2):
                        pb = 4 + half
                        for c in range(8):
                            fw.pe.op(lambda c=c: nc.tensor.matmul(self.ps(pb), lhsT=XT[k][:, c, tb * 128:(tb + 1) * 128],
                                                                  rhs=WIN[:, c, 2 * D + half * 512:2 * D + (half + 1) * 512],
                                                                  start=(c == 0), stop=(c == 7)),
                                     reads=[b_win, bXT[k]], writes=[self.bps[pb]], inc=(c == 7))
                        if half == 0:
                            fw.act.op(lambda: nc.scalar.copy(VT[k][:, tb, 0:512], self.ps(pb)), reads=[self.bps[pb]], writes=[bVT[k]])
                        else:
                            fw.dve.op(lambda: nc.vector.tensor_copy(VT[k][:, tb, 512:1024], self.ps(pb)), reads=[self.bps[pb]], writes=[bVT[k]])
                if m == 0:
                    for tb in range(2):
                        pb = 6
                        wk = (tt * 2 + tb) % 2
                        for c in range(8):
                            fw.pe.op(lambda c=c: nc.tensor.matmul(self.ps(pb, 128, 8), lhsT=XO[k][:, c, tb * 128:(tb + 1) * 128],
                                                                  rhs=WIN[:, c, 4 * D + 512:4 * D + 520], start=(c == 0), stop=(c == 7)),
                                     reads=[b_win, bXO[k]], writes=[self.bps[pb]], inc=(c == 7))
                        fw.dve.op(lambda: nc.vector.tensor_scalar(out=WIt[wk][:], in0=self.ps(pb, 128, 8), scalar1=float(8 ** -0.5 * 64 ** -0.5),
                                                                  scalar2=None, op0=ALU.mult), reads=[self.bps[pb]], writes=[bWI[wk]])
                        r0 = l0 + tb * 128
                        fw.st.op(lambda: nc.gpsimd.dma_start(out=self.WI[r0:r0 + 128, :], in_=WIt[wk][:]), reads=[bWI[wk]], writes=[self.b_WI])
                for h in range(NH):
                    fw.st.op(lambda h=h: nc.gpsimd.dma_start(out=self.Vs[h, :, tt * 4:(tt + 1) * 4, :], in_=VT[k][:, :, h * 64:(h + 1) * 64]),
                             reads=[bVT[k]], writes=[self.b_Vs])
            fw.barrier()

    def phase_c(self, li, cur, last):
        nc, fw = self.nc, self.fw
        first = li == 0
        src = self.x_own if first else self.xres[cur]
        bsrc = Buf() if first else self.b_xres[cur]
        dst = self.out if last else self.xres[cur ^ 1]
        bdst = Buf() if last else self.b_xres[cur ^ 1]
        wset = (li + 1) % 2
        with ExitStack() as es:
            sb = self.sbf(es)
            WO = sb("WO", [128, 8, D], BF16); b_wo = Buf()
            stg = [sb(f"wostg{i}", [128, D], F32) for i in range(2)]; bstg = [Buf(), Buf()]
            for c in range(8):
                k = c % 2
                fw.ld.op(lambda: nc.sync.dma_start(out=stg[k][:], in_=self.w_out[li][c * 128:(c + 1) * 128, :]), writes=[bstg[k]])
                fw.act.op(lambda: nc.scalar.copy(WO[:, c, :], stg[k][:]), reads=[bstg[k]], writes=[b_wo])
            G = sb("lnG", [128, D], F32); Bt = sb("lnB", [128, D], F32); b_gb = Buf()
            fw.ld.op(lambda: nc.sync.dma_start(out=G[:], in_=self.lng[li:li + 1, :].broadcast_to([128, D])), writes=[b_gb])
            fw.ld.op(lambda: nc.sync.dma_start(out=Bt[:], in_=self.lnb[li:li + 1, :].broadcast_to([128, D])), writes=[b_gb])
            OTt = [sb(f"pc_ot{i}", [128, 8, 512], BF16) for i in range(2)]; bOT = [Buf(), Buf()]
            XR = [sb(f"pc_x{i}", [128, D], F32) for i in range(2)]; bXR = [Buf(), Buf()]
            U = [sb(f"pc_u{i}", [128, D], F32) for i in range(2)]; bU = [Buf(), Buf()]
            XN = [sb(f"pc_xn{i}", [128, D], F32) for i in range(2)]; bXN = [Buf(), Buf()]
            SQ = sb("pc_sq", [128, D], F32); bSQ = Buf()
            ST = [sb(f"pc_st{i}", [128, 8], F32) for i in range(2)]; bST = [Buf(), Buf()]
            XTT = [sb(f"pc_xtt{i}", [128, 8, 128], BF16) for i in range(2)]; bXTT = [Buf(), Buf()]
            for tt in range(self.SO // 512):
                kk = tt % 2
                t0 = tt * 512
                fw.ld.op(lambda: nc.sync.dma_start(out=OTt[kk][:], in_=self.OT[:, t0:t0 + 512].rearrange("(c p) t -> p c t", p=128)),
                         reads=[self.b_OT], writes=[bOT[kk]])
                for tb4 in range(4):
                    tb = tt * 4 + tb4
                    k = tb % 2
                    fw.ld.op(lambda: nc.sync.dma_start(out=XR[k][:], in_=src[tb * 128:(tb + 1) * 128, :]), reads=[bsrc], writes=[bXR[k]])
                    pbs = (4 * k, 4 * k + 1)
                    for half in range(2):
                        for c in range(8):
                            fw.pe.op(lambda c=c, half=half: nc.tensor.matmul(self.ps(pbs[half]), lhsT=OTt[kk][:, c, tb4 * 128:(tb4 + 1) * 128],
                                                                             rhs=WO[:, c, half * 512:(half + 1) * 512], start=(c == 0), stop=(c == 7)),
                                     reads=[b_wo, bOT[kk]], writes=[self.bps[pbs[half]]], inc=(c == 7))
                    st = ST[k]
                    for half in range(2):
                        hs = slice(half * 512, (half + 1) * 512)
                        fw.dve.op(lambda half=half, hs=hs: nc.vector.scalar_tensor_tensor(out=U[k][:, hs], in0=XR[k][:, hs], scalar=float(ALPHA),
                                                                                          in1=self.ps(pbs[half]), op0=ALU.mult, op1=ALU.add),
                                  reads=[bXR[k], self.bps[pbs[half]]], writes=[bU[k]])
                    fw.act.op(lambda: nc.scalar.activation(out=SQ[:], in_=U[k][:], func=AF.Copy, accum_out=st[:, 0:1]), reads=[bU[k]], writes=[bSQ, bST[k]])
                    fw.act.op(lambda: nc.scalar.activation(out=SQ[:], in_=U[k][:], func=AF.Square, accum_out=st[:, 1:2]), reads=[bU[k]], writes=[bSQ, bST[k]])
                    fw.dve.op(lambda: nc.vector.tensor_scalar(out=st[:, 2:3], in0=st[:, 0:1], scalar1=float(1.0 / D), scalar2=None, op0=ALU.mult),
                              reads=[bST[k]], writes=[bST[k]])
                    fw.dve.op(lambda: nc.vector.tensor_tensor(out=st[:, 3:4], in0=st[:, 2:3], in1=st[:, 2:3], op=ALU.mult), reads=[bST[k]], writes=[bST[k]])
                    fw.dve.op(lambda: nc.vector.scalar_tensor_tensor(out=st[:, 4:5], in0=st[:, 1:2], scalar=float(1.0 / D), in1=st[:, 3:4],
                                                                     op0=ALU.mult, op1=ALU.subtract), reads=[bST[k]], writes=[bST[k]])
                    fw.act.op(lambda: nc.scalar.activation(out=st[:, 5:6], in_=st[:, 4:5], func=AF.Sqrt, bias=self.cst[:, 0:1], scale=1.0),
                              reads=[bST[k], self.b_cst], writes=[bST[k]])
                    fw.dve.op(lambda: nc.vector.reciprocal(st[:, 5:6], st[:, 5:6]), reads=[bST[k]], writes=[bST[k]])
                    fw.dve.op(lambda: nc.vector.tensor_scalar(out=XN[k][:], in0=U[k][:], scalar1=st[:, 2:3], scalar2=st[:, 5:6], op0=ALU.subtract, op1=ALU.mult),
                              reads=[bU[k], bST[k]], writes=[bXN[k]])
                    fw.pool.op(lambda: nc.gpsimd.tensor_tensor(out=XN[k][:], in0=XN[k][:], in1=G[:], op=ALU.mult), reads=[bXN[k], b_gb], writes=[bXN[k]])
                    fw.pool.op(lambda: nc.gpsimd.tensor_tensor(out=XN[k][:], in0=XN[k][:], in1=Bt[:], op=ALU.add), reads=[bXN[k], b_gb], writes=[bXN[k]])
                    fw.st.op(lambda: nc.gpsimd.dma_start(out=dst[tb * 128:(tb + 1) * 128, :], in_=XN[k][:]), reads=[bXN[k]], writes=[bdst])
                    if not last:
                        self.emit_transpose_store(XN[k], bXN[k], self.xto_dst(wset, tb), self.b_xTo[wset], XTT[k], bXTT[k], (4 * k + 2, 4 * k + 3))
            fw.barrier()

    def build_masks(self, sb, kind):
        nc, fw = self.nc, self.fw
        Ms = [sb(f"mask_{kind}{j}", [128, TW2], BF16) for j in range(8)]
        b = Buf()
        for j in range(8):
            if kind == "before":
                fw.dve.op(lambda j=j: nc.vector.tensor_scalar(out=Ms[j][:], in0=self.QR2[:], scalar1=float(-128 * j), scalar2=self.cst[:, 1:2],
                                                              op0=ALU.add, op1=ALU.is_gt), reads=[self.b_cst], writes=[b])
            else:
                fw.dve.op(lambda j=j: nc.vector.tensor_scalar(out=Ms[j][:], in0=self.QR2[:], scalar1=float(-128 * j), scalar2=self.cst[:, 2:3],
                                                              op0=ALU.add, op1=ALU.is_ge), reads=[self.b_cst], writes=[b])
        return Ms, b

    def attn_diff(self, L):
        nc, fw, S, NB, NT = self.nc, self.fw, self.S, self.NB, self.NT
        lambda_init = 0.8 - 0.6 * math.exp(-0.3 * L)
        with ExitStack() as es:
            sb = self.sbf(es)
            lv = sb("lv", [128, 4, 64], F32); b_lv = Buf()
            fw.ld.op(lambda: nc.sync.dma_start(out=lv[:].rearrange("p a b -> p (a b)"), in_=self.lam[0:1, :].broadcast_to([128, 256])), writes=[b_lv])
            lt = sb("lt", [128, 8], F32); b_lt = Buf()
            pr = sb("lpr", [128, 2, 64], F32)
            fw.dve.op(lambda: nc.vector.tensor_tensor(out=pr[:, 0, :], in0=lv[:, 0, :], in1=lv[:, 1, :], op=ALU.mult), reads=[b_lv], writes=[b_lt])
            fw.dve.op(lambda: nc.vector.tensor_tensor(out=pr[:, 1, :], in0=lv[:, 2, :], in1=lv[:, 3, :], op=ALU.mult), reads=[b_lv], writes=[b_lt])
            fw.dve.op(lambda: nc.vector.reduce_sum(out=lt[:, 0:1], in_=pr[:, 0, :], axis=mybir.AxisListType.X), reads=[b_lt], writes=[b_lt])
            fw.dve.op(lambda: nc.vector.reduce_sum(out=lt[:, 1:2], in_=pr[:, 1, :], axis=mybir.AxisListType.X), reads=[b_lt], writes=[b_lt])
            fw.act.op(lambda: nc.scalar.activation(out=lt[:, 2:4], in_=lt[:, 0:2], func=AF.Exp), reads=[b_lt], writes=[b_lt])
            fw.dve.op(lambda: nc.vector.tensor_tensor(out=lt[:, 4:5], in0=lt[:, 3:4], in1=lt[:, 2:3], op=ALU.subtract), reads=[b_lt], writes=[b_lt])
            fw.dve.op(lambda: nc.vector.tensor_scalar(out=lt[:, 5:6], in0=lt[:, 4:5], scalar1=float(-lambda_init), scalar2=None, op0=ALU.add),
                      reads=[b_lt], writes=[b_lt])
            sg = sb("sg", [128, 2], F32); b_sg = Buf()
            fw.ld.op(lambda: nc.sync.dma_start(out=sg[:, 0:1], in_=self.subg[:, :]), writes=[b_sg])
            fw.dve.op(lambda: nc.vector.tensor_scalar(out=sg[:, 1:2], in0=sg[:, 0:1], scalar1=float(1.0 - lambda_init), scalar2=None, op0=ALU.mult),
                      reads=[b_sg], writes=[b_sg])
            ones_b = sb("ones_b", [128, 128], BF16); ones_f = sb("ones_f", [128, 128], F32); b_ones = Buf()
            fw.pool.op(lambda: nc.gpsimd.memset(ones_b[:], 1.0), writes=[b_ones])
            fw.pool.op(lambda: nc.gpsimd.memset(ones_f[:], 1.0), writes=[b_ones])
            CM, b_cm = self.build_masks(sb, "chunk")
            KTt = [sb(f"df_k{i}", [64, 2, S], BF16) for i in range(2)]; bK = [Buf(), Buf()]
            Vt = [sb(f"df_v{i}", [128, NB, 128], BF16) for i in range(2)]; bV = [Buf(), Buf()]
            Qt = [sb(f"df_q{i}", [64, 2, TW2], BF16) for i in range(3)]; bQ = [Buf() for _ in range(3)]
            Gt = [sb(f"df_g{i}", [128, TW2], BF16) for i in range(3)]; bG = [Buf() for _ in range(3)]
            P = [sb(f"df_p{i}", [128, TW2], BF16) for i in range(10)]; bP = [Buf() for _ in range(10)]
            PF = [sb(f"df_pf{i}", [128, TW2], F32) for i in range(2)]; bPF = [Buf(), Buf()]
            R0 = sb("df_r0", [128, TW2], F32); R1 = sb("df_r1", [128, TW2], F32); bR = [Buf(), Buf()]
            O0 = sb("df_o0", [128, TW2], F32); O1 = sb("df_o1", [128, TW2], F32); bO = [Buf(), Buf()]
            SQ = sb("df_sq", [128, TW2], F32); bSQ = Buf()
            RS = sb("df_rs", [128, TW2], F32); bRS = Buf()
            OG = [sb(f"df_og{i}", [128, TW2], BF16) for i in range(2)]; bOG = [Buf(), Buf()]
            N = TW2
            sctr = [0]
            pctr = [0]
            items = []
            loaders = []
            for hd in range(8):
                hk = hd % 2

                def load_head(hd=hd, hk=hk):
                    for mm in range(2):
                        fw.ld.op(lambda mm=mm: nc.sync.dma_start(out=KTt[hk][:, mm, :], in_=self.KT[(2 * hd + mm) * 64:(2 * hd + mm + 1) * 64, :]),
                                 reads=[self.b_KT], writes=[bK[hk]])
                        fw.ld.op(lambda mm=mm: nc.sync.dma_start(out=Vt[hk][:, :, mm * 64:(mm + 1) * 64], in_=self.Vs[2 * hd + mm]),
                                 reads=[self.b_Vs], writes=[bV[hk]])
                for tt in range(NT // 2):
                    qk = (hd * (NT // 2) + tt) % 3
                    t0 = tt * TW2
                    nkb = 8 * tt + 8

                    def load_q(hd=hd, qk=qk, t0=t0, tt=tt, hk=hk, lh=load_head):
                        if tt == 0:
                            lh()
                        for mm in range(2):
                            fw.ld.op(lambda mm=mm: nc.sync.dma_start(out=Qt[qk][:, mm, :], in_=self.QT[(2 * hd + mm) * 64:(2 * hd + mm + 1) * 64, t0:t0 + TW2]),
                                     reads=[self.b_QT], writes=[bQ[qk]])
                        fw.ld.op(lambda: nc.sync.dma_start(out=Gt[qk][:], in_=self.GT[hd * 128:(hd + 1) * 128, t0:t0 + TW2]),
                                 reads=[self.b_GT], writes=[bG[qk]])
                    for kb in range(nkb):
                        j = kb - 8 * tt
                        sbk = []
                        pidx = []
                        for mm in range(2):
                            sbk.append(sctr[0] % 3); sctr[0] += 1
                            pidx.append(pctr[0] % 10); pctr[0] += 1
                        if kb == 0:
                            loaders.append(load_q)
                        tidx = len(loaders) - 1

                        def st0(hk=hk, qk=qk, kb=kb, sbk=sbk, first=(kb == 0), tidx=tidx):
                            if first:
                                if tidx == 0:
                                    loaders[0]()
                                if tidx + 1 < len(loaders):
                                    loaders[tidx + 1]()
                            for mm in range(2):
                                fw.pe.op(lambda mm=mm: nc.tensor.matmul(self.ps(sbk[mm], 128, N), lhsT=KTt[hk][:, mm, kb * 128:(kb + 1) * 128], rhs=Qt[qk][:, mm, :],
                                                                        start=True, stop=True),
                                         reads=[bK[hk], bQ[qk]], writes=[self.bps[sbk[mm]]], inc=(mm == 1))

                        def st1(sbk=sbk, pidx=pidx, j=j):
                            for mm in range(2):
                                if j >= 0:
                                    fw.act.op(lambda mm=mm: nc.scalar.activation(out=PF[mm][:], in_=self.ps(sbk[mm], 128, N), func=AF.Exp, scale=0.125),
                                              reads=[self.bps[sbk[mm]]], writes=[bPF[mm]])
                                    fw.dve.op(lambda mm=mm: nc.vector.tensor_tensor(out=P[pidx[mm]][:], in0=PF[mm][:], in1=CM[j][:], op=ALU.mult),
                                              reads=[bPF[mm], b_cm], writes=[bP[pidx[mm]]])
                                else:
                                    fw.act.op(lambda mm=mm: nc.scalar.activation(out=P[pidx[mm]][:], in_=self.ps(sbk[mm], 128, N), func=AF.Exp, scale=0.125),
                                              reads=[self.bps[sbk[mm]]], writes=[bP[pidx[mm]]])

                        def st2(hk=hk, kb=kb, pidx=pidx, nkb=nkb, hd=hd, tt=tt, qk=qk, t0=t0):
                            for mm in range(2):
                                fw.pe.op(lambda mm=mm: nc.tensor.matmul(self.ps(3 + mm, 128, N), lhsT=Vt[hk][:, kb, :], rhs=P[pidx[mm]][:],
                                                                        start=(kb == 0), stop=(kb == nkb - 1)),
                                         reads=[bV[hk], bP[pidx[mm]]], writes=[self.bps[3 + mm]], inc=False)
                                fw.pe.op(lambda mm=mm: nc.tensor.matmul(self.ps(5 + mm, 128, N), lhsT=ones_b[:], rhs=P[pidx[mm]][:],
                                                                        start=(kb == 0), stop=(kb == nkb - 1)),
                                         reads=[b_ones, bP[pidx[mm]]], writes=[self.bps[5 + mm]], inc=(mm == 1))
                            if kb == nkb - 1:
                                ok = (hd * (NT // 2) + tt) % 2
                                fw.dve.op(lambda: nc.vector.reciprocal(R0[:], self.ps(5, 128, N)), reads=[self.bps[5]], writes=[bR[0]])
                                fw.dve.op(lambda: nc.vector.reciprocal(R1[:], self.ps(6, 128, N)), reads=[self.bps[6]], writes=[bR[1]])
                                fw.dve.op(lambda: nc.vector.tensor_tensor(out=O0[:], in0=self.ps(3, 128, N), in1=R0[:], op=ALU.mult), reads=[self.bps[3], bR[0]], writes=[bO[0]])
                                fw.dve.op(lambda: nc.vector.tensor_tensor(out=O1[:], in0=self.ps(4, 128, N), in1=R1[:], op=ALU.mult), reads=[self.bps[4], bR[1]], writes=[bO[1]])
                                fw.dve.op(lambda: nc.vector.scalar_tensor_tensor(out=O0[:], in0=O1[:], scalar=lt[:, 5:6], in1=O0[:], op0=ALU.mult, op1=ALU.add),
                                          reads=[bO[0], bO[1], b_lt], writes=[bO[0]])
                                fw.act.op(lambda: nc.scalar.activation(out=SQ[:], in_=O0[:], func=AF.Square), reads=[bO[0]], writes=[bSQ])
                                fw.pe.op(lambda: nc.tensor.matmul(self.ps(7, 128, N), lhsT=ones_f[:], rhs=SQ[:], start=True, stop=True),
                                         reads=[b_ones, bSQ], writes=[self.bps[7]])
                                fw.act.op(lambda: nc.scalar.activation(out=RS[:], in_=self.ps(7, 128, N), func=AF.Sqrt, bias=self.cst[:, 0:1], scale=float(1.0 / 128)),
                                          reads=[self.bps[7], self.b_cst], writes=[bRS])
                                fw.dve.op(lambda: nc.vector.reciprocal(RS[:], RS[:]), reads=[bRS], writes=[bRS])
                                fw.dve.op(lambda: nc.vector.scalar_tensor_tensor(out=O0[:], in0=O0[:], scalar=sg[:, 1:2], in1=RS[:], op0=ALU.mult, op1=ALU.mult),
                                          reads=[bO[0], bRS, b_sg], writes=[bO[0]])
                                fw.pool.op(lambda: nc.gpsimd.tensor_tensor(out=OG[ok][:], in0=O0[:], in1=Gt[qk][:], op=ALU.mult), reads=[bO[0], bG[qk]], writes=[bOG[ok]])
                                fw.st.op(lambda: nc.gpsimd.dma_start(out=self.OT[hd * 128:(hd + 1) * 128, t0:t0 + TW2], in_=OG[ok][:]),
                                         reads=[bOG[ok]], writes=[self.b_OT])
                        items.append([st0, st1, None, None, st2])
            pipeline(items, 5)
            fw.barrier()

    def attn_sb(self):
        nc, fw, S, NB, NT = self.nc, self.fw, self.S, self.NB, self.NT
        N = TW2
        with ExitStack() as es:
            sb = self.sbf(es)
            b_c = Buf()
            BEF, b_bef = self.build_masks(sb, "before")
            n8 = sb("sb_n8", [128, 128], BF16); tri = sb("sb_tri", [128, 128], BF16)
            fw.pool.op(lambda: nc.gpsimd.memset(n8[:], -8.0), writes=[b_c])
            fw.pool.op(lambda: nc.gpsimd.affine_select(out=tri[:], in_=n8[:], pattern=[[-1, 128]], compare_op=ALU.is_ge, fill=0.0,
                                                       base=0, channel_multiplier=1), reads=[b_c], writes=[b_c])
            one1 = self.cst[:, 3:4]
            KTt = [sb(f"sb_k{i}", [64, S], BF16) for i in range(2)]; bK = [Buf(), Buf()]
            Vt = [sb(f"sb_v{i}", [128, NB, 64], BF16) for i in range(2)]; bV = [Buf(), Buf()]
            Qt = [sb(f"sb_q{i}", [64, N], BF16) for i in range(4)]; bQ = [Buf() for _ in range(4)]
            Gt = [sb(f"sb_g{i}", [64, N], BF16) for i in range(4)]; bG = [Buf() for _ in range(4)]
            E1 = [sb(f"sb_e{i}", [128, N], F32) for i in range(2)]; bE = [Buf(), Buf()]
            SPf = sb("sb_spf", [128, N], F32); bSPf = Buf()
            SP = [sb(f"sb_sp{i}", [128, N], BF16) for i in range(6)]; bSP = [Buf() for _ in range(6)]
            AC = [sb(f"sb_ac{i}", [128, N], BF16) for i in range(6)]; bAC = [Buf() for _ in range(6)]
            A = [sb(f"sb_a{i}", [128, N], BF16) for i in range(6)]; bA = [Buf() for _ in range(6)]
            Af = sb("sb_af", [128, N], F32); bAf = Buf()
            OG = [sb(f"sb_og{i}", [64, N], BF16) for i in range(2)]; bOG = [Buf(), Buf()]
            items = []
            loaders = []
            ic = 0
            tile_i = 0
            for h in range(NH):
                hk = h % 2

                def load_head(h=h, hk=hk):
                    fw.ld.op(lambda: nc.sync.dma_start(out=KTt[hk][:], in_=self.KT[h * 64:(h + 1) * 64, :]), reads=[self.b_KT], writes=[bK[hk]])
                    fw.ld.op(lambda: nc.sync.dma_start(out=Vt[hk][:], in_=self.Vs[h]), reads=[self.b_Vs], writes=[bV[hk]])
                for tt in range(NT // 2):
                    qk = tile_i % 4
                    ob = 5 + tile_i % 2
                    ogk = tile_i % 2
                    tile_i += 1
                    t0 = tt * TW2
                    hi = 8 * tt + 7
                    lo = max(0, 8 * tt - SB_WIN)
                    kbs = list(range(hi, lo - 1, -1))

                    def load_q(h=h, qk=qk, t0=t0, tt=tt, lh=load_head):
                        if tt == 0:
                            lh()
                        fw.ld.op(lambda: nc.sync.dma_start(out=Qt[qk][:], in_=self.QT[h * 64:(h + 1) * 64, t0:t0 + N]), reads=[self.b_QT], writes=[bQ[qk]])
                        fw.ld.op(lambda: nc.sync.dma_start(out=Gt[qk][:], in_=self.GT[h * 64:(h + 1) * 64, t0:t0 + N]), reads=[self.b_GT], writes=[bG[qk]])
                    loaders.append(load_q)
                    tidx = len(loaders) - 1
                    prev_acc = None
                    for n, kb in enumerate(kbs):
                        j = kb - 8 * tt
                        diag = j >= 0
                        sbank = ic % 5
                        spk = ic % 6
                        ek = ic % 2
                        ic += 1
                        first = n == 0
                        lastk = n == len(kbs) - 1
                        pacc = prev_acc
                        acc_out = (SP[spk], bSP[spk]) if first else (AC[spk], bAC[spk])
                        prev_acc = acc_out

                        def st0(hk=hk, qk=qk, kb=kb, sbank=sbank, first=first, tidx=tidx):
                            if first:
                                if tidx == 0:
                                    loaders[0]()
                                if tidx + 1 < len(loaders):
                                    loaders[tidx + 1]()
                            fw.pe.op(lambda: nc.tensor.matmul(self.ps(sbank, 128, N), lhsT=KTt[hk][:, kb * 128:(kb + 1) * 128], rhs=Qt[qk][:], start=True, stop=False),
                                     reads=[bK[hk], bQ[qk]], writes=[self.bps[sbank]])

                        def st1(sbank=sbank, spk=spk, ek=ek, diag=diag, j=j, first=first, pacc=pacc, acc_out=acc_out):
                            fw.act.op(lambda: nc.scalar.activation(out=E1[ek][:], in_=self.ps(sbank, 128, N), func=AF.Exp, scale=0.125),
                                      reads=[self.bps[sbank]], writes=[bE[ek]])
                            if diag:
                                fw.act.op(lambda: nc.scalar.activation(out=SPf[:], in_=E1[ek][:], func=AF.Ln, bias=one1, scale=1.0),
                                          reads=[bE[ek], self.b_cst], writes=[bSPf])
                                fw.dve.op(lambda: nc.vector.tensor_tensor(out=SP[spk][:], in0=SPf[:], in1=BEF[j][:], op=ALU.mult),
                                          reads=[bSPf, b_bef], writes=[bSP[spk]])
                            else:
                                fw.act.op(lambda: nc.scalar.activation(out=SP[spk][:], in_=E1[ek][:], func=AF.Ln, bias=one1, scale=1.0),
                                          reads=[bE[ek], self.b_cst], writes=[bSP[spk]])
                            if not first:
                                fw.pool.op(lambda: nc.gpsimd.tensor_tensor(out=acc_out[0][:], in0=pacc[0][:], in1=SP[spk][:], op=ALU.add),
                                           reads=[pacc[1], bSP[spk]], writes=[acc_out[1]])

                        def st2(sbank=sbank, spk=spk, first=first, pacc=pacc):
                            fw.pe.op(lambda: nc.tensor.matmul(self.ps(sbank, 128, N), lhsT=tri[:], rhs=SP[spk][:], start=False, stop=first),
                                     reads=[b_c, bSP[spk]], writes=[self.bps[sbank]], inc=first)
                            if not first:
                                fw.pe.op(lambda: nc.tensor.matmul(self.ps(sbank, 128, N), lhsT=n8[:], rhs=pacc[0][:], start=False, stop=True),
                                         reads=[b_c, pacc[1]], writes=[self.bps[sbank]])

                        def st3(sbank=sbank, spk=spk, diag=diag, j=j):
                            if diag:
                                fw.act.op(lambda: nc.scalar.activation(out=Af[:], in_=self.ps(sbank, 128, N), func=AF.Exp, scale=0.125),
                                          reads=[self.bps[sbank]], writes=[bAf])
                                fw.dve.op(lambda: nc.vector.tensor_tensor(out=A[spk][:], in0=Af[:], in1=BEF[j][:], op=ALU.mult),
                                          reads=[bAf, b_bef], writes=[bA[spk]])
                            else:
                                fw.act.op(lambda: nc.scalar.activation(out=A[spk][:], in_=self.ps(sbank, 128, N), func=AF.Exp, scale=0.125),
                                          reads=[self.bps[sbank]], writes=[bA[spk]])

                        def st4(hk=hk, kb=kb, spk=spk, first=first, lastk=lastk, ob=ob, ogk=ogk, qk=qk, h=h, t0=t0):
                            fw.pe.op(lambda: nc.tensor.matmul(self.ps(ob, 64, N), lhsT=Vt[hk][:, kb, :], rhs=A[spk][:], start=first, stop=lastk),
                                     reads=[bV[hk], bA[spk]], writes=[self.bps[ob]])
                            if lastk:
                                fw.dve.op(lambda: nc.vector.tensor_tensor(out=OG[ogk][:], in0=self.ps(ob, 64, N), in1=Gt[qk][:], op=ALU.mult),
                                          reads=[self.bps[ob], bG[qk]], writes=[bOG[ogk]])
                                fw.st.op(lambda: nc.gpsimd.dma_start(out=self.OT[h * 64:(h + 1) * 64, t0:t0 + N], in_=OG[ogk][:]),
                                         reads=[bOG[ogk]], writes=[self.b_OT])
                        items.append([st0, st1, None, st2, st3, None, st4])
            pipeline(items, 7)
            fw.barrier()

    def attn_dsa(self):
        nc, fw, S, NB, NT = self.nc, self.fw, self.S, self.NB, self.NT
        N = TW2
        with ExitStack() as es:
            sb = self.sbf(es)
            b_c = Buf()
            kcb = sb("ds_kcb", [128, 128], F32)
            fw.pool.op(lambda: nc.gpsimd.memset(kcb[:, 0:64], 0.0), writes=[b_c])
            fw.pool.op(lambda: nc.gpsimd.memset(kcb[:, 64:128], 64.0), writes=[b_c])
            ones_b = sb("ds_ones", [128, 64], BF16)
            fw.pool.op(lambda: nc.gpsimd.memset(ones_b[:], 1.0), writes=[b_c])
            kiT = sb("ds_ki", [64, S], BF16); b_ki = Buf()
            fw.ld.op(lambda: nc.sync.dma_start(out=kiT[:], in_=self.KT[D:D + 64, :]), reads=[self.b_KT], writes=[b_ki])
            SC = sb("ds_sc", [128, S], F32); bSC = Buf()
            JK = sb("ds_jk", [128, 2048], BF16); bJK = Buf()
            CN = sb("ds_cn", [128, 8], F32)
            MT0 = sb("ds_mt0", [128, NB, N], BF16); bMT0 = Buf()
            MTs = [MT0, MT0]; bMTs = [bMT0, bMT0]
            QI = [sb(f"ds_qi{i}", [64, 8, 128], BF16) for i in range(2)]; bQI = [Buf(), Buf()]
            WQ = [sb(f"ds_wq{i}", [128, 8], F32) for i in range(2)]; bWQ = [Buf(), Buf()]
            PS_ = [sb(f"ds_pos{i}", [128, 2], F32) for i in range(2)]; bPS = [Buf(), Buf()]
            PM = sb("ds_pm", [128, 128], F32); bPM = Buf()
            RL = [sb(f"ds_rl{i}", [128, 512], F32) for i in range(2)]; bRL = [Buf() for _ in range(2)]
            BS = sb("ds_bs", [128, 8], F32); bBS = Buf()
            MK0 = sb("ds_mk0", [128, 512], F32); MK = [MK0, MK0]; bMK0 = Buf(); bMK = [bMK0, bMK0]
            KTt = [sb(f"ds_k{i}", [64, S], BF16) for i in range(2)]; bK = [Buf(), Buf()]
            Vt = [sb(f"ds_v{i}", [128, NB, 65], BF16) for i in range(3)]; bV = [Buf() for _ in range(3)]
            for i_ in range(3):
                fw.pool.op(lambda i_=i_: nc.gpsimd.memset(Vt[i_][:, :, 64:65], 1.0), writes=[bV[i_]])
            onesr = sb("ds_onesr", [65, 64], F32)
            fw.pool.op(lambda: nc.gpsimd.memset(onesr[:], 1.0), writes=[b_c])
            R1 = sb("ds_r1", [65, N], F32); bR1 = Buf()
            Qt = [sb(f"ds_q{i}", [64, N], BF16) for i in range(3)]; bQ = [Buf() for _ in range(3)]
            Gt = [sb(f"ds_g{i}", [64, N], BF16) for i in range(3)]; bG = [Buf() for _ in range(3)]
            P = [sb(f"ds_p{i}", [128, N], BF16) for i in range(6)]; bP = [Buf() for _ in range(6)]
            O1 = sb("ds_o1", [64, N], F32); bO1 = Buf()
            OG = [sb(f"ds_og{i}", [64, N], BF16) for i in range(2)]; bOG = [Buf(), Buf()]
            ctr = {"rl": 0, "qi": 0, "ps": 0, "tile": 0}

            def nk_of(tt, i2):
                return 4 * tt + (2 if i2 == 0 else 4)


            def part1(tt):
                st_ = tt
                MT, bMT = MTs[0], bMTs[0]
                for i4 in range(4):
                    tt = 2 * st_ + i4 // 2
                    i2 = i4 % 2
                    lb = 4 * st_ + i4
                    nk = nk_of(tt, i2)
                    ncols_tot = nk * 128
                    qk = ctr["qi"] % 2
                    ctr["qi"] += 1
                    fw.ld.op(lambda: nc.sync.dma_start(out=QI[qk][:], in_=self.QT[D:D + 512, lb * 128:(lb + 1) * 128].rearrange("(h d) t -> d h t", d=64)),
                             reads=[self.b_QT], writes=[bQI[qk]])
                    fw.ld.op(lambda: nc.sync.dma_start(out=WQ[qk][:], in_=self.WI[lb * 128:(lb + 1) * 128, :]), reads=[self.b_WI], writes=[bWQ[qk]])
                    fw.ld.op(lambda: nc.sync.dma_start(out=PS_[qk][:, 0:1], in_=self.poscol[lb * 128:(lb + 1) * 128, :]), writes=[bPS[qk]])
                    for c0 in range(0, ncols_tot, 512):
                        ncol = min(512, ncols_tot - c0)
                        for hh in range(8):
                            pb = 3 + ctr["ps"] % 2
                            ctr["ps"] += 1
                            rk = ctr["rl"] % 2
                            ctr["rl"] += 1
                            fw.pe.op(lambda: nc.tensor.matmul(self.ps(pb, 128, ncol), lhsT=QI[qk][:, hh, :], rhs=kiT[:, c0:c0 + ncol], start=True, stop=True),
                                     reads=[bQI[qk], b_ki], writes=[self.bps[pb]])
                            fw.act.op(lambda: nc.scalar.activation(out=RL[rk][:, 0:ncol], in_=self.ps(pb, 128, ncol), func=AF.Relu),
                                      reads=[self.bps[pb]], writes=[bRL[rk]])
                            if hh == 0:
                                fw.dve.op(lambda: nc.vector.tensor_scalar(out=SC[:, c0:c0 + ncol], in0=RL[rk][:, 0:ncol], scalar1=WQ[qk][:, 0:1], scalar2=None,
                                                                          op0=ALU.mult), reads=[bRL[rk], bWQ[qk]], writes=[bSC])
                            else:
                                fw.dve.op(lambda: nc.vector.scalar_tensor_tensor(out=SC[:, c0:c0 + ncol], in0=RL[rk][:, 0:ncol], scalar=WQ[qk][:, hh:hh + 1],
                                                                                 in1=SC[:, c0:c0 + ncol], op0=ALU.mult, op1=ALU.add),
                                          reads=[bRL[rk], bWQ[qk], bSC], writes=[bSC])
                            yield
                    for kb in range(4 * tt, nk):
                        dsl = slice(kb * 128, (kb + 1) * 128)
                        fw.dve.op(lambda kb=kb: nc.vector.tensor_scalar(out=PS_[qk][:, 1:2], in0=PS_[qk][:, 0:1], scalar1=float(-128 * kb), scalar2=None, op0=ALU.add),
                                  reads=[bPS[qk]], writes=[bPS[qk]])
                        fw.dve.op(lambda: nc.vector.tensor_scalar(out=PM[:], in0=kcb[:], scalar1=PS_[qk][:, 1:2], scalar2=-1e30, op0=ALU.is_gt, op1=ALU.mult),
                                  reads=[b_c, bPS[qk]], writes=[bPM])
                        fw.dve.op(lambda dsl=dsl: nc.vector.tensor_tensor(out=SC[:, dsl], in0=SC[:, dsl], in1=PM[:], op=ALU.add), reads=[bSC, bPM], writes=[bSC])
                    fw.dve.op(lambda: nc.vector.memset(BS[:, 0:1], -BIS_R), writes=[bBS])
                    fw.dve.op(lambda: nc.vector.memset(BS[:, 1:2], 0.0), writes=[bBS])
                    yield
                    hstep = BIS_R
                    for it in range(BIS_ITERS):
                        nchk = (ncols_tot + 2047) // 2048
                        for ci in range(nchk):
                            cc0 = ci * 2048
                            nn = min(2048, ncols_tot - cc0)
                            dcol = BS[:, 2:3] if nchk == 1 else CN[:, ci:ci + 1]
                            fw.dve.op(lambda cc0=cc0, nn=nn, dcol=dcol: nc.vector.tensor_scalar(out=JK[:, 0:nn], in0=SC[:, cc0:cc0 + nn], scalar1=BS[:, 1:2], scalar2=0.0,
                                                                                                op0=ALU.is_ge, op1=ALU.add, accum_out=dcol),
                                      reads=[bSC, bBS], writes=[bJK, bBS])
                        if nchk > 1:
                            fw.dve.op(lambda: nc.vector.reduce_sum(out=BS[:, 2:3], in_=CN[:, 0:nchk], axis=mybir.AxisListType.X), reads=[bBS], writes=[bBS])
                        fw.dve.op(lambda: nc.vector.tensor_scalar(out=BS[:, 3:4], in0=BS[:, 2:3], scalar1=float(TOPK) - 0.5, scalar2=float(hstep),
                                                                  op0=ALU.is_ge, op1=ALU.mult), reads=[bBS], writes=[bBS])
                        fw.dve.op(lambda: nc.vector.tensor_tensor(out=BS[:, 0:1], in0=BS[:, 0:1], in1=BS[:, 3:4], op=ALU.add), reads=[bBS], writes=[bBS])
                        hstep = hstep / 2
                        fw.dve.op(lambda: nc.vector.tensor_scalar(out=BS[:, 1:2], in0=BS[:, 0:1], scalar1=float(hstep), scalar2=None, op0=ALU.add),
                                  reads=[bBS], writes=[bBS])
                        yield
                    for c0 in range(0, ncols_tot, 512):
                        ncol = min(512, ncols_tot - c0)
                        mk = (c0 // 512) % 2
                        pb = 5
                        fw.dve.op(lambda: nc.vector.tensor_scalar(out=MK[mk][:, 0:ncol], in0=SC[:, c0:c0 + ncol], scalar1=BS[:, 0:1], scalar2=None, op0=ALU.is_ge),
                                  reads=[bSC, bBS], writes=[bMK[mk]])
                        nb4 = ncol // 128
                        for b4 in range(nb4):
                            fw.pe.op(lambda b4=b4: nc.tensor.transpose(self.ps(pb)[:, b4 * 128:(b4 + 1) * 128], MK[mk][:, b4 * 128:(b4 + 1) * 128], self.ident[:]),
                                     reads=[bMK[mk], self.b_ident], writes=[self.bps[pb]], inc=(b4 == nb4 - 1))
                        kb0 = c0 // 128
                        fw.act.op(lambda: nc.scalar.copy(MT[:, kb0:kb0 + nb4, i4 * 128:(i4 + 1) * 128],
                                                         self.ps(pb, 128, ncol).rearrange("p (b t) -> p b t", t=128)),
                                  reads=[self.bps[pb]], writes=[bMT])
                        yield
                    if nk < 8 * st_ + 8:
                        fw.pool.op(lambda: nc.gpsimd.memset(MT[:, nk:8 * st_ + 8, i4 * 128:(i4 + 1) * 128], 0.0), writes=[bMT])

            def part2_items(tt):
                MT, bMT = MTs[0], bMTs[0]
                t0 = tt * N
                nkb = 8 * tt + 8
                items = []
                loaders = []
                ic = 0
                for h in range(NH):
                    hk = h % 2
                    vk = h % 3
                    qk = ctr["tile"] % 3
                    ogk = ctr["tile"] % 2
                    ctr["tile"] += 1

                    def load_q(h=h, qk=qk, hk=hk, vk=vk):
                        fw.ld.op(lambda: nc.sync.dma_start(out=KTt[hk][:, 0:nkb * 128], in_=self.KT[h * 64:(h + 1) * 64, 0:nkb * 128]), reads=[self.b_KT], writes=[bK[hk]])
                        fw.ld.op(lambda: nc.sync.dma_start(out=Vt[vk][:, 0:nkb, 0:64], in_=self.Vs[h, :, 0:nkb, :]), reads=[self.b_Vs], writes=[bV[vk]])
                        fw.ld.op(lambda: nc.sync.dma_start(out=Qt[qk][:], in_=self.QT[h * 64:(h + 1) * 64, t0:t0 + N]), reads=[self.b_QT], writes=[bQ[qk]])
                        fw.ld.op(lambda: nc.sync.dma_start(out=Gt[qk][:], in_=self.GT[h * 64:(h + 1) * 64, t0:t0 + N]), reads=[self.b_GT], writes=[bG[qk]])
                    loaders.append(load_q)
                    for kb in range(nkb):
                        sbank = ic % 3
                        pk = ic % 6
                        ic += 1
                        first = kb == 0
                        lastk = kb == nkb - 1

                        def st0(h=h, hk=hk, qk=qk, kb=kb, sbank=sbank, first=first):
                            if first:
                                if h == 0:
                                    loaders[0]()
                                if h + 1 < NH:
                                    loaders[h + 1]()
                            fw.pe.op(lambda: nc.tensor.matmul(self.ps(sbank, 128, N), lhsT=KTt[hk][:, kb * 128:(kb + 1) * 128], rhs=Qt[qk][:], start=True, stop=True),
                                     reads=[bK[hk], bQ[qk]], writes=[self.bps[sbank]])

                        def st1(sbank=sbank, pk=pk, kb=kb, onpool=(ic % 3 == 0)):
                            fw.act.op(lambda: nc.scalar.activation(out=P[pk][:], in_=self.ps(sbank, 128, N), func=AF.Exp, scale=0.125),
                                      reads=[self.bps[sbank]], writes=[bP[pk]])
                            if onpool:
                                fw.pool.op(lambda: nc.gpsimd.tensor_tensor(out=P[pk][:], in0=P[pk][:], in1=MT[:, kb, :], op=ALU.mult),
                                           reads=[bP[pk], bMT], writes=[bP[pk]])
                            else:
                                fw.dve.op(lambda: nc.vector.tensor_tensor(out=P[pk][:], in0=P[pk][:], in1=MT[:, kb, :], op=ALU.mult),
                                          reads=[bP[pk], bMT], writes=[bP[pk]])

                        def st2(h=h, vk=vk, kb=kb, pk=pk, first=first, lastk=lastk, qk=qk, ogk=ogk):
                            fw.pe.op(lambda: nc.tensor.matmul(self.ps(6, 65, N), lhsT=Vt[vk][:, kb, :], rhs=P[pk][:], start=first, stop=lastk),
                                     reads=[bV[vk], bP[pk]], writes=[self.bps[6]])
                            if lastk:
                                fw.act.op(lambda: nc.scalar.copy(O1[:], self.ps(6, 64, N)), reads=[self.bps[6]], writes=[bO1])
                                fw.dve.op(lambda: nc.vector.reciprocal(R1[64:65, :], self.psall[64:65, 6 * 512:6 * 512 + N]), reads=[self.bps[6]], writes=[bR1])
                                fw.pe.op(lambda: nc.tensor.matmul(self.ps(7, 64, N), lhsT=onesr[64:65, :], rhs=R1[64:65, :], start=True, stop=True),
                                         reads=[b_c, bR1], writes=[self.bps[7]])
                                fw.dve.op(lambda: nc.vector.tensor_tensor(out=O1[:], in0=O1[:], in1=self.ps(7, 64, N), op=ALU.mult), reads=[bO1, self.bps[7]], writes=[bO1])
                                fw.pool.op(lambda: nc.gpsimd.tensor_tensor(out=OG[ogk][:], in0=O1[:], in1=Gt[qk][:], op=ALU.mult), reads=[bO1, bG[qk]], writes=[bOG[ogk]])
                                fw.st.op(lambda: nc.gpsimd.dma_start(out=self.OT[h * 64:(h + 1) * 64, t0:t0 + N], in_=OG[ogk][:]),
                                         reads=[bOG[ogk]], writes=[self.b_OT])
                        items.append([st0, st1, None, None, st2])
                return items

            for st_ in range(NT // 2):
                for _ in part1(st_):
                    pass
                pipeline(part2_items(st_), 5)
            fw.barrier()


_LAYERS = [(i % 3, i // 3, i) for i in range(DEPTH)]


def _consts():
    p = np.arange(128)
    i = (p % 32).astype(np.float32)
    inv = (1.0 / (np.float32(10000.0) ** (2 * i / np.float32(64)))).astype(np.float32)
    sign = np.where((p % 64) < 32, -1.0, 1.0).astype(np.float32)
    return np.stack([inv, sign], axis=1).astype(np.float32), np.eye(128, dtype=np.float32)


def own_rows(S, rho):
    idx = []
    for tt in range(S // 512):
        for c in own_blocks(rho):
            g = 4 * tt + c
            idx.append(np.arange(g * 128, (g + 1) * 128))
    return np.concatenate(idx)


def make_in_map(xb, rho, layers, w):
    S = xb.shape[0]
    ropec, ident = _consts()
    rows = own_rows(S, rho)
    im = {"x": np.ascontiguousarray(xb), "x_own": np.ascontiguousarray(xb[rows]), "ropec": ropec, "ident": ident}
    im["posrow"] = rows.astype(np.float32).reshape(1, -1)
    im["poscol"] = rows.astype(np.float32).reshape(-1, 1)
    im["qrel"] = (rows[:TW2] % 1024).astype(np.float32).reshape(1, TW2)
    ws_in = {0: w["w_in_a"], 1: w["w_in_b"], 2: w["w_in_c"]}
    ws_out = {0: w["w_out_a"], 1: w["w_out_b"], 2: w["w_out_c"]}
    for li, (m, j, L) in enumerate(layers):
        im[f"w_in{li}"] = np.ascontiguousarray(ws_in[m][j])
        im[f"w_out{li}"] = np.ascontiguousarray(ws_out[m][j])
    im["ln_g"] = np.ascontiguousarray(np.stack([w["ln_g"][L] for (_, _, L) in layers]))
    im["ln_b"] = np.ascontiguousarray(np.stack([w["ln_b"][L] for (_, _, L) in layers]))
    im["lam"] = np.ascontiguousarray(np.stack([w["lambda_q1"][0], w["lambda_k1"][0], w["lambda_q2"][0], w["lambda_k2"][0]]).reshape(1, 256))
    im["subg"] = np.ascontiguousarray(w["subln_g"][0].reshape(128, 1))
    return im


def run_layers(x, w, layers, ncores):
    B, S, _ = x.shape
    assert ncores == 2 * B
    prog = Prog(S, layers, ncores=ncores)
    in_maps = [make_in_map(x[c // 2], c % 2, layers, w) for c in range(ncores)]
    res = run_bass_kernel_spmd(prog.nc, in_maps, core_ids=list(range(ncores)))
    out = np.empty((B, S, D), np.float32)
    for c in range(ncores):
        out[c // 2][own_rows(S, c % 2)] = res.results[c]["out"]
    return out


def kernel(**inputs):
    w = {k: np.asarray(v, dtype=np.float32) for k, v in inputs.items()}
    return run_layers(w["x"], w, _LAYERS, 8)
```

```python
import math
from contextlib import ExitStack
import numpy as np
import concourse.bass as bass
import concourse.mybir as mybir
from concourse.bass_utils import run_bass_kernel_spmd

F32 = mybir.dt.float32
BF16 = mybir.dt.bfloat16
I32 = mybir.dt.int32
AF = mybir.ActivationFunctionType
ALU = mybir.AluOpType

D = 1024
NH = 16
HD = 64
DEPTH = 4
ALPHA = (2.0 * DEPTH) ** 0.25
LN_EPS = 1e-5
RMS_EPS = 1e-5
A_IN = 4 * D + 8 * 64 + 8 + 64
TOPK = 256
SB_WIN = 3
NEGM = 30000.0
BIS_ITERS = 20
BIS_R = 16.0


class Buf:
    __slots__ = ("w", "r")

    def __init__(self):
        self.w = None
        self.r = {}


class Q:
    def __init__(self, fw, eng, name, is_dma, nsems, same_engine_waits=True):
        self.fw = fw
        self.eng = eng
        self.is_dma = is_dma
        self.inc = 16 if is_dma else 1
        self.sems = []
        for i in range(nsems):
            h = fw.nc.alloc_semaphore(name=f"s_{name}{i}")
            self.sems.append(len(fw.semh))
            fw.semh.append(h)
        self.cnt = [0] * nsems
        self.rr = 0
        self.waited = {}
        self.sew = same_engine_waits
        self.pending = False

    def _wait(self, sem, val):
        if self.waited.get(sem, 0) < val:
            self.eng.wait_ge(self.fw.semh[sem], val)
            self.waited[sem] = val

    def op(self, fn, reads=(), writes=(), inc=True):
        deps = {}
        for b in reads:
            if b.w is not None:
                s, v = b.w
                if deps.get(s, 0) < v:
                    deps[s] = v
        for b in writes:
            if b.w is not None:
                s, v = b.w
                if deps.get(s, 0) < v:
                    deps[s] = v
            for s, v in b.r.items():
                if deps.get(s, 0) < v:
                    deps[s] = v
        i = self.rr
        if self.is_dma:
            self.rr = (self.rr + 1) % len(self.sems)
            if self.cnt[i] > 0:
                deps[self.sems[i]] = max(deps.get(self.sems[i], 0), self.cnt[i])
        for s, v in deps.items():
            if (not self.sew) and s == self.sems[0]:
                continue
            self._wait(s, v)
        ins = fn()
        if inc:
            self.cnt[i] += self.inc
            ins.then_inc(self.fw.semh[self.sems[i]], self.inc)
            tag = (self.sems[i], self.cnt[i])
            self.pending = False
        else:
            tag = (self.sems[i], self.cnt[i] + self.inc)
            self.pending = True
        for b in reads:
            if b.r.get(tag[0], 0) < tag[1]:
                b.r[tag[0]] = tag[1]
        for b in writes:
            b.w = tag
            b.r = {}
        return tag

    def wait_all_of(self, other):
        for i, s in enumerate(other.sems):
            if other.cnt[i] > 0:
                self._wait(s, other.cnt[i])


class FW:
    def __init__(self, nc):
        self.nc = nc
        self.semh = []
        self.pe = Q(self, nc.tensor, "pe", False, 1, same_engine_waits=False)
        self.act = Q(self, nc.scalar, "act", False, 1)
        self.dve = Q(self, nc.vector, "dve", False, 1)
        self.pool = Q(self, nc.gpsimd, "pool", False, 1)
        self.ld = Q(self, nc.sync, "ld", True, 8)
        self.st = Q(self, nc.gpsimd, "st", True, 8)
        self.qs = [self.pe, self.act, self.dve, self.pool, self.ld, self.st]

    def barrier(self):
        for q in self.qs:
            assert not q.pending
        for q in self.qs:
            for o in self.qs:
                if o is not q:
                    q.wait_all_of(o)


def pipeline(items, nstages):
    n = len(items)
    for step in range(n + nstages - 1):
        for s in range(nstages - 1, -1, -1):
            i = step - s
            if 0 <= i < n and items[i][s] is not None:
                items[i][s]()


TW2 = 512
TW = 256


def own_blocks(rho):
    return (0, 3) if rho == 0 else (1, 2)


class Prog:
    def __init__(self, S, layers, ncores=8, debug=False):
        self.S = S
        self.SO = S // 2
        self.NB = S // 128
        self.NT = S // 512
        self.CW = min(1024, self.SO)
        self.NCH = self.SO // self.CW
        self.groups = [[2 * i, 2 * i + 1] for i in range(ncores // 2)]
        self.layers = layers
        self.debug = debug
        nc = self.nc = bass.Bass("TRN2", target_bir_lowering=False)
        self.fw = FW(nc)
        self.fw.cc = Q(self.fw, nc.gpsimd, "cc", False, 1)
        self.fw.qs.append(self.fw.cc)
        SO = self.SO
        dt = nc.dram_tensor
        self.x_in = dt("x", [S, D], F32, kind="ExternalInput").ap()
        self.x_own = dt("x_own", [SO, D], F32, kind="ExternalInput").ap()
        self.posrow = dt("posrow", [1, SO], F32, kind="ExternalInput").ap()
        self.poscol = dt("poscol", [SO, 1], F32, kind="ExternalInput").ap()
        self.qrel = dt("qrel", [1, TW2], F32, kind="ExternalInput").ap()
        self.out = dt("out", [SO, D], F32, kind="ExternalOutput").ap()
        self.w_in = {}
        self.w_out = {}
        for li, (m, j, L) in enumerate(layers):
            width = A_IN if m == 0 else 4 * D
            self.w_in[li] = dt(f"w_in{li}", [D, width], F32, kind="ExternalInput").ap()
            self.w_out[li] = dt(f"w_out{li}", [D, D], F32, kind="ExternalInput").ap()
        self.lng = dt("ln_g", [len(layers), D], F32, kind="ExternalInput").ap()
        self.lnb = dt("ln_b", [len(layers), D], F32, kind="ExternalInput").ap()
        self.lam = dt("lam", [1, 256], F32, kind="ExternalInput").ap()
        self.subg = dt("subg", [128, 1], F32, kind="ExternalInput").ap()
        self.ropec = dt("ropec", [128, 2], F32, kind="ExternalInput").ap()
        self.ident_in = dt("ident", [128, 128], F32, kind="ExternalInput").ap()
        IK = "ExternalOutput" if debug else "Internal"
        self.xres = [dt(f"xres{i}", [SO, D], F32, kind="Internal").ap() for i in range(2)]
        self.xTf = dt("xTf", [D, S], BF16, kind="Internal").ap()
        self.xTo = [[dt(f"xTo{s_}_{c}", [D, self.CW], BF16) for c in range(self.NCH)] for s_ in range(2)]
        self.gath = [[dt(f"gath{l}_{c}", [2 * D, self.CW], BF16) for c in range(self.NCH)] for l in range(max(1, len(layers) - 1))]
        self.QT = dt("QT", [24 * 64, SO], BF16, kind=IK).ap()
        self.KT = dt("KT", [17 * 64, S], BF16, kind=IK).ap()
        self.GT = dt("GT", [D, SO], BF16, kind=IK).ap()
        self.OT = dt("OT", [D, SO], BF16, kind=IK).ap()
        self.Vs = dt("Vs", [NH, 128, self.NB, 64], BF16, kind=IK).ap()
        self.WI = dt("WI", [SO, 8], F32, kind="Internal").ap()
        self.CC = dt("CCt", [128, S], F32, kind="Internal").ap()
        self.SS = dt("SSt", [128, S], F32, kind="Internal").ap()
        self.CCo = dt("CCo", [128, SO], F32, kind="Internal").ap()
        self.SSo = dt("SSo", [128, SO], F32, kind="Internal").ap()
        if debug:
            self.dbgSC = dt("dbgSC", [128, 512], F32, kind="ExternalOutput").ap()
            self.dbgBS = dt("dbgBS", [128, 8], F32, kind="ExternalOutput").ap()
            self.dbgMT = dt("dbgMT", [128, 4, TW], BF16, kind="ExternalOutput").ap()
            self.dbgMT2 = dt("dbgMT2", [128, 4, TW], BF16, kind="ExternalOutput").ap()
            self.dbgP = dt("dbgP", [128, 4, TW], BF16, kind="ExternalOutput").ap()
            self.dbgO = dt("dbgO", [64, 2, TW], F32, kind="ExternalOutput").ap()
        self.b_xres = [Buf(), Buf()]
        self.b_xTf, self.b_QT, self.b_KT, self.b_GT, self.b_OT, self.b_Vs, self.b_WI, self.b_tab = (Buf() for _ in range(8))
        self.b_xTo = [Buf(), Buf()]
        self.b_gath = [Buf() for _ in self.gath]
        self.psall = nc.alloc_psum_tensor("psall", [128, 4096], F32)
        self.bps = [Buf() for _ in range(8)]
        self.build()

    def sbf(self, es):
        def f(n, sh, d):
            self._uid = getattr(self, "_uid", 0) + 1
            return es.enter_context(self.nc.sbuf_tensor(f"{n}_u{self._uid}", sh, d))
        return f

    def ps(self, i, parts=128, n=512):
        return self.psall[0:parts, i * 512:i * 512 + n]

    def build(self):
        nc, fw = self.nc, self.fw
        with ExitStack() as es:
            sb = self.sbf(es)
            self.ident = sb("identsb", [128, 128], F32)
            self.b_ident = Buf()
            fw.ld.op(lambda: nc.sync.dma_start(out=self.ident[:], in_=self.ident_in[:, :]), writes=[self.b_ident])
            self.cst = sb("cst", [128, 6], F32)
            self.identb = sb("identb", [128, 128], BF16)
            self.b_cst = Buf()
            fw.pool.op(lambda: nc.gpsimd.memset(self.cst[:, 0:1], float(LN_EPS)), writes=[self.b_cst])
            fw.pool.op(lambda: nc.gpsimd.iota(self.cst[:, 1:2], pattern=[[0, 1]], base=0, channel_multiplier=1,
                                              allow_small_or_imprecise_dtypes=True), writes=[self.b_cst])
            fw.pool.op(lambda: nc.gpsimd.memset(self.cst[0:64, 2:3], 0.0), writes=[self.b_cst])
            fw.pool.op(lambda: nc.gpsimd.memset(self.cst[64:128, 2:3], 64.0), writes=[self.b_cst])
            fw.pool.op(lambda: nc.gpsimd.memset(self.cst[:, 3:4], 1.0), writes=[self.b_cst])
            fw.pool.op(lambda: nc.gpsimd.memset(self.cst[:, 4:5], -NEGM), writes=[self.b_cst])
            fw.dve.op(lambda: nc.vector.tensor_copy(self.identb[:], self.ident[:]), reads=[self.b_ident], writes=[self.b_cst])
            self.QR2 = sb("qrelsb", [128, TW2], F32)
            fw.ld.op(lambda: nc.sync.dma_start(out=self.QR2[:], in_=self.qrel[0:1, :].broadcast_to([128, TW2])), writes=[self.b_cst])
            self.rope_tables(self.CC, self.SS, self.S, None)
            self.rope_tables(self.CCo, self.SSo, self.SO, self.posrow)
            self.transpose_input()
            cur = 0
            for li, (m, j, L) in enumerate(self.layers):
                last = li == len(self.layers) - 1
                self.phase_a(li, m)
                if m == 0:
                    self.attn_dsa()
                elif m == 1:
                    self.attn_sb()
                else:
                    self.attn_diff(L)
                self.phase_c(li, cur, last)
                if not last:
                    self.exchange(li)
                cur ^= 1
            fw.barrier()

    def exchange(self, li):
        nc, fw = self.nc, self.fw
        sset = (li + 1) % 2
        for ch in range(self.NCH):
            fw.cc.op(lambda ch=ch: nc.gpsimd.collective_compute("AllGather", ALU.bypass, replica_groups=self.groups,
                                                                ins=[self.xTo[sset][ch].ap().opt()], outs=[self.gath[li][ch].ap().opt()]),
                     reads=[self.b_xTo[sset]], writes=[self.b_gath[li]])
        fw.barrier()

    def xt_all_src(self, li, tt, c):
        if li == 0:
            g = 4 * tt + c
            return self.xTf[:, g * 128:(g + 1) * 128], self.b_xTf
        rho = 0 if c in (0, 3) else 1
        lb = 2 * tt + (0 if c in (0, 1) else 1)
        ch, col = divmod(lb * 128, self.CW)
        return self.gath[li - 1][ch].ap()[rho * D:(rho + 1) * D, col:col + 128], self.b_gath[li - 1]

    def rope_tables(self, CCd, SSd, n, posrow):
        nc, fw = self.nc, self.fw
        with ExitStack() as es:
            sb = self.sbf(es)
            rc = sb("rc", [128, 2], F32); brc = Buf()
            fw.ld.op(lambda: nc.sync.dma_start(out=rc[:], in_=self.ropec[:, :]), writes=[brc])
            C1 = 6.28125
            C2 = 2 * math.pi - C1
            pools = {}
            for nm, dtp in [("it", F32), ("ang", F32), ("a2", F32), ("ki", I32), ("kf", F32), ("r0", F32), ("r1", F32)]:
                pools[nm] = [(sb(f"rt_{nm}{q}", [128, 512], dtp), Buf()) for q in range(2)]
            for t0 in range(0, n, 512):
                sl = (t0 // 512) % 2
                it, b_it = pools["it"][sl]
                ang, b_ang = pools["ang"][sl]
                for which in range(2):
                    a2, b_a2 = pools["a2"][sl]
                    ki, b_ki = pools["ki"][sl]
                    kf, b_kf = pools["kf"][sl]
                    r, b_r = pools["r0" if which == 0 else "r1"][sl]
                    if which == 0:
                        if posrow is None:
                            fw.pool.op(lambda: nc.gpsimd.iota(it[:], pattern=[[1, 512]], base=t0, channel_multiplier=0,
                                                              allow_small_or_imprecise_dtypes=True), writes=[b_it])
                        else:
                            fw.ld.op(lambda: nc.sync.dma_start(out=it[:], in_=posrow[0:1, t0:t0 + 512].broadcast_to([128, 512])), writes=[b_it])
                        fw.dve.op(lambda: nc.vector.tensor_scalar(out=ang[:], in0=it[:], scalar1=rc[:, 0:1], scalar2=None,
                                                                  op0=ALU.mult), reads=[b_it, brc], writes=[b_ang])
                        src = ang
                    else:
                        fw.dve.op(lambda: nc.vector.tensor_scalar(out=a2[:], in0=ang[:], scalar1=float(math.pi / 2), scalar2=None,
                                                                  op0=ALU.add), reads=[b_ang], writes=[b_a2])
                        src = a2
                    bsrc = b_ang if which == 0 else b_a2
                    fw.dve.op(lambda: nc.vector.tensor_scalar(out=ki[:], in0=src[:], scalar1=float(1 / (2 * math.pi)), scalar2=None,
                                                              op0=ALU.mult), reads=[bsrc], writes=[b_ki])
                    fw.dve.op(lambda: nc.vector.tensor_copy(kf[:], ki[:]), reads=[b_ki], writes=[b_kf])
                    fw.dve.op(lambda: nc.vector.scalar_tensor_tensor(out=r[:], in0=kf[:], scalar=-C1, in1=src[:], op0=ALU.mult,
                                                                     op1=ALU.add), reads=[b_kf, bsrc], writes=[b_r])
                    fw.dve.op(lambda: nc.vector.scalar_tensor_tensor(out=r[:], in0=kf[:], scalar=-C2, in1=r[:], op0=ALU.mult,
                                                                     op1=ALU.add), reads=[b_kf, b_r], writes=[b_r])
                    fw.dve.op(lambda: nc.vector.tensor_scalar(out=r[:], in0=r[:], scalar1=float(math.pi), scalar2=float(-math.pi),
                                                              op0=ALU.min, op1=ALU.max), reads=[b_r], writes=[b_r])
                    fw.act.op(lambda: nc.scalar.activation(out=r[:], in_=r[:], func=AF.Sin), reads=[b_r], writes=[b_r])
                    if which == 0:
                        fw.dve.op(lambda: nc.vector.tensor_scalar(out=r[:], in0=r[:], scalar1=rc[:, 1:2], scalar2=None,
                                                                  op0=ALU.mult), reads=[b_r, brc], writes=[b_r])
                        fw.st.op(lambda: nc.gpsimd.dma_start(out=SSd[:, t0:t0 + 512], in_=r[:]), reads=[b_r], writes=[self.b_tab])
                    else:
                        fw.st.op(lambda: nc.gpsimd.dma_start(out=CCd[:, t0:t0 + 512], in_=r[:]), reads=[b_r], writes=[self.b_tab])
            fw.barrier()

    def emit_transpose_store(self, xn, b_xn, dst_ap, b_dst, xtt, b_xtt, pbank):
        nc, fw = self.nc, self.fw
        for half in range(2):
            bank = pbank[half]
            for c4 in range(4):
                c = half * 4 + c4
                fw.pe.op(lambda c=c, c4=c4: nc.tensor.transpose(self.ps(bank)[:, c4 * 128:(c4 + 1) * 128], xn[:, c * 128:(c + 1) * 128],
                                                                self.ident[:]),
                         reads=[b_xn, self.b_ident], writes=[self.bps[bank]], inc=(c4 == 3))
            fw.act.op(lambda half=half: nc.scalar.copy(xtt[:, half * 4:(half + 1) * 4, :],
                                                        self.ps(bank).rearrange("p (c t) -> p c t", c=4)),
                      reads=[self.bps[bank]], writes=[b_xtt])
        fw.st.op(lambda: nc.gpsimd.dma_start(out=dst_ap.rearrange("(c p) t -> p c t", p=128), in_=xtt[:]),
                 reads=[b_xtt], writes=[b_dst])

    def xto_dst(self, sset, lb):
        ch, col = divmod(lb * 128, self.CW)
        return self.xTo[sset][ch].ap()[:, col:col + 128]

    def transpose_input(self):
        nc, fw = self.nc, self.fw
        with ExitStack() as es:
            sb = self.sbf(es)
            xt = [sb(f"ti_x{i}", [128, D], F32) for i in range(2)]
            bx = [Buf(), Buf()]
            xtt = [sb(f"ti_xt{i}", [128, 8, 128], BF16) for i in range(2)]
            bxtt = [Buf(), Buf()]
            n = 0
            for tb in range(self.NB):
                k = n % 2
                n += 1
                fw.ld.op(lambda: nc.sync.dma_start(out=xt[k][:], in_=self.x_in[tb * 128:(tb + 1) * 128, :]), writes=[bx[k]])
                self.emit_transpose_store(xt[k], bx[k], self.xTf[:, tb * 128:(tb + 1) * 128], self.b_xTf, xtt[k], bxtt[k], (2 * k, 2 * k + 1))
            for lb in range(self.SO // 128):
                k = n % 2
                n += 1
                fw.ld.op(lambda: nc.sync.dma_start(out=xt[k][:], in_=self.x_own[lb * 128:(lb + 1) * 128, :]), writes=[bx[k]])
                self.emit_transpose_store(xt[k], bx[k], self.xto_dst(0, lb), self.b_xTo[0], xtt[k], bxtt[k], (2 * k, 2 * k + 1))
            fw.barrier()

    def phase_a(self, li, m):
        nc, fw, S = self.nc, self.fw, self.S
        width = A_IN if m == 0 else 4 * D
        rope = m in (0, 2)
        rot_blocks = []
        if rope:
            rot_blocks = [(0, 16), (D, 16)]
            if m == 0:
                rot_blocks += [(4 * D, 8), (4 * D + 520, 1)]
        nrot = sum(nh for _, nh in rot_blocks) * 64
        sset = li % 2
        with ExitStack() as es:
            sb = self.sbf(es)
            WIN = sb("WIN", [128, 8, width], BF16); b_win = Buf()
            WROT = sb("WROT", [128, 8, max(nrot, 64)], BF16)
            stg0 = sb("wstg0", [128, width], F32)
            stg = [stg0, stg0]
            bstg0 = Buf()
            bstg = [bstg0, bstg0]
            rot_off = {}
            off = 0
            for col0, nh in rot_blocks:
                rot_off[col0] = off
                off += nh * 64
            for c in range(8):
                k = c % 2
                fw.ld.op(lambda: nc.sync.dma_start(out=stg[k][:], in_=self.w_in[li][c * 128:(c + 1) * 128, :]), writes=[bstg[k]])
                h2 = width // 2
                fw.act.op(lambda: nc.scalar.copy(WIN[:, c, 0:h2], stg[k][:, 0:h2]), reads=[bstg[k]], writes=[b_win])
                fw.dve.op(lambda: nc.vector.tensor_copy(WIN[:, c, h2:width], stg[k][:, h2:width]), reads=[bstg[k]], writes=[b_win])
                for col0, nh in rot_blocks:
                    ro = rot_off[col0]
                    src = stg[k][:, col0:col0 + nh * 64].rearrange("p (h t i) -> p h t i", t=2, i=32)
                    dst = WROT[:, c, ro:ro + nh * 64].rearrange("p (h t i) -> p h t i", t=2, i=32)
                    fw.pool.op(lambda: nc.gpsimd.tensor_copy(dst[:, :, 0, :], src[:, :, 1, :]), reads=[bstg[k]], writes=[b_win])
                    fw.pool.op(lambda: nc.gpsimd.tensor_copy(dst[:, :, 1, :], src[:, :, 0, :]), reads=[bstg[k]], writes=[b_win])
            kgroups = [(self.KT, self.b_KT, g * 128, D + g * 128, 128, rope, None) for g in range(8)]
            if m == 0:
                kgroups.append((self.KT, self.b_KT, D, 4 * D + 520, 64, True, None))
            qgroups = [(self.QT, self.b_QT, g * 128, g * 128, 128, rope, None) for g in range(8)]
            qgroups += [(self.GT, self.b_GT, g * 128, 3 * D + g * 128, 128, False, "silu") for g in range(8)]
            if m == 0:
                qgroups += [(self.QT, self.b_QT, D + g * 128, 4 * D + g * 128, 128, True, None) for g in range(4)]

            def rotcol(col):
                for col0, nh in rot_blocks:
                    if col0 <= col < col0 + nh * 64:
                        return rot_off[col0] + (col - col0)
                raise AssertionError

            XT = [sb(f"pa_xt{i}", [128, 8, 512], BF16) for i in range(2)]; bXT = [Buf(), Buf()]
            XO = [sb(f"pa_xo{i}", [128, 8, TW], BF16) for i in range(2)]; bXO = [Buf(), Buf()]
            CCt = [sb(f"pa_cc{i}", [128, 512], F32) for i in range(2)]; bCC = [Buf(), Buf()]
            SSt = [sb(f"pa_ss{i}", [128, 512], F32) for i in range(2)]; bSS = [Buf(), Buf()]
            CCq = [sb(f"pa_ccq{i}", [128, TW], F32) for i in range(2)]; bCCq = [Buf(), Buf()]
            SSq = [sb(f"pa_ssq{i}", [128, TW], F32) for i in range(2)]; bSSq = [Buf(), Buf()]
            T1 = [sb(f"pa_t1{i}", [128, 512], F32) for i in range(2)]; bT1 = [Buf(), Buf()]
            T2 = [sb(f"pa_t2{i}", [128, 512], F32) for i in range(2)]; bT2 = [Buf(), Buf()]
            OB = [sb(f"pa_ob{i}", [128, 512], BF16) for i in range(3)]; bOB = [Buf() for _ in range(3)]
            VT0 = sb("pa_vt0", [128, 4, D], BF16); VT = [VT0, VT0]; bVT0 = Buf(); bVT = [bVT0, bVT0]
            WIt = [sb(f"pa_wi{i}", [128, 8], F32) for i in range(2)]; bWI = [Buf(), Buf()]
            gi = [0]

            def do_group(grp, xin, bxin, N, t0, cc, bcc, ss, bss):
                (dst, bdst, row0, col0, M, rp, act) = grp
                pb = (gi[0] % 2) * 2
                ob = gi[0] % 3
                tk = gi[0] % 2
                gi[0] += 1
                for c in range(8):
                    fw.pe.op(lambda c=c: nc.tensor.matmul(self.ps(pb, M, N), lhsT=WIN[:, c, col0:col0 + M], rhs=xin[:, c, :],
                                                          start=(c == 0), stop=(c == 7)),
                             reads=[b_win, bxin], writes=[self.bps[pb]], inc=(c == 7))
                if rp:
                    rc0 = rotcol(col0)
                    for c in range(8):
                        fw.pe.op(lambda c=c: nc.tensor.matmul(self.ps(pb + 1, M, N), lhsT=WROT[:, c, rc0:rc0 + M], rhs=xin[:, c, :],
                                                              start=(c == 0), stop=(c == 7)),
                                 reads=[b_win, bxin], writes=[self.bps[pb + 1]], inc=(c == 7))
                    fw.dve.op(lambda: nc.vector.tensor_tensor(out=T1[tk][0:M, 0:N], in0=self.ps(pb, M, N), in1=cc[0:M, :], op=ALU.mult),
                              reads=[self.bps[pb], bcc], writes=[bT1[tk]])
                    fw.dve.op(lambda: nc.vector.tensor_tensor(out=T2[tk][0:M, 0:N], in0=self.ps(pb + 1, M, N), in1=ss[0:M, :], op=ALU.mult),
                              reads=[self.bps[pb + 1], bss], writes=[bT2[tk]])
                    fw.pool.op(lambda: nc.gpsimd.tensor_tensor(out=OB[ob][0:M, 0:N], in0=T1[tk][0:M, 0:N], in1=T2[tk][0:M, 0:N], op=ALU.add),
                               reads=[bT1[tk], bT2[tk]], writes=[bOB[ob]])
                elif act == "silu":
                    fw.act.op(lambda: nc.scalar.activation(out=OB[ob][0:M, 0:N], in_=self.ps(pb, M, N), func=AF.Silu),
                              reads=[self.bps[pb]], writes=[bOB[ob]])
                else:
                    fw.act.op(lambda: nc.scalar.copy(OB[ob][0:M, 0:N], self.ps(pb, M, N)), reads=[self.bps[pb]], writes=[bOB[ob]])
                fw.st.op(lambda: nc.gpsimd.dma_start(out=dst[row0:row0 + M, t0:t0 + N], in_=OB[ob][0:M, 0:N]),
                         reads=[bOB[ob]], writes=[bdst])

            for tt in range(self.NT):
                k = tt % 2
                t0 = tt * 512
                for c in range(4):
                    src, bsrc = self.xt_all_src(li, tt, c)
                    fw.ld.op(lambda: nc.sync.dma_start(out=XT[k][:, :, c * 128:(c + 1) * 128], in_=src.rearrange("(c p) t -> p c t", p=128)),
                             reads=[bsrc], writes=[bXT[k]])
                l0 = tt * TW
                ch, col = divmod(l0, self.CW)
                fw.ld.op(lambda: nc.sync.dma_start(out=XO[k][:], in_=self.xTo[sset][ch].ap()[:, col:col + TW].rearrange("(c p) t -> p c t", p=128)),
                         reads=[self.b_xTo[sset]], writes=[bXO[k]])
                if rope:
                    fw.ld.op(lambda: nc.sync.dma_start(out=CCt[k][:], in_=self.CC[:, t0:t0 + 512]), reads=[self.b_tab], writes=[bCC[k]])
                    fw.ld.op(lambda: nc.sync.dma_start(out=SSt[k][:], in_=self.SS[:, t0:t0 + 512]), reads=[self.b_tab], writes=[bSS[k]])
                    fw.ld.op(lambda: nc.sync.dma_start(out=CCq[k][:], in_=self.CCo[:, l0:l0 + TW]), reads=[self.b_tab], writes=[bCCq[k]])
                    fw.ld.op(lambda: nc.sync.dma_start(out=SSq[k][:], in_=self.SSo[:, l0:l0 + TW]), reads=[self.b_tab], writes=[bSSq[k]])
                for grp in kgroups:
                    do_group(grp, XT[k], bXT[k], 512, t0, CCt[k], bCC[k], SSt[k], bSS[k])
                for grp in qgroups:
                    do_group(grp, XO[k], bXO[k], TW, l0, CCq[k], bCCq[k], SSq[k], bSSq[k])
                for tb in range(4):
                    for half in range(2):
                        pb = 4 + half
                        for c in range(8):
                            fw.pe.op(lambda c=c: nc.tensor.matmul(self.ps(pb), lhsT=XT[k][:, c, tb * 128:(tb + 1) * 128],
                                                                  rhs=WIN[:, c, 2 * D + half * 512:2 * D + (half + 1) * 512],
                                                                  start=(c == 0), stop=(c == 7)),
                                     reads=[b_win, bXT[k]], writes=[self.bps[pb]], inc=(c == 7))
                        if half == 0:
                            fw.act.op(lambda: nc.scalar.copy(VT[k][:, tb, 0:512], self.ps(pb)), reads=[self.bps[pb]], writes=[bVT[k]])
                        else:
                            fw.dve.op(lambda: nc.vector.tensor_copy(VT[k][:, tb, 512:1024], self.ps(pb)), reads=[self.bps[pb]], writes=[bVT[k]])
                if m == 0:
                    for tb in range(2):
                        pb = 6
                        wk = (tt * 2 + tb) % 2
                        for c in range(8):
                            fw.pe.op(lambda c=c: nc.tensor.matmul(self.ps(pb, 128, 8), lhsT=XO[k][:, c, tb * 128:(tb + 1) * 128],
                                                                  rhs=WIN[:, c, 4 * D + 512:4 * D + 520], start=(c == 0), stop=(c == 7)),
                                     reads=[b_win, bXO[k]], writes=[self.bps[pb]], inc=(c == 7))
                        fw.dve.op(lambda: nc.vector.tensor_scalar(out=WIt[wk][:], in0=self.ps(pb, 128, 8), scalar1=float(8 ** -0.5 * 64 ** -0.5),
                                                                  scalar2=None, op0=ALU.mult), reads=[self.bps[pb]], writes=[bWI[wk]])
                        r0 = l0 + tb * 128
                        fw.st.op(lambda: nc.gpsimd.dma_start(out=self.WI[r0:r0 + 128, :], in_=WIt[wk][:]), reads=[bWI[wk]], writes=[self.b_WI])
                for h in range(NH):
                    fw.st.op(lambda h=h: nc.gpsimd.dma_start(out=self.Vs[h, :, tt * 4:(tt + 1) * 4, :], in_=VT[k][:, :, h * 64:(h + 1) * 64]),
                             reads=[bVT[k]], writes=[self.b_Vs])
            fw.barrier()

    def phase_c(self, li, cur, last):
        nc, fw = self.nc, self.fw
        first = li == 0
        src = self.x_own if first else self.xres[cur]
        bsrc = Buf() if first else self.b_xres[cur]
        dst = self.out if last else self.xres[cur ^ 1]
        bdst = Buf() if last else self.b_xres[cur ^ 1]
        wset = (li + 1) % 2
        with ExitStack() as es:
            sb = self.sbf(es)
            WO = sb("WO", [128, 8, D], BF16); b_wo = Buf()
            stg = [sb(f"wostg{i}", [128, D], F32) for i in range(2)]; bstg = [Buf(), Buf()]
            for c in range(8):
                k = c % 2
                fw.ld.op(lambda: nc.sync.dma_start(out=stg[k][:], in_=self.w_out[li][c * 128:(c + 1) * 128, :]), writes=[bstg[k]])
                fw.act.op(lambda: nc.scalar.copy(WO[:, c, :], stg[k][:]), reads=[bstg[k]], writes=[b_wo])
            G = sb("lnG", [128, D], F32); Bt = sb("lnB", [128, D], F32); b_gb = Buf()
            fw.ld.op(lambda: nc.sync.dma_start(out=G[:], in_=self.lng[li:li + 1, :].broadcast_to([128, D])), writes=[b_gb])
            fw.ld.op(lambda: nc.sync.dma_start(out=Bt[:], in_=self.lnb[li:li + 1, :].broadcast_to([128, D])), writes=[b_gb])
            OTt = [sb(f"pc_ot{i}", [128, 8, 512], BF16) for i in range(2)]; bOT = [Buf(), Buf()]
            XR = [sb(f"pc_x{i}", [128, D], F32) for i in range(2)]; bXR = [Buf(), Buf()]
            U = [sb(f"pc_u{i}", [128, D], F32) for i in range(2)]; bU = [Buf(), Buf()]
            XN = [sb(f"pc_xn{i}", [128, D], F32) for i in range(2)]; bXN = [Buf(), Buf()]
            SQ = sb("pc_sq", [128, D], F32); bSQ = Buf()
            ST = [sb(f"pc_st{i}", [128, 8], F32) for i in range(2)]; bST = [Buf(), Buf()]
            XTT = [sb(f"pc_xtt{i}", [128, 8, 128], BF16) for i in range(2)]; bXTT = [Buf(), Buf()]
            for tt in range(self.SO // 512):
                kk = tt % 2
                t0 = tt * 512
                fw.ld.op(lambda: nc.sync.dma_start(out=OTt[kk][:], in_=self.OT[:, t0:t0 + 512].rearrange("(c p) t -> p c t", p=128)),
                         reads=[self.b_OT], writes=[bOT[kk]])
                for tb4 in range(4):
                    tb = tt * 4 + tb4
                    k = tb % 2
                    fw.ld.op(lambda: nc.sync.dma_start(out=XR[k][:], in_=src[tb * 128:(tb + 1) * 128, :]), reads=[bsrc], writes=[bXR[k]])
                    pbs = (4 * k, 4 * k + 1)
                    for half in range(2):
                        for c in range(8):
                            fw.pe.op(lambda c=c, half=half: nc.tensor.matmul(self.ps(pbs[half]), lhsT=OTt[kk][:, c, tb4 * 128:(tb4 + 1) * 128],
                                                                             rhs=WO[:, c, half * 512:(half + 1) * 512], start=(c == 0), stop=(c == 7)),
                                     reads=[b_wo, bOT[kk]], writes=[self.bps[pbs[half]]], inc=(c == 7))
                    st = ST[k]
                    for half in range(2):
                        hs = slice(half * 512, (half + 1) * 512)
                        fw.dve.op(lambda half=half, hs=hs: nc.vector.scalar_tensor_tensor(out=U[k][:, hs], in0=XR[k][:, hs], scalar=float(ALPHA),
                                                                                          in1=self.ps(pbs[half]), op0=ALU.mult, op1=ALU.add),
                                  reads=[bXR[k], self.bps[pbs[half]]], writes=[bU[k]])
                    fw.act.op(lambda: nc.scalar.activation(out=SQ[:], in_=U[k][:], func=AF.Copy, accum_out=st[:, 0:1]), reads=[bU[k]], writes=[bSQ, bST[k]])
                    fw.act.op(lambda: nc.scalar.activation(out=SQ[:], in_=U[k][:], func=AF.Square, accum_out=st[:, 1:2]), reads=[bU[k]], writes=[bSQ, bST[k]])
                    fw.dve.op(lambda: nc.vector.tensor_scalar(out=st[:, 2:3], in0=st[:, 0:1], scalar1=float(1.0 / D), scalar2=None, op0=ALU.mult),
                              reads=[bST[k]], writes=[bST[k]])
                    fw.dve.op(lambda: nc.vector.tensor_tensor(out=st[:, 3:4], in0=st[:, 2:3], in1=st[:, 2:3], op=ALU.mult), reads=[bST[k]], writes=[bST[k]])
                    fw.dve.op(lambda: nc.vector.scalar_tensor_tensor(out=st[:, 4:5], in0=st[:, 1:2], scalar=float(1.0 / D), in1=st[:, 3:4],
                                                                     op0=ALU.mult, op1=ALU.subtract), reads=[bST[k]], writes=[bST[k]])
                    fw.act.op(lambda: nc.scalar.activation(out=st[:, 5:6], in_=st[:, 4:5], func=AF.Sqrt, bias=self.cst[:, 0:1], scale=1.0),
                              reads=[bST[k], self.b_cst], writes=[bST[k]])
                    fw.dve.op(lambda: nc.vector.reciprocal(st[:, 5:6], st[:, 5:6]), reads=[bST[k]], writes=[bST[k]])
                    fw.dve.op(lambda: nc.vector.tensor_scalar(out=XN[k][:], in0=U[k][:], scalar1=st[:, 2:3], scalar2=st[:, 5:6], op0=ALU.subtract, op1=ALU.mult),
                              reads=[bU[k], bST[k]], writes=[bXN[k]])
                    fw.pool.op(lambda: nc.gpsimd.tensor_tensor(out=XN[k][:], in0=XN[k][:], in1=G[:], op=ALU.mult), reads=[bXN[k], b_gb], writes=[bXN[k]])
                    fw.pool.op(lambda: nc.gpsimd.tensor_tensor(out=XN[k][:], in0=XN[k][:], in1=Bt[:], op=ALU.add), reads=[bXN[k], b_gb], writes=[bXN[k]])
                    fw.st.op(lambda: nc.gpsimd.dma_start(out=dst[tb * 128:(tb + 1) * 128, :], in_=XN[k][:]), reads=[bXN[k]], writes=[bdst])
                    if not last:
                        self.emit_transpose_store(XN[k], bXN[k], self.xto_dst(wset, tb), self.b_xTo[wset], XTT[k], bXTT[k], (4 * k + 2, 4 * k + 3))
            fw.barrier()

    def build_masks(self, sb, kind):
        nc, fw = self.nc, self.fw
        Ms = [sb(f"mask_{kind}{j}", [128, TW2], BF16) for j in range(8)]
        b = Buf()
        for j in range(8):
            if kind == "before":
                fw.dve.op(lambda j=j: nc.vector.tensor_scalar(out=Ms[j][:], in0=self.QR2[:], scalar1=float(-128 * j), scalar2=self.cst[:, 1:2],
                                                              op0=ALU.add, op1=ALU.is_gt), reads=[self.b_cst], writes=[b])
            else:
                fw.dve.op(lambda j=j: nc.vector.tensor_scalar(out=Ms[j][:], in0=self.QR2[:], scalar1=float(-128 * j), scalar2=self.cst[:, 2:3],
                                                              op0=ALU.add, op1=ALU.is_ge), reads=[self.b_cst], writes=[b])
        return Ms, b

    def attn_diff(self, L):
        nc, fw, S, NB, NT = self.nc, self.fw, self.S, self.NB, self.NT
        lambda_init = 0.8 - 0.6 * math.exp(-0.3 * L)
        with ExitStack() as es:
            sb = self.sbf(es)
            lv = sb("lv", [128, 4, 64], F32); b_lv = Buf()
            fw.ld.op(lambda: nc.sync.dma_start(out=lv[:].rearrange("p a b -> p (a b)"), in_=self.lam[0:1, :].broadcast_to([128, 256])), writes=[b_lv])
            lt = sb("lt", [128, 8], F32); b_lt = Buf()
            pr = sb("lpr", [128, 2, 64], F32)
            fw.dve.op(lambda: nc.vector.tensor_tensor(out=pr[:, 0, :], in0=lv[:, 0, :], in1=lv[:, 1, :], op=ALU.mult), reads=[b_lv], writes=[b_lt])
            fw.dve.op(lambda: nc.vector.tensor_tensor(out=pr[:, 1, :], in0=lv[:, 2, :], in1=lv[:, 3, :], op=ALU.mult), reads=[b_lv], writes=[b_lt])
            fw.dve.op(lambda: nc.vector.reduce_sum(out=lt[:, 0:1], in_=pr[:, 0, :], axis=mybir.AxisListType.X), reads=[b_lt], writes=[b_lt])
            fw.dve.op(lambda: nc.vector.reduce_sum(out=lt[:, 1:2], in_=pr[:, 1, :], axis=mybir.AxisListType.X), reads=[b_lt], writes=[b_lt])
            fw.act.op(lambda: nc.scalar.activation(out=lt[:, 2:4], in_=lt[:, 0:2], func=AF.Exp), reads=[b_lt], writes=[b_lt])
            fw.dve.op(lambda: nc.vector.tensor_tensor(out=lt[:, 4:5], in0=lt[:, 3:4], in1=lt[:, 2:3], op=ALU.subtract), reads=[b_lt], writes=[b_lt])
            fw.dve.op(lambda: nc.vector.tensor_scalar(out=lt[:, 5:6], in0=lt[:, 4:5], scalar1=float(-lambda_init), scalar2=None, op0=ALU.add),
                      reads=[b_lt], writes=[b_lt])
            sg = sb("sg", [128, 2], F32); b_sg = Buf()
            fw.ld.op(lambda: nc.sync.dma_start(out=sg[:, 0:1], in_=self.subg[:, :]), writes=[b_sg])
            fw.dve.op(lambda: nc.vector.tensor_scalar(out=sg[:, 1:2], in0=sg[:, 0:1], scalar1=float(1.0 - lambda_init), scalar2=None, op0=ALU.mult),
                      reads=[b_sg], writes=[b_sg])
            ones_b = sb("ones_b", [128, 128], BF16); ones_f = sb("ones_f", [128, 128], F32); b_ones = Buf()
            fw.pool.op(lambda: nc.gpsimd.memset(ones_b[:], 1.0), writes=[b_ones])
            fw.pool.op(lambda: nc.gpsimd.memset(ones_f[:], 1.0), writes=[b_ones])
            CM, b_cm = self.build_masks(sb, "chunk")
            KTt = [sb(f"df_k{i}", [64, 2, S], BF16) for i in range(2)]; bK = [Buf(), Buf()]
            Vt = [sb(f"df_v{i}", [128, NB, 128], BF16) for i in range(2)]; bV = [Buf(), Buf()]
            Qt = [sb(f"df_q{i}", [64, 2, TW2], BF16) for i in range(3)]; bQ = [Buf() for _ in range(3)]
            Gt = [sb(f"df_g{i}", [128, TW2], BF16) for i in range(3)]; bG = [Buf() for _ in range(3)]
            P = [sb(f"df_p{i}", [128, TW2], BF16) for i in range(10)]; bP = [Buf() for _ in range(10)]
            PF = [sb(f"df_pf{i}", [128, TW2], F32) for i in range(2)]; bPF = [Buf(), Buf()]
            R0 = sb("df_r0", [128, TW2], F32); R1 = sb("df_r1", [128, TW2], F32); bR = [Buf(), Buf()]
            O0 = sb("df_o0", [128, TW2], F32); O1 = sb("df_o1", [128, TW2], F32); bO = [Buf(), Buf()]
            SQ = sb("df_sq", [128, TW2], F32); bSQ = Buf()
            RS = sb("df_rs", [128, TW2], F32); bRS = Buf()
            OG = [sb(f"df_og{i}", [128, TW2], BF16) for i in range(2)]; bOG = [Buf(), Buf()]
            N = TW2
            sctr = [0]
            pctr = [0]
            items = []
            loaders = []
            for hd in range(8):
                hk = hd % 2

                def load_head(hd=hd, hk=hk):
                    for mm in range(2):
                        fw.ld.op(lambda mm=mm: nc.sync.dma_start(out=KTt[hk][:, mm, :], in_=self.KT[(2 * hd + mm) * 64:(2 * hd + mm + 1) * 64, :]),
                                 reads=[self.b_KT], writes=[bK[hk]])
                        fw.ld.op(lambda mm=mm: nc.sync.dma_start(out=Vt[hk][:, :, mm * 64:(mm + 1) * 64], in_=self.Vs[2 * hd + mm]),
                                 reads=[self.b_Vs], writes=[bV[hk]])
                for tt in range(NT // 2):
                    qk = (hd * (NT // 2) + tt) % 3
                    t0 = tt * TW2
                    nkb = 8 * tt + 8

                    def load_q(hd=hd, qk=qk, t0=t0, tt=tt, hk=hk, lh=load_head):
                        if tt == 0:
                            lh()
                        for mm in range(2):
                            fw.ld.op(lambda mm=mm: nc.sync.dma_start(out=Qt[qk][:, mm, :], in_=self.QT[(2 * hd + mm) * 64:(2 * hd + mm + 1) * 64, t0:t0 + TW2]),
                                     reads=[self.b_QT], writes=[bQ[qk]])
                        fw.ld.op(lambda: nc.sync.dma_start(out=Gt[qk][:], in_=self.GT[hd * 128:(hd + 1) * 128, t0:t0 + TW2]),
                                 reads=[self.b_GT], writes=[bG[qk]])
                    for kb in range(nkb):
                        j = kb - 8 * tt
                        sbk = []
                        pidx = []
                        for mm in range(2):
                            sbk.append(sctr[0] % 3); sctr[0] += 1
                            pidx.append(pctr[0] % 10); pctr[0] += 1
                        if kb == 0:
                            loaders.append(load_q)
                        tidx = len(loaders) - 1

                        def st0(hk=hk, qk=qk, kb=kb, sbk=sbk, first=(kb == 0), tidx=tidx):
                            if first:
                                if tidx == 0:
                                    loaders[0]()
                                if tidx + 1 < len(loaders):
                                    loaders[tidx + 1]()
                            for mm in range(2):
                                fw.pe.op(lambda mm=mm: nc.tensor.matmul(self.ps(sbk[mm], 128, N), lhsT=KTt[hk][:, mm, kb * 128:(kb + 1) * 128], rhs=Qt[qk][:, mm, :],
                                                                        start=True, stop=True),
                                         reads=[bK[hk], bQ[qk]], writes=[self.bps[sbk[mm]]], inc=(mm == 1))

                        def st1(sbk=sbk, pidx=pidx, j=j):
                            for mm in range(2):
                                if j >= 0:
                                    fw.act.op(lambda mm=mm: nc.scalar.activation(out=PF[mm][:], in_=self.ps(sbk[mm], 128, N), func=AF.Exp, scale=0.125),
                                              reads=[self.bps[sbk[mm]]], writes=[bPF[mm]])
                                    fw.dve.op(lambda mm=mm: nc.vector.tensor_tensor(out=P[pidx[mm]][:], in0=PF[mm][:], in1=CM[j][:], op=ALU.mult),
                                              reads=[bPF[mm], b_cm], writes=[bP[pidx[mm]]])
                                else:
                                    fw.act.op(lambda mm=mm: nc.scalar.activation(out=P[pidx[mm]][:], in_=self.ps(sbk[mm], 128, N), func=AF.Exp, scale=0.125),
                                              reads=[self.bps[sbk[mm]]], writes=[bP[pidx[mm]]])

                        def st2(hk=hk, kb=kb, pidx=pidx, nkb=nkb, hd=hd, tt=tt, qk=qk, t0=t0):
                            for mm in range(2):
                                fw.pe.op(lambda mm=mm: nc.tensor.matmul(self.ps(3 + mm, 128, N), lhsT=Vt[hk][:, kb, :], rhs=P[pidx[mm]][:],
                                                                        start=(kb == 0), stop=(kb == nkb - 1)),
                                         reads=[bV[hk], bP[pidx[mm]]], writes=[self.bps[3 + mm]], inc=False)
                                fw.pe.op(lambda mm=mm: nc.tensor.matmul(self.ps(5 + mm, 128, N), lhsT=ones_b[:], rhs=P[pidx[mm]][:],
                                                                        start=(kb == 0), stop=(kb == nkb - 1)),
                                         reads=[b_ones, bP[pidx[mm]]], writes=[self.bps[5 + mm]], inc=(mm == 1))
                            if kb == nkb - 1:
                                ok = (hd * (NT // 2) + tt) % 2
                                fw.dve.op(lambda: nc.vector.reciprocal(R0[:], self.ps(5, 128, N)), reads=[self.bps[5]], writes=[bR[0]])
                                fw.dve.op(lambda: nc.vector.reciprocal(R1[:], self.ps(6, 128, N)), reads=[self.bps[6]], writes=[bR[1]])
                                fw.dve.op(lambda: nc.vector.tensor_tensor(out=O0[:], in0=self.ps(3, 128, N), in1=R0[:], op=ALU.mult), reads=[self.bps[3], bR[0]], writes=[bO[0]])
                                fw.dve.op(lambda: nc.vector.tensor_tensor(out=O1[:], in0=self.ps(4, 128, N), in1=R1[:], op=ALU.mult), reads=[self.bps[4], bR[1]], writes=[bO[1]])
                                fw.dve.op(lambda: nc.vector.scalar_tensor_tensor(out=O0[:], in0=O1[:], scalar=lt[:, 5:6], in1=O0[:], op0=ALU.mult, op1=ALU.add),
                                          reads=[bO[0], bO[1], b_lt], writes=[bO[0]])
                                fw.act.op(lambda: nc.scalar.activation(out=SQ[:], in_=O0[:], func=AF.Square), reads=[bO[0]], writes=[bSQ])
                                fw.pe.op(lambda: nc.tensor.matmul(self.ps(7, 128, N), lhsT=ones_f[:], rhs=SQ[:], start=True, stop=True),
                                         reads=[b_ones, bSQ], writes=[self.bps[7]])
                                fw.act.op(lambda: nc.scalar.activation(out=RS[:], in_=self.ps(7, 128, N), func=AF.Sqrt, bias=self.cst[:, 0:1], scale=float(1.0 / 128)),
                                          reads=[self.bps[7], self.b_cst], writes=[bRS])
                                fw.dve.op(lambda: nc.vector.reciprocal(RS[:], RS[:]), reads=[bRS], writes=[bRS])
                                fw.dve.op(lambda: nc.vector.scalar_tensor_tensor(out=O0[:], in0=O0[:], scalar=sg[:, 1:2], in1=RS[:], op0=ALU.mult, op1=ALU.mult),
                                          reads=[bO[0], bRS, b_sg], writes=[bO[0]])
                                fw.pool.op(lambda: nc.gpsimd.tensor_tensor(out=OG[ok][:], in0=O0[:], in1=Gt[qk][:], op=ALU.mult), reads=[bO[0], bG[qk]], writes=[bOG[ok]])
                                fw.st.op(lambda: nc.gpsimd.dma_start(out=self.OT[hd * 128:(hd + 1) * 128, t0:t0 + TW2], in_=OG[ok][:]),
                                         reads=[bOG[ok]], writes=[self.b_OT])
                        items.append([st0, st1, None, None, st2])
            pipeline(items, 5)
            fw.barrier()

    def attn_sb(self):
        nc, fw, S, NB, NT = self.nc, self.fw, self.S, self.NB, self.NT
        N = TW2
        with ExitStack() as es:
            sb = self.sbf(es)
            b_c = Buf()
            BEF, b_bef = self.build_masks(sb, "before")
            n8 = sb("sb_n8", [128, 128], BF16); tri = sb("sb_tri", [128, 128], BF16)
            fw.pool.op(lambda: nc.gpsimd.memset(n8[:], -8.0), writes=[b_c])
            fw.pool.op(lambda: nc.gpsimd.affine_select(out=tri[:], in_=n8[:], pattern=[[-1, 128]], compare_op=ALU.is_ge, fill=0.0,
                                                       base=0, channel_multiplier=1), reads=[b_c], writes=[b_c])
            one1 = self.cst[:, 3:4]
            KTt = [sb(f"sb_k{i}", [64, S], BF16) for i in range(2)]; bK = [Buf(), Buf()]
            Vt = [sb(f"sb_v{i}", [128, NB, 64], BF16) for i in range(2)]; bV = [Buf(), Buf()]
            Qt = [sb(f"sb_q{i}", [64, N], BF16) for i in range(4)]; bQ = [Buf() for _ in range(4)]
            Gt = [sb(f"sb_g{i}", [64, N], BF16) for i in range(4)]; bG = [Buf() for _ in range(4)]
            E1 = [sb(f"sb_e{i}", [128, N], F32) for i in range(2)]; bE = [Buf(), Buf()]
            SPf = sb("sb_spf", [128, N], F32); bSPf = Buf()
            SP = [sb(f"sb_sp{i}", [128, N], BF16) for i in range(6)]; bSP = [Buf() for _ in range(6)]
            AC = [sb(f"sb_ac{i}", [128, N], BF16) for i in range(6)]; bAC = [Buf() for _ in range(6)]
            A = [sb(f"sb_a{i}", [128, N], BF16) for i in range(6)]; bA = [Buf() for _ in range(6)]
            Af = sb("sb_af", [128, N], F32); bAf = Buf()
            OG = [sb(f"sb_og{i}", [64, N], BF16) for i in range(2)]; bOG = [Buf(), Buf()]
            items = []
            loaders = []
            ic = 0
            tile_i = 0
            for h in range(NH):
                hk = h % 2

                def load_head(h=h, hk=hk):
                    fw.ld.op(lambda: nc.sync.dma_start(out=KTt[hk][:], in_=self.KT[h * 64:(h + 1) * 64, :]), reads=[self.b_KT], writes=[bK[hk]])
                    fw.ld.op(lambda: nc.sync.dma_start(out=Vt[hk][:], in_=self.Vs[h]), reads=[self.b_Vs], writes=[bV[hk]])
                for tt in range(NT // 2):
                    qk = tile_i % 4
                    ob = 5 + tile_i % 2
                    ogk = tile_i % 2
                    tile_i += 1
                    t0 = tt * TW2
                    hi = 8 * tt + 7
                    lo = max(0, 8 * tt - SB_WIN)
                    kbs = list(range(hi, lo - 1, -1))

                    def load_q(h=h, qk=qk, t0=t0, tt=tt, lh=load_head):
                        if tt == 0:
                            lh()
                        fw.ld.op(lambda: nc.sync.dma_start(out=Qt[qk][:], in_=self.QT[h * 64:(h + 1) * 64, t0:t0 + N]), reads=[self.b_QT], writes=[bQ[qk]])
                        fw.ld.op(lambda: nc.sync.dma_start(out=Gt[qk][:], in_=self.GT[h * 64:(h + 1) * 64, t0:t0 + N]), reads=[self.b_GT], writes=[bG[qk]])
                    loaders.append(load_q)
                    tidx = len(loaders) - 1
                    prev_acc = None
                    for n, kb in enumerate(kbs):
                        j = kb - 8 * tt
                        diag = j >= 0
                        sbank = ic % 5
                        spk = ic % 6
                        ek = ic % 2
                        ic += 1
                        first = n == 0
                        lastk = n == len(kbs) - 1
                        pacc = prev_acc
                        acc_out = (SP[spk], bSP[spk]) if first else (AC[spk], bAC[spk])
                        prev_acc = acc_out

                        def st0(hk=hk, qk=qk, kb=kb, sbank=sbank, first=first, tidx=tidx):
                            if first:
                                if tidx == 0:
                                    loaders[0]()
                                if tidx + 1 < len(loaders):
                                    loaders[tidx + 1]()
                            fw.pe.op(lambda: nc.tensor.matmul(self.ps(sbank, 128, N), lhsT=KTt[hk][:, kb * 128:(kb + 1) * 128], rhs=Qt[qk][:], start=True, stop=False),
                                     reads=[bK[hk], bQ[qk]], writes=[self.bps[sbank]])

                        def st1(sbank=sbank, spk=spk, ek=ek, diag=diag, j=j, first=first, pacc=pacc, acc_out=acc_out):
                            fw.act.op(lambda: nc.scalar.activation(out=E1[ek][:], in_=self.ps(sbank, 128, N), func=AF.Exp, scale=0.125),
                                      reads=[self.bps[sbank]], writes=[bE[ek]])
                            if diag:
                                fw.act.op(lambda: nc.scalar.activation(out=SPf[:], in_=E1[ek][:], func=AF.Ln, bias=one1, scale=1.0),
                                          reads=[bE[ek], self.b_cst], writes=[bSPf])
                                fw.dve.op(lambda: nc.vector.tensor_tensor(out=SP[spk][:], in0=SPf[:], in1=BEF[j][:], op=ALU.mult),
                                          reads=[bSPf, b_bef], writes=[bSP[spk]])
                            else:
                                fw.act.op(lambda: nc.scalar.activation(out=SP[spk][:], in_=E1[ek][:], func=AF.Ln, bias=one1, scale=1.0),
                                          reads=[bE[ek], self.b_cst], writes=[bSP[spk]])
                            if not first:
                                fw.pool.op(lambda: nc.gpsimd.tensor_tensor(out=acc_out[0][:], in0=pacc[0][:], in1=SP[spk][:], op=ALU.add),
                                           reads=[pacc[1], bSP[spk]], writes=[acc_out[1]])

                        def st2(sbank=sbank, spk=spk, first=first, pacc=pacc):
                            fw.pe.op(lambda: nc.tensor.matmul(self.ps(sbank, 128, N), lhsT=tri[:], rhs=SP[spk][:], start=False, stop=first),
                                     reads=[b_c, bSP[spk]], writes=[self.bps[sbank]], inc=first)
                            if not first:
                                fw.pe.op(lambda: nc.tensor.matmul(self.ps(sbank, 128, N), lhsT=n8[:], rhs=pacc[0][:], start=False, stop=True),
                                         reads=[b_c, pacc[1]], writes=[self.bps[sbank]])

                        def st3(sbank=sbank, spk=spk, diag=diag, j=j):
                            if diag:
                                fw.act.op(lambda: nc.scalar.activation(out=Af[:], in_=self.ps(sbank, 128, N), func=AF.Exp, scale=0.125),
                                          reads=[self.bps[sbank]], writes=[bAf])
                                fw.dve.op(lambda: nc.vector.tensor_tensor(out=A[spk][:], in0=Af[:], in1=BEF[j][:], op=ALU.mult),
                                          reads=[bAf, b_bef], writes=[bA[spk]])
                            else:
                                fw.act.op(lambda: nc.scalar.activation(out=A[spk][:], in_=self.ps(sbank, 128, N), func=AF.Exp, scale=0.125),
                                          reads=[self.bps[sbank]], writes=[bA[spk]])

                        def st4(hk=hk, kb=kb, spk=spk, first=first, lastk=lastk, ob=ob, ogk=ogk, qk=qk, h=h, t0=t0):
                            fw.pe.op(lambda: nc.tensor.matmul(self.ps(ob, 64, N), lhsT=Vt[hk][:, kb, :], rhs=A[spk][:], start=first, stop=lastk),
                                     reads=[bV[hk], bA[spk]], writes=[self.bps[ob]])
                            if lastk:
                                fw.dve.op(lambda: nc.vector.tensor_tensor(out=OG[ogk][:], in0=self.ps(ob, 64, N), in1=Gt[qk][:], op=ALU.mult),
                                          reads=[self.bps[ob], bG[qk]], writes=[bOG[ogk]])
                                fw.st.op(lambda: nc.gpsimd.dma_start(out=self.OT[h * 64:(h + 1) * 64, t0:t0 + N], in_=OG[ogk][:]),
                                         reads=[bOG[ogk]], writes=[self.b_OT])
                        items.append([st0, st1, None, st2, st3, None, st4])
            pipeline(items, 7)
            fw.barrier()

    def attn_dsa(self):
        nc, fw, S, NB, NT = self.nc, self.fw, self.S, self.NB, self.NT
        N = TW2
        with ExitStack() as es:
            sb = self.sbf(es)
            b_c = Buf()
            kcb = sb("ds_kcb", [128, 128], F32)
            fw.pool.op(lambda: nc.gpsimd.memset(kcb[:, 0:64], 0.0), writes=[b_c])
            fw.pool.op(lambda: nc.gpsimd.memset(kcb[:, 64:128], 64.0), writes=[b_c])
            ones_b = sb("ds_ones", [128, 64], BF16)
            fw.pool.op(lambda: nc.gpsimd.memset(ones_b[:], 1.0), writes=[b_c])
            kiT = sb("ds_ki", [64, S], BF16); b_ki = Buf()
            fw.ld.op(lambda: nc.sync.dma_start(out=kiT[:], in_=self.KT[D:D + 64, :]), reads=[self.b_KT], writes=[b_ki])
            SC = sb("ds_sc", [128, S], F32); bSC = Buf()
            JK = sb("ds_jk", [128, 2048], BF16); bJK = Buf()
            CN = sb("ds_cn", [128, 8], F32)
            MT0 = sb("ds_mt0", [128, NB, N], BF16); bMT0 = Buf()
            MTs = [MT0, MT0]; bMTs = [bMT0, bMT0]
            QI = [sb(f"ds_qi{i}", [64, 8, 128], BF16) for i in range(2)]; bQI = [Buf(), Buf()]
            WQ = [sb(f"ds_wq{i}", [128, 8], F32) for i in range(2)]; bWQ = [Buf(), Buf()]
            PS_ = [sb(f"ds_pos{i}", [128, 2], F32) for i in range(2)]; bPS = [Buf(), Buf()]
            PM = sb("ds_pm", [128, 128], F32); bPM = Buf()
            RL = [sb(f"ds_rl{i}", [128, 512], F32) for i in range(2)]; bRL = [Buf() for _ in range(2)]
            BS = sb("ds_bs", [128, 8], F32); bBS = Buf()
            MK0 = sb("ds_mk0", [128, 512], F32); MK = [MK0, MK0]; bMK0 = Buf(); bMK = [bMK0, bMK0]
            KTt = [sb(f"ds_k{i}", [64, S], BF16) for i in range(2)]; bK = [Buf(), Buf()]
            Vt = [sb(f"ds_v{i}", [128, NB, 65], BF16) for i in range(3)]; bV = [Buf() for _ in range(3)]
            for i_ in range(3):
                fw.pool.op(lambda i_=i_: nc.gpsimd.memset(Vt[i_][:, :, 64:65], 1.0), writes=[bV[i_]])
            onesr = sb("ds_onesr", [65, 64], F32)
            fw.pool.op(lambda: nc.gpsimd.memset(onesr[:], 1.0), writes=[b_c])
            R1 = sb("ds_r1", [65, N], F32); bR1 = Buf()
            Qt = [sb(f"ds_q{i}", [64, N], BF16) for i in range(3)]; bQ = [Buf() for _ in range(3)]
            Gt = [sb(f"ds_g{i}", [64, N], BF16) for i in range(3)]; bG = [Buf() for _ in range(3)]
            P = [sb(f"ds_p{i}", [128, N], BF16) for i in range(6)]; bP = [Buf() for _ in range(6)]
            O1 = sb("ds_o1", [64, N], F32); bO1 = Buf()
            OG = [sb(f"ds_og{i}", [64, N], BF16) for i in range(2)]; bOG = [Buf(), Buf()]
            ctr = {"rl": 0, "qi": 0, "ps": 0, "tile": 0}

            def nk_of(tt, i2):
                return 4 * tt + (2 if i2 == 0 else 4)


            def part1(tt):
                st_ = tt
                MT, bMT = MTs[0], bMTs[0]
                for i4 in range(4):
                    tt = 2 * st_ + i4 // 2
                    i2 = i4 % 2
                    lb = 4 * st_ + i4
                    nk = nk_of(tt, i2)
                    ncols_tot = nk * 128
                    qk = ctr["qi"] % 2
                    ctr["qi"] += 1
                    fw.ld.op(lambda: nc.sync.dma_start(out=QI[qk][:], in_=self.QT[D:D + 512, lb * 128:(lb + 1) * 128].rearrange("(h d) t -> d h t", d=64)),
                             reads=[self.b_QT], writes=[bQI[qk]])
                    fw.ld.op(lambda: nc.sync.dma_start(out=WQ[qk][:], in_=self.WI[lb * 128:(lb + 1) * 128, :]), reads=[self.b_WI], writes=[bWQ[qk]])
                    fw.ld.op(lambda: nc.sync.dma_start(out=PS_[qk][:, 0:1], in_=self.poscol[lb * 128:(lb + 1) * 128, :]), writes=[bPS[qk]])
                    for c0 in range(0, ncols_tot, 512):
                        ncol = min(512, ncols_tot - c0)
                        for hh in range(8):
                            pb = 3 + ctr["ps"] % 2
                            ctr["ps"] += 1
                            rk = ctr["rl"] % 2
                            ctr["rl"] += 1
                            fw.pe.op(lambda: nc.tensor.matmul(self.ps(pb, 128, ncol), lhsT=QI[qk][:, hh, :], rhs=kiT[:, c0:c0 + ncol], start=True, stop=True),
                                     reads=[bQI[qk], b_ki], writes=[self.bps[pb]])
                            fw.act.op(lambda: nc.scalar.activation(out=RL[rk][:, 0:ncol], in_=self.ps(pb, 128, ncol), func=AF.Relu),
                                      reads=[self.bps[pb]], writes=[bRL[rk]])
                            if hh == 0:
                                fw.dve.op(lambda: nc.vector.tensor_scalar(out=SC[:, c0:c0 + ncol], in0=RL[rk][:, 0:ncol], scalar1=WQ[qk][:, 0:1], scalar2=None,
                                                                          op0=ALU.mult), reads=[bRL[rk], bWQ[qk]], writes=[bSC])
                            else:
                                fw.dve.op(lambda: nc.vector.scalar_tensor_tensor(out=SC[:, c0:c0 + ncol], in0=RL[rk][:, 0:ncol], scalar=WQ[qk][:, hh:hh + 1],
                                                                                 in1=SC[:, c0:c0 + ncol], op0=ALU.mult, op1=ALU.add),
                                          reads=[bRL[rk], bWQ[qk], bSC], writes=[bSC])
                            yield
                    for kb in range(4 * tt, nk):
                        dsl = slice(kb * 128, (kb + 1) * 128)
                        fw.dve.op(lambda kb=kb: nc.vector.tensor_scalar(out=PS_[qk][:, 1:2], in0=PS_[qk][:, 0:1], scalar1=float(-128 * kb), scalar2=None, op0=ALU.add),
                                  reads=[bPS[qk]], writes=[bPS[qk]])
                        fw.dve.op(lambda: nc.vector.tensor_scalar(out=PM[:], in0=kcb[:], scalar1=PS_[qk][:, 1:2], scalar2=-1e30, op0=ALU.is_gt, op1=ALU.mult),
                                  reads=[b_c, bPS[qk]], writes=[bPM])
                        fw.dve.op(lambda dsl=dsl: nc.vector.tensor_tensor(out=SC[:, dsl], in0=SC[:, dsl], in1=PM[:], op=ALU.add), reads=[bSC, bPM], writes=[bSC])
                    fw.dve.op(lambda: nc.vector.memset(BS[:, 0:1], -BIS_R), writes=[bBS])
                    fw.dve.op(lambda: nc.vector.memset(BS[:, 1:2], 0.0), writes=[bBS])
                    yield
                    hstep = BIS_R
                    for it in range(BIS_ITERS):
                        nchk = (ncols_tot + 2047) // 2048
                        for ci in range(nchk):
                            cc0 = ci * 2048
                            nn = min(2048, ncols_tot - cc0)
                            dcol = BS[:, 2:3] if nchk == 1 else CN[:, ci:ci + 1]
                            fw.dve.op(lambda cc0=cc0, nn=nn, dcol=dcol: nc.vector.tensor_scalar(out=JK[:, 0:nn], in0=SC[:, cc0:cc0 + nn], scalar1=BS[:, 1:2], scalar2=0.0,
                                                                                                op0=ALU.is_ge, op1=ALU.add, accum_out=dcol),
                                      reads=[bSC, bBS], writes=[bJK, bBS])
                        if nchk > 1:
                            fw.dve.op(lambda: nc.vector.reduce_sum(out=BS[:, 2:3], in_=CN[:, 0:nchk], axis=mybir.AxisListType.X), reads=[bBS], writes=[bBS])
                        fw.dve.op(lambda: nc.vector.tensor_scalar(out=BS[:, 3:4], in0=BS[:, 2:3], scalar1=float(TOPK) - 0.5, scalar2=float(hstep),
                                                                  op0=ALU.is_ge, op1=ALU.mult), reads=[bBS], writes=[bBS])
                        fw.dve.op(lambda: nc.vector.tensor_tensor(out=BS[:, 0:1], in0=BS[:, 0:1], in1=BS[:, 3:4], op=ALU.add), reads=[bBS], writes=[bBS])
                        hstep = hstep / 2
                        fw.dve.op(lambda: nc.vector.tensor_scalar(out=BS[:, 1:2], in0=BS[:, 0:1], scalar1=float(hstep), scalar2=None, op0=ALU.add),
                                  reads=[bBS], writes=[bBS])
                        yield
                    for c0 in range(0, ncols_tot, 512):
                        ncol = min(512, ncols_tot - c0)
                        mk = (c0 // 512) % 2
                        pb = 5
                        fw.dve.op(lambda: nc.vector.tensor_scalar(out=MK[mk][:, 0:ncol], in0=SC[:, c0:c0 + ncol], scalar1=BS[:, 0:1], scalar2=None, op0=ALU.is_ge),
                                  reads=[bSC, bBS], writes=[bMK[mk]])
                        nb4 = ncol // 128
                        for b4 in range(nb4):
                            fw.pe.op(lambda b4=b4: nc.tensor.transpose(self.ps(pb)[:, b4 * 128:(b4 + 1) * 128], MK[mk][:, b4 * 128:(b4 + 1) * 128], self.ident[:]),
                                     reads=[bMK[mk], self.b_ident], writes=[self.bps[pb]], inc=(b4 == nb4 - 1))
                        kb0 = c0 // 128
                        fw.act.op(lambda: nc.scalar.copy(MT[:, kb0:kb0 + nb4, i4 * 128:(i4 + 1) * 128],
                                                         self.ps(pb, 128, ncol).rearrange("p (b t) -> p b t", t=128)),
                                  reads=[self.bps[pb]], writes=[bMT])
                        yield
                    if nk < 8 * st_ + 8:
                        fw.pool.op(lambda: nc.gpsimd.memset(MT[:, nk:8 * st_ + 8, i4 * 128:(i4 + 1) * 128], 0.0), writes=[bMT])

            def part2_items(tt):
                MT, bMT = MTs[0], bMTs[0]
                t0 = tt * N
                nkb = 8 * tt + 8
                items = []
                loaders = []
                ic = 0
                for h in range(NH):
                    hk = h % 2
                    vk = h % 3
                    qk = ctr["tile"] % 3
                    ogk = ctr["tile"] % 2
                    ctr["tile"] += 1

                    def load_q(h=h, qk=qk, hk=hk, vk=vk):
                        fw.ld.op(lambda: nc.sync.dma_start(out=KTt[hk][:, 0:nkb * 128], in_=self.KT[h * 64:(h + 1) * 64, 0:nkb * 128]), reads=[self.b_KT], writes=[bK[hk]])
                        fw.ld.op(lambda: nc.sync.dma_start(out=Vt[vk][:, 0:nkb, 0:64], in_=self.Vs[h, :, 0:nkb, :]), reads=[self.b_Vs], writes=[bV[vk]])
                        fw.ld.op(lambda: nc.sync.dma_start(out=Qt[qk][:], in_=self.QT[h * 64:(h + 1) * 64, t0:t0 + N]), reads=[self.b_QT], writes=[bQ[qk]])
                        fw.ld.op(lambda: nc.sync.dma_start(out=Gt[qk][:], in_=self.GT[h * 64:(h + 1) * 64, t0:t0 + N]), reads=[self.b_GT], writes=[bG[qk]])
                    loaders.append(load_q)
                    for kb in range(nkb):
                        sbank = ic % 3
                        pk = ic % 6
                        ic += 1
                        first = kb == 0
                        lastk = kb == nkb - 1

                        def st0(h=h, hk=hk, qk=qk, kb=kb, sbank=sbank, first=first):
                            if first:
                                if h == 0:
                                    loaders[0]()
                                if h + 1 < NH:
                                    loaders[h + 1]()
                            fw.pe.op(lambda: nc.tensor.matmul(self.ps(sbank, 128, N), lhsT=KTt[hk][:, kb * 128:(kb + 1) * 128], rhs=Qt[qk][:], start=True, stop=True),
                                     reads=[bK[hk], bQ[qk]], writes=[self.bps[sbank]])

                        def st1(sbank=sbank, pk=pk, kb=kb, onpool=False):
                            fw.act.op(lambda: nc.scalar.activation(out=P[pk][:], in_=self.ps(sbank, 128, N), func=AF.Exp, scale=0.125),
                                      reads=[self.bps[sbank]], writes=[bP[pk]])
                            if onpool:
                                fw.pool.op(lambda: nc.gpsimd.tensor_tensor(out=P[pk][:], in0=P[pk][:], in1=MT[:, kb, :], op=ALU.mult),
                                           reads=[bP[pk], bMT], writes=[bP[pk]])
                            else:
                                fw.dve.op(lambda: nc.vector.tensor_tensor(out=P[pk][:], in0=P[pk][:], in1=MT[:, kb, :], op=ALU.mult),
                                          reads=[bP[pk], bMT], writes=[bP[pk]])

                        def st2(h=h, vk=vk, kb=kb, pk=pk, first=first, lastk=lastk, qk=qk, ogk=ogk):
                            fw.pe.op(lambda: nc.tensor.matmul(self.ps(6, 65, N), lhsT=Vt[vk][:, kb, :], rhs=P[pk][:], start=first, stop=lastk),
                                     reads=[bV[vk], bP[pk]], writes=[self.bps[6]])
                            if lastk:
                                fw.act.op(lambda: nc.scalar.copy(O1[:], self.ps(6, 64, N)), reads=[self.bps[6]], writes=[bO1])
                                fw.dve.op(lambda: nc.vector.reciprocal(R1[64:65, :], self.psall[64:65, 6 * 512:6 * 512 + N]), reads=[self.bps[6]], writes=[bR1])
                                fw.pe.op(lambda: nc.tensor.matmul(self.ps(7, 64, N), lhsT=onesr[64:65, :], rhs=R1[64:65, :], start=True, stop=True),
                                         reads=[b_c, bR1], writes=[self.bps[7]])
                                fw.dve.op(lambda: nc.vector.tensor_tensor(out=O1[:], in0=O1[:], in1=self.ps(7, 64, N), op=ALU.mult), reads=[bO1, self.bps[7]], writes=[bO1])
                                fw.pool.op(lambda: nc.gpsimd.tensor_tensor(out=OG[ogk][:], in0=O1[:], in1=Gt[qk][:], op=ALU.mult), reads=[bO1, bG[qk]], writes=[bOG[ogk]])
                                fw.st.op(lambda: nc.gpsimd.dma_start(out=self.OT[h * 64:(h + 1) * 64, t0:t0 + N], in_=OG[ogk][:]),
                                         reads=[bOG[ogk]], writes=[self.b_OT])
                        items.append([st0, st1, None, None, st2])
                return items

            for st_ in range(NT // 2):
                for _ in part1(st_):
                    pass
                pipeline(part2_items(st_), 5)
            fw.barrier()


_LAYERS = [(i % 3, i // 3, i) for i in range(DEPTH)]


def _consts():
    p = np.arange(128)
    i = (p % 32).astype(np.float32)
    inv = (1.0 / (np.float32(10000.0) ** (2 * i / np.float32(64)))).astype(np.float32)
    sign = np.where((p % 64) < 32, -1.0, 1.0).astype(np.float32)
    return np.stack([inv, sign], axis=1).astype(np.float32), np.eye(128, dtype=np.float32)


def own_rows(S, rho):
    idx = []
    for tt in range(S // 512):
        for c in own_blocks(rho):
            g = 4 * tt + c
            idx.append(np.arange(g * 128, (g + 1) * 128))
    return np.concatenate(idx)


def make_in_map(xb, rho, layers, w):
    S = xb.shape[0]
    ropec, ident = _consts()
    rows = own_rows(S, rho)
    im = {"x": np.ascontiguousarray(xb), "x_own": np.ascontiguousarray(xb[rows]), "ropec": ropec, "ident": ident}
    im["posrow"] = rows.astype(np.float32).reshape(1, -1)
    im["poscol"] = rows.astype(np.float32).reshape(-1, 1)
    im["qrel"] = (rows[:TW2] % 1024).astype(np.float32).reshape(1, TW2)
    ws_in = {0: w["w_in_a"], 1: w["w_in_b"], 2: w["w_in_c"]}
    ws_out = {0: w["w_out_a"], 1: w["w_out_b"], 2: w["w_out_c"]}
    for li, (m, j, L) in enumerate(layers):
        im[f"w_in{li}"] = np.ascontiguousarray(ws_in[m][j])
        im[f"w_out{li}"] = np.ascontiguousarray(ws_out[m][j])
    im["ln_g"] = np.ascontiguousarray(np.stack([w["ln_g"][L] for (_, _, L) in layers]))
    im["ln_b"] = np.ascontiguousarray(np.stack([w["ln_b"][L] for (_, _, L) in layers]))
    im["lam"] = np.ascontiguousarray(np.stack([w["lambda_q1"][0], w["lambda_k1"][0], w["lambda_q2"][0], w["lambda_k2"][0]]).reshape(1, 256))
    im["subg"] = np.ascontiguousarray(w["subln_g"][0].reshape(128, 1))
    return im


def run_layers(x, w, layers, ncores):
    B, S, _ = x.shape
    assert ncores == 2 * B
    prog = Prog(S, layers, ncores=ncores)
    in_maps = [make_in_map(x[c // 2], c % 2, layers, w) for c in range(ncores)]
    res = run_bass_kernel_spmd(prog.nc, in_maps, core_ids=list(range(ncores)))
    out = np.empty((B, S, D), np.float32)
    for c in range(ncores):
        out[c // 2][own_rows(S, c % 2)] = res.results[c]["out"]
    return out


def kernel(**inputs):
    w = {k: np.asarray(v, dtype=np.float32) for k, v in inputs.items()}
    return run_layers(w["x"], w, _LAYERS, 8)
```
